# Optimizing a Trainium2 kernel written in Bass

```python
import jax, jax.numpy as jnp
from jax import lax
import numpy as np

D_MODEL = 1024
BATCH = 2
SEQ = 8192
DEPTH = 2

HGRN_HEADS = 4
HGRN_KEY = 128
HGRN_VAL = 128
HGRN_KW = HGRN_HEADS * HGRN_KEY
HGRN_VW = HGRN_HEADS * HGRN_VAL
HGRN_CHUNK = 64
POOL_WINDOWS = (2, 4, 8, 16)
POOL_GROUP = 128
POOL_WIDTH = POOL_GROUP * len(POOL_WINDOWS)
FOX_HEADS = 16
FOX_HEAD_DIM = 64
FOX_WIDTH = FOX_HEADS * FOX_HEAD_DIM
FOX_BLOCK = 128
EVEN_IN = 2 * HGRN_KW + 2 * HGRN_VW + 2 * POOL_WIDTH
EVEN_MIX = HGRN_VW + POOL_WIDTH
ODD_IN = 4 * FOX_WIDTH + FOX_HEADS
N_EVEN = (DEPTH + 1) // 2
N_ODD = DEPTH // 2
EPS = 1e-6

kernel_name = 'hybrid_hgrn2_pool_fox_adaln'


def rms_norm(x, g):
    xf = x.astype(jnp.float32)
    y = xf * lax.rsqrt(jnp.mean(xf * xf, axis=-1, keepdims=True) + EPS)
    return (y * g.astype(jnp.float32)).astype(x.dtype)


def hgrn2_chunked(q, k, v, logf):
    b_, s_, h_, kd = q.shape
    vd = v.shape[-1]
    nc = s_ // HGRN_CHUNK

    def to_chunks(t):
        return t.reshape(b_, nc, HGRN_CHUNK, h_, t.shape[-1]).transpose(1, 0, 3, 2, 4)

    causal = jnp.tril(jnp.ones((HGRN_CHUNK, HGRN_CHUNK), dtype=bool))

    def step(state, inp):
        qc, kc, vc, gc = inp
        cum = jnp.cumsum(gc, axis=2)
        diff = cum[:, :, :, None, :] - cum[:, :, None, :, :]
        decay = jnp.where(causal[:, :, None], jnp.exp(jnp.minimum(diff, 0.0)), 0.0)
        scores = jnp.einsum('bhtk,bhsk,bhtsk->bhts', qc, kc, decay)
        out = (jnp.einsum('bhts,bhsv->bhtv', scores, vc)
               + jnp.einsum('bhtk,bhkv->bhtv', qc * jnp.exp(cum), state))
        last = cum[:, :, -1:, :]
        state = (jnp.exp(last[:, :, 0, :])[..., None] * state
                 + jnp.einsum('bhsk,bhsv->bhkv', kc * jnp.exp(last - cum), vc))
        return state, out

    s0 = jnp.zeros((b_, h_, kd, vd), jnp.float32)
    _, out = lax.scan(step, s0, (to_chunks(q), to_chunks(k), to_chunks(v), to_chunks(logf)))
    return out.transpose(1, 0, 3, 2, 4).reshape(b_, s_, h_, vd)


def multiscale_pool(u):
    b_, s_, _ = u.shape
    groups = u.reshape(b_, s_, len(POOL_WINDOWS), POOL_GROUP)
    csum = jnp.cumsum(groups, axis=1)
    csum = jnp.concatenate([jnp.zeros_like(csum[:, :1]), csum], axis=1)
    pos = jnp.arange(s_)
    outs = []
    for gi, w in enumerate(POOL_WINDOWS):
        cg = csum[:, :, gi]
        lagged = jnp.concatenate([jnp.zeros((b_, w - 1, POOL_GROUP), cg.dtype), cg[:, :s_ - w + 1]], axis=1)
        count = jnp.minimum(pos + 1, w).astype(cg.dtype)[None, :, None]
        outs.append((cg[:, 1:] - lagged) / count - groups[:, :, gi])
    return jnp.stack(outs, axis=2)


def even_mixer(h, w_in, lower, onorm_g, pool_w, pool_scale, w_out):
    b_, s_, _ = h.shape
    proj = (h @ w_in).astype(jnp.float32)
    cuts = [HGRN_KW, 2 * HGRN_KW, 2 * HGRN_KW + HGRN_VW, 2 * HGRN_KW + 2 * HGRN_VW,
            2 * HGRN_KW + 2 * HGRN_VW + POOL_WIDTH]
    q, f, i, g_a, u, g_b = jnp.split(proj, cuts, axis=-1)
    lb = lower.astype(jnp.float32)
    forget = lb + (1.0 - lb) * jax.nn.sigmoid(f)
    logf = jnp.log(forget)
    key = 1.0 - forget
    kh = lambda t: t.reshape(b_, s_, HGRN_HEADS, HGRN_KEY)
    o_a = hgrn2_chunked(kh(q), kh(key), i.reshape(b_, s_, HGRN_HEADS, HGRN_VAL), kh(logf))
    o_a = rms_norm(o_a, onorm_g.reshape(HGRN_HEADS, HGRN_VAL)).reshape(b_, s_, HGRN_VW) * jax.nn.silu(g_a)
    pooled = multiscale_pool(u)
    o_b = jnp.einsum('bsgc,gcd->bsgd', pooled, pool_w.astype(jnp.float32)).reshape(b_, s_, POOL_WIDTH)
    o_b = o_b * pool_scale.astype(jnp.float32) * jax.nn.silu(g_b)
    mixed = jnp.concatenate([o_a, o_b], axis=-1).astype(h.dtype)
    return mixed @ w_out


def odd_mixer(h, w_in, b_f, qnorm_g, knorm_g, w_out):
    b_, s_, _ = h.shape
    proj = (h @ w_in).astype(jnp.float32)
    q, k, v, g, fl = jnp.split(proj, [FOX_WIDTH, 2 * FOX_WIDTH, 3 * FOX_WIDTH, 4 * FOX_WIDTH], axis=-1)
    hd = lambda t: t.reshape(b_, s_, FOX_HEADS, FOX_HEAD_DIM)
    q = rms_norm(hd(q), qnorm_g).transpose(0, 2, 1, 3)
    k = rms_norm(hd(k), knorm_g).transpose(0, 2, 1, 3)
    v = hd(v).transpose(0, 2, 1, 3)
    logf = jax.nn.log_sigmoid(fl + b_f.astype(jnp.float32))
    cumf = jnp.cumsum(logf, axis=1).transpose(0, 2, 1)
    scale = FOX_HEAD_DIM ** -0.5
    outs = []
    for blk in range(s_ // FOX_BLOCK):
        q0 = blk * FOX_BLOCK
        q1 = q0 + FOX_BLOCK
        logits = (jnp.einsum('bhqd,bhkd->bhqk', q[:, :, q0:q1], k[:, :, :q1]) * scale
                  + cumf[:, :, q0:q1, None] - cumf[:, :, None, :q1])
        mask = jnp.arange(q0, q1)[:, None] >= jnp.arange(q1)[None, :]
        probs = jax.nn.softmax(jnp.where(mask, logits, -jnp.inf), axis=-1)
        outs.append(jnp.einsum('bhqk,bhkd->bhqd', probs, v[:, :, :q1]))
    o = jnp.concatenate(outs, axis=2).transpose(0, 2, 1, 3).reshape(b_, s_, FOX_WIDTH)
    o = o * jax.nn.silu(g)
    return o.astype(h.dtype) @ w_out


def setup_inputs(seed: int = 0) -> dict:
    key = jax.random.key(seed)
    ks = jax.random.split(key, 16)
    nrm = jax.random.normal
    f32 = jnp.float32
    return {
        'x': nrm(ks[0], (BATCH, SEQ, D_MODEL), f32),
        'c': nrm(ks[1], (BATCH, D_MODEL), f32),
        'norm_g': 1.0 + 0.05 * nrm(ks[2], (DEPTH, D_MODEL), f32),
        'ada_w': 0.5 * D_MODEL ** -0.5 * nrm(ks[3], (DEPTH, D_MODEL, 3 * D_MODEL), f32),
        'ada_b': 0.02 * nrm(ks[4], (DEPTH, 3 * D_MODEL), f32),
        'hgrn_lb': 0.1 * nrm(ks[5], (DEPTH + 1, HGRN_KW), f32),
        'even_w_in': D_MODEL ** -0.5 * nrm(ks[6], (N_EVEN, D_MODEL, EVEN_IN), f32),
        'hgrn_onorm_g': 1.0 + 0.05 * nrm(ks[7], (N_EVEN, HGRN_VW), f32),
        'pool_w': POOL_GROUP ** -0.5 * nrm(ks[8], (N_EVEN, len(POOL_WINDOWS), POOL_GROUP, POOL_GROUP), f32),
        'pool_scale': 1.0 + 0.05 * nrm(ks[9], (N_EVEN, POOL_WIDTH), f32),
        'even_w_out': EVEN_MIX ** -0.5 * nrm(ks[10], (N_EVEN, EVEN_MIX, D_MODEL), f32),
        'odd_w_in': D_MODEL ** -0.5 * nrm(ks[11], (N_ODD, D_MODEL, ODD_IN), f32),
        'fox_b_f': jax.random.uniform(ks[12], (N_ODD, FOX_HEADS), f32, 1.0, 4.0),
        'fox_qnorm_g': 1.0 + 0.05 * nrm(ks[13], (N_ODD, FOX_HEAD_DIM), f32),
        'fox_knorm_g': 1.0 + 0.05 * nrm(ks[14], (N_ODD, FOX_HEAD_DIM), f32),
        'odd_w_out': FOX_WIDTH ** -0.5 * nrm(ks[15], (N_ODD, FOX_WIDTH, D_MODEL), f32),
    }


def reference(x, c, norm_g, ada_w, ada_b, hgrn_lb, even_w_in, hgrn_onorm_g, pool_w, pool_scale,
              even_w_out, odd_w_in, fox_b_f, fox_qnorm_g, fox_knorm_g, odd_w_out):
    lower = jnp.cumsum(jax.nn.softmax(hgrn_lb.astype(jnp.float32), axis=0), axis=0)
    cond = jax.nn.silu(c)
    for l in range(DEPTH):
        mod = cond @ ada_w[l] + ada_b[l]
        shift, scale, gate = jnp.split(mod, 3, axis=-1)
        h = rms_norm(x, norm_g[l]) * (1.0 + scale[:, None, :]) + shift[:, None, :]
        j = l // 2
        if l % 2 == 0:
            y = even_mixer(h, even_w_in[j], lower[l], hgrn_onorm_g[j], pool_w[j], pool_scale[j], even_w_out[j])
        else:
            y = odd_mixer(h, odd_w_in[j], fox_b_f[j], fox_qnorm_g[j], fox_knorm_g[j], odd_w_out[j])
        x = x + gate[:, None, :] * y
    return x
```

```python
import numpy as np
import ml_dtypes
from contextlib import ExitStack
import concourse.bass as bass
import concourse.mybir as mybir
from concourse.bass_utils import run_bass_kernel_spmd

F32 = mybir.dt.float32
BF16 = mybir.dt.bfloat16
AF = mybir.ActivationFunctionType
ALU = mybir.AluOpType
NPBF = ml_dtypes.bfloat16

T = 8192
D = 1024
TOK = 2048
EPS = 1e-6
NCORES = 8
CC_QOS = "P3"


class Sched:
    def __init__(self, nc, es, n_lanes=16):
        self.nc = nc
        self.engs = {"pe": nc.tensor, "act": nc.scalar, "dve": nc.vector, "pool": nc.gpsimd, "sp": nc.sync}
        self.sem = {}
        self.cnt = {}
        for n in ["pe", "act", "dve", "pool", "cc"]:
            self.sem[n] = es.enter_context(nc.semaphore("s_" + n))
            self.cnt[n] = 0
        self.lanes = []
        for i in range(n_lanes):
            nm = "lane%d" % i
            self.sem[nm] = es.enter_context(nc.semaphore("s_" + nm))
            self.cnt[nm] = 0
            self.lanes.append(nm)
        self.planes = []
        for i in range(4):
            nm = "plane%d" % i
            self.sem[nm] = es.enter_context(nc.semaphore("s_" + nm))
            self.cnt[nm] = 0
            self.planes.append(nm)
        self.lane_rr = 0
        self.plane_rr = 0
        self.waited = {n: {} for n in self.engs}
        self.lastw = {}
        self.readers = {}

    def _need(self, reads, writes):
        need = {}

        def add(ev):
            if ev is None:
                return
            s, v = ev
            if need.get(s, 0) < v:
                need[s] = v

        for k in reads:
            add(self.lastw.get(k))
        for k in writes:
            add(self.lastw.get(k))
            for ev in self.readers.get(k, []):
                add(ev)
        return need

    def _emit_waits(self, e, need):
        eng = self.engs[e]
        for s, v in need.items():
            if self.waited[e].get(s, 0) >= v:
                continue
            eng.wait_ge(self.sem[s], v)
            self.waited[e][s] = v

    def _record(self, ev, reads, writes):
        for k in reads:
            lst = self.readers.setdefault(k, [])
            lst.append(ev)
            if len(lst) > 64:
                mx = {}
                for s, v in lst:
                    mx[s] = max(mx.get(s, 0), v)
                self.readers[k] = list(mx.items())
        for k in writes:
            self.lastw[k] = ev
            self.readers[k] = []

    def op(self, e, fn, reads=(), writes=()):
        need = self._need(reads, writes)
        if e == "pe":
            need.pop("pe", None)
        self._emit_waits(e, need)
        ins = fn(self.engs[e])
        self.cnt[e] += 1
        ins.then_inc(self.sem[e], 1)
        self._record((e, self.cnt[e]), reads, writes)
        return ins

    def dma(self, q, out, in_, reads=(), writes=(), **kw):
        if q == "pool":
            lane = self.planes[self.plane_rr % len(self.planes)]
            self.plane_rr += 1
        else:
            lane = self.lanes[self.lane_rr % len(self.lanes)]
            self.lane_rr += 1
        need = self._need(reads, writes)
        if self.cnt[lane] > 0:
            need[lane] = max(need.get(lane, 0), self.cnt[lane])
        self._emit_waits(q, need)
        ins = self.engs[q].dma_start(out=out, in_=in_, **kw)
        self.cnt[lane] += 16
        ins.then_inc(self.sem[lane], 16)
        self._record((lane, self.cnt[lane]), reads, writes)
        return ins

    def barrier(self):
        for e in self.engs:
            need = {n: c for n, c in self.cnt.items() if c > 0 and n != "cc"}
            self._emit_waits(e, need)

    def collective(self, kind, ins, outs, groups, reads=(), writes=()):
        need = self._need(reads, writes)
        self._emit_waits("pool", need)
        ins_ = self.nc.gpsimd.collective_compute(kind, ALU.bypass, replica_groups=groups, ins=ins, outs=outs, dma_qos=CC_QOS)
        self.cnt["cc"] += 1
        ins_.then_inc(self.sem["cc"], 1)
        self._record(("cc", self.cnt["cc"]), reads, writes)
        return ins_

    def wait_all(self, e, keys):
        need = self._need(keys, keys)
        self._emit_waits(e, need)


class _Phase:
    def __init__(self, kb, tag):
        self.kb, self.tag = kb, tag

    def __enter__(self):
        self.kb.tag = self.tag
        self.kb.pes = ExitStack()
        return self

    def __exit__(self, *a):
        self.kb.S.barrier()
        self.kb.pes.close()
        self.kb.pes = self.kb.es
        self.kb.tag = ""
        return False


class BankView:
    def __init__(self, t, off):
        self.t, self.off = t, off

    def __getitem__(self, key):
        rows, cols = key
        a = self.off + (cols.start or 0)
        b = self.off + (512 if cols.stop is None else cols.stop)
        return self.t[rows, a:b]


class KB:
    def __init__(self):
        self.nc = bass.Bass("TRN2", target_bir_lowering=False)
        self.es = ExitStack()
        self.pes = self.es
        self.S = Sched(self.nc, self.es)
        self.outs = []
        self.tag = ""
        self.shared = {}
        self._banks = None

    def phase(self, tag):
        return _Phase(self, tag)

    def din(self, name, shape, dt, shared=False):
        if shared:
            if name not in self.shared:
                self.shared[name] = self.nc.dram_tensor(name, list(shape), dt, kind="ExternalInput").ap()
            return self.shared[name]
        return self.nc.dram_tensor(self.tag + name, list(shape), dt, kind="ExternalInput").ap()

    def dout(self, name, shape, dt):
        self.outs.append(self.tag + name)
        return self.nc.dram_tensor(self.tag + name, list(shape), dt, kind="ExternalOutput").ap()

    def sb(self, name, shape, dt):
        return self.pes.enter_context(self.nc.sbuf_tensor("sb_" + self.tag + name, list(shape), dt))

    def banks(self):
        if self._banks is None:
            self.dbanks = [self.es.enter_context(self.nc.psum_tensor("dbank%d" % i, [128, 1024], F32)) for i in range(4)]
            self._banks = [BankView(self.dbanks[i // 2], (i % 2) * 512) for i in range(8)]
        return self._banks

    def finish(self, out_keys=None):
        self.S.wait_all("sp", out_keys if out_keys is not None else self.outs)
        self.es.close()
        return self.nc


def emit_cond_rep(kb, cT_d):
    S = kb.S
    cT = kb.sb("cT", [128, 8], F32)
    cE = kb.sb("cE", [128, 8], F32)
    cond = kb.sb("cond", [128, 8], F32)
    ones = kb.sb("ones128", [128, 128], F32)
    crep = kb.sb("cond_rep", [128, 8, 128], F32)
    S.dma("sp", cT[:], cT_d[:, :], writes=["cT"])
    S.op("dve", lambda e: e.memset(ones[:], 1.0), writes=["ones128"])
    S.op("act", lambda e: e.activation(out=cE[:], in_=cT[:], func=AF.Exp, scale=-1.0), reads=["cT"], writes=["cE"])
    S.op("dve", lambda e: e.tensor_scalar_add(out=cE[:], in0=cE[:], scalar1=1.0), reads=["cE"], writes=["cE"])
    S.op("dve", lambda e: e.reciprocal(out=cE[:], in_=cE[:]), reads=["cE"], writes=["cE"])
    S.op("dve", lambda e: e.tensor_mul(out=cond[:], in0=cT[:], in1=cE[:]), reads=["cE", "cT"], writes=["cond"])
    for k in range(8):
        S.op("dve", lambda e, k=k: e.tensor_scalar(out=crep[:, k, :], in0=ones[:], scalar1=cond[:, k:k + 1], scalar2=None, op0=ALU.mult),
             reads=["cond", "ones128"], writes=["crep"])
    return crep


def emit_mod_bcast(kb, crep, w_d, b_d, out_t, out_key, ncols, banks, stage, tag, b0=0, c0=0):
    S = kb.S
    nb = ncols // 512
    S.dma("sp", out_t[:, 0:ncols], b_d[:, c0:c0 + ncols], writes=[out_key])
    idx = 0
    for k in range(8):
        st = stage[k % len(stage)]
        skey = "%s_st%d" % (tag, k % len(stage))
        S.dma("sp", st[:, 0:ncols], w_d[k * 128:(k + 1) * 128, c0:c0 + ncols], writes=[skey])
        for n in range(nb):
            S.op("pe", lambda e, n=n, k=k, st=st: e.matmul(banks[b0 + n][:, :], lhsT=crep[:, k, :], rhs=st[:, n * 512:(n + 1) * 512], start=(k == 0), stop=(k == 7)),
                 reads=["crep", skey], writes=["bank%d" % (b0 + n)])
    for n in range(nb):
        S.op("dve", lambda e, n=n: e.tensor_tensor(out=out_t[:, n * 512:(n + 1) * 512], in0=out_t[:, n * 512:(n + 1) * 512], in1=banks[b0 + n][:, :], op=ALU.add),
             reads=["bank%d" % (b0 + n), out_key], writes=[out_key])


MC = 1536


def emit_mods(kb, modown_d):
    S = kb.S
    cT_d = kb.din("cT", [128, 8], F32, shared=True)
    w_d = kb.din("w", [D, MC], F32)
    b_d = kb.din("b", [128, MC], F32)
    banks = kb.banks()
    crep = emit_cond_rep(kb, cT_d)
    stage = [kb.sb("stg%d" % i, [128, MC], F32) for i in range(4)]
    out_t = kb.sb("mo", [128, MC], F32)
    emit_mod_bcast(kb, crep, w_d, b_d, out_t, "mo", MC, banks, stage, "g", b0=0, c0=0)
    S.dma("sp", modown_d[:, :], out_t[:], reads=["mo"], writes=["modown"])


def load_mods(kb, dst, dst_key, modall_d, c0, n):
    S = kb.S
    c = c0
    while c < c0 + n:
        r, lo = c // MC, c % MC
        m = min(MC - lo, c0 + n - c)
        S.dma("sp", dst[:, c - c0:c - c0 + m], modall_d[r * 128:(r + 1) * 128, lo:lo + m], reads=["modall"], writes=[dst_key])
        c += m


def build_tok(mode):
    kb = KB()
    emit_tok(kb, mode, {})
    return kb.finish(["hT", "xo"])


def emit_tok(kb, mode, io):
    S = kb.S
    do_proj = mode in ("PB", "PD")
    do_norm = mode in ("P0", "PB")
    x_d = kb.din("x", [TOK, D], F32, shared=True) if "x_src" not in io else None
    cT_d = kb.din("cT", [128, 8], F32, shared=True)
    ident_d = kb.din("ident", [128, 128], BF16, shared=True)
    if do_proj:
        mT_d = kb.din("mT", [D, TOK], BF16) if "mT_src" not in io else None
        wout_d = kb.din("wout", [D, D], F32)
        gw_d = kb.din("gw", [D, D], F32) if "mods" not in io else None
        gb_d = kb.din("gb", [128, D], F32) if "mods" not in io else None
    if do_norm:
        nw_d = kb.din("nw", [D, 2 * D], F32) if "mods" not in io else None
        nb_d = kb.din("nb", [128, 2 * D], F32) if "mods" not in io else None
        ng_d = kb.din("ng", [128, D], F32)
        hT_d = kb.dout("hT", [D, TOK], BF16) if "hT_dst" not in io else None
    if mode != "P0":
        xo_d = kb.dout("xo", [TOK, D], F32) if "xo_dst" not in io else None
    banks = kb.banks()
    mods = io.get("mods")
    if mods is None:
        crep = emit_cond_rep(kb, cT_d)
    stage = [kb.sb("stg%d" % i, [128, 2048 if mods is None else 1024], F32) for i in range(3)]
    ident = kb.sb("ident", [128, 128], BF16)
    S.dma("sp", ident[:], ident_d[:, :], writes=["ident"])
    if do_proj:
        gate = kb.sb("gate", [128, D], F32)
        if mods is None:
            emit_mod_bcast(kb, crep, gw_d, gb_d, gate, "gate", D, banks, stage, "g")
        else:
            load_mods(kb, gate, "gate", mods[0], io["layer"] * 3 * D + 2 * D, D)
        Wp = kb.sb("Wp", [128, 8, D], BF16)
        for k in range(8):
            st = stage[k % 3]
            skey = "g_st%d" % (k % 3)
            S.dma("sp", st[:, 0:D], wout_d[k * 128:(k + 1) * 128, :], writes=[skey])
            S.op("dve", lambda e, k=k, st=st: e.tensor_tensor(out=Wp[:, k, :], in0=st[:, 0:D], in1=gate[:], op=ALU.mult),
                 reads=[skey, "gate"], writes=["Wp"])
    if do_norm:
        modn = kb.sb("modn", [128, 2 * D], F32)
        if mods is None:
            emit_mod_bcast(kb, crep, nw_d, nb_d, modn, "modn", 2 * D, banks, stage, "g", b0=4)
        else:
            load_mods(kb, modn, "modn", mods[0], io["nlayer"] * 3 * D, 2 * D)
        Gb = kb.sb("Gb", [128, D], F32)
        S.dma("sp", Gb[:], ng_d[:, :], writes=["Gb"])
        S.op("dve", lambda e: e.scalar_tensor_tensor(out=Gb[:], in0=modn[:, D:2 * D], scalar=1.0, in1=Gb[:], op0=ALU.add, op1=ALU.mult),
             reads=["modn", "Gb"], writes=["Gb"])
    xt = [kb.sb("xt%d" % i, [128, D], F32) for i in range(3)]
    if do_proj:
        mTt = [kb.sb("mTt%d" % i, [128, 8, 512], BF16) for i in range(2)]
        x1t = [kb.sb("x1t%d" % i, [128, D], F32) for i in range(2)]
    if do_norm:
        junk = kb.sb("junk", [128, D], F32)
        ss = [kb.sb("ss%d" % i, [128, 4], F32) for i in range(2)]
        tt = kb.sb("tt", [128, D], F32)
        hb = [kb.sb("hb%d" % i, [128, D], BF16) for i in range(2)]
        hst = [kb.sb("hst%d" % i, [128, 8, 512], BF16) for i in range(2)]
        epsb = kb.sb("epsb", [128, 1], F32)
        S.op("dve", lambda e: e.memset(epsb[:], EPS), writes=["epsb"])
    if do_proj:
        if "mT_src" in io:
            mT_src, mT_key = io["mT_src"], io["mT_key"]
        else:
            mT_v = mT_d.rearrange("(k p) t -> p k t", p=128)
            mT_src, mT_key = (lambda g: mT_v[:, :, g * 512:(g + 1) * 512]), "mT_ext"
    if do_norm:
        if "hT_dst" in io:
            hT_dst, hT_key = io["hT_dst"], io["hT_key"]
        else:
            hT_v = hT_d.rearrange("(k p) t -> p k t", p=128)
            hT_dst, hT_key = (lambda g: hT_v[:, :, g * 512:(g + 1) * 512]), "hT"
    if "x_src" in io:
        x_src, x_key = io["x_src"], io["x_key"]
    else:
        x_src, x_key = (lambda ti: x_d[ti * 128:(ti + 1) * 128, :]), "x_ext"
    if mode != "P0":
        if "xo_dst" in io:
            xo_dst, xo_key = io["xo_dst"], io["xo_key"]
        else:
            xo_dst, xo_key = (lambda ti: xo_d[ti * 128:(ti + 1) * 128, :]), "xo"
    NT = TOK // 128

    def part1(ti):
        g, j = ti // 4, ti % 4
        if do_proj:
            mt = mTt[g % 2]
            mkey = "mTt%d" % (g % 2)
            mk = (lambda gg: mT_key(gg)) if callable(mT_key) else (lambda gg: mT_key)
            if j == 0 and g == 0:
                S.dma("sp", mTt[0][:], mT_src(0), reads=[mk(0)], writes=["mTt0"])
            if j == 2 and g + 1 < TOK // 512:
                S.dma("sp", mTt[(g + 1) % 2][:], mT_src(g + 1), reads=[mk(g + 1)], writes=["mTt%d" % ((g + 1) % 2)])
        x_ = xt[ti % 3]
        xkey = "xt%d" % (ti % 3)
        if ti == 0:
            S.dma("sp", xt[0][:], x_src(0), reads=[x_key], writes=["xt0"])
        if ti + 1 < NT:
            S.dma("sp", xt[(ti + 1) % 3][:], x_src(ti + 1), reads=[x_key], writes=["xt%d" % ((ti + 1) % 3)])
        cur = x_
        curkey = xkey
        if do_proj:
            for half in range(2):
                bk = banks[half]
                for k in range(8):
                    S.op("pe", lambda e, k=k, half=half, bk=bk, mt=mt, j=j: e.matmul(bk[:, :], lhsT=mt[:, k, j * 128:(j + 1) * 128], rhs=Wp[:, k, half * 512:(half + 1) * 512], start=(k == 0), stop=(k == 7)),
                         reads=[mkey, "Wp"], writes=["bank%d" % half])
            x1 = x1t[ti % 2]
            x1key = "x1t%d" % (ti % 2)
            for half in range(2):
                S.op("dve", lambda e, half=half, x1=x1, x_=x_: e.tensor_tensor(out=x1[:, half * 512:(half + 1) * 512], in0=x_[:, half * 512:(half + 1) * 512], in1=banks[half][:, :], op=ALU.add),
                     reads=[xkey, "bank%d" % half], writes=[x1key])
            S.dma("sp", xo_dst(ti), x1[:], reads=[x1key], writes=[xo_key])
            cur = x1
            curkey = x1key
        if do_norm:
            s_ = ss[ti % 2]
            skey = "ss%d" % (ti % 2)
            S.op("act", lambda e, cur=cur, s_=s_: e.activation(out=junk[:], in_=cur[:], func=AF.Square, accum_out=s_[:, 0:1]),
                 reads=[curkey], writes=["junk", skey])
            S.op("act", lambda e, s_=s_: e.activation(out=s_[:, 1:2], in_=s_[:, 0:1], func=AF.Ln, scale=1.0 / D, bias=epsb[:, 0:1]),
                 reads=[skey, "epsb"], writes=[skey])
            S.op("act", lambda e, s_=s_: e.activation(out=s_[:, 2:3], in_=s_[:, 1:2], func=AF.Exp, scale=-0.5),
                 reads=[skey], writes=[skey])
            S.op("dve", lambda e, cur=cur, s_=s_: e.scalar_tensor_tensor(out=tt[:], in0=cur[:], scalar=s_[:, 2:3], in1=Gb[:], op0=ALU.mult, op1=ALU.mult),
                 reads=[curkey, skey, "Gb"], writes=["tt"])
            h_ = hb[ti % 2]
            hkey = "hb%d" % (ti % 2)
            S.op("dve", lambda e, h_=h_: e.tensor_tensor(out=h_[:], in0=tt[:], in1=modn[:, 0:D], op=ALU.add),
                 reads=["tt", "modn"], writes=[hkey])

    def part2(ti):
        if not do_norm:
            return
        g, j = ti // 4, ti % 4
        h_ = hb[ti % 2]
        hkey = "hb%d" % (ti % 2)
        pbank = banks[2 + (ti % 2)]
        pkey = "bank%d" % (2 + (ti % 2))
        pT = pbank[:, :].bitcast(BF16)
        for k in range(8):
            S.op("pe", lambda e, k=k, h_=h_, pT=pT: e.transpose(pT[:, k * 128:(k + 1) * 128], h_[:, k * 128:(k + 1) * 128], ident[:]),
                 reads=[hkey, "ident"], writes=[pkey])
        hs = hst[g % 2]
        hskey = "hst%d" % (g % 2)
        S.op("act", lambda e, hs=hs, pT=pT, j=j: e.activation(out=hs[:, :, j * 128:(j + 1) * 128], in_=pT.rearrange("p (k t) -> p k t", k=8), func=AF.Copy),
             reads=[pkey], writes=[hskey])
        if j == 3:
            S.dma("sp", hT_dst(g), hs[:], reads=[hskey], writes=[hT_key(g) if callable(hT_key) else hT_key])
            if "on_hT" in io:
                io["on_hT"](g)

    for ti in range(NT + 1):
        if ti < NT:
            part1(ti)
        if ti >= 1:
            part2(ti - 1)

def build_C(ngroups=16):
    kb = KB()
    emit_C(kb, {}, ngroups)
    return kb.finish(["oT"])


def emit_C(kb, io, ngroups=16):
    S = kb.S
    hT_d = kb.din("hT", [D, T], BF16) if "hT_src" not in io else None
    w_d = kb.din("w", [D, 1028], F32)
    cst_d = kb.din("cst", [128, 16], F32)
    cb_d = kb.din("cb", [128, 384], BF16)
    oT_d = kb.dout("oT", [256, T], BF16) if "oT_dst" not in io else None
    banks = kb.banks()
    if "hT_src" in io:
        hT_src, hT_key = io["hT_src"], io["hT_key"]
    else:
        hT_v = hT_d.rearrange("(k p) t -> p k t", p=128)
        hT_src, hT_key = (lambda G: hT_v[:, :, G * 512:(G + 1) * 512]), "hT_ext"
    if "oT_dst" in io:
        oT_dst, oT_key = io["oT_dst"], io["oT_key"]
    else:
        oT_dst, oT_key = (lambda p, G: oT_d[p * 128:(p + 1) * 128, G * 512:(G + 1) * 512]), "oT"

    cst = kb.sb("cst", [128, 16], F32)
    cb = kb.sb("cb", [128, 384], BF16)
    S.dma("sp", cst[:], cst_d[:, :], writes=["cst"])
    S.dma("sp", cb[:], cb_d[:, :], writes=["cb"])
    ident = cb[:, 0:128]
    bd64 = cb[:, 128:256]
    tri = cb[:, 256:384]
    negbf = kb.sb("negbf", [128, 1], F32)
    S.op("dve", lambda e: e.tensor_scalar(out=negbf[:], in0=cst[:, 2:3], scalar1=-1.0, scalar2=None, op0=ALU.mult), reads=["cst"], writes=["negbf"])
    eps64 = kb.sb("eps64", [128, 2], F32)
    S.op("dve", lambda e: e.memset(eps64[:, 0:1], 64.0 * EPS), writes=["eps64"])
    S.op("dve", lambda e: e.memset(eps64[:, 1:2], EPS), writes=["eps64"])
    ones512 = kb.sb("ones512", [128, 512], F32)
    S.op("dve", lambda e: e.memset(ones512[:], 1.0), writes=["ones512"])

    Wb = kb.sb("Wb", [128, 8, 1028], BF16)
    wst = [kb.sb("wst%d" % i, [128, 1028], F32) for i in range(2)]
    for k in range(8):
        st = wst[k % 2]
        S.dma("sp", st[:], w_d[k * 128:(k + 1) * 128, :], writes=["wst%d" % (k % 2)])
        S.op("dve", lambda e, k=k, st=st: e.tensor_copy(out=Wb[:, k, :], in_=st[:]), reads=["wst%d" % (k % 2)], writes=["Wb"])
    Wrep = kb.sb("Wrep", [128, 8, 128], BF16)
    for h in range(4):
        S.op("dve", lambda e, h=h: e.tensor_copy(out=Wrep[:, :, 32 * h:32 * h + 32], in_=Wb[:, :, 1024 + h:1025 + h].to_broadcast([128, 8, 32])),
             reads=["Wb"], writes=["Wrep"])

    hTg = [kb.sb("hTg%d" % i, [128, 8, 512], BF16) for i in range(2)]
    QT = [kb.sb("QT%d" % i, [128, T], BF16) for i in range(2)]
    KT = [kb.sb("KT%d" % i, [128, T], BF16) for i in range(2)]
    Vall = kb.sb("Vall", [128, 64, 192], BF16)
    V = [Vall[:, :, 0:128], Vall[:, :, 64:192]]
    Gt = kb.sb("Gt", [128, T], BF16)
    sqbs = {n: kb.sb("sqb" + n, [128, 512], BF16) for n in ("q", "k")}
    lnvs = {n: kb.sb("lnv" + n, [128, 512], F32) for n in ("q", "k")}
    rss = {n: kb.sb("rs" + n, [128, 512], F32) for n in ("q", "k")}
    ge = kb.sb("ge", [128, 512], F32)
    fe = kb.sb("fe", [128, 512], F32)
    cl = [kb.sb("cl%d" % i, [128, 512], F32) for i in range(2)]
    c1b = kb.sb("c1b", [128, 512], BF16)
    c2b = kb.sb("c2b", [128, 512], BF16)
    c3b = kb.sb("c3b", [128, 512], BF16)
    r1 = kb.sb("r1", [128, 512], F32)
    r2 = kb.sb("r2", [128, 512], F32)
    KA = kb.sb("KA", [128, 512], BF16)
    QA = kb.sb("QA", [128, 512], BF16)
    r1b = kb.sb("r1b", [128, 512], BF16)
    r2b = kb.sb("r2b", [128, 512], BF16)
    AUG = kb.sb("AUG", [128, T], BF16)
    Pt = [kb.sb("P%d" % i, [128, 1024], BF16) for i in range(3)]
    rec = kb.sb("rec", [128, 512], F32)
    tmp = kb.sb("tmp", [128, 512], F32)
    OT = [kb.sb("OT%d" % i, [128, 512], BF16) for i in range(3)]

    S.op("dve", lambda e: e.memset(Vall[:, :, 64:128], 1.0), writes=["V0ones", "V1ones"])

    bstate = {"i": 0}

    def nextbank():
        b = bstate["i"] % 8
        bstate["i"] += 1
        return b

    for p in range(2):
        for G in range(ngroups):
            t0 = G * 512
            hg = hTg[G % 2]
            hkey = "hTg%d" % (G % 2)
            S.dma("sp", hg[:], hT_src(G), reads=[hT_key(G) if callable(hT_key) else hT_key], writes=[hkey])
            pb = {}
            for name, blk in (("q", 0), ("k", 1), ("g", 3)):
                b = nextbank()
                pb[name] = b
                c0 = p * 512 + blk * 128
                for k in range(8):
                    S.op("pe", lambda e, k=k, b=b, c0=c0, hg=hg: e.matmul(banks[b][:, :], lhsT=Wb[:, k, c0:c0 + 128], rhs=hg[:, k, :], start=(k == 0), stop=(k == 7)),
                         reads=["Wb", hkey], writes=["bank%d" % b])
            bv = nextbank()
            bvk = "bank%d" % bv
            c0 = p * 512 + 256
            for j in range(4):
                for k in range(8):
                    S.op("pe", lambda e, k=k, j=j, bv=bv, c0=c0, hg=hg: e.matmul(banks[bv][:, j * 128:(j + 1) * 128], lhsT=hg[:, k, j * 128:(j + 1) * 128], rhs=Wb[:, k, c0:c0 + 128], start=(k == 0), stop=(k == 7)),
                         reads=["Wb", hkey], writes=[bvk])
            if p == 0:
                bf = nextbank()
                bfk = "bank%d" % bf
                for k in range(8):
                    S.op("pe", lambda e, k=k, bf=bf, hg=hg: e.matmul(banks[bf][:, :], lhsT=Wrep[:, k, :], rhs=hg[:, k, :], start=(k == 0), stop=(k == 7)),
                         reads=["Wrep", hkey], writes=[bfk])
            bm = {}
            for name in ("q", "k"):
                b = pb[name]
                S.op("act", lambda e, b=b, name=name: e.activation(out=sqbs[name][:], in_=banks[b][:, :], func=AF.Square), reads=["bank%d" % b], writes=["sqb" + name])
            for name in ("q", "k"):
                bm[name] = nextbank()
                S.op("pe", lambda e, name=name: e.matmul(banks[bm[name]][:, :], lhsT=bd64, rhs=sqbs[name][:], start=True, stop=True), reads=["cb", "sqb" + name], writes=["bank%d" % bm[name]])
            S.op("act", lambda e: e.activation(out=lnvs["q"][:], in_=banks[bm["q"]][:, :], func=AF.Ln, scale=64.0, bias=eps64[:, 0:1]), reads=["bank%d" % bm["q"], "eps64"], writes=["lnvq"])
            S.op("act", lambda e: e.activation(out=lnvs["k"][:], in_=banks[bm["k"]][:, :], func=AF.Ln, scale=1.0, bias=eps64[:, 1:2]), reads=["bank%d" % bm["k"], "eps64"], writes=["lnvk"])
            for name in ("q", "k"):
                S.op("act", lambda e, name=name: e.activation(out=rss[name][:], in_=lnvs[name][:], func=AF.Exp, scale=-0.5), reads=["lnv" + name], writes=["rs" + name])
            for name, blk, dst, dn in (("q", 0, QT, "QT"), ("k", 1, KT, "KT")):
                b = pb[name]
                for hl in range(2):
                    pr = slice(hl * 64, hl * 64 + 64)
                    S.op("dve", lambda e, hl=hl, pr=pr, b=b, dst=dst, blk=blk, name=name: e.scalar_tensor_tensor(out=dst[hl][0:64, t0:t0 + 512], in0=banks[b][pr, :], scalar=cst[pr, blk:blk + 1], in1=rss[name][pr, :], op0=ALU.mult, op1=ALU.mult),
                         reads=["bank%d" % b, "cst", "rs" + name], writes=[(dn, hl, G)])
            bg = pb["g"]
            bgk = "bank%d" % bg
            S.op("act", lambda e, bg=bg: e.activation(out=ge[:], in_=banks[bg][:, :], func=AF.Tanh, scale=0.5), reads=[bgk], writes=["ge"])
            S.op("dve", lambda e: e.tensor_scalar(out=ge[:], in0=ge[:], scalar1=0.5, scalar2=0.5, op0=ALU.mult, op1=ALU.add), reads=["ge"], writes=["ge"])
            S.op("dve", lambda e, bg=bg: e.tensor_tensor(out=Gt[:, t0:t0 + 512], in0=banks[bg][:, :], in1=ge[:], op=ALU.mult), reads=[bgk, "ge"], writes=[("Gt", G)])
            bvv = banks[bv][:, :].rearrange("p (j c) -> p j c", j=4)
            S.op("dve", lambda e, bvv=bvv: e.tensor_copy(out=V[0][:, 4 * G:4 * G + 4, 0:64], in_=bvv[:, :, 0:64]), reads=[bvk], writes=[("V", 0, G)])
            S.op("dve", lambda e, bvv=bvv: e.tensor_copy(out=V[1][:, 4 * G:4 * G + 4, 64:128], in_=bvv[:, :, 64:128]), reads=[bvk], writes=[("V", 1, G)])
            if p == 0:
                S.op("act", lambda e, bf=bf: e.activation(out=fe[:], in_=banks[bf][:, :], func=AF.Exp, scale=-1.0, bias=negbf[:, 0:1]), reads=[bfk, "negbf"], writes=["fe"])
                S.op("act", lambda e: e.activation(out=fe[:], in_=fe[:], func=AF.Ln, scale=1.0, bias=ones512[:, 0:1]), reads=["fe", "ones512"], writes=["fe"])
                c_ = cl[G % 2]
                ck = "cl%d" % (G % 2)
                if G == 0:
                    S.op("dve", lambda e, c_=c_: e.tensor_tensor_scan(out=c_[:], data0=ones512[:], data1=fe[:], initial=0.0, op0=ALU.mult, op1=ALU.add),
                         reads=["fe", "ones512"], writes=[ck])
                else:
                    cp = cl[(G - 1) % 2]
                    S.op("dve", lambda e, c_=c_, cp=cp: e.tensor_tensor_scan(out=c_[:], data0=ones512[:], data1=fe[:], initial=cp[:, 511:512], op0=ALU.mult, op1=ALU.add),
                         reads=["fe", "ones512", "cl%d" % ((G - 1) % 2)], writes=[ck])
                S.op("dve", lambda e, c_=c_: e.tensor_copy(out=c1b[:], in_=c_[:]), reads=[ck], writes=["c1b"])
                S.op("dve", lambda e, c_=c_: e.tensor_tensor(out=r1[:], in0=c_[:], in1=c1b[:], op=ALU.subtract), reads=[ck, "c1b"], writes=["r1"])
                S.op("dve", lambda e: e.tensor_copy(out=c2b[:], in_=r1[:]), reads=["r1"], writes=["c2b"])
                S.op("dve", lambda e: e.tensor_tensor(out=r2[:], in0=r1[:], in1=c2b[:], op=ALU.subtract), reads=["r1", "c2b"], writes=["r2"])
                S.op("dve", lambda e: e.tensor_copy(out=c3b[:], in_=r2[:]), reads=["r2"], writes=["c3b"])
                for (dstt, dk, mc) in ((KA, "KA", 3), (QA, "QA", 7)):
                    S.op("dve", lambda e, mc=mc: e.tensor_scalar(out=r1b[:], in0=c1b[:], scalar1=cst[:, mc:mc + 1], scalar2=cst[:, mc + 3:mc + 4], op0=ALU.mult, op1=ALU.add),
                         reads=["c1b", "cst"], writes=["r1b"])
                    S.op("dve", lambda e, mc=mc: e.scalar_tensor_tensor(out=r2b[:], in0=c2b[:], scalar=cst[:, mc + 1:mc + 2], in1=r1b[:], op0=ALU.mult, op1=ALU.add),
                         reads=["c2b", "cst", "r1b"], writes=["r2b"])
                    S.op("dve", lambda e, mc=mc, dstt=dstt: e.scalar_tensor_tensor(out=dstt[:], in0=c3b[:], scalar=cst[:, mc + 2:mc + 3], in1=r2b[:], op0=ALU.mult, op1=ALU.add),
                         reads=["c3b", "cst", "r2b"], writes=[dk])
                for hl in range(2):
                    S.op("dve", lambda e, hl=hl: e.tensor_copy(out=KT[hl][64:70, t0:t0 + 512], in_=KA[32 * hl:32 * hl + 6, :]), reads=["KA"], writes=[("KTa", hl, G)])
                    S.op("dve", lambda e, hl=hl: e.tensor_copy(out=QT[hl][64:70, t0:t0 + 512], in_=QA[32 * hl:32 * hl + 6, :]), reads=["QA"], writes=[("QTa", hl, G)])
                    S.op("dve", lambda e, hl=hl: e.tensor_copy(out=AUG[32 * hl:32 * hl + 6, t0:t0 + 512], in_=KA[64 + 32 * hl:64 + 32 * hl + 6, :]), reads=["KA"], writes=[("AUG", G)])
                    S.op("dve", lambda e, hl=hl: e.tensor_copy(out=AUG[64 + 32 * hl:64 + 32 * hl + 6, t0:t0 + 512], in_=QA[64 + 32 * hl:64 + 32 * hl + 6, :]), reads=["QA"], writes=[("AUG", G)])
        if p == 1:
            TT_ = ngroups * 512
            allG = list(range(ngroups))
            for hl in range(2):
                S.op("dve", lambda e, hl=hl: e.tensor_copy(out=KT[hl][64:70, 0:TT_], in_=AUG[32 * hl:32 * hl + 6, 0:TT_]), reads=[("AUG", G) for G in allG], writes=[("KTa", hl, G) for G in allG])
                S.op("dve", lambda e, hl=hl: e.tensor_copy(out=QT[hl][64:70, 0:TT_], in_=AUG[64 + 32 * hl:64 + 32 * hl + 6, 0:TT_]), reads=[("AUG", G) for G in allG], writes=[("QTa", hl, G) for G in allG])
        units = []
        for G in range(ngroups):
            for hl in range(2):
                us = [(i, i + 1) for i in range(0, 4 * G, 2)] + [(i,) for i in range(4 * G, 4 * G + 4)]
                for n_, u in enumerate(us):
                    units.append((G, hl, u, n_ == len(us) - 1))

        def geom(G, i):
            q_lo = max(G * 512, i * 128)
            return q_lo, (G + 1) * 512 - q_lo

        def QK(ui):
            G, hl, u, _ = units[ui]
            d = 1 + (ui % 3)
            for hh, i in enumerate(u):
                q_lo, n = geom(G, i)
                bi = 2 * d + hh
                S.op("pe", lambda e, i=i, bi=bi, q_lo=q_lo, n=n, hl=hl: e.matmul(banks[bi][:, 0:n], lhsT=KT[hl][0:70, i * 128:(i + 1) * 128], rhs=QT[hl][0:70, q_lo:q_lo + n], start=True, stop=True),
                     reads=[("QT", hl, G), ("QTa", hl, G), ("KT", hl, i // 4), ("KTa", hl, i // 4)], writes=["bank%d" % bi])

        def EXPV(ui):
            G, hl, u, last = units[ui]
            nkb = 4 * G + 4
            ob = hl
            obk = "bank%d" % ob
            d = 1 + (ui % 3)
            P_ = Pt[ui % 3]
            pk = "P%d" % (ui % 3)
            if len(u) == 2:
                S.op("act", lambda e: e.activation(out=P_[:, 0:1024], in_=kb.dbanks[d][:, 0:1024], func=AF.Exp),
                     reads=["bank%d" % (2 * d), "bank%d" % (2 * d + 1)], writes=[pk])
            else:
                q_lo, n = geom(G, u[0])
                S.op("act", lambda e: e.activation(out=P_[:, 0:n], in_=banks[2 * d][:, 0:n], func=AF.Exp), reads=["bank%d" % (2 * d)], writes=[pk])
                S.op("dve", lambda e: e.tensor_tensor(out=P_[:, 0:128], in0=P_[:, 0:128], in1=tri, op=ALU.mult), reads=[pk, "cb"], writes=[pk])
            for hh, i in enumerate(u):
                q_lo, n = geom(G, i)
                c_lo = q_lo - G * 512
                S.op("pe", lambda e, i=i, hh=hh, n=n, c_lo=c_lo: e.matmul(banks[ob][:, c_lo:512], lhsT=V[hl][:, i, :], rhs=P_[:, hh * 512:hh * 512 + n], start=(i == 0), stop=(i == nkb - 1)),
                     reads=[pk, ("V", hl, i // 4), "V%dones" % hl], writes=[obk])
            if last:
                ot = OT[G % 3]
                otk = "OT%d" % (G % 3)
                num = slice(0, 64) if hl == 0 else slice(64, 128)
                den = slice(64, 128) if hl == 0 else slice(0, 64)
                S.op("dve", lambda e: e.reciprocal(out=rec[num, :], in_=banks[ob][den, :]), reads=[obk], writes=[("rec", hl)])
                S.op("dve", lambda e: e.tensor_tensor(out=tmp[num, :], in0=banks[ob][num, :], in1=rec[num, :], op=ALU.mult), reads=[obk, ("rec", hl)], writes=[("tmp", hl)])
                S.op("dve", lambda e: e.tensor_tensor(out=ot[num, :], in0=tmp[num, :], in1=Gt[num, G * 512:(G + 1) * 512], op=ALU.mult), reads=[("tmp", hl), ("Gt", G)], writes=[(otk, hl)])
                if hl == 1:
                    S.dma("pool", oT_dst(p, G), ot[:], reads=[(otk, 0), (otk, 1)], writes=[oT_key(G) if callable(oT_key) else oT_key])
                    S._record(S.lastw[oT_key(G) if callable(oT_key) else oT_key], [(otk, 0), (otk, 1)], [])
                    if "on_oT" in io and p == 1 and G % 4 == 3:
                        io["on_oT"](G // 4)

        for ui in range(len(units) + 2):
            if ui < len(units):
                QK(ui)
            if ui >= 2:
                EXPV(ui - 2)

def consts_C():
    ident = np.eye(128, dtype=np.float32)
    bd = np.zeros((128, 128), np.float32)
    bd[0:64, 0:64] = 1.0 / 64
    bd[64:128, 64:128] = 1.0 / 64
    tri = (np.arange(128)[None, :] >= np.arange(128)[:, None]).astype(np.float32)
    return np.ascontiguousarray(np.concatenate([ident, bd, tri], axis=1).astype(NPBF))


def prep_C(inp, core, hT):
    b, g = core // 4, core % 4
    w_in = inp["odd_w_in"][0]
    cols = []
    for p in range(2):
        hc = (4 * g + 2 * p) * 64
        for base in (0, 1024, 2048, 3072):
            cols.append(w_in[:, base + hc:base + hc + 128])
    cols.append(w_in[:, 4096 + 4 * g:4096 + 4 * g + 4])
    w = np.ascontiguousarray(np.concatenate(cols, axis=1))
    cst = np.zeros((128, 16), np.float32)
    cst[:, 0] = np.tile(inp["fox_qnorm_g"][0], 2)
    cst[:, 1] = np.tile(inp["fox_knorm_g"][0], 2)
    cst[:, 2] = np.repeat(inp["fox_b_f"][0][4 * g:4 * g + 4], 32)
    r = np.arange(128) % 32
    cst[:, 3] = (r == 3)
    cst[:, 4] = (r == 4)
    cst[:, 5] = (r == 5)
    cst[:, 6] = (r < 3)
    cst[:, 7] = -1.0 * (r == 0)
    cst[:, 8] = -1.0 * (r == 1)
    cst[:, 9] = -1.0 * (r == 2)
    cst[:, 10] = (r >= 3) & (r < 6)
    return dict(hT=hT, w=w, cst=cst, cb=consts_C())


def build_A(ngroups=16, stage=9):
    kb = KB()
    emit_A(kb, {}, ngroups, stage)
    return kb.finish(["mo"])


def emit_A(kb, io, ngroups=16, stage=9):
    S = kb.S
    hT_d = kb.din("hT", [D, T], BF16) if "hT_src" not in io else None
    w_d = kb.din("w", [D, 768], F32)
    cst_d = kb.din("cst", [128, 8], F32)
    pw_d = kb.din("pw", [128, 128], F32)
    cb_d = kb.din("cb", [128, 1152], BF16)
    rm_d = kb.din("rm", [128, 512], F32)
    mo_d = kb.dout("mo", [256, T], BF16) if "mo_dst" not in io else None
    banks = kb.banks()
    if "hT_src" in io:
        hT_src, hT_key = io["hT_src"], io["hT_key"]
    else:
        hT_v = hT_d.rearrange("(k p) t -> p k t", p=128)
        hT_src, hT_key = (lambda G: hT_v[:, :, G * 512:(G + 1) * 512]), "hT_ext"
    if "mo_dst" in io:
        mo_dst, mo_key = io["mo_dst"], io["mo_key"]
    else:
        mo_v = mo_d.rearrange("(r p) t -> p r t", p=128)
        mo_dst, mo_key = (lambda G: mo_v[:, :, G * 512:(G + 1) * 512]), "mo"

    cst = kb.sb("cst", [128, 8], F32)
    cb = kb.sb("cb", [128, 1152], BF16)
    rm = kb.sb("rm", [128, 512], F32)
    pwf = kb.sb("pwf", [128, 128], F32)
    pwb = kb.sb("pwb", [128, 128], BF16)
    S.dma("sp", cst[:], cst_d[:, :], writes=["cst"])
    S.dma("sp", cb[:], cb_d[:, :], writes=["cb"])
    S.dma("sp", rm[:], rm_d[:, :], writes=["rm"])
    S.dma("sp", pwf[:], pw_d[:, :], writes=["pwf"])
    S.op("dve", lambda e: e.tensor_copy(out=pwb[:], in_=pwf[:]), reads=["pwf"], writes=["pwb"])
    ident = cb[:, 0:128]
    o128 = cb[:, 128:256]
    trim = cb[:, 256:768]
    Bmain = cb[:, 768:896]
    Bprev = cb[:, 896:1024]
    B0 = cb[:, 1024:1152]
    lbt = kb.sb("lbt", [128, 8], F32)
    S.op("act", lambda e: e.activation(out=lbt[:, 0:3], in_=cst[:, 0:3], func=AF.Exp), reads=["cst"], writes=["lbt"])
    S.op("dve", lambda e: e.tensor_tensor(out=lbt[:, 3:4], in0=lbt[:, 0:1], in1=lbt[:, 1:2], op=ALU.add), reads=["lbt"], writes=["lbt"])
    S.op("dve", lambda e: e.tensor_tensor(out=lbt[:, 3:4], in0=lbt[:, 3:4], in1=lbt[:, 2:3], op=ALU.add), reads=["lbt"], writes=["lbt"])
    S.op("dve", lambda e: e.reciprocal(out=lbt[:, 3:4], in_=lbt[:, 3:4]), reads=["lbt"], writes=["lbt"])
    S.op("dve", lambda e: e.tensor_tensor(out=lbt[:, 4:5], in0=lbt[:, 0:1], in1=lbt[:, 3:4], op=ALU.mult), reads=["lbt"], writes=["lbt"])
    S.op("dve", lambda e: e.tensor_scalar(out=lbt[:, 5:6], in0=lbt[:, 4:5], scalar1=-1.0, scalar2=1.0, op0=ALU.mult, op1=ALU.add), reads=["lbt"], writes=["lbt"])
    S.op("dve", lambda e: e.tensor_scalar(out=lbt[:, 6:7], in0=lbt[:, 4:5], scalar1=-1.0, scalar2=None, op0=ALU.add), reads=["lbt"], writes=["lbt"])
    S.op("dve", lambda e: e.memset(lbt[:, 7:8], EPS), reads=[], writes=["lbt7"])
    lbc = lbt[:, 4:5]
    omlb = lbt[:, 5:6]
    nomlb = lbt[:, 6:7]
    epsc = lbt[:, 7:8]

    Wb = kb.sb("Wb", [128, 8, 768], BF16)
    wst = [kb.sb("wst%d" % i, [128, 768], F32) for i in range(2)]
    for k in range(8):
        st = wst[k % 2]
        S.dma("sp", st[:], w_d[k * 128:(k + 1) * 128, :], writes=["wst%d" % (k % 2)])
        S.op("dve", lambda e, k=k, st=st: e.tensor_copy(out=Wb[:, k, :], in_=st[:]), reads=["wst%d" % (k % 2)], writes=["Wb"])

    hTg = [kb.sb("hTg%d" % i, [128, 8, 512], BF16) for i in range(2)]
    t = {n: kb.sb(n, [128, 512], F32) for n in ["e", "L1", "L2", "logf", "cum", "key", "dq", "dl", "Eq", "Ek", "El", "lnm", "rstd", "ega", "egb", "t1"]}
    sga = [kb.sb("sga%d" % i, [128, 512], F32) for i in range(2)]
    sgb = [kb.sb("sgb%d" % i, [128, 512], F32) for i in range(2)]
    Qs = [kb.sb("Qs%d" % i, [128, 512], BF16) for i in range(2)]
    Ks = [kb.sb("Ks%d" % i, [128, 512], BF16) for i in range(2)]
    Kh = [kb.sb("Kh%d" % i, [128, 512], BF16) for i in range(2)]
    sq = kb.sb("sq", [128, 512], BF16)
    pooledT = kb.sb("pooledT", [128, 512], BF16)
    KhT = kb.sb("KhT", [128, 4, 2, 128], BF16)
    S.op("dve", lambda e: e.memset(KhT[:], 0.0), writes=["KhT"])
    Vt = [kb.sb("Vt%d" % i, [128, 4, 128], BF16) for i in range(2)]
    Ut = [kb.sb("Ut%d" % i, [128, 4, 128], BF16) for i in range(3)]
    At = kb.sb("At", [128, 512], BF16)
    er = [kb.sb("er%d" % i, [128, 16], F32) for i in range(2)]
    state = kb.sb("state", [128, 128], F32)
    stb = [kb.sb("stb%d" % i, [128, 128], BF16) for i in range(2)]
    Ost = [kb.sb("Ost%d" % i, [128, 2, 512], BF16) for i in range(4)]
    S.op("dve", lambda e: e.memset(state[:], 0.0), writes=["state"])

    bst = {"1": 0, "2": 0}

    def nb1():
        b = bst["1"] % 4
        bst["1"] += 1
        return b, "bank%d" % b

    def nb2():
        b = 4 + bst["2"] % 4
        bst["2"] += 1
        return b, "bank%d" % b

    def s1(G):
        pz = G % 2
        hg = hTg[pz]
        hkey = "hTg%d" % pz
        S.dma("sp", hg[:], hT_src(G), reads=[hT_key(G) if callable(hT_key) else hT_key], writes=[hkey])
        yield
        pe_ops = []
        f_ops = []

        def proj_fm(blk):
            b, bk = nb1()
            for k in range(8):
                pe_ops.append(("pe", lambda e, k=k, b=b: e.matmul(banks[b][:, :], lhsT=Wb[:, k, blk * 128:(blk + 1) * 128], rhs=hg[:, k, :], start=(k == 0), stop=(k == 7)),
                               ["Wb", hkey], [bk]))
            return b, bk

        bfm, bfk = proj_fm(1)
        bq, bqk = proj_fm(0)
        biu = []
        for half in range(2):
            b, bk = nb1()
            for jj in range(2):
                j = half * 2 + jj
                for k in range(8):
                    pe_ops.append(("pe", lambda e, k=k, j=j, jj=jj, b=b: e.matmul(banks[b][:, jj * 256:(jj + 1) * 256], lhsT=hg[:, k, j * 128:(j + 1) * 128], rhs=Wb[:, k, 512:768], start=(k == 0), stop=(k == 7)),
                                   ["Wb", hkey], [bk]))
            biu.append((b, bk))
        bga, bgak = proj_fm(2)
        bgb, bgbk = proj_fm(3)
        cum3 = t["cum"][:].rearrange("p (c s) -> p c s", c=8)
        er_ = er[pz]
        erk = "er%d" % pz
        f_ops += [
            ("act", lambda e: e.activation(out=t["e"][:], in_=banks[bfm][:, :], func=AF.Exp, scale=-1.0), [bfk], ["e"]),
            ("act", lambda e: e.activation(out=t["L1"][:], in_=t["e"][:], func=AF.Ln, scale=lbc, bias=rm[:, 1:2]), ["e", "lbt", "rm"], ["L1"]),
            ("act", lambda e: e.activation(out=t["L2"][:], in_=t["e"][:], func=AF.Ln, scale=1.0, bias=rm[:, 1:2]), ["e", "rm"], ["L2"]),
            ("dve", lambda e: e.tensor_tensor(out=t["logf"][:], in0=t["L1"][:], in1=t["L2"][:], op=ALU.subtract), ["L1", "L2"], ["logf"]),
            ("dve", lambda e: e.tensor_tensor_scan(out=t["cum"][:], data0=rm[:], data1=t["logf"][:], initial=0.0, op0=ALU.mult, op1=ALU.add), ["rm", "logf"], ["cum"]),
            ("act", lambda e: e.activation(out=t["e"][:], in_=t["L2"][:], func=AF.Exp, scale=-1.0), ["L2", "L1"], ["e"]),
            ("dve", lambda e: e.tensor_scalar(out=t["key"][:], in0=t["e"][:], scalar1=nomlb, scalar2=omlb, op0=ALU.mult, op1=ALU.add), ["e", "lbt"], ["key"]),
            ("dve", lambda e: e.tensor_tensor(out=t["dq"][:].rearrange("p (c s) -> p c s", c=8), in0=cum3, in1=cum3[:, :, 31:32].to_broadcast([128, 8, 64]), op=ALU.subtract), ["cum"], ["dq"]),
            ("dve", lambda e: e.tensor_tensor(out=t["dl"][:].rearrange("p (c s) -> p c s", c=8), in0=cum3[:, :, 63:64].to_broadcast([128, 8, 64]), in1=cum3, op=ALU.subtract), ["cum"], ["dl"]),
            ("act", lambda e: e.activation(out=t["Eq"][:], in_=t["dq"][:], func=AF.Exp), ["dq"], ["Eq"]),
            ("act", lambda e: e.activation(out=t["Ek"][:], in_=t["dq"][:], func=AF.Exp, scale=-1.0), ["dq"], ["Ek"]),
            ("act", lambda e: e.activation(out=t["El"][:], in_=t["dl"][:], func=AF.Exp), ["dl"], ["El"]),
            ("act", lambda e: e.activation(out=er_[:, 0:8], in_=cum3[:, :, 31], func=AF.Exp), ["cum"], [erk]),
            ("act", lambda e: e.activation(out=er_[:, 8:16], in_=cum3[:, :, 63], func=AF.Exp), ["cum"], [erk]),
            ("dve", lambda e: e.tensor_tensor(out=Qs[pz][:], in0=banks[bq][:, :], in1=t["Eq"][:], op=ALU.mult), [bqk, "Eq"], ["Qs%d" % pz]),
            ("dve", lambda e: e.tensor_tensor(out=Ks[pz][:], in0=t["key"][:], in1=t["Ek"][:], op=ALU.mult), ["key", "Ek"], ["Ks%d" % pz]),
            ("dve", lambda e: e.tensor_tensor(out=Kh[pz][:], in0=t["key"][:], in1=t["El"][:], op=ALU.mult), ["key", "El"], ["Kh%d" % pz]),
        ]
        Uc = Ut[G % 3]
        ukey = "Ut%d" % (G % 3)
        for half in range(2):
            b, bk = biu[half]
            v3 = banks[b][:, :].rearrange("p (j c) -> p j c", j=2)
            f_ops.append(("dve", lambda e, v3=v3, half=half: e.tensor_copy(out=Vt[pz][:, 2 * half:2 * half + 2, :], in_=v3[:, :, 0:128]), [bk], ["Vt%d" % pz]))
            f_ops.append(("dve", lambda e, v3=v3, half=half: e.tensor_copy(out=Uc[:, 2 * half:2 * half + 2, :], in_=v3[:, :, 128:256]), [bk], [ukey]))
        for (bg, bgk, eg, sg, sgk) in ((bga, bgak, "ega", sga[pz], "sga%d" % pz), (bgb, bgbk, "egb", sgb[pz], "sgb%d" % pz)):
            f_ops.append(("act", lambda e, bg=bg, eg=eg: e.activation(out=t[eg][:], in_=banks[bg][:, :], func=AF.Tanh, scale=0.5), [bgk], [eg]))
            f_ops.append(("dve", lambda e, eg=eg: e.tensor_scalar(out=t[eg][:], in0=t[eg][:], scalar1=0.5, scalar2=0.5, op0=ALU.mult, op1=ALU.add), [eg], [eg]))
            f_ops.append(("dve", lambda e, bg=bg, eg=eg, sg=sg: e.tensor_tensor(out=sg[:], in0=banks[bg][:, :], in1=t[eg][:], op=ALU.mult), [bgk, eg], [sgk]))
        pi = 0
        for _ in range(8):
            en, fn, rd, wr = pe_ops[pi]
            S.op(en, fn, reads=rd, writes=wr)
            pi += 1
            yield
        for (en, fn, rd, wr) in f_ops:
            S.op(en, fn, reads=rd, writes=wr)
            yield
            for _ in range(3):
                if pi < len(pe_ops):
                    en2, fn2, rd2, wr2 = pe_ops[pi]
                    S.op(en2, fn2, reads=rd2, writes=wr2)
                    pi += 1
                    yield
        while pi < len(pe_ops):
            en2, fn2, rd2, wr2 = pe_ops[pi]
            S.op(en2, fn2, reads=rd2, writes=wr2)
            pi += 1
            yield

    def s2(G):
        pz = G % 2
        er_ = er[pz]
        erk = "er%d" % pz
        Qk, Kk, Khk, Vk = "Qs%d" % pz, "Ks%d" % pz, "Kh%d" % pz, "Vt%d" % pz
        bt, btk = nb2()
        ptv = banks[bt][:, :].bitcast(BF16)
        for j in range(4):
            S.op("pe", lambda e, j=j: e.transpose(ptv[:, j * 128:(j + 1) * 128], Kh[pz][:, j * 128:(j + 1) * 128], ident), reads=[Khk, "cb"], writes=[btk])
            yield
        ptv3 = ptv[:, 0:512].rearrange("p (j k) -> p j k", j=4)
        S.op("dve", lambda e: e.tensor_copy(out=KhT[0:64, :, 0, :], in_=ptv3[0:64, :, :]), reads=[btk], writes=["KhT"])
        yield
        S.op("dve", lambda e: e.tensor_copy(out=KhT[64:128, :, 1, :], in_=ptv3[64:128, :, :]), reads=[btk], writes=["KhT"])
        yield
        bs, bsk = nb2()
        for pair in range(4):
            S.op("pe", lambda e, pair=pair: e.matmul(banks[bs][:, pair * 128:(pair + 1) * 128], lhsT=Ks[pz][:, pair * 128:(pair + 1) * 128], rhs=Qs[pz][:, pair * 128:(pair + 1) * 128], start=True, stop=True),
                 reads=[Kk, Qk], writes=[bsk])
            yield
        S.op("dve", lambda e: e.tensor_tensor(out=At[:], in0=banks[bs][:, :], in1=trim, op=ALU.mult), reads=[bsk, "cb"], writes=["At"])
        yield
        bz = []
        for zh in range(2):
            b, bk = nb2()
            for cc in range(4):
                c = zh * 4 + cc
                j, half = c // 2, c % 2
                S.op("pe", lambda e, cc=cc, j=j, half=half, b=b: e.matmul(banks[b][:, cc * 128:(cc + 1) * 128], lhsT=KhT[:, j, half, :], rhs=Vt[pz][:, j, :], start=True, stop=True),
                     reads=["KhT", Vk], writes=[bk])
                yield
            bz.append((b, bk))
        bo, bok = nb2()
        for c in range(8):
            gc = 8 * G + c
            j, half = c // 2, c % 2
            sb_ = stb[gc % 2]
            sbk = "stb%d" % (gc % 2)
            S.op("act", lambda e, c=c, sb_=sb_: e.activation(out=sb_[:], in_=state[:], func=AF.Copy, scale=er_[:, c:c + 1]), reads=["state", erk], writes=[sbk])
            if half == 0:
                S.op("pe", lambda e, j=j: e.matmul(banks[bo][:, j * 128:(j + 1) * 128], lhsT=Vt[pz][:, j, :], rhs=At[:, j * 128:(j + 1) * 128], start=True, stop=False),
                     reads=[Vk, "At"], writes=[bok])
            S.op("pe", lambda e, c=c, sb_=sb_, half=half: e.matmul(banks[bo][:, c * 64:(c + 1) * 64], lhsT=sb_[:], rhs=Qs[pz][:, c * 64:(c + 1) * 64], start=False, stop=(half == 1)),
                 reads=[sbk, Qk], writes=[bok])
            zb, zbk = bz[c // 4]
            S.op("dve", lambda e, c=c, zb=zb: e.scalar_tensor_tensor(out=state[:], in0=state[:], scalar=er_[:, 8 + c:9 + c], in1=banks[zb][:, (c % 4) * 128:(c % 4 + 1) * 128], op0=ALU.mult, op1=ALU.add),
                 reads=["state", erk, zbk], writes=["state"])
            yield
        if stage <= 3:
            return
        S.op("act", lambda e: e.activation(out=sq[:], in_=banks[bo][:, :], func=AF.Square), reads=[bok], writes=["sq"])
        yield
        bm, bmk = nb2()
        S.op("pe", lambda e: e.matmul(banks[bm][:, :], lhsT=o128, rhs=sq[:], start=True, stop=True), reads=["sq", "cb"], writes=[bmk])
        yield
        S.op("act", lambda e: e.activation(out=t["lnm"][:], in_=banks[bm][:, :], func=AF.Ln, scale=1.0, bias=epsc), reads=[bmk, "lbt7"], writes=["lnm"])
        yield
        S.op("act", lambda e: e.activation(out=t["rstd"][:], in_=t["lnm"][:], func=AF.Exp, scale=-0.5), reads=["lnm"], writes=["rstd"])
        yield
        S.op("dve", lambda e: e.scalar_tensor_tensor(out=t["t1"][:], in0=banks[bo][:, :], scalar=cst[:, 3:4], in1=t["rstd"][:], op0=ALU.mult, op1=ALU.mult), reads=[bok, "cst", "rstd"], writes=["t1"])
        yield
        os_ = Ost[G % 4]
        osk = "Ost%d" % (G % 4)
        S.op("dve", lambda e: e.tensor_tensor(out=os_[:, 0, :], in0=t["t1"][:], in1=sga[pz][:], op=ALU.mult), reads=["t1", "sga%d" % pz], writes=[osk])
        yield
        bp, bpk = nb2()
        Uc = Ut[G % 3]
        ukey = "Ut%d" % (G % 3)
        Up = Ut[(G - 1) % 3]
        upkey = "Ut%d" % ((G - 1) % 3)
        for j in range(4):
            tj = 4 * G + j
            if tj == 0:
                S.op("pe", lambda e, j=j: e.matmul(banks[bp][:, j * 128:(j + 1) * 128], lhsT=Uc[:, j, :], rhs=B0, start=True, stop=True), reads=[ukey, "cb"], writes=[bpk])
            else:
                S.op("pe", lambda e, j=j: e.matmul(banks[bp][:, j * 128:(j + 1) * 128], lhsT=Uc[:, j, :], rhs=Bmain, start=True, stop=False), reads=[ukey, "cb"], writes=[bpk])
                if j == 0:
                    S.op("pe", lambda e, j=j: e.matmul(banks[bp][:, j * 128:(j + 1) * 128], lhsT=Up[:, 3, :], rhs=Bprev, start=False, stop=True), reads=[upkey, "cb"], writes=[bpk])
                else:
                    S.op("pe", lambda e, j=j: e.matmul(banks[bp][:, j * 128:(j + 1) * 128], lhsT=Uc[:, j - 1, :], rhs=Bprev, start=False, stop=True), reads=[ukey, "cb"], writes=[bpk])
            yield
        S.op("dve", lambda e: e.tensor_copy(out=pooledT[:], in_=banks[bp][:, :]), reads=[bpk], writes=["pooledT"])
        yield
        bob, bobk = nb2()
        S.op("pe", lambda e: e.matmul(banks[bob][:, :], lhsT=pwb[:], rhs=pooledT[:], start=True, stop=True), reads=["pwb", "pooledT"], writes=[bobk])
        yield
        S.op("dve", lambda e: e.scalar_tensor_tensor(out=os_[:, 1, :], in0=banks[bob][:, :], scalar=cst[:, 4:5], in1=sgb[pz][:], op0=ALU.mult, op1=ALU.mult), reads=[bobk, "cst", "sgb%d" % pz], writes=[osk])
        yield
        S.dma("pool", mo_dst(G), os_[:], reads=[osk], writes=[mo_key(G) if callable(mo_key) else mo_key])
        if "on_mo" in io and G % 4 == 3:
            io["on_mo"](G // 4)
        yield

    RA, RB = 1, 2

    def drain(*gens):
        act_ = [g for g in gens if g is not None]
        reps = [RA, RB]
        while act_:
            for gi, g in enumerate(list(act_)):
                for _ in range(reps[gi] if len(gens) > 1 and gi < 2 else 1):
                    try:
                        next(g)
                    except StopIteration:
                        if g in act_:
                            act_.remove(g)
                        break

    drain(s1(0))
    for G in range(ngroups):
        drain(s2(G), s1(G + 1) if G + 1 < ngroups else None)

WINDOWS = (2, 4, 8, 16)


def consts_A(g):
    w = WINDOWS[g]
    ident = np.eye(128, dtype=np.float32)
    o128 = np.full((128, 128), 1.0 / 128, np.float32)
    s = np.arange(128)[:, None]
    t = np.arange(128)[None, :]
    tri1 = ((s // 64 == t // 64) & (s % 64 <= t % 64)).astype(np.float32)
    trim = np.tile(tri1, (1, 4))
    t = np.arange(128)[None, :]
    Bmain = ((s <= t) & (s >= t - w + 1)).astype(np.float32) / w - (s == t)
    Bprev = ((s - 128) >= (t - w + 1)).astype(np.float32) / w
    cnt = np.minimum(t + 1, w).astype(np.float32)
    B0 = ((s <= t) & (s >= t - w + 1)).astype(np.float32) / cnt - (s == t)
    cb = np.concatenate([ident, o128, trim, Bmain, Bprev, B0], axis=1).astype(NPBF)
    rm = np.ones((128, 512), np.float32)
    rm[:, 0::64] = 0.0
    return np.ascontiguousarray(cb), rm


def prep_A(inp, core, hT):
    b, g = core // 4, core % 4
    w_in = inp["even_w_in"][0]
    sl = slice(g * 128, (g + 1) * 128)
    cols = [w_in[:, 0:512][:, sl], w_in[:, 512:1024][:, sl], w_in[:, 1536:2048][:, sl], w_in[:, 2560:3072][:, sl],
            w_in[:, 1024:1536][:, sl], w_in[:, 2048:2560][:, sl]]
    w = np.ascontiguousarray(np.concatenate(cols, axis=1))
    cst = np.zeros((128, 8), np.float32)
    cst[:, 0:3] = inp["hgrn_lb"][:, sl].T
    cst[:, 3] = inp["hgrn_onorm_g"][0][sl]
    cst[:, 4] = inp["pool_scale"][0][sl]
    cb, rm = consts_A(g)
    return dict(hT=hT, w=w, cst=cst, pw=np.ascontiguousarray(inp["pool_w"][0, g]), cb=cb, rm=rm)


_PROGS = {}
GROUPS = [[0, 1, 2, 3], [4, 5, 6, 7]]


def build_fused(upto=99):
    kb = KB()
    nc, S = kb.nc, kb.S
    dt_ = lambda n, sh, d: nc.dram_tensor(n, sh, d).ap()
    h0own = [dt_("i_h0own%d" % g, [D, 512], BF16) for g in range(4)]
    h0all = [dt_("i_h0all%d" % g, [4 * D, 512], BF16) for g in range(4)]
    moown = [dt_("i_moown%d" % q, [256, TOK], BF16) for q in range(4)]
    moall = dt_("i_moall", [4 * D, TOK], BF16)
    x1 = dt_("i_x1", [TOK, D], F32)
    h1own = [dt_("i_h1own%d" % g, [D, 512], BF16) for g in range(4)]
    h1all = [dt_("i_h1all%d" % g, [4 * D, 512], BF16) for g in range(4)]
    oown = [dt_("i_oown%d" % q, [256, TOK], BF16) for q in range(4)]
    oall = dt_("i_oall", [4 * D, TOK], BF16)
    out = nc.dram_tensor("out", [TOK, D], F32, kind="ExternalOutput").ap()
    jq = nc.sync.partition_id() % 4

    def own_dst(ts):
        return lambda g: ts[g].rearrange("(k p) t -> p k t", p=128)

    def all_src(ts):
        return lambda G: ts[G // 4].rearrange("(r k p) t -> p r k t", r=4, p=128)[:, G % 4, :, :]

    def dyn_src(t):
        v = t.rearrange("(q k p) t -> p (q k) t", q=4, p=128)
        return lambda g: v[:, g * 8:(g + 1) * 8, bass.ds(jq * 512, 512)]

    def gather_h(own, al, nm):
        return lambda g: S.collective("AllGather", [own[g][:, :]], [al[g][:, :]], GROUPS, reads=[nm + "own%d" % g], writes=[nm + "all%d" % g])

    def gather_q(own, al, nm):
        return lambda q: S.collective("AllGather", [own[q][:, :]], [al[q * D:(q + 1) * D, :]], GROUPS, reads=[nm + "own%d" % q], writes=[nm + "all%d" % q])

    modown = dt_("i_modown", [128, MC], F32)
    modall = dt_("i_modall", [512, MC], F32)
    with kb.phase("M_"):
        emit_mods(kb, modown)
        S.collective("AllGather", [modown[:, :]], [modall[:, :]], GROUPS, reads=["modown"], writes=["modall"])
    MODS = (modall, "modall")
    if upto <= 0:
        return kb.finish(["modall"])
    with kb.phase("P0_"):
        emit_tok(kb, "P0", dict(mods=MODS, nlayer=0, hT_dst=own_dst(h0own), hT_key=lambda g: "h0own%d" % g, on_hT=gather_h(h0own, h0all, "h0")))
    if upto <= 1:
        return kb.finish(["h0own%d" % g for g in range(4)])
    if upto <= 2:
        return kb.finish(["h0all%d" % g for g in range(4)])
    with kb.phase("A_"):
        emit_A(kb, dict(hT_src=all_src(h0all), hT_key=lambda G: "h0all%d" % (G // 4),
                        mo_dst=(lambda G: moown[G // 4].rearrange("(r p) t -> p r t", p=128)[:, :, (G % 4) * 512:(G % 4 + 1) * 512]),
                        mo_key=lambda G: "moown%d" % (G // 4), on_mo=gather_q(moown, moall, "mo")))
    if upto <= 3:
        return kb.finish(["moown%d" % g for g in range(4)])
    if upto <= 4:
        return kb.finish(["moall%d" % q for q in range(4)])
    with kb.phase("B_"):
        emit_tok(kb, "PB", dict(mods=MODS, layer=0, nlayer=1, mT_src=dyn_src(moall), mT_key=(lambda g: "moall%d" % g), hT_dst=own_dst(h1own), hT_key=lambda g: "h1own%d" % g, on_hT=gather_h(h1own, h1all, "h1"),
                                xo_dst=(lambda ti: x1[ti * 128:(ti + 1) * 128, :]), xo_key="x1"))
    if upto <= 5:
        return kb.finish(["x1"] + ["h1own%d" % g for g in range(4)])
    with kb.phase("C_"):
        emit_C(kb, dict(hT_src=all_src(h1all), hT_key=lambda G: "h1all%d" % (G // 4),
                        oT_dst=(lambda p, G: oown[G // 4][p * 128:(p + 1) * 128, (G % 4) * 512:(G % 4 + 1) * 512]),
                        oT_key=lambda G: "oown%d" % (G // 4), on_oT=gather_q(oown, oall, "o")))
    if upto <= 7:
        return kb.finish(["oown%d" % g for g in range(4)])
    with kb.phase("D_"):
        emit_tok(kb, "PD", dict(mods=MODS, layer=1, mT_src=dyn_src(oall), mT_key=(lambda g: "oall%d" % g), x_src=(lambda ti: x1[ti * 128:(ti + 1) * 128, :]), x_key="x1",
                                xo_dst=(lambda ti: out[ti * 128:(ti + 1) * 128, :]), xo_key="out"))
    return kb.finish(["out"])


def _bc(v, n=128):
    return np.ascontiguousarray(np.broadcast_to(np.asarray(v, np.float32), (n, v.shape[-1])))


def _inputs(x, c, norm_g, ada_w, ada_b, hgrn_lb, even_w_in, hgrn_onorm_g, pool_w, pool_scale,
            even_w_out, odd_w_in, fox_b_f, fox_qnorm_g, fox_knorm_g, odd_w_out):
    f = lambda a: np.asarray(a, np.float32)
    return dict(x=f(x), c=f(c), norm_g=f(norm_g), ada_w=f(ada_w), ada_b=f(ada_b), hgrn_lb=f(hgrn_lb), even_w_in=f(even_w_in),
                hgrn_onorm_g=f(hgrn_onorm_g), pool_w=f(pool_w), pool_scale=f(pool_scale), even_w_out=f(even_w_out),
                odd_w_in=f(odd_w_in), fox_b_f=f(fox_b_f), fox_qnorm_g=f(fox_qnorm_g), fox_knorm_g=f(fox_knorm_g), odd_w_out=f(odd_w_out))


def kernel(_upto=99, **kw):
    inp = _inputs(**kw)
    if "F" not in _PROGS:
        _PROGS["F"] = build_fused(_upto)
    ident = np.ascontiguousarray(np.eye(128, dtype=np.float32).astype(NPBF))
    perm = np.concatenate([np.concatenate([np.arange(g * 128, (g + 1) * 128), 512 + np.arange(g * 128, (g + 1) * 128)]) for g in range(4)])
    wout0 = np.ascontiguousarray(inp["even_w_out"][0][perm, :])
    wout1 = np.ascontiguousarray(inp["odd_w_out"][0])
    aw, ab = inp["ada_w"], inp["ada_b"]
    shared = {
        "P0_ng": _bc(inp["norm_g"][0]), "B_wout": wout0, "B_ng": _bc(inp["norm_g"][1]), "D_wout": wout1,
        "ident": ident,
    }
    maps = []
    Wall = np.concatenate([aw[0], aw[1]], axis=1)
    ball = np.concatenate([ab[0], ab[1]])
    for core in range(NCORES):
        b, j = core // 4, core % 4
        m = dict(shared)
        m["x"] = np.ascontiguousarray(inp["x"][b].reshape(16, 512, D)[j::4].reshape(TOK, D))
        m["cT"] = np.ascontiguousarray(inp["c"][b].reshape(8, 128).T)
        m["M_w"] = np.ascontiguousarray(Wall[:, j * MC:(j + 1) * MC])
        m["M_b"] = _bc(ball[j * MC:(j + 1) * MC])
        pa = prep_A(inp, core, None)
        pc = prep_C(inp, core, None)
        for k, v in pa.items():
            if k != "hT":
                m["A_" + k] = v
        for k, v in pc.items():
            if k != "hT":
                m["C_" + k] = v
        maps.append(m)
    res = run_bass_kernel_spmd(_PROGS["F"], maps, core_ids=list(range(NCORES)))
    r = res.results
    out = np.empty((2, 16, 512, D), np.float32)
    for b in range(2):
        for j in range(4):
            out[b, j::4] = np.asarray(r[b * 4 + j]["out"], np.float32).reshape(4, 512, D)
    return np.ascontiguousarray(out.reshape(2, T, D))
```

```python
import numpy as np
import ml_dtypes
from contextlib import ExitStack
import concourse.bass as bass
import concourse.mybir as mybir
from concourse.bass_utils import run_bass_kernel_spmd

F32 = mybir.dt.float32
BF16 = mybir.dt.bfloat16
AF = mybir.ActivationFunctionType
ALU = mybir.AluOpType
NPBF = ml_dtypes.bfloat16

T = 8192
D = 1024
TOK = 2048
EPS = 1e-6
NCORES = 8
CC_QOS = "P3"


class Sched:
    def __init__(self, nc, es, n_lanes=16):
        self.nc = nc
        self.engs = {"pe": nc.tensor, "act": nc.scalar, "dve": nc.vector, "pool": nc.gpsimd, "sp": nc.sync}
        self.sem = {}
        self.cnt = {}
        for n in ["pe", "act", "dve", "pool", "cc"]:
            self.sem[n] = es.enter_context(nc.semaphore("s_" + n))
            self.cnt[n] = 0
        self.lanes = []
        for i in range(n_lanes):
            nm = "lane%d" % i
            self.sem[nm] = es.enter_context(nc.semaphore("s_" + nm))
            self.cnt[nm] = 0
            self.lanes.append(nm)
        self.planes = []
        for i in range(4):
            nm = "plane%d" % i
            self.sem[nm] = es.enter_context(nc.semaphore("s_" + nm))
            self.cnt[nm] = 0
            self.planes.append(nm)
        self.lane_rr = 0
        self.plane_rr = 0
        self.waited = {n: {} for n in self.engs}
        self.lastw = {}
        self.readers = {}

    def _need(self, reads, writes):
        need = {}

        def add(ev):
            if ev is None:
                return
            s, v = ev
            if need.get(s, 0) < v:
                need[s] = v

        for k in reads:
            add(self.lastw.get(k))
        for k in writes:
            add(self.lastw.get(k))
            for ev in self.readers.get(k, []):
                add(ev)
        return need

    def _emit_waits(self, e, need):
        eng = self.engs[e]
        for s, v in need.items():
            if self.waited[e].get(s, 0) >= v:
                continue
            eng.wait_ge(self.sem[s], v)
            self.waited[e][s] = v

    def _record(self, ev, reads, writes):
        for k in reads:
            lst = self.readers.setdefault(k, [])
            lst.append(ev)
            if len(lst) > 64:
                mx = {}
                for s, v in lst:
                    mx[s] = max(mx.get(s, 0), v)
                self.readers[k] = list(mx.items())
        for k in writes:
            self.lastw[k] = ev
            self.readers[k] = []

    def op(self, e, fn, reads=(), writes=()):
        need = self._need(reads, writes)
        if e == "pe":
            need.pop("pe", None)
        self._emit_waits(e, need)
        ins = fn(self.engs[e])
        self.cnt[e] += 1
        ins.then_inc(self.sem[e], 1)
        self._record((e, self.cnt[e]), reads, writes)
        return ins

    def dma(self, q, out, in_, reads=(), writes=(), **kw):
        if q == "pool":
            lane = self.planes[self.plane_rr % len(self.planes)]
            self.plane_rr += 1
        else:
            lane = self.lanes[self.lane_rr % len(self.lanes)]
            self.lane_rr += 1
        need = self._need(reads, writes)
        if self.cnt[lane] > 0:
            need[lane] = max(need.get(lane, 0), self.cnt[lane])
        self._emit_waits(q, need)
        ins = self.engs[q].dma_start(out=out, in_=in_, **kw)
        self.cnt[lane] += 16
        ins.then_inc(self.sem[lane], 16)
        self._record((lane, self.cnt[lane]), reads, writes)
        return ins

    def barrier(self):
        for e in self.engs:
            need = {n: c for n, c in self.cnt.items() if c > 0 and n != "cc"}
            self._emit_waits(e, need)

    def collective(self, kind, ins, outs, groups, reads=(), writes=()):
        need = self._need(reads, writes)
        self._emit_waits("pool", need)
        ins_ = self.nc.gpsimd.collective_compute(kind, ALU.bypass, replica_groups=groups, ins=ins, outs=outs, dma_qos=CC_QOS)
        self.cnt["cc"] += 1
        ins_.then_inc(self.sem["cc"], 1)
        self._record(("cc", self.cnt["cc"]), reads, writes)
        return ins_

    def wait_all(self, e, keys):
        need = self._need(keys, keys)
        self._emit_waits(e, need)


class _Phase:
    def __init__(self, kb, tag):
        self.kb, self.tag = kb, tag

    def __enter__(self):
        self.kb.tag = self.tag
        self.kb.pes = ExitStack()
        return self

    def __exit__(self, *a):
        self.kb.S.barrier()
        self.kb.pes.close()
        self.kb.pes = self.kb.es
        self.kb.tag = ""
        return False


class BankView:
    def __init__(self, t, off):
        self.t, self.off = t, off

    def __getitem__(self, key):
        rows, cols = key
        a = self.off + (cols.start or 0)
        b = self.off + (512 if cols.stop is None else cols.stop)
        return self.t[rows, a:b]


class KB:
    def __init__(self):
        self.nc = bass.Bass("TRN2", target_bir_lowering=False)
        self.es = ExitStack()
        self.pes = self.es
        self.S = Sched(self.nc, self.es)
        self.outs = []
        self.tag = ""
        self.shared = {}
        self._banks = None

    def phase(self, tag):
        return _Phase(self, tag)

    def din(self, name, shape, dt, shared=False):
        if shared:
            if name not in self.shared:
                self.shared[name] = self.nc.dram_tensor(name, list(shape), dt, kind="ExternalInput").ap()
            return self.shared[name]
        return self.nc.dram_tensor(self.tag + name, list(shape), dt, kind="ExternalInput").ap()

    def dout(self, name, shape, dt):
        self.outs.append(self.tag + name)
        return self.nc.dram_tensor(self.tag + name, list(shape), dt, kind="ExternalOutput").ap()

    def sb(self, name, shape, dt):
        return self.pes.enter_context(self.nc.sbuf_tensor("sb_" + self.tag + name, list(shape), dt))

    def banks(self):
        if self._banks is None:
            self.dbanks = [self.es.enter_context(self.nc.psum_tensor("dbank%d" % i, [128, 1024], F32)) for i in range(4)]
            self._banks = [BankView(self.dbanks[i // 2], (i % 2) * 512) for i in range(8)]
        return self._banks

    def finish(self, out_keys=None):
        self.S.wait_all("sp", out_keys if out_keys is not None else self.outs)
        self.es.close()
        return self.nc


def emit_cond_rep(kb, cT_d):
    S = kb.S
    cT = kb.sb("cT", [128, 8], F32)
    cE = kb.sb("cE", [128, 8], F32)
    cond = kb.sb("cond", [128, 8], F32)
    ones = kb.sb("ones128", [128, 128], F32)
    crep = kb.sb("cond_rep", [128, 8, 128], F32)
    S.dma("sp", cT[:], cT_d[:, :], writes=["cT"])
    S.op("dve", lambda e: e.memset(ones[:], 1.0), writes=["ones128"])
    S.op("act", lambda e: e.activation(out=cE[:], in_=cT[:], func=AF.Exp, scale=-1.0), reads=["cT"], writes=["cE"])
    S.op("dve", lambda e: e.tensor_scalar_add(out=cE[:], in0=cE[:], scalar1=1.0), reads=["cE"], writes=["cE"])
    S.op("dve", lambda e: e.reciprocal(out=cE[:], in_=cE[:]), reads=["cE"], writes=["cE"])
    S.op("dve", lambda e: e.tensor_mul(out=cond[:], in0=cT[:], in1=cE[:]), reads=["cE", "cT"], writes=["cond"])
    for k in range(8):
        S.op("dve", lambda e, k=k: e.tensor_scalar(out=crep[:, k, :], in0=ones[:], scalar1=cond[:, k:k + 1], scalar2=None, op0=ALU.mult),
             reads=["cond", "ones128"], writes=["crep"])
    return crep


def emit_mod_bcast(kb, crep, w_d, b_d, out_t, out_key, ncols, banks, stage, tag, b0=0, c0=0):
    S = kb.S
    nb = ncols // 512
    S.dma("sp", out_t[:, 0:ncols], b_d[:, c0:c0 + ncols], writes=[out_key])
    idx = 0
    for k in range(8):
        st = stage[k % len(stage)]
        skey = "%s_st%d" % (tag, k % len(stage))
        S.dma("sp", st[:, 0:ncols], w_d[k * 128:(k + 1) * 128, c0:c0 + ncols], writes=[skey])
        for n in range(nb):
            S.op("pe", lambda e, n=n, k=k, st=st: e.matmul(banks[b0 + n][:, :], lhsT=crep[:, k, :], rhs=st[:, n * 512:(n + 1) * 512], start=(k == 0), stop=(k == 7)),
                 reads=["crep", skey], writes=["bank%d" % (b0 + n)])
    for n in range(nb):
        S.op("dve", lambda e, n=n: e.tensor_tensor(out=out_t[:, n * 512:(n + 1) * 512], in0=out_t[:, n * 512:(n + 1) * 512], in1=banks[b0 + n][:, :], op=ALU.add),
             reads=["bank%d" % (b0 + n), out_key], writes=[out_key])


MC = 1536


def emit_mods(kb, modown_d):
    S = kb.S
    cT_d = kb.din("cT", [128, 8], F32, shared=True)
    w_d = kb.din("w", [D, MC], F32)
    b_d = kb.din("b", [128, MC], F32)
    banks = kb.banks()
    crep = emit_cond_rep(kb, cT_d)
    stage = [kb.sb("stg%d" % i, [128, MC], F32) for i in range(4)]
    out_t = kb.sb("mo", [128, MC], F32)
    emit_mod_bcast(kb, crep, w_d, b_d, out_t, "mo", MC, banks, stage, "g", b0=0, c0=0)
    S.dma("sp", modown_d[:, :], out_t[:], reads=["mo"], writes=["modown"])


def load_mods(kb, dst, dst_key, modall_d, c0, n):
    S = kb.S
    c = c0
    while c < c0 + n:
        r, lo = c // MC, c % MC
        m = min(MC - lo, c0 + n - c)
        S.dma("sp", dst[:, c - c0:c - c0 + m], modall_d[r * 128:(r + 1) * 128, lo:lo + m], reads=["modall"], writes=[dst_key])
        c += m


def build_tok(mode):
    kb = KB()
    emit_tok(kb, mode, {})
    return kb.finish(["hT", "xo"])


def emit_tok(kb, mode, io):
    S = kb.S
    do_proj = mode in ("PB", "PD")
    do_norm = mode in ("P0", "PB")
    x_d = kb.din("x", [TOK, D], F32, shared=True) if "x_src" not in io else None
    cT_d = kb.din("cT", [128, 8], F32, shared=True)
    ident_d = kb.din("ident", [128, 128], BF16, shared=True)
    if do_proj:
        mT_d = kb.din("mT", [D, TOK], BF16) if "mT_src" not in io else None
        wout_d = kb.din("wout", [D, D], F32)
        gw_d = kb.din("gw", [D, D], F32) if "mods" not in io else None
        gb_d = kb.din("gb", [128, D], F32) if "mods" not in io else None
    if do_norm:
        nw_d = kb.din("nw", [D, 2 * D], F32) if "mods" not in io else None
        nb_d = kb.din("nb", [128, 2 * D], F32) if "mods" not in io else None
        ng_d = kb.din("ng", [128, D], F32)
        hT_d = kb.dout("hT", [D, TOK], BF16) if "hT_dst" not in io else None
    if mode != "P0":
        xo_d = kb.dout("xo", [TOK, D], F32) if "xo_dst" not in io else None
    banks = kb.banks()
    mods = io.get("mods")
    if mods is None:
        crep = emit_cond_rep(kb, cT_d)
    stage = [kb.sb("stg%d" % i, [128, 2048 if mods is None else 1024], F32) for i in range(3)]
    ident = kb.sb("ident", [128, 128], BF16)
    S.dma("sp", ident[:], ident_d[:, :], writes=["ident"])
    if do_proj:
        gate = kb.sb("gate", [128, D], F32)
        if mods is None:
            emit_mod_bcast(kb, crep, gw_d, gb_d, gate, "gate", D, banks, stage, "g")
        else:
            load_mods(kb, gate, "gate", mods[0], io["layer"] * 3 * D + 2 * D, D)
        Wp = kb.sb("Wp", [128, 8, D], BF16)
        for k in range(8):
            st = stage[k % 3]
            skey = "g_st%d" % (k % 3)
            S.dma("sp", st[:, 0:D], wout_d[k * 128:(k + 1) * 128, :], writes=[skey])
            S.op("dve", lambda e, k=k, st=st: e.tensor_tensor(out=Wp[:, k, :], in0=st[:, 0:D], in1=gate[:], op=ALU.mult),
                 reads=[skey, "gate"], writes=["Wp"])
    if do_norm:
        modn = kb.sb("modn", [128, 2 * D], F32)
        if mods is None:
            emit_mod_bcast(kb, crep, nw_d, nb_d, modn, "modn", 2 * D, banks, stage, "g", b0=4)
        else:
            load_mods(kb, modn, "modn", mods[0], io["nlayer"] * 3 * D, 2 * D)
        Gb = kb.sb("Gb", [128, D], F32)
        S.dma("sp", Gb[:], ng_d[:, :], writes=["Gb"])
        S.op("dve", lambda e: e.scalar_tensor_tensor(out=Gb[:], in0=modn[:, D:2 * D], scalar=1.0, in1=Gb[:], op0=ALU.add, op1=ALU.mult),
             reads=["modn", "Gb"], writes=["Gb"])
    xt = [kb.sb("xt%d" % i, [128, D], F32) for i in range(TOK // 128)]
    if do_proj:
        mTt = [kb.sb("mTt%d" % i, [128, 8, 512], BF16) for i in range(2)]
    if do_norm:
        junk = kb.sb("junk", [128, D], F32)
        ss = [kb.sb("ss%d" % i, [128, 4], F32) for i in range(2)]
        tt = kb.sb("tt", [128, D], F32)
        hb = [kb.sb("hb%d" % i, [128, D], BF16) for i in range(2)]
        hst = [kb.sb("hst%d" % i, [128, 8, 512], BF16) for i in range(2)]
        epsb = kb.sb("epsb", [128, 1], F32)
        S.op("dve", lambda e: e.memset(epsb[:], EPS), writes=["epsb"])
    if do_proj:
        if "mT_src" in io:
            mT_src, mT_key = io["mT_src"], io["mT_key"]
        else:
            mT_v = mT_d.rearrange("(k p) t -> p k t", p=128)
            mT_src, mT_key = (lambda g: mT_v[:, :, g * 512:(g + 1) * 512]), "mT_ext"
    if do_norm:
        if "hT_dst" in io:
            hT_dst, hT_key = io["hT_dst"], io["hT_key"]
        else:
            hT_v = hT_d.rearrange("(k p) t -> p k t", p=128)
            hT_dst, hT_key = (lambda g: hT_v[:, :, g * 512:(g + 1) * 512]), "hT"
    if "x_src" in io:
        x_src, x_key = io["x_src"], io["x_key"]
    else:
        x_src, x_key = (lambda ti: x_d[ti * 128:(ti + 1) * 128, :]), "x_ext"
    if mode != "P0":
        if "xo_dst" in io:
            xo_dst, xo_key = io["xo_dst"], io["xo_key"]
        else:
            xo_dst, xo_key = (lambda ti: xo_d[ti * 128:(ti + 1) * 128, :]), "xo"
    NT = TOK // 128

    def part1(ti):
        g, j = ti // 4, ti % 4
        if do_proj:
            mt = mTt[g % 2]
            mkey = "mTt%d" % (g % 2)
            mk = (lambda gg: mT_key(gg)) if callable(mT_key) else (lambda gg: mT_key)
            if j == 0 and g == 0:
                S.dma("sp", mTt[0][:], mT_src(0), reads=[mk(0)], writes=["mTt0"])
            if j == 2 and g + 1 < TOK // 512:
                S.dma("sp", mTt[(g + 1) % 2][:], mT_src(g + 1), reads=[mk(g + 1)], writes=["mTt%d" % ((g + 1) % 2)])
        x_ = xt[ti]
        xkey = "xt%d" % ti
        if ti == 0:
            for t2 in range(NT):
                S.dma("sp", xt[t2][:], x_src(t2), reads=[x_key], writes=["xt%d" % t2])
        cur = x_
        curkey = xkey
        if do_proj:
            for half in range(2):
                bk = banks[half]
                for k in range(8):
                    S.op("pe", lambda e, k=k, half=half, bk=bk, mt=mt, j=j: e.matmul(bk[:, :], lhsT=mt[:, k, j * 128:(j + 1) * 128], rhs=Wp[:, k, half * 512:(half + 1) * 512], start=(k == 0), stop=(k == 7)),
                         reads=[mkey, "Wp"], writes=["bank%d" % half])
            x1 = x_
            x1key = xkey
            for half in range(2):
                S.op("dve", lambda e, half=half, x1=x1, x_=x_: e.tensor_tensor(out=x1[:, half * 512:(half + 1) * 512], in0=x_[:, half * 512:(half + 1) * 512], in1=banks[half][:, :], op=ALU.add),
                     reads=[xkey, "bank%d" % half], writes=[x1key])
            S.dma("sp", xo_dst(ti), x1[:], reads=[x1key], writes=[xo_key])
            cur = x1
            curkey = x1key
        if do_norm:
            s_ = ss[ti % 2]
            skey = "ss%d" % (ti % 2)
            S.op("act", lambda e, cur=cur, s_=s_: e.activation(out=junk[:], in_=cur[:], func=AF.Square, accum_out=s_[:, 0:1]),
                 reads=[curkey], writes=["junk", skey])
            S.op("act", lambda e, s_=s_: e.activation(out=s_[:, 1:2], in_=s_[:, 0:1], func=AF.Ln, scale=1.0 / D, bias=epsb[:, 0:1]),
                 reads=[skey, "epsb"], writes=[skey])
            S.op("act", lambda e, s_=s_: e.activation(out=s_[:, 2:3], in_=s_[:, 1:2], func=AF.Exp, scale=-0.5),
                 reads=[skey], writes=[skey])
            S.op("dve", lambda e, cur=cur, s_=s_: e.scalar_tensor_tensor(out=tt[:], in0=cur[:], scalar=s_[:, 2:3], in1=Gb[:], op0=ALU.mult, op1=ALU.mult),
                 reads=[curkey, skey, "Gb"], writes=["tt"])
            h_ = hb[ti % 2]
            hkey = "hb%d" % (ti % 2)
            S.op("dve", lambda e, h_=h_: e.tensor_tensor(out=h_[:], in0=tt[:], in1=modn[:, 0:D], op=ALU.add),
                 reads=["tt", "modn"], writes=[hkey])

    def part2(ti):
        if not do_norm:
            return
        g, j = ti // 4, ti % 4
        h_ = hb[ti % 2]
        hkey = "hb%d" % (ti % 2)
        pbank = banks[2 + (ti % 2)]
        pkey = "bank%d" % (2 + (ti % 2))
        pT = pbank[:, :].bitcast(BF16)
        for k in range(8):
            S.op("pe", lambda e, k=k, h_=h_, pT=pT: e.transpose(pT[:, k * 128:(k + 1) * 128], h_[:, k * 128:(k + 1) * 128], ident[:]),
                 reads=[hkey, "ident"], writes=[pkey])
        hs = hst[g % 2]
        hskey = "hst%d" % (g % 2)
        S.op("act", lambda e, hs=hs, pT=pT, j=j: e.activation(out=hs[:, :, j * 128:(j + 1) * 128], in_=pT.rearrange("p (k t) -> p k t", k=8), func=AF.Copy),
             reads=[pkey], writes=[hskey])
        if j == 3:
            S.dma("sp", hT_dst(g), hs[:], reads=[hskey], writes=[hT_key(g) if callable(hT_key) else hT_key])
            if "on_hT" in io:
                io["on_hT"](g)

    for ti in range(NT + 1):
        if ti < NT:
            part1(ti)
        if ti >= 1:
            part2(ti - 1)

def build_C(ngroups=16):
    kb = KB()
    emit_C(kb, {}, ngroups)
    return kb.finish(["oT"])


def emit_C(kb, io, ngroups=16):
    S = kb.S
    hT_d = kb.din("hT", [D, T], BF16) if "hT_src" not in io else None
    w_d = kb.din("w", [D, 1028], F32)
    cst_d = kb.din("cst", [128, 16], F32)
    cb_d = kb.din("cb", [128, 384], BF16)
    oT_d = kb.dout("oT", [256, T], BF16) if "oT_dst" not in io else None
    banks = kb.banks()
    if "hT_src" in io:
        hT_src, hT_key = io["hT_src"], io["hT_key"]
    else:
        hT_v = hT_d.rearrange("(k p) t -> p k t", p=128)
        hT_src, hT_key = (lambda G: hT_v[:, :, G * 512:(G + 1) * 512]), "hT_ext"
    if "oT_dst" in io:
        oT_dst, oT_key = io["oT_dst"], io["oT_key"]
    else:
        oT_dst, oT_key = (lambda p, G: oT_d[p * 128:(p + 1) * 128, G * 512:(G + 1) * 512]), "oT"

    cst = kb.sb("cst", [128, 16], F32)
    cb = kb.sb("cb", [128, 384], BF16)
    S.dma("sp", cst[:], cst_d[:, :], writes=["cst"])
    S.dma("sp", cb[:], cb_d[:, :], writes=["cb"])
    ident = cb[:, 0:128]
    bd64 = cb[:, 128:256]
    tri = cb[:, 256:384]
    negbf = kb.sb("negbf", [128, 1], F32)
    S.op("dve", lambda e: e.tensor_scalar(out=negbf[:], in0=cst[:, 2:3], scalar1=-1.0, scalar2=None, op0=ALU.mult), reads=["cst"], writes=["negbf"])
    eps64 = kb.sb("eps64", [128, 2], F32)
    S.op("dve", lambda e: e.memset(eps64[:, 0:1], 64.0 * EPS), writes=["eps64"])
    S.op("dve", lambda e: e.memset(eps64[:, 1:2], EPS), writes=["eps64"])
    ones512 = kb.sb("ones512", [128, 512], F32)
    S.op("dve", lambda e: e.memset(ones512[:], 1.0), writes=["ones512"])

    Wb = kb.sb("Wb", [128, 8, 1028], BF16)
    wst = [kb.sb("wst%d" % i, [128, 1028], F32) for i in range(2)]
    for k in range(8):
        st = wst[k % 2]
        S.dma("sp", st[:], w_d[k * 128:(k + 1) * 128, :], writes=["wst%d" % (k % 2)])
        S.op("dve", lambda e, k=k, st=st: e.tensor_copy(out=Wb[:, k, :], in_=st[:]), reads=["wst%d" % (k % 2)], writes=["Wb"])
    Wrep = kb.sb("Wrep", [128, 8, 128], BF16)
    for h in range(4):
        S.op("dve", lambda e, h=h: e.tensor_copy(out=Wrep[:, :, 32 * h:32 * h + 32], in_=Wb[:, :, 1024 + h:1025 + h].to_broadcast([128, 8, 32])),
             reads=["Wb"], writes=["Wrep"])

    hTg = [kb.sb("hTg%d" % i, [128, 8, 512], BF16) for i in range(2)]
    QT = [kb.sb("QT%d" % i, [128, T], BF16) for i in range(2)]
    KT = [kb.sb("KT%d" % i, [128, T], BF16) for i in range(2)]
    Vall = kb.sb("Vall", [128, 64, 192], BF16)
    V = [Vall[:, :, 0:128], Vall[:, :, 64:192]]
    Gt = kb.sb("Gt", [128, T], BF16)
    sqbs = {n: kb.sb("sqb" + n, [128, 512], BF16) for n in ("q", "k")}
    lnvs = {n: kb.sb("lnv" + n, [128, 512], F32) for n in ("q", "k")}
    rss = {n: kb.sb("rs" + n, [128, 512], F32) for n in ("q", "k")}
    ge = kb.sb("ge", [128, 512], F32)
    fe = kb.sb("fe", [128, 512], F32)
    cl = [kb.sb("cl%d" % i, [128, 512], F32) for i in range(2)]
    c1b = kb.sb("c1b", [128, 512], BF16)
    c2b = kb.sb("c2b", [128, 512], BF16)
    c3b = kb.sb("c3b", [128, 512], BF16)
    r1 = kb.sb("r1", [128, 512], F32)
    r2 = kb.sb("r2", [128, 512], F32)
    KA = kb.sb("KA", [128, 512], BF16)
    QA = kb.sb("QA", [128, 512], BF16)
    r1b = kb.sb("r1b", [128, 512], BF16)
    r2b = kb.sb("r2b", [128, 512], BF16)
    AUG = kb.sb("AUG", [128, T], BF16)
    Pt = [kb.sb("P%d" % i, [128, 1024], BF16) for i in range(3)]
    rec = kb.sb("rec", [128, 512], F32)
    tmp = kb.sb("tmp", [128, 512], F32)
    OT = [kb.sb("OT%d" % i, [128, 512], BF16) for i in range(3)]

    S.op("dve", lambda e: e.memset(Vall[:, :, 64:128], 1.0), writes=["V0ones", "V1ones"])

    bstate = {"i": 0}

    def nextbank():
        b = bstate["i"] % 8
        bstate["i"] += 1
        return b

    for p in range(2):
        for G in range(ngroups):
            t0 = G * 512
            hg = hTg[G % 2]
            hkey = "hTg%d" % (G % 2)
            S.dma("sp", hg[:], hT_src(G), reads=[hT_key(G) if callable(hT_key) else hT_key], writes=[hkey])
            pb = {}
            for name, blk in (("q", 0), ("k", 1), ("g", 3)):
                b = nextbank()
                pb[name] = b
                c0 = p * 512 + blk * 128
                for k in range(8):
                    S.op("pe", lambda e, k=k, b=b, c0=c0, hg=hg: e.matmul(banks[b][:, :], lhsT=Wb[:, k, c0:c0 + 128], rhs=hg[:, k, :], start=(k == 0), stop=(k == 7)),
                         reads=["Wb", hkey], writes=["bank%d" % b])
            bv = nextbank()
            bvk = "bank%d" % bv
            c0 = p * 512 + 256
            for j in range(4):
                for k in range(8):
                    S.op("pe", lambda e, k=k, j=j, bv=bv, c0=c0, hg=hg: e.matmul(banks[bv][:, j * 128:(j + 1) * 128], lhsT=hg[:, k, j * 128:(j + 1) * 128], rhs=Wb[:, k, c0:c0 + 128], start=(k == 0), stop=(k == 7)),
                         reads=["Wb", hkey], writes=[bvk])
            if p == 0:
                bf = nextbank()
                bfk = "bank%d" % bf
                for k in range(8):
                    S.op("pe", lambda e, k=k, bf=bf, hg=hg: e.matmul(banks[bf][:, :], lhsT=Wrep[:, k, :], rhs=hg[:, k, :], start=(k == 0), stop=(k == 7)),
                         reads=["Wrep", hkey], writes=[bfk])
            bm = {}
            for name in ("q", "k"):
                b = pb[name]
                S.op("act", lambda e, b=b, name=name: e.activation(out=sqbs[name][:], in_=banks[b][:, :], func=AF.Square), reads=["bank%d" % b], writes=["sqb" + name])
            for name in ("q", "k"):
                bm[name] = nextbank()
                S.op("pe", lambda e, name=name: e.matmul(banks[bm[name]][:, :], lhsT=bd64, rhs=sqbs[name][:], start=True, stop=True), reads=["cb", "sqb" + name], writes=["bank%d" % bm[name]])
            S.op("act", lambda e: e.activation(out=lnvs["q"][:], in_=banks[bm["q"]][:, :], func=AF.Ln, scale=64.0, bias=eps64[:, 0:1]), reads=["bank%d" % bm["q"], "eps64"], writes=["lnvq"])
            S.op("act", lambda e: e.activation(out=lnvs["k"][:], in_=banks[bm["k"]][:, :], func=AF.Ln, scale=1.0, bias=eps64[:, 1:2]), reads=["bank%d" % bm["k"], "eps64"], writes=["lnvk"])
            for name in ("q", "k"):
                S.op("act", lambda e, name=name: e.activation(out=rss[name][:], in_=lnvs[name][:], func=AF.Exp, scale=-0.5), reads=["lnv" + name], writes=["rs" + name])
            for name, blk, dst, dn in (("q", 0, QT, "QT"), ("k", 1, KT, "KT")):
                b = pb[name]
                for hl in range(2):
                    pr = slice(hl * 64, hl * 64 + 64)
                    S.op("dve", lambda e, hl=hl, pr=pr, b=b, dst=dst, blk=blk, name=name: e.scalar_tensor_tensor(out=dst[hl][0:64, t0:t0 + 512], in0=banks[b][pr, :], scalar=cst[pr, blk:blk + 1], in1=rss[name][pr, :], op0=ALU.mult, op1=ALU.mult),
                         reads=["bank%d" % b, "cst", "rs" + name], writes=[(dn, hl, G)])
            bg = pb["g"]
            bgk = "bank%d" % bg
            S.op("act", lambda e, bg=bg: e.activation(out=ge[:], in_=banks[bg][:, :], func=AF.Tanh, scale=0.5), reads=[bgk], writes=["ge"])
            S.op("dve", lambda e: e.tensor_scalar(out=ge[:], in0=ge[:], scalar1=0.5, scalar2=0.5, op0=ALU.mult, op1=ALU.add), reads=["ge"], writes=["ge"])
            S.op("dve", lambda e, bg=bg: e.tensor_tensor(out=Gt[:, t0:t0 + 512], in0=banks[bg][:, :], in1=ge[:], op=ALU.mult), reads=[bgk, "ge"], writes=[("Gt", G)])
            bvv = banks[bv][:, :].rearrange("p (j c) -> p j c", j=4)
            S.op("dve", lambda e, bvv=bvv: e.tensor_copy(out=V[0][:, 4 * G:4 * G + 4, 0:64], in_=bvv[:, :, 0:64]), reads=[bvk], writes=[("V", 0, G)])
            S.op("dve", lambda e, bvv=bvv: e.tensor_copy(out=V[1][:, 4 * G:4 * G + 4, 64:128], in_=bvv[:, :, 64:128]), reads=[bvk], writes=[("V", 1, G)])
            if p == 0:
                S.op("act", lambda e, bf=bf: e.activation(out=fe[:], in_=banks[bf][:, :], func=AF.Exp, scale=-1.0, bias=negbf[:, 0:1]), reads=[bfk, "negbf"], writes=["fe"])
                S.op("act", lambda e: e.activation(out=fe[:], in_=fe[:], func=AF.Ln, scale=1.0, bias=ones512[:, 0:1]), reads=["fe", "ones512"], writes=["fe"])
                c_ = cl[G % 2]
                ck = "cl%d" % (G % 2)
                if G == 0:
                    S.op("dve", lambda e, c_=c_: e.tensor_tensor_scan(out=c_[:], data0=ones512[:], data1=fe[:], initial=0.0, op0=ALU.mult, op1=ALU.add),
                         reads=["fe", "ones512"], writes=[ck])
                else:
                    cp = cl[(G - 1) % 2]
                    S.op("dve", lambda e, c_=c_, cp=cp: e.tensor_tensor_scan(out=c_[:], data0=ones512[:], data1=fe[:], initial=cp[:, 511:512], op0=ALU.mult, op1=ALU.add),
                         reads=["fe", "ones512", "cl%d" % ((G - 1) % 2)], writes=[ck])
                S.op("dve", lambda e, c_=c_: e.tensor_copy(out=c1b[:], in_=c_[:]), reads=[ck], writes=["c1b"])
                S.op("dve", lambda e, c_=c_: e.tensor_tensor(out=r1[:], in0=c_[:], in1=c1b[:], op=ALU.subtract), reads=[ck, "c1b"], writes=["r1"])
                S.op("dve", lambda e: e.tensor_copy(out=c2b[:], in_=r1[:]), reads=["r1"], writes=["c2b"])
                S.op("dve", lambda e: e.tensor_tensor(out=r2[:], in0=r1[:], in1=c2b[:], op=ALU.subtract), reads=["r1", "c2b"], writes=["r2"])
                S.op("dve", lambda e: e.tensor_copy(out=c3b[:], in_=r2[:]), reads=["r2"], writes=["c3b"])
                for (dstt, dk, mc) in ((KA, "KA", 3), (QA, "QA", 7)):
                    S.op("dve", lambda e, mc=mc: e.tensor_scalar(out=r1b[:], in0=c1b[:], scalar1=cst[:, mc:mc + 1], scalar2=cst[:, mc + 3:mc + 4], op0=ALU.mult, op1=ALU.add),
                         reads=["c1b", "cst"], writes=["r1b"])
                    S.op("dve", lambda e, mc=mc: e.scalar_tensor_tensor(out=r2b[:], in0=c2b[:], scalar=cst[:, mc + 1:mc + 2], in1=r1b[:], op0=ALU.mult, op1=ALU.add),
                         reads=["c2b", "cst", "r1b"], writes=["r2b"])
                    S.op("dve", lambda e, mc=mc, dstt=dstt: e.scalar_tensor_tensor(out=dstt[:], in0=c3b[:], scalar=cst[:, mc + 2:mc + 3], in1=r2b[:], op0=ALU.mult, op1=ALU.add),
                         reads=["c3b", "cst", "r2b"], writes=[dk])
                for hl in range(2):
                    S.op("dve", lambda e, hl=hl: e.tensor_copy(out=KT[hl][64:70, t0:t0 + 512], in_=KA[32 * hl:32 * hl + 6, :]), reads=["KA"], writes=[("KTa", hl, G)])
                    S.op("dve", lambda e, hl=hl: e.tensor_copy(out=QT[hl][64:70, t0:t0 + 512], in_=QA[32 * hl:32 * hl + 6, :]), reads=["QA"], writes=[("QTa", hl, G)])
                    S.op("dve", lambda e, hl=hl: e.tensor_copy(out=AUG[32 * hl:32 * hl + 6, t0:t0 + 512], in_=KA[64 + 32 * hl:64 + 32 * hl + 6, :]), reads=["KA"], writes=[("AUG", G)])
                    S.op("dve", lambda e, hl=hl: e.tensor_copy(out=AUG[64 + 32 * hl:64 + 32 * hl + 6, t0:t0 + 512], in_=QA[64 + 32 * hl:64 + 32 * hl + 6, :]), reads=["QA"], writes=[("AUG", G)])
        if p == 1:
            TT_ = ngroups * 512
            allG = list(range(ngroups))
            for hl in range(2):
                S.op("dve", lambda e, hl=hl: e.tensor_copy(out=KT[hl][64:70, 0:TT_], in_=AUG[32 * hl:32 * hl + 6, 0:TT_]), reads=[("AUG", G) for G in allG], writes=[("KTa", hl, G) for G in allG])
                S.op("dve", lambda e, hl=hl: e.tensor_copy(out=QT[hl][64:70, 0:TT_], in_=AUG[64 + 32 * hl:64 + 32 * hl + 6, 0:TT_]), reads=[("AUG", G) for G in allG], writes=[("QTa", hl, G) for G in allG])
        units = []
        for G in range(ngroups):
            for hl in range(2):
                us = [(i, i + 1) for i in range(0, 4 * G, 2)] + [(i,) for i in range(4 * G, 4 * G + 4)]
                for n_, u in enumerate(us):
                    units.append((G, hl, u, n_ == len(us) - 1))

        def geom(G, i):
            q_lo = max(G * 512, i * 128)
            return q_lo, (G + 1) * 512 - q_lo

        def QK(ui):
            G, hl, u, _ = units[ui]
            d = 1 + (ui % 3)
            for hh, i in enumerate(u):
                q_lo, n = geom(G, i)
                bi = 2 * d + hh
                S.op("pe", lambda e, i=i, bi=bi, q_lo=q_lo, n=n, hl=hl: e.matmul(banks[bi][:, 0:n], lhsT=KT[hl][0:70, i * 128:(i + 1) * 128], rhs=QT[hl][0:70, q_lo:q_lo + n], start=True, stop=True),
                     reads=[("QT", hl, G), ("QTa", hl, G), ("KT", hl, i // 4), ("KTa", hl, i // 4)], writes=["bank%d" % bi])

        def EXPV(ui):
            G, hl, u, last = units[ui]
            nkb = 4 * G + 4
            ob = hl
            obk = "bank%d" % ob
            d = 1 + (ui % 3)
            P_ = Pt[ui % 3]
            pk = "P%d" % (ui % 3)
            if len(u) == 2:
                S.op("act", lambda e: e.activation(out=P_[:, 0:1024], in_=kb.dbanks[d][:, 0:1024], func=AF.Exp),
                     reads=["bank%d" % (2 * d), "bank%d" % (2 * d + 1)], writes=[pk])
            else:
                q_lo, n = geom(G, u[0])
                S.op("act", lambda e: e.activation(out=P_[:, 0:n], in_=banks[2 * d][:, 0:n], func=AF.Exp), reads=["bank%d" % (2 * d)], writes=[pk])
                S.op("dve", lambda e: e.tensor_tensor(out=P_[:, 0:128], in0=P_[:, 0:128], in1=tri, op=ALU.mult), reads=[pk, "cb"], writes=[pk])
            for hh, i in enumerate(u):
                q_lo, n = geom(G, i)
                c_lo = q_lo - G * 512
                S.op("pe", lambda e, i=i, hh=hh, n=n, c_lo=c_lo: e.matmul(banks[ob][:, c_lo:512], lhsT=V[hl][:, i, :], rhs=P_[:, hh * 512:hh * 512 + n], start=(i == 0), stop=(i == nkb - 1)),
                     reads=[pk, ("V", hl, i // 4), "V%dones" % hl], writes=[obk])
            if last:
                ot = OT[G % 3]
                otk = "OT%d" % (G % 3)
                num = slice(0, 64) if hl == 0 else slice(64, 128)
                den = slice(64, 128) if hl == 0 else slice(0, 64)
                S.op("dve", lambda e: e.reciprocal(out=rec[num, :], in_=banks[ob][den, :]), reads=[obk], writes=[("rec", hl)])
                S.op("dve", lambda e: e.tensor_tensor(out=tmp[num, :], in0=banks[ob][num, :], in1=rec[num, :], op=ALU.mult), reads=[obk, ("rec", hl)], writes=[("tmp", hl)])
                S.op("dve", lambda e: e.tensor_tensor(out=ot[num, :], in0=tmp[num, :], in1=Gt[num, G * 512:(G + 1) * 512], op=ALU.mult), reads=[("tmp", hl), ("Gt", G)], writes=[(otk, hl)])
                if hl == 1:
                    S.dma("pool", oT_dst(p, G), ot[:], reads=[(otk, 0), (otk, 1)], writes=[oT_key(G) if callable(oT_key) else oT_key])
                    S._record(S.lastw[oT_key(G) if callable(oT_key) else oT_key], [(otk, 0), (otk, 1)], [])
                    if "on_oT" in io and p == 1 and G % 4 == 3:
                        io["on_oT"](G // 4)

        for ui in range(len(units) + 2):
            if ui < len(units):
                QK(ui)
            if ui >= 2:
                EXPV(ui - 2)

def consts_C():
    ident = np.eye(128, dtype=np.float32)
    bd = np.zeros((128, 128), np.float32)
    bd[0:64, 0:64] = 1.0 / 64
    bd[64:128, 64:128] = 1.0 / 64
    tri = (np.arange(128)[None, :] >= np.arange(128)[:, None]).astype(np.float32)
    return np.ascontiguousarray(np.concatenate([ident, bd, tri], axis=1).astype(NPBF))


def prep_C(inp, core, hT):
    b, g = core // 4, core % 4
    w_in = inp["odd_w_in"][0]
    cols = []
    for p in range(2):
        hc = (4 * g + 2 * p) * 64
        for base in (0, 1024, 2048, 3072):
            cols.append(w_in[:, base + hc:base + hc + 128])
    cols.append(w_in[:, 4096 + 4 * g:4096 + 4 * g + 4])
    w = np.ascontiguousarray(np.concatenate(cols, axis=1))
    cst = np.zeros((128, 16), np.float32)
    cst[:, 0] = np.tile(inp["fox_qnorm_g"][0], 2)
    cst[:, 1] = np.tile(inp["fox_knorm_g"][0], 2)
    cst[:, 2] = np.repeat(inp["fox_b_f"][0][4 * g:4 * g + 4], 32)
    r = np.arange(128) % 32
    cst[:, 3] = (r == 3)
    cst[:, 4] = (r == 4)
    cst[:, 5] = (r == 5)
    cst[:, 6] = (r < 3)
    cst[:, 7] = -1.0 * (r == 0)
    cst[:, 8] = -1.0 * (r == 1)
    cst[:, 9] = -1.0 * (r == 2)
    cst[:, 10] = (r >= 3) & (r < 6)
    return dict(hT=hT, w=w, cst=cst, cb=consts_C())


def build_A(ngroups=16, stage=9):
    kb = KB()
    emit_A(kb, {}, ngroups, stage)
    return kb.finish(["mo"])


def emit_A(kb, io, ngroups=16, stage=9):
    S = kb.S
    hT_d = kb.din("hT", [D, T], BF16) if "hT_src" not in io else None
    w_d = kb.din("w", [D, 768], F32)
    cst_d = kb.din("cst", [128, 8], F32)
    pw_d = kb.din("pw", [128, 128], F32)
    cb_d = kb.din("cb", [128, 1152], BF16)
    rm_d = kb.din("rm", [128, 512], F32)
    mo_d = kb.dout("mo", [256, T], BF16) if "mo_dst" not in io else None
    banks = kb.banks()
    if "hT_src" in io:
        hT_src, hT_key = io["hT_src"], io["hT_key"]
    else:
        hT_v = hT_d.rearrange("(k p) t -> p k t", p=128)
        hT_src, hT_key = (lambda G: hT_v[:, :, G * 512:(G + 1) * 512]), "hT_ext"
    if "mo_dst" in io:
        mo_dst, mo_key = io["mo_dst"], io["mo_key"]
    else:
        mo_v = mo_d.rearrange("(r p) t -> p r t", p=128)
        mo_dst, mo_key = (lambda G: mo_v[:, :, G * 512:(G + 1) * 512]), "mo"

    cst = kb.sb("cst", [128, 8], F32)
    cb = kb.sb("cb", [128, 1152], BF16)
    rm = kb.sb("rm", [128, 512], F32)
    pwf = kb.sb("pwf", [128, 128], F32)
    pwb = kb.sb("pwb", [128, 128], BF16)
    S.dma("sp", cst[:], cst_d[:, :], writes=["cst"])
    S.dma("sp", cb[:], cb_d[:, :], writes=["cb"])
    S.dma("sp", rm[:], rm_d[:, :], writes=["rm"])
    S.dma("sp", pwf[:], pw_d[:, :], writes=["pwf"])
    S.op("dve", lambda e: e.tensor_copy(out=pwb[:], in_=pwf[:]), reads=["pwf"], writes=["pwb"])
    ident = cb[:, 0:128]
    o128 = cb[:, 128:256]
    trim = cb[:, 256:768]
    Bmain = cb[:, 768:896]
    Bprev = cb[:, 896:1024]
    B0 = cb[:, 1024:1152]
    lbt = kb.sb("lbt", [128, 8], F32)
    S.op("act", lambda e: e.activation(out=lbt[:, 0:3], in_=cst[:, 0:3], func=AF.Exp), reads=["cst"], writes=["lbt"])
    S.op("dve", lambda e: e.tensor_tensor(out=lbt[:, 3:4], in0=lbt[:, 0:1], in1=lbt[:, 1:2], op=ALU.add), reads=["lbt"], writes=["lbt"])
    S.op("dve", lambda e: e.tensor_tensor(out=lbt[:, 3:4], in0=lbt[:, 3:4], in1=lbt[:, 2:3], op=ALU.add), reads=["lbt"], writes=["lbt"])
    S.op("dve", lambda e: e.reciprocal(out=lbt[:, 3:4], in_=lbt[:, 3:4]), reads=["lbt"], writes=["lbt"])
    S.op("dve", lambda e: e.tensor_tensor(out=lbt[:, 4:5], in0=lbt[:, 0:1], in1=lbt[:, 3:4], op=ALU.mult), reads=["lbt"], writes=["lbt"])
    S.op("dve", lambda e: e.tensor_scalar(out=lbt[:, 5:6], in0=lbt[:, 4:5], scalar1=-1.0, scalar2=1.0, op0=ALU.mult, op1=ALU.add), reads=["lbt"], writes=["lbt"])
    S.op("dve", lambda e: e.tensor_scalar(out=lbt[:, 6:7], in0=lbt[:, 4:5], scalar1=-1.0, scalar2=None, op0=ALU.add), reads=["lbt"], writes=["lbt"])
    S.op("dve", lambda e: e.memset(lbt[:, 7:8], EPS), reads=[], writes=["lbt7"])
    lbc = lbt[:, 4:5]
    omlb = lbt[:, 5:6]
    nomlb = lbt[:, 6:7]
    epsc = lbt[:, 7:8]

    Wb = kb.sb("Wb", [128, 8, 768], BF16)
    wst = [kb.sb("wst%d" % i, [128, 768], F32) for i in range(2)]
    for k in range(8):
        st = wst[k % 2]
        S.dma("sp", st[:], w_d[k * 128:(k + 1) * 128, :], writes=["wst%d" % (k % 2)])
        S.op("dve", lambda e, k=k, st=st: e.tensor_copy(out=Wb[:, k, :], in_=st[:]), reads=["wst%d" % (k % 2)], writes=["Wb"])

    hTg = [kb.sb("hTg%d" % i, [128, 8, 512], BF16) for i in range(2)]
    t = {n: kb.sb(n, [128, 512], F32) for n in ["e", "L1", "L2", "logf", "cum", "key", "dq", "dl", "Eq", "Ek", "El", "lnm", "rstd", "ega", "egb", "t1"]}
    sga = [kb.sb("sga%d" % i, [128, 512], F32) for i in range(2)]
    sgb = [kb.sb("sgb%d" % i, [128, 512], F32) for i in range(2)]
    Qs = [kb.sb("Qs%d" % i, [128, 512], BF16) for i in range(2)]
    Ks = [kb.sb("Ks%d" % i, [128, 512], BF16) for i in range(2)]
    Kh = [kb.sb("Kh%d" % i, [128, 512], BF16) for i in range(2)]
    sq = kb.sb("sq", [128, 512], BF16)
    pooledT = kb.sb("pooledT", [128, 512], BF16)
    KhT = kb.sb("KhT", [128, 4, 2, 128], BF16)
    S.op("dve", lambda e: e.memset(KhT[:], 0.0), writes=["KhT"])
    Vt = [kb.sb("Vt%d" % i, [128, 4, 128], BF16) for i in range(2)]
    Ut = [kb.sb("Ut%d" % i, [128, 4, 128], BF16) for i in range(3)]
    At = kb.sb("At", [128, 512], BF16)
    er = [kb.sb("er%d" % i, [128, 16], F32) for i in range(2)]
    state = kb.sb("state", [128, 128], F32)
    stb = [kb.sb("stb%d" % i, [128, 128], BF16) for i in range(2)]
    Ost = [kb.sb("Ost%d" % i, [128, 2, 512], BF16) for i in range(4)]
    S.op("dve", lambda e: e.memset(state[:], 0.0), writes=["state"])

    bst = {"1": 0, "2": 0}

    def nb1():
        b = bst["1"] % 4
        bst["1"] += 1
        return b, "bank%d" % b

    def nb2():
        b = 4 + bst["2"] % 4
        bst["2"] += 1
        return b, "bank%d" % b

    def s1(G):
        pz = G % 2
        hg = hTg[pz]
        hkey = "hTg%d" % pz
        S.dma("sp", hg[:], hT_src(G), reads=[hT_key(G) if callable(hT_key) else hT_key], writes=[hkey])
        yield
        pe_ops = []
        f_ops = []

        def proj_fm(blk):
            b, bk = nb1()
            for k in range(8):
                pe_ops.append(("pe", lambda e, k=k, b=b: e.matmul(banks[b][:, :], lhsT=Wb[:, k, blk * 128:(blk + 1) * 128], rhs=hg[:, k, :], start=(k == 0), stop=(k == 7)),
                               ["Wb", hkey], [bk]))
            return b, bk

        bfm, bfk = proj_fm(1)
        bq, bqk = proj_fm(0)
        biu = []
        for half in range(2):
            b, bk = nb1()
            for jj in range(2):
                j = half * 2 + jj
                for k in range(8):
                    pe_ops.append(("pe", lambda e, k=k, j=j, jj=jj, b=b: e.matmul(banks[b][:, jj * 256:(jj + 1) * 256], lhsT=hg[:, k, j * 128:(j + 1) * 128], rhs=Wb[:, k, 512:768], start=(k == 0), stop=(k == 7)),
                                   ["Wb", hkey], [bk]))
            biu.append((b, bk))
        bga, bgak = proj_fm(2)
        bgb, bgbk = proj_fm(3)
        cum3 = t["cum"][:].rearrange("p (c s) -> p c s", c=8)
        er_ = er[pz]
        erk = "er%d" % pz
        f_ops += [
            ("act", lambda e: e.activation(out=t["e"][:], in_=banks[bfm][:, :], func=AF.Exp, scale=-1.0), [bfk], ["e"]),
            ("act", lambda e: e.activation(out=t["L1"][:], in_=t["e"][:], func=AF.Ln, scale=lbc, bias=rm[:, 1:2]), ["e", "lbt", "rm"], ["L1"]),
            ("act", lambda e: e.activation(out=t["L2"][:], in_=t["e"][:], func=AF.Ln, scale=1.0, bias=rm[:, 1:2]), ["e", "rm"], ["L2"]),
            ("dve", lambda e: e.tensor_tensor(out=t["logf"][:], in0=t["L1"][:], in1=t["L2"][:], op=ALU.subtract), ["L1", "L2"], ["logf"]),
            ("dve", lambda e: e.tensor_tensor_scan(out=t["cum"][:], data0=rm[:], data1=t["logf"][:], initial=0.0, op0=ALU.mult, op1=ALU.add), ["rm", "logf"], ["cum"]),
            ("act", lambda e: e.activation(out=t["e"][:], in_=t["L2"][:], func=AF.Exp, scale=-1.0), ["L2", "L1"], ["e"]),
            ("dve", lambda e: e.tensor_scalar(out=t["key"][:], in0=t["e"][:], scalar1=nomlb, scalar2=omlb, op0=ALU.mult, op1=ALU.add), ["e", "lbt"], ["key"]),
            ("dve", lambda e: e.tensor_tensor(out=t["dq"][:].rearrange("p (c s) -> p c s", c=8), in0=cum3, in1=cum3[:, :, 31:32].to_broadcast([128, 8, 64]), op=ALU.subtract), ["cum"], ["dq"]),
            ("dve", lambda e: e.tensor_tensor(out=t["dl"][:].rearrange("p (c s) -> p c s", c=8), in0=cum3[:, :, 63:64].to_broadcast([128, 8, 64]), in1=cum3, op=ALU.subtract), ["cum"], ["dl"]),
            ("act", lambda e: e.activation(out=t["Eq"][:], in_=t["dq"][:], func=AF.Exp), ["dq"], ["Eq"]),
            ("act", lambda e: e.activation(out=t["Ek"][:], in_=t["dq"][:], func=AF.Exp, scale=-1.0), ["dq"], ["Ek"]),
            ("act", lambda e: e.activation(out=t["El"][:], in_=t["dl"][:], func=AF.Exp), ["dl"], ["El"]),
            ("act", lambda e: e.activation(out=er_[:, 0:8], in_=cum3[:, :, 31], func=AF.Exp), ["cum"], [erk]),
            ("act", lambda e: e.activation(out=er_[:, 8:16], in_=cum3[:, :, 63], func=AF.Exp), ["cum"], [erk]),
            ("dve", lambda e: e.tensor_tensor(out=Qs[pz][:], in0=banks[bq][:, :], in1=t["Eq"][:], op=ALU.mult), [bqk, "Eq"], ["Qs%d" % pz]),
            ("dve", lambda e: e.tensor_tensor(out=Ks[pz][:], in0=t["key"][:], in1=t["Ek"][:], op=ALU.mult), ["key", "Ek"], ["Ks%d" % pz]),
            ("dve", lambda e: e.tensor_tensor(out=Kh[pz][:], in0=t["key"][:], in1=t["El"][:], op=ALU.mult), ["key", "El"], ["Kh%d" % pz]),
        ]
        Uc = Ut[G % 3]
        ukey = "Ut%d" % (G % 3)
        for half in range(2):
            b, bk = biu[half]
            v3 = banks[b][:, :].rearrange("p (j c) -> p j c", j=2)
            f_ops.append(("dve", lambda e, v3=v3, half=half: e.tensor_copy(out=Vt[pz][:, 2 * half:2 * half + 2, :], in_=v3[:, :, 0:128]), [bk], ["Vt%d" % pz]))
            f_ops.append(("dve", lambda e, v3=v3, half=half: e.tensor_copy(out=Uc[:, 2 * half:2 * half + 2, :], in_=v3[:, :, 128:256]), [bk], [ukey]))
        for (bg, bgk, eg, sg, sgk) in ((bga, bgak, "ega", sga[pz], "sga%d" % pz), (bgb, bgbk, "egb", sgb[pz], "sgb%d" % pz)):
            f_ops.append(("act", lambda e, bg=bg, eg=eg: e.activation(out=t[eg][:], in_=banks[bg][:, :], func=AF.Tanh, scale=0.5), [bgk], [eg]))
            f_ops.append(("dve", lambda e, eg=eg: e.tensor_scalar(out=t[eg][:], in0=t[eg][:], scalar1=0.5, scalar2=0.5, op0=ALU.mult, op1=ALU.add), [eg], [eg]))
            f_ops.append(("dve", lambda e, bg=bg, eg=eg, sg=sg: e.tensor_tensor(out=sg[:], in0=banks[bg][:, :], in1=t[eg][:], op=ALU.mult), [bgk, eg], [sgk]))
        pi = 0
        for _ in range(8):
            en, fn, rd, wr = pe_ops[pi]
            S.op(en, fn, reads=rd, writes=wr)
            pi += 1
            yield
        for (en, fn, rd, wr) in f_ops:
            S.op(en, fn, reads=rd, writes=wr)
            yield
            for _ in range(3):
                if pi < len(pe_ops):
                    en2, fn2, rd2, wr2 = pe_ops[pi]
                    S.op(en2, fn2, reads=rd2, writes=wr2)
                    pi += 1
                    yield
        while pi < len(pe_ops):
            en2, fn2, rd2, wr2 = pe_ops[pi]
            S.op(en2, fn2, reads=rd2, writes=wr2)
            pi += 1
            yield

    def s2(G):
        pz = G % 2
        er_ = er[pz]
        erk = "er%d" % pz
        Qk, Kk, Khk, Vk = "Qs%d" % pz, "Ks%d" % pz, "Kh%d" % pz, "Vt%d" % pz
        bt, btk = nb2()
        ptv = banks[bt][:, :].bitcast(BF16)
        for j in range(4):
            S.op("pe", lambda e, j=j: e.transpose(ptv[:, j * 128:(j + 1) * 128], Kh[pz][:, j * 128:(j + 1) * 128], ident), reads=[Khk, "cb"], writes=[btk])
            yield
        ptv3 = ptv[:, 0:512].rearrange("p (j k) -> p j k", j=4)
        S.op("dve", lambda e: e.tensor_copy(out=KhT[0:64, :, 0, :], in_=ptv3[0:64, :, :]), reads=[btk], writes=["KhT"])
        yield
        S.op("dve", lambda e: e.tensor_copy(out=KhT[64:128, :, 1, :], in_=ptv3[64:128, :, :]), reads=[btk], writes=["KhT"])
        yield
        bs, bsk = nb2()
        for pair in range(4):
            S.op("pe", lambda e, pair=pair: e.matmul(banks[bs][:, pair * 128:(pair + 1) * 128], lhsT=Ks[pz][:, pair * 128:(pair + 1) * 128], rhs=Qs[pz][:, pair * 128:(pair + 1) * 128], start=True, stop=True),
                 reads=[Kk, Qk], writes=[bsk])
            yield
        S.op("dve", lambda e: e.tensor_tensor(out=At[:], in0=banks[bs][:, :], in1=trim, op=ALU.mult), reads=[bsk, "cb"], writes=["At"])
        yield
        bz = []
        for zh in range(2):
            b, bk = nb2()
            for cc in range(4):
                c = zh * 4 + cc
                j, half = c // 2, c % 2
                S.op("pe", lambda e, cc=cc, j=j, half=half, b=b: e.matmul(banks[b][:, cc * 128:(cc + 1) * 128], lhsT=KhT[:, j, half, :], rhs=Vt[pz][:, j, :], start=True, stop=True),
                     reads=["KhT", Vk], writes=[bk])
                yield
            bz.append((b, bk))
        bo, bok = nb2()
        for c in range(8):
            gc = 8 * G + c
            j, half = c // 2, c % 2
            sb_ = stb[gc % 2]
            sbk = "stb%d" % (gc % 2)
            S.op("act", lambda e, c=c, sb_=sb_: e.activation(out=sb_[:], in_=state[:], func=AF.Copy, scale=er_[:, c:c + 1]), reads=["state", erk], writes=[sbk])
            if half == 0:
                S.op("pe", lambda e, j=j: e.matmul(banks[bo][:, j * 128:(j + 1) * 128], lhsT=Vt[pz][:, j, :], rhs=At[:, j * 128:(j + 1) * 128], start=True, stop=False),
                     reads=[Vk, "At"], writes=[bok])
            S.op("pe", lambda e, c=c, sb_=sb_, half=half: e.matmul(banks[bo][:, c * 64:(c + 1) * 64], lhsT=sb_[:], rhs=Qs[pz][:, c * 64:(c + 1) * 64], start=False, stop=(half == 1)),
                 reads=[sbk, Qk], writes=[bok])
            zb, zbk = bz[c // 4]
            S.op("dve", lambda e, c=c, zb=zb: e.scalar_tensor_tensor(out=state[:], in0=state[:], scalar=er_[:, 8 + c:9 + c], in1=banks[zb][:, (c % 4) * 128:(c % 4 + 1) * 128], op0=ALU.mult, op1=ALU.add),
                 reads=["state", erk, zbk], writes=["state"])
            yield
        if stage <= 3:
            return
        S.op("act", lambda e: e.activation(out=sq[:], in_=banks[bo][:, :], func=AF.Square), reads=[bok], writes=["sq"])
        yield
        bm, bmk = nb2()
        S.op("pe", lambda e: e.matmul(banks[bm][:, :], lhsT=o128, rhs=sq[:], start=True, stop=True), reads=["sq", "cb"], writes=[bmk])
        yield
        S.op("act", lambda e: e.activation(out=t["lnm"][:], in_=banks[bm][:, :], func=AF.Ln, scale=1.0, bias=epsc), reads=[bmk, "lbt7"], writes=["lnm"])
        yield
        S.op("act", lambda e: e.activation(out=t["rstd"][:], in_=t["lnm"][:], func=AF.Exp, scale=-0.5), reads=["lnm"], writes=["rstd"])
        yield
        S.op("dve", lambda e: e.scalar_tensor_tensor(out=t["t1"][:], in0=banks[bo][:, :], scalar=cst[:, 3:4], in1=t["rstd"][:], op0=ALU.mult, op1=ALU.mult), reads=[bok, "cst", "rstd"], writes=["t1"])
        yield
        os_ = Ost[G % 4]
        osk = "Ost%d" % (G % 4)
        S.op("dve", lambda e: e.tensor_tensor(out=os_[:, 0, :], in0=t["t1"][:], in1=sga[pz][:], op=ALU.mult), reads=["t1", "sga%d" % pz], writes=[osk])
        yield
        bp, bpk = nb2()
        Uc = Ut[G % 3]
        ukey = "Ut%d" % (G % 3)
        Up = Ut[(G - 1) % 3]
        upkey = "Ut%d" % ((G - 1) % 3)
        for j in range(4):
            tj = 4 * G + j
            if tj == 0:
                S.op("pe", lambda e, j=j: e.matmul(banks[bp][:, j * 128:(j + 1) * 128], lhsT=Uc[:, j, :], rhs=B0, start=True, stop=True), reads=[ukey, "cb"], writes=[bpk])
            else:
                S.op("pe", lambda e, j=j: e.matmul(banks[bp][:, j * 128:(j + 1) * 128], lhsT=Uc[:, j, :], rhs=Bmain, start=True, stop=False), reads=[ukey, "cb"], writes=[bpk])
                if j == 0:
                    S.op("pe", lambda e, j=j: e.matmul(banks[bp][:, j * 128:(j + 1) * 128], lhsT=Up[:, 3, :], rhs=Bprev, start=False, stop=True), reads=[upkey, "cb"], writes=[bpk])
                else:
                    S.op("pe", lambda e, j=j: e.matmul(banks[bp][:, j * 128:(j + 1) * 128], lhsT=Uc[:, j - 1, :], rhs=Bprev, start=False, stop=True), reads=[ukey, "cb"], writes=[bpk])
            yield
        S.op("dve", lambda e: e.tensor_copy(out=pooledT[:], in_=banks[bp][:, :]), reads=[bpk], writes=["pooledT"])
        yield
        bob, bobk = nb2()
        S.op("pe", lambda e: e.matmul(banks[bob][:, :], lhsT=pwb[:], rhs=pooledT[:], start=True, stop=True), reads=["pwb", "pooledT"], writes=[bobk])
        yield
        S.op("dve", lambda e: e.scalar_tensor_tensor(out=os_[:, 1, :], in0=banks[bob][:, :], scalar=cst[:, 4:5], in1=sgb[pz][:], op0=ALU.mult, op1=ALU.mult), reads=[bobk, "cst", "sgb%d" % pz], writes=[osk])
        yield
        S.dma("pool", mo_dst(G), os_[:], reads=[osk], writes=[mo_key(G) if callable(mo_key) else mo_key])
        if "on_mo" in io and G % 4 == 3:
            io["on_mo"](G // 4)
        yield

    RA, RB = 1, 2

    def drain(*gens):
        act_ = [g for g in gens if g is not None]
        reps = [RA, RB]
        while act_:
            for gi, g in enumerate(list(act_)):
                for _ in range(reps[gi] if len(gens) > 1 and gi < 2 else 1):
                    try:
                        next(g)
                    except StopIteration:
                        if g in act_:
                            act_.remove(g)
                        break

    drain(s1(0))
    for G in range(ngroups):
        drain(s2(G), s1(G + 1) if G + 1 < ngroups else None)

WINDOWS = (2, 4, 8, 16)


def consts_A(g):
    w = WINDOWS[g]
    ident = np.eye(128, dtype=np.float32)
    o128 = np.full((128, 128), 1.0 / 128, np.float32)
    s = np.arange(128)[:, None]
    t = np.arange(128)[None, :]
    tri1 = ((s // 64 == t // 64) & (s % 64 <= t % 64)).astype(np.float32)
    trim = np.tile(tri1, (1, 4))
    t = np.arange(128)[None, :]
    Bmain = ((s <= t) & (s >= t - w + 1)).astype(np.float32) / w - (s == t)
    Bprev = ((s - 128) >= (t - w + 1)).astype(np.float32) / w
    cnt = np.minimum(t + 1, w).astype(np.float32)
    B0 = ((s <= t) & (s >= t - w + 1)).astype(np.float32) / cnt - (s == t)
    cb = np.concatenate([ident, o128, trim, Bmain, Bprev, B0], axis=1).astype(NPBF)
    rm = np.ones((128, 512), np.float32)
    rm[:, 0::64] = 0.0
    return np.ascontiguousarray(cb), rm


def prep_A(inp, core, hT):
    b, g = core // 4, core % 4
    w_in = inp["even_w_in"][0]
    sl = slice(g * 128, (g + 1) * 128)
    cols = [w_in[:, 0:512][:, sl], w_in[:, 512:1024][:, sl], w_in[:, 1536:2048][:, sl], w_in[:, 2560:3072][:, sl],
            w_in[:, 1024:1536][:, sl], w_in[:, 2048:2560][:, sl]]
    w = np.ascontiguousarray(np.concatenate(cols, axis=1))
    cst = np.zeros((128, 8), np.float32)
    cst[:, 0:3] = inp["hgrn_lb"][:, sl].T
    cst[:, 3] = inp["hgrn_onorm_g"][0][sl]
    cst[:, 4] = inp["pool_scale"][0][sl]
    cb, rm = consts_A(g)
    return dict(hT=hT, w=w, cst=cst, pw=np.ascontiguousarray(inp["pool_w"][0, g]), cb=cb, rm=rm)


_PROGS = {}
GROUPS = [[0, 1, 2, 3], [4, 5, 6, 7]]


def build_fused(upto=99):
    kb = KB()
    nc, S = kb.nc, kb.S
    dt_ = lambda n, sh, d: nc.dram_tensor(n, sh, d).ap()
    h0own = [dt_("i_h0own%d" % g, [D, 512], BF16) for g in range(4)]
    h0all = [dt_("i_h0all%d" % g, [4 * D, 512], BF16) for g in range(4)]
    moown = [dt_("i_moown%d" % q, [256, TOK], BF16) for q in range(4)]
    moall = dt_("i_moall", [4 * D, TOK], BF16)
    x1 = dt_("i_x1", [TOK, D], F32)
    h1own = [dt_("i_h1own%d" % g, [D, 512], BF16) for g in range(4)]
    h1all = [dt_("i_h1all%d" % g, [4 * D, 512], BF16) for g in range(4)]
    oown = [dt_("i_oown%d" % q, [256, TOK], BF16) for q in range(4)]
    oall = dt_("i_oall", [4 * D, TOK], BF16)
    out = nc.dram_tensor("out", [TOK, D], F32, kind="ExternalOutput").ap()
    jq = nc.sync.partition_id() % 4

    def own_dst(ts):
        return lambda g: ts[g].rearrange("(k p) t -> p k t", p=128)

    def all_src(ts):
        return lambda G: ts[G // 4].rearrange("(r k p) t -> p r k t", r=4, p=128)[:, G % 4, :, :]

    def dyn_src(t):
        v = t.rearrange("(q k p) t -> p (q k) t", q=4, p=128)
        return lambda g: v[:, g * 8:(g + 1) * 8, bass.ds(jq * 512, 512)]

    def gather_h(own, al, nm):
        return lambda g: S.collective("AllGather", [own[g][:, :]], [al[g][:, :]], GROUPS, reads=[nm + "own%d" % g], writes=[nm + "all%d" % g])

    def gather_q(own, al, nm):
        return lambda q: S.collective("AllGather", [own[q][:, :]], [al[q * D:(q + 1) * D, :]], GROUPS, reads=[nm + "own%d" % q], writes=[nm + "all%d" % q])

    modown = dt_("i_modown", [128, MC], F32)
    modall = dt_("i_modall", [512, MC], F32)
    with kb.phase("M_"):
        emit_mods(kb, modown)
        S.collective("AllGather", [modown[:, :]], [modall[:, :]], GROUPS, reads=["modown"], writes=["modall"])
    MODS = (modall, "modall")
    if upto <= 0:
        return kb.finish(["modall"])
    with kb.phase("P0_"):
        emit_tok(kb, "P0", dict(mods=MODS, nlayer=0, hT_dst=own_dst(h0own), hT_key=lambda g: "h0own%d" % g, on_hT=gather_h(h0own, h0all, "h0")))
    if upto <= 1:
        return kb.finish(["h0own%d" % g for g in range(4)])
    if upto <= 2:
        return kb.finish(["h0all%d" % g for g in range(4)])
    with kb.phase("A_"):
        emit_A(kb, dict(hT_src=all_src(h0all), hT_key=lambda G: "h0all%d" % (G // 4),
                        mo_dst=(lambda G: moown[G // 4].rearrange("(r p) t -> p r t", p=128)[:, :, (G % 4) * 512:(G % 4 + 1) * 512]),
                        mo_key=lambda G: "moown%d" % (G // 4), on_mo=gather_q(moown, moall, "mo")))
    if upto <= 3:
        return kb.finish(["moown%d" % g for g in range(4)])
    if upto <= 4:
        return kb.finish(["moall%d" % q for q in range(4)])
    with kb.phase("B_"):
        emit_tok(kb, "PB", dict(mods=MODS, layer=0, nlayer=1, mT_src=dyn_src(moall), mT_key=(lambda g: "moall%d" % g), hT_dst=own_dst(h1own), hT_key=lambda g: "h1own%d" % g, on_hT=gather_h(h1own, h1all, "h1"),
                                xo_dst=(lambda ti: x1[ti * 128:(ti + 1) * 128, :]), xo_key="x1"))
    if upto <= 5:
        return kb.finish(["x1"] + ["h1own%d" % g for g in range(4)])
    with kb.phase("C_"):
        emit_C(kb, dict(hT_src=all_src(h1all), hT_key=lambda G: "h1all%d" % (G // 4),
                        oT_dst=(lambda p, G: oown[G // 4][p * 128:(p + 1) * 128, (G % 4) * 512:(G % 4 + 1) * 512]),
                        oT_key=lambda G: "oown%d" % (G // 4), on_oT=gather_q(oown, oall, "o")))
    if upto <= 7:
        return kb.finish(["oown%d" % g for g in range(4)])
    with kb.phase("D_"):
        emit_tok(kb, "PD", dict(mods=MODS, layer=1, mT_src=dyn_src(oall), mT_key=(lambda g: "oall%d" % g), x_src=(lambda ti: x1[ti * 128:(ti + 1) * 128, :]), x_key="x1",
                                xo_dst=(lambda ti: out[ti * 128:(ti + 1) * 128, :]), xo_key="out"))
    return kb.finish(["out"])


def _bc(v, n=128):
    return np.ascontiguousarray(np.broadcast_to(np.asarray(v, np.float32), (n, v.shape[-1])))


def _inputs(x, c, norm_g, ada_w, ada_b, hgrn_lb, even_w_in, hgrn_onorm_g, pool_w, pool_scale,
            even_w_out, odd_w_in, fox_b_f, fox_qnorm_g, fox_knorm_g, odd_w_out):
    f = lambda a: np.asarray(a, np.float32)
    return dict(x=f(x), c=f(c), norm_g=f(norm_g), ada_w=f(ada_w), ada_b=f(ada_b), hgrn_lb=f(hgrn_lb), even_w_in=f(even_w_in),
                hgrn_onorm_g=f(hgrn_onorm_g), pool_w=f(pool_w), pool_scale=f(pool_scale), even_w_out=f(even_w_out),
                odd_w_in=f(odd_w_in), fox_b_f=f(fox_b_f), fox_qnorm_g=f(fox_qnorm_g), fox_knorm_g=f(fox_knorm_g), odd_w_out=f(odd_w_out))


def kernel(_upto=99, **kw):
    inp = _inputs(**kw)
    if "F" not in _PROGS:
        _PROGS["F"] = build_fused(_upto)
    ident = np.ascontiguousarray(np.eye(128, dtype=np.float32).astype(NPBF))
    perm = np.concatenate([np.concatenate([np.arange(g * 128, (g + 1) * 128), 512 + np.arange(g * 128, (g + 1) * 128)]) for g in range(4)])
    wout0 = np.ascontiguousarray(inp["even_w_out"][0][perm, :])
    wout1 = np.ascontiguousarray(inp["odd_w_out"][0])
    aw, ab = inp["ada_w"], inp["ada_b"]
    shared = {
        "P0_ng": _bc(inp["norm_g"][0]), "B_wout": wout0, "B_ng": _bc(inp["norm_g"][1]), "D_wout": wout1,
        "ident": ident,
    }
    maps = []
    Wall = np.concatenate([aw[0], aw[1]], axis=1)
    ball = np.concatenate([ab[0], ab[1]])
    for core in range(NCORES):
        b, j = core // 4, core % 4
        m = dict(shared)
        m["x"] = np.ascontiguousarray(inp["x"][b].reshape(16, 512, D)[j::4].reshape(TOK, D))
        m["cT"] = np.ascontiguousarray(inp["c"][b].reshape(8, 128).T)
        m["M_w"] = np.ascontiguousarray(Wall[:, j * MC:(j + 1) * MC])
        m["M_b"] = _bc(ball[j * MC:(j + 1) * MC])
        pa = prep_A(inp, core, None)
        pc = prep_C(inp, core, None)
        for k, v in pa.items():
            if k != "hT":
                m["A_" + k] = v
        for k, v in pc.items():
            if k != "hT":
                m["C_" + k] = v
        maps.append(m)
    res = run_bass_kernel_spmd(_PROGS["F"], maps, core_ids=list(range(NCORES)))
    r = res.results
    out = np.empty((2, 16, 512, D), np.float32)
    for b in range(2):
        for j in range(4):
            out[b, j::4] = np.asarray(r[b * 4 + j]["out"], np.float32).reshape(4, 512, D)
    return np.ascontiguousarray(out.reshape(2, T, D))
```

```python
import numpy as np
import ml_dtypes
from contextlib import ExitStack
import concourse.bass as bass
import concourse.mybir as mybir
from concourse.bass_utils import run_bass_kernel_spmd

F32 = mybir.dt.float32
BF16 = mybir.dt.bfloat16
AF = mybir.ActivationFunctionType
ALU = mybir.AluOpType
NPBF = ml_dtypes.bfloat16

T = 8192
D = 1024
TOK = 2048
EPS = 1e-6
NCORES = 8
CC_QOS = "P3"


class Sched:
    def __init__(self, nc, es, n_lanes=16):
        self.nc = nc
        self.engs = {"pe": nc.tensor, "act": nc.scalar, "dve": nc.vector, "pool": nc.gpsimd, "sp": nc.sync}
        self.sem = {}
        self.cnt = {}
        for n in ["pe", "act", "dve", "pool", "cc"]:
            self.sem[n] = es.enter_context(nc.semaphore("s_" + n))
            self.cnt[n] = 0
        self.lanes = []
        for i in range(n_lanes):
            nm = "lane%d" % i
            self.sem[nm] = es.enter_context(nc.semaphore("s_" + nm))
            self.cnt[nm] = 0
            self.lanes.append(nm)
        self.planes = []
        for i in range(4):
            nm = "plane%d" % i
            self.sem[nm] = es.enter_context(nc.semaphore("s_" + nm))
            self.cnt[nm] = 0
            self.planes.append(nm)
        self.lane_rr = 0
        self.plane_rr = 0
        self.waited = {n: {} for n in self.engs}
        self.lastw = {}
        self.readers = {}

    def _need(self, reads, writes):
        need = {}

        def add(ev):
            if ev is None:
                return
            s, v = ev
            if need.get(s, 0) < v:
                need[s] = v

        for k in reads:
            add(self.lastw.get(k))
        for k in writes:
            add(self.lastw.get(k))
            for ev in self.readers.get(k, []):
                add(ev)
        return need

    def _emit_waits(self, e, need):
        eng = self.engs[e]
        for s, v in need.items():
            if self.waited[e].get(s, 0) >= v:
                continue
            eng.wait_ge(self.sem[s], v)
            self.waited[e][s] = v

    def _record(self, ev, reads, writes):
        for k in reads:
            lst = self.readers.setdefault(k, [])
            lst.append(ev)
            if len(lst) > 64:
                mx = {}
                for s, v in lst:
                    mx[s] = max(mx.get(s, 0), v)
                self.readers[k] = list(mx.items())
        for k in writes:
            self.lastw[k] = ev
            self.readers[k] = []

    def op(self, e, fn, reads=(), writes=()):
        need = self._need(reads, writes)
        if e == "pe":
            need.pop("pe", None)
        self._emit_waits(e, need)
        ins = fn(self.engs[e])
        self.cnt[e] += 1
        ins.then_inc(self.sem[e], 1)
        self._record((e, self.cnt[e]), reads, writes)
        return ins

    def dma(self, q, out, in_, reads=(), writes=(), **kw):
        if q == "pool":
            lane = self.planes[self.plane_rr % len(self.planes)]
            self.plane_rr += 1
        else:
            lane = self.lanes[self.lane_rr % len(self.lanes)]
            self.lane_rr += 1
        need = self._need(reads, writes)
        if self.cnt[lane] > 0:
            need[lane] = max(need.get(lane, 0), self.cnt[lane])
        self._emit_waits(q, need)
        ins = self.engs[q].dma_start(out=out, in_=in_, **kw)
        self.cnt[lane] += 16
        ins.then_inc(self.sem[lane], 16)
        self._record((lane, self.cnt[lane]), reads, writes)
        return ins

    def barrier(self):
        for e in self.engs:
            need = {n: c for n, c in self.cnt.items() if c > 0 and n != "cc"}
            self._emit_waits(e, need)

    def collective(self, kind, ins, outs, groups, reads=(), writes=()):
        need = self._need(reads, writes)
        self._emit_waits("pool", need)
        ins_ = self.nc.gpsimd.collective_compute(kind, ALU.bypass, replica_groups=groups, ins=ins, outs=outs, dma_qos=CC_QOS)
        self.cnt["cc"] += 1
        ins_.then_inc(self.sem["cc"], 1)
        self._record(("cc", self.cnt["cc"]), reads, writes)
        return ins_

    def wait_all(self, e, keys):
        need = self._need(keys, keys)
        self._emit_waits(e, need)


class _Phase:
    def __init__(self, kb, tag):
        self.kb, self.tag = kb, tag

    def __enter__(self):
        self.kb.tag = self.tag
        self.kb.pes = ExitStack()
        return self

    def __exit__(self, *a):
        self.kb.S.barrier()
        self.kb.pes.close()
        self.kb.pes = self.kb.es
        self.kb.tag = ""
        return False


class BankView:
    def __init__(self, t, off):
        self.t, self.off = t, off

    def __getitem__(self, key):
        rows, cols = key
        a = self.off + (cols.start or 0)
        b = self.off + (512 if cols.stop is None else cols.stop)
        return self.t[rows, a:b]


class KB:
    def __init__(self):
        self.nc = bass.Bass("TRN2", target_bir_lowering=False)
        self.es = ExitStack()
        self.pes = self.es
        self.S = Sched(self.nc, self.es)
        self.outs = []
        self.tag = ""
        self.shared = {}
        self._banks = None

    def phase(self, tag):
        return _Phase(self, tag)

    def din(self, name, shape, dt, shared=False):
        if shared:
            if name not in self.shared:
                self.shared[name] = self.nc.dram_tensor(name, list(shape), dt, kind="ExternalInput").ap()
            return self.shared[name]
        return self.nc.dram_tensor(self.tag + name, list(shape), dt, kind="ExternalInput").ap()

    def dout(self, name, shape, dt):
        self.outs.append(self.tag + name)
        return self.nc.dram_tensor(self.tag + name, list(shape), dt, kind="ExternalOutput").ap()

    def sb(self, name, shape, dt):
        return self.pes.enter_context(self.nc.sbuf_tensor("sb_" + self.tag + name, list(shape), dt))

    def banks(self):
        if self._banks is None:
            self.dbanks = [self.es.enter_context(self.nc.psum_tensor("dbank%d" % i, [128, 1024], F32)) for i in range(4)]
            self._banks = [BankView(self.dbanks[i // 2], (i % 2) * 512) for i in range(8)]
        return self._banks

    def finish(self, out_keys=None):
        self.S.wait_all("sp", out_keys if out_keys is not None else self.outs)
        self.es.close()
        return self.nc


def emit_cond_rep(kb, cT_d):
    S = kb.S
    cT = kb.sb("cT", [128, 8], F32)
    cE = kb.sb("cE", [128, 8], F32)
    cond = kb.sb("cond", [128, 8], F32)
    ones = kb.sb("ones128", [128, 128], F32)
    crep = kb.sb("cond_rep", [128, 8, 128], F32)
    S.dma("sp", cT[:], cT_d[:, :], writes=["cT"])
    S.op("dve", lambda e: e.memset(ones[:], 1.0), writes=["ones128"])
    S.op("act", lambda e: e.activation(out=cE[:], in_=cT[:], func=AF.Exp, scale=-1.0), reads=["cT"], writes=["cE"])
    S.op("dve", lambda e: e.tensor_scalar_add(out=cE[:], in0=cE[:], scalar1=1.0), reads=["cE"], writes=["cE"])
    S.op("dve", lambda e: e.reciprocal(out=cE[:], in_=cE[:]), reads=["cE"], writes=["cE"])
    S.op("dve", lambda e: e.tensor_mul(out=cond[:], in0=cT[:], in1=cE[:]), reads=["cE", "cT"], writes=["cond"])
    for k in range(8):
        S.op("dve", lambda e, k=k: e.tensor_scalar(out=crep[:, k, :], in0=ones[:], scalar1=cond[:, k:k + 1], scalar2=None, op0=ALU.mult),
             reads=["cond", "ones128"], writes=["crep"])
    return crep


def emit_mod_bcast(kb, crep, w_d, b_d, out_t, out_key, ncols, banks, stage, tag, b0=0, c0=0):
    S = kb.S
    nb = ncols // 512
    S.dma("sp", out_t[:, 0:ncols], b_d[:, c0:c0 + ncols], writes=[out_key])
    idx = 0
    for k in range(8):
        st = stage[k % len(stage)]
        skey = "%s_st%d" % (tag, k % len(stage))
        S.dma("sp", st[:, 0:ncols], w_d[k * 128:(k + 1) * 128, c0:c0 + ncols], writes=[skey])
        for n in range(nb):
            S.op("pe", lambda e, n=n, k=k, st=st: e.matmul(banks[b0 + n][:, :], lhsT=crep[:, k, :], rhs=st[:, n * 512:(n + 1) * 512], start=(k == 0), stop=(k == 7)),
                 reads=["crep", skey], writes=["bank%d" % (b0 + n)])
    for n in range(nb):
        S.op("dve", lambda e, n=n: e.tensor_tensor(out=out_t[:, n * 512:(n + 1) * 512], in0=out_t[:, n * 512:(n + 1) * 512], in1=banks[b0 + n][:, :], op=ALU.add),
             reads=["bank%d" % (b0 + n), out_key], writes=[out_key])


MC = 1536


def emit_mods(kb, modown_d):
    S = kb.S
    cT_d = kb.din("cT", [128, 8], F32, shared=True)
    w_d = kb.din("w", [D, MC], F32)
    b_d = kb.din("b", [128, MC], F32)
    banks = kb.banks()
    crep = emit_cond_rep(kb, cT_d)
    stage = [kb.sb("stg%d" % i, [128, MC], F32) for i in range(4)]
    out_t = kb.sb("mo", [128, MC], F32)
    emit_mod_bcast(kb, crep, w_d, b_d, out_t, "mo", MC, banks, stage, "g", b0=0, c0=0)
    S.dma("sp", modown_d[:, :], out_t[:], reads=["mo"], writes=["modown"])


def load_mods(kb, dst, dst_key, modall_d, c0, n):
    S = kb.S
    c = c0
    while c < c0 + n:
        r, lo = c // MC, c % MC
        m = min(MC - lo, c0 + n - c)
        S.dma("sp", dst[:, c - c0:c - c0 + m], modall_d[r * 128:(r + 1) * 128, lo:lo + m], reads=["modall"], writes=[dst_key])
        c += m


def build_tok(mode):
    kb = KB()
    emit_tok(kb, mode, {})
    return kb.finish(["hT", "xo"])


def emit_tok(kb, mode, io):
    S = kb.S
    do_proj = mode in ("PB", "PD")
    do_norm = mode in ("P0", "PB")
    x_d = kb.din("x", [TOK, D], F32, shared=True) if "x_src" not in io else None
    cT_d = kb.din("cT", [128, 8], F32, shared=True)
    ident_d = kb.din("ident", [128, 128], BF16, shared=True)
    if do_proj:
        mT_d = kb.din("mT", [D, TOK], BF16) if "mT_src" not in io else None
        wout_d = kb.din("wout", [D, D], F32)
        gw_d = kb.din("gw", [D, D], F32) if "mods" not in io else None
        gb_d = kb.din("gb", [128, D], F32) if "mods" not in io else None
    if do_norm:
        nw_d = kb.din("nw", [D, 2 * D], F32) if "mods" not in io else None
        nb_d = kb.din("nb", [128, 2 * D], F32) if "mods" not in io else None
        ng_d = kb.din("ng", [128, D], F32)
        hT_d = kb.dout("hT", [D, TOK], BF16) if "hT_dst" not in io else None
    if mode != "P0":
        xo_d = kb.dout("xo", [TOK, D], F32) if "xo_dst" not in io else None
    banks = kb.banks()
    mods = io.get("mods")
    if mods is None:
        crep = emit_cond_rep(kb, cT_d)
    stage = [kb.sb("stg%d" % i, [128, 2048 if mods is None else 1024], F32) for i in range(3)]
    ident = kb.sb("ident", [128, 128], BF16)
    S.dma("sp", ident[:], ident_d[:, :], writes=["ident"])
    if do_proj:
        gate = kb.sb("gate", [128, D], F32)
        if mods is None:
            emit_mod_bcast(kb, crep, gw_d, gb_d, gate, "gate", D, banks, stage, "g")
        else:
            load_mods(kb, gate, "gate", mods[0], io["layer"] * 3 * D + 2 * D, D)
        Wp = kb.sb("Wp", [128, 8, D], BF16)
        for k in range(8):
            st = stage[k % 3]
            skey = "g_st%d" % (k % 3)
            S.dma("sp", st[:, 0:D], wout_d[k * 128:(k + 1) * 128, :], writes=[skey])
            S.op("dve", lambda e, k=k, st=st: e.tensor_tensor(out=Wp[:, k, :], in0=st[:, 0:D], in1=gate[:], op=ALU.mult),
                 reads=[skey, "gate"], writes=["Wp"])
    if do_norm:
        modn = kb.sb("modn", [128, 2 * D], F32)
        if mods is None:
            emit_mod_bcast(kb, crep, nw_d, nb_d, modn, "modn", 2 * D, banks, stage, "g", b0=4)
        else:
            load_mods(kb, modn, "modn", mods[0], io["nlayer"] * 3 * D, 2 * D)
        Gb = kb.sb("Gb", [128, D], F32)
        S.dma("sp", Gb[:], ng_d[:, :], writes=["Gb"])
        S.op("dve", lambda e: e.scalar_tensor_tensor(out=Gb[:], in0=modn[:, D:2 * D], scalar=1.0, in1=Gb[:], op0=ALU.add, op1=ALU.mult),
             reads=["modn", "Gb"], writes=["Gb"])
    xt = [kb.sb("xt%d" % i, [128, D], F32) for i in range(TOK // 128)]
    if do_proj:
        mTt = [kb.sb("mTt%d" % i, [128, 8, 512], BF16) for i in range(2)]
    if do_norm:
        junk = kb.sb("junk", [128, D], F32)
        ss = [kb.sb("ss%d" % i, [128, 4], F32) for i in range(2)]
        tt = kb.sb("tt", [128, D], F32)
        hb = [kb.sb("hb%d" % i, [128, D], BF16) for i in range(2)]
        hst = [kb.sb("hst%d" % i, [128, 8, 512], BF16) for i in range(2)]
        epsb = kb.sb("epsb", [128, 1], F32)
        S.op("dve", lambda e: e.memset(epsb[:], EPS), writes=["epsb"])
    if do_proj:
        if "mT_src" in io:
            mT_src, mT_key = io["mT_src"], io["mT_key"]
        else:
            mT_v = mT_d.rearrange("(k p) t -> p k t", p=128)
            mT_src, mT_key = (lambda g: mT_v[:, :, g * 512:(g + 1) * 512]), "mT_ext"
    if do_norm:
        if "hT_dst" in io:
            hT_dst, hT_key = io["hT_dst"], io["hT_key"]
        else:
            hT_v = hT_d.rearrange("(k p) t -> p k t", p=128)
            hT_dst, hT_key = (lambda g: hT_v[:, :, g * 512:(g + 1) * 512]), "hT"
    if "x_src" in io:
        x_src, x_key = io["x_src"], io["x_key"]
    else:
        x_src, x_key = (lambda ti: x_d[ti * 128:(ti + 1) * 128, :]), "x_ext"
    if mode != "P0":
        if "xo_dst" in io:
            xo_dst, xo_key = io["xo_dst"], io["xo_key"]
        else:
            xo_dst, xo_key = (lambda ti: xo_d[ti * 128:(ti + 1) * 128, :]), "xo"
    NT = TOK // 128

    def part1(ti):
        g, j = ti // 4, ti % 4
        if do_proj:
            mt = mTt[g % 2]
            mkey = "mTt%d" % (g % 2)
            mk = (lambda gg: mT_key(gg)) if callable(mT_key) else (lambda gg: mT_key)
            if j == 0 and g == 0:
                S.dma("sp", mTt[0][:], mT_src(0), reads=[mk(0)], writes=["mTt0"])
            if j == 2 and g + 1 < TOK // 512:
                S.dma("sp", mTt[(g + 1) % 2][:], mT_src(g + 1), reads=[mk(g + 1)], writes=["mTt%d" % ((g + 1) % 2)])
        x_ = xt[ti]
        xkey = "xt%d" % ti
        if ti == 0:
            for t2 in range(NT):
                S.dma("sp", xt[t2][:], x_src(t2), reads=[x_key], writes=["xt%d" % t2])
        cur = x_
        curkey = xkey
        if do_proj:
            for half in range(2):
                bk = banks[half]
                for k in range(8):
                    S.op("pe", lambda e, k=k, half=half, bk=bk, mt=mt, j=j: e.matmul(bk[:, :], lhsT=mt[:, k, j * 128:(j + 1) * 128], rhs=Wp[:, k, half * 512:(half + 1) * 512], start=(k == 0), stop=(k == 7)),
                         reads=[mkey, "Wp"], writes=["bank%d" % half])
            x1 = x_
            x1key = xkey
            for half in range(2):
                S.op("dve", lambda e, half=half, x1=x1, x_=x_: e.tensor_tensor(out=x1[:, half * 512:(half + 1) * 512], in0=x_[:, half * 512:(half + 1) * 512], in1=banks[half][:, :], op=ALU.add),
                     reads=[xkey, "bank%d" % half], writes=[x1key])
            S.dma("sp", xo_dst(ti), x1[:], reads=[x1key], writes=[xo_key])
            cur = x1
            curkey = x1key
        if do_norm:
            s_ = ss[ti % 2]
            skey = "ss%d" % (ti % 2)
            S.op("act", lambda e, cur=cur, s_=s_: e.activation(out=junk[:], in_=cur[:], func=AF.Square, accum_out=s_[:, 0:1]),
                 reads=[curkey], writes=["junk", skey])
            S.op("act", lambda e, s_=s_: e.activation(out=s_[:, 1:2], in_=s_[:, 0:1], func=AF.Ln, scale=1.0 / D, bias=epsb[:, 0:1]),
                 reads=[skey, "epsb"], writes=[skey])
            S.op("act", lambda e, s_=s_: e.activation(out=s_[:, 2:3], in_=s_[:, 1:2], func=AF.Exp, scale=-0.5),
                 reads=[skey], writes=[skey])
            S.op("dve", lambda e, cur=cur, s_=s_: e.scalar_tensor_tensor(out=tt[:], in0=cur[:], scalar=s_[:, 2:3], in1=Gb[:], op0=ALU.mult, op1=ALU.mult),
                 reads=[curkey, skey, "Gb"], writes=["tt"])
            h_ = hb[ti % 2]
            hkey = "hb%d" % (ti % 2)
            S.op("dve", lambda e, h_=h_: e.tensor_tensor(out=h_[:], in0=tt[:], in1=modn[:, 0:D], op=ALU.add),
                 reads=["tt", "modn"], writes=[hkey])

    def part2(ti):
        if not do_norm:
            return
        g, j = ti // 4, ti % 4
        h_ = hb[ti % 2]
        hkey = "hb%d" % (ti % 2)
        pbank = banks[2 + (ti % 2)]
        pkey = "bank%d" % (2 + (ti % 2))
        pT = pbank[:, :].bitcast(BF16)
        for k in range(8):
            S.op("pe", lambda e, k=k, h_=h_, pT=pT: e.transpose(pT[:, k * 128:(k + 1) * 128], h_[:, k * 128:(k + 1) * 128], ident[:]),
                 reads=[hkey, "ident"], writes=[pkey])
        hs = hst[g % 2]
        hskey = "hst%d" % (g % 2)
        S.op("act", lambda e, hs=hs, pT=pT, j=j: e.activation(out=hs[:, :, j * 128:(j + 1) * 128], in_=pT.rearrange("p (k t) -> p k t", k=8), func=AF.Copy),
             reads=[pkey], writes=[hskey])
        if j == 3:
            S.dma("sp", hT_dst(g), hs[:], reads=[hskey], writes=[hT_key(g) if callable(hT_key) else hT_key])
            if "on_hT" in io:
                io["on_hT"](g)

    for ti in range(NT + 1):
        if ti < NT:
            part1(ti)
        if ti >= 1:
            part2(ti - 1)

def build_C(ngroups=16):
    kb = KB()
    for _ in emit_C(kb, {}, ngroups):
        pass
    return kb.finish(["oT"])


def emit_C(kb, io, ngroups=16):
    S = kb.S
    hT_d = kb.din("hT", [D, T], BF16) if "hT_src" not in io else None
    w_d = kb.din("w", [D, 1028], F32)
    cst_d = kb.din("cst", [128, 16], F32)
    cb_d = kb.din("cb", [128, 384], BF16)
    oT_d = kb.dout("oT", [256, T], BF16) if "oT_dst" not in io else None
    banks = kb.banks()
    if "hT_src" in io:
        hT_src, hT_key = io["hT_src"], io["hT_key"]
    else:
        hT_v = hT_d.rearrange("(k p) t -> p k t", p=128)
        hT_src, hT_key = (lambda G: hT_v[:, :, G * 512:(G + 1) * 512]), "hT_ext"
    if "oT_dst" in io:
        oT_dst, oT_key = io["oT_dst"], io["oT_key"]
    else:
        oT_dst, oT_key = (lambda p, G: oT_d[p * 128:(p + 1) * 128, G * 512:(G + 1) * 512]), "oT"

    cst = kb.sb("cst", [128, 16], F32)
    cb = kb.sb("cb", [128, 384], BF16)
    S.dma("sp", cst[:], cst_d[:, :], writes=["cst"])
    S.dma("sp", cb[:], cb_d[:, :], writes=["cb"])
    ident = cb[:, 0:128]
    bd64 = cb[:, 128:256]
    tri = cb[:, 256:384]
    negbf = kb.sb("negbf", [128, 1], F32)
    S.op("dve", lambda e: e.tensor_scalar(out=negbf[:], in0=cst[:, 2:3], scalar1=-1.0, scalar2=None, op0=ALU.mult), reads=["cst"], writes=["negbf"])
    eps64 = kb.sb("eps64", [128, 2], F32)
    S.op("dve", lambda e: e.memset(eps64[:, 0:1], 64.0 * EPS), writes=["eps64"])
    S.op("dve", lambda e: e.memset(eps64[:, 1:2], EPS), writes=["eps64"])
    ones512 = kb.sb("ones512", [128, 512], F32)
    S.op("dve", lambda e: e.memset(ones512[:], 1.0), writes=["ones512"])

    Wb = kb.sb("Wb", [128, 8, 1028], BF16)
    wst = [kb.sb("wst%d" % i, [128, 1028], F32) for i in range(2)]
    for k in range(8):
        st = wst[k % 2]
        S.dma("sp", st[:], w_d[k * 128:(k + 1) * 128, :], writes=["wst%d" % (k % 2)])
        S.op("dve", lambda e, k=k, st=st: e.tensor_copy(out=Wb[:, k, :], in_=st[:]), reads=["wst%d" % (k % 2)], writes=["Wb"])
    Wrep = kb.sb("Wrep", [128, 8, 128], BF16)
    for h in range(4):
        S.op("dve", lambda e, h=h: e.tensor_copy(out=Wrep[:, :, 32 * h:32 * h + 32], in_=Wb[:, :, 1024 + h:1025 + h].to_broadcast([128, 8, 32])),
             reads=["Wb"], writes=["Wrep"])

    yield
    hTg = [kb.sb("hTg%d" % i, [128, 8, 512], BF16) for i in range(2)]
    QT = [kb.sb("QT%d" % i, [128, T], BF16) for i in range(2)]
    KT = [kb.sb("KT%d" % i, [128, T], BF16) for i in range(2)]
    Vall = kb.sb("Vall", [128, 64, 192], BF16)
    V = [Vall[:, :, 0:128], Vall[:, :, 64:192]]
    Gt = kb.sb("Gt", [128, T], BF16)
    sqbs = {n: kb.sb("sqb" + n, [128, 512], BF16) for n in ("q", "k")}
    lnvs = {n: kb.sb("lnv" + n, [128, 512], F32) for n in ("q", "k")}
    rss = {n: kb.sb("rs" + n, [128, 512], F32) for n in ("q", "k")}
    ge = kb.sb("ge", [128, 512], F32)
    fe = kb.sb("fe", [128, 512], F32)
    cl = [kb.sb("cl%d" % i, [128, 512], F32) for i in range(2)]
    c1b = kb.sb("c1b", [128, 512], BF16)
    c2b = kb.sb("c2b", [128, 512], BF16)
    c3b = kb.sb("c3b", [128, 512], BF16)
    r1 = kb.sb("r1", [128, 512], F32)
    r2 = kb.sb("r2", [128, 512], F32)
    KA = kb.sb("KA", [128, 512], BF16)
    QA = kb.sb("QA", [128, 512], BF16)
    r1b = kb.sb("r1b", [128, 512], BF16)
    r2b = kb.sb("r2b", [128, 512], BF16)
    AUG = kb.sb("AUG", [128, T], BF16)
    Pt = [kb.sb("P%d" % i, [128, 1024], BF16) for i in range(3)]
    rec = kb.sb("rec", [128, 512], F32)
    tmp = kb.sb("tmp", [128, 512], F32)
    OT = [kb.sb("OT%d" % i, [128, 512], BF16) for i in range(3)]

    S.op("dve", lambda e: e.memset(Vall[:, :, 64:128], 1.0), writes=["V0ones", "V1ones"])

    bstate = {"i": 0}

    def nextbank():
        b = bstate["i"] % 8
        bstate["i"] += 1
        return b

    for p in range(2):
        for G in range(ngroups):
            t0 = G * 512
            hg = hTg[G % 2]
            hkey = "hTg%d" % (G % 2)
            S.dma("sp", hg[:], hT_src(G), reads=[hT_key(G) if callable(hT_key) else hT_key], writes=[hkey])
            pb = {}
            for name, blk in (("q", 0), ("k", 1), ("g", 3)):
                b = nextbank()
                pb[name] = b
                c0 = p * 512 + blk * 128
                for k in range(8):
                    S.op("pe", lambda e, k=k, b=b, c0=c0, hg=hg: e.matmul(banks[b][:, :], lhsT=Wb[:, k, c0:c0 + 128], rhs=hg[:, k, :], start=(k == 0), stop=(k == 7)),
                         reads=["Wb", hkey], writes=["bank%d" % b])
            bv = nextbank()
            bvk = "bank%d" % bv
            c0 = p * 512 + 256
            for j in range(4):
                for k in range(8):
                    S.op("pe", lambda e, k=k, j=j, bv=bv, c0=c0, hg=hg: e.matmul(banks[bv][:, j * 128:(j + 1) * 128], lhsT=hg[:, k, j * 128:(j + 1) * 128], rhs=Wb[:, k, c0:c0 + 128], start=(k == 0), stop=(k == 7)),
                         reads=["Wb", hkey], writes=[bvk])
            if p == 0:
                bf = nextbank()
                bfk = "bank%d" % bf
                for k in range(8):
                    S.op("pe", lambda e, k=k, bf=bf, hg=hg: e.matmul(banks[bf][:, :], lhsT=Wrep[:, k, :], rhs=hg[:, k, :], start=(k == 0), stop=(k == 7)),
                         reads=["Wrep", hkey], writes=[bfk])
            bm = {}
            for name in ("q", "k"):
                b = pb[name]
                S.op("act", lambda e, b=b, name=name: e.activation(out=sqbs[name][:], in_=banks[b][:, :], func=AF.Square), reads=["bank%d" % b], writes=["sqb" + name])
            for name in ("q", "k"):
                bm[name] = nextbank()
                S.op("pe", lambda e, name=name: e.matmul(banks[bm[name]][:, :], lhsT=bd64, rhs=sqbs[name][:], start=True, stop=True), reads=["cb", "sqb" + name], writes=["bank%d" % bm[name]])
            S.op("act", lambda e: e.activation(out=lnvs["q"][:], in_=banks[bm["q"]][:, :], func=AF.Ln, scale=64.0, bias=eps64[:, 0:1]), reads=["bank%d" % bm["q"], "eps64"], writes=["lnvq"])
            S.op("act", lambda e: e.activation(out=lnvs["k"][:], in_=banks[bm["k"]][:, :], func=AF.Ln, scale=1.0, bias=eps64[:, 1:2]), reads=["bank%d" % bm["k"], "eps64"], writes=["lnvk"])
            for name in ("q", "k"):
                S.op("act", lambda e, name=name: e.activation(out=rss[name][:], in_=lnvs[name][:], func=AF.Exp, scale=-0.5), reads=["lnv" + name], writes=["rs" + name])
            for name, blk, dst, dn in (("q", 0, QT, "QT"), ("k", 1, KT, "KT")):
                b = pb[name]
                for hl in range(2):
                    pr = slice(hl * 64, hl * 64 + 64)
                    S.op("dve", lambda e, hl=hl, pr=pr, b=b, dst=dst, blk=blk, name=name: e.scalar_tensor_tensor(out=dst[hl][0:64, t0:t0 + 512], in0=banks[b][pr, :], scalar=cst[pr, blk:blk + 1], in1=rss[name][pr, :], op0=ALU.mult, op1=ALU.mult),
                         reads=["bank%d" % b, "cst", "rs" + name], writes=[(dn, hl, G)])
            bg = pb["g"]
            bgk = "bank%d" % bg
            S.op("act", lambda e, bg=bg: e.activation(out=ge[:], in_=banks[bg][:, :], func=AF.Tanh, scale=0.5), reads=[bgk], writes=["ge"])
            S.op("dve", lambda e: e.tensor_scalar(out=ge[:], in0=ge[:], scalar1=0.5, scalar2=0.5, op0=ALU.mult, op1=ALU.add), reads=["ge"], writes=["ge"])
            S.op("dve", lambda e, bg=bg: e.tensor_tensor(out=Gt[:, t0:t0 + 512], in0=banks[bg][:, :], in1=ge[:], op=ALU.mult), reads=[bgk, "ge"], writes=[("Gt", G)])
            bvv = banks[bv][:, :].rearrange("p (j c) -> p j c", j=4)
            S.op("dve", lambda e, bvv=bvv: e.tensor_copy(out=V[0][:, 4 * G:4 * G + 4, 0:64], in_=bvv[:, :, 0:64]), reads=[bvk], writes=[("V", 0, G)])
            S.op("dve", lambda e, bvv=bvv: e.tensor_copy(out=V[1][:, 4 * G:4 * G + 4, 64:128], in_=bvv[:, :, 64:128]), reads=[bvk], writes=[("V", 1, G)])
            if p == 0:
                S.op("act", lambda e, bf=bf: e.activation(out=fe[:], in_=banks[bf][:, :], func=AF.Exp, scale=-1.0, bias=negbf[:, 0:1]), reads=[bfk, "negbf"], writes=["fe"])
                S.op("act", lambda e: e.activation(out=fe[:], in_=fe[:], func=AF.Ln, scale=1.0, bias=ones512[:, 0:1]), reads=["fe", "ones512"], writes=["fe"])
                c_ = cl[G % 2]
                ck = "cl%d" % (G % 2)
                if G == 0:
                    S.op("dve", lambda e, c_=c_: e.tensor_tensor_scan(out=c_[:], data0=ones512[:], data1=fe[:], initial=0.0, op0=ALU.mult, op1=ALU.add),
                         reads=["fe", "ones512"], writes=[ck])
                else:
                    cp = cl[(G - 1) % 2]
                    S.op("dve", lambda e, c_=c_, cp=cp: e.tensor_tensor_scan(out=c_[:], data0=ones512[:], data1=fe[:], initial=cp[:, 511:512], op0=ALU.mult, op1=ALU.add),
                         reads=["fe", "ones512", "cl%d" % ((G - 1) % 2)], writes=[ck])
                S.op("dve", lambda e, c_=c_: e.tensor_copy(out=c1b[:], in_=c_[:]), reads=[ck], writes=["c1b"])
                S.op("dve", lambda e, c_=c_: e.tensor_tensor(out=r1[:], in0=c_[:], in1=c1b[:], op=ALU.subtract), reads=[ck, "c1b"], writes=["r1"])
                S.op("dve", lambda e: e.tensor_copy(out=c2b[:], in_=r1[:]), reads=["r1"], writes=["c2b"])
                S.op("dve", lambda e: e.tensor_tensor(out=r2[:], in0=r1[:], in1=c2b[:], op=ALU.subtract), reads=["r1", "c2b"], writes=["r2"])
                S.op("dve", lambda e: e.tensor_copy(out=c3b[:], in_=r2[:]), reads=["r2"], writes=["c3b"])
                for (dstt, dk, mc) in ((KA, "KA", 3), (QA, "QA", 7)):
                    S.op("dve", lambda e, mc=mc: e.tensor_scalar(out=r1b[:], in0=c1b[:], scalar1=cst[:, mc:mc + 1], scalar2=cst[:, mc + 3:mc + 4], op0=ALU.mult, op1=ALU.add),
                         reads=["c1b", "cst"], writes=["r1b"])
                    S.op("dve", lambda e, mc=mc: e.scalar_tensor_tensor(out=r2b[:], in0=c2b[:], scalar=cst[:, mc + 1:mc + 2], in1=r1b[:], op0=ALU.mult, op1=ALU.add),
                         reads=["c2b", "cst", "r1b"], writes=["r2b"])
                    S.op("dve", lambda e, mc=mc, dstt=dstt: e.scalar_tensor_tensor(out=dstt[:], in0=c3b[:], scalar=cst[:, mc + 2:mc + 3], in1=r2b[:], op0=ALU.mult, op1=ALU.add),
                         reads=["c3b", "cst", "r2b"], writes=[dk])
                for hl in range(2):
                    S.op("dve", lambda e, hl=hl: e.tensor_copy(out=KT[hl][64:70, t0:t0 + 512], in_=KA[32 * hl:32 * hl + 6, :]), reads=["KA"], writes=[("KTa", hl, G)])
                    S.op("dve", lambda e, hl=hl: e.tensor_copy(out=QT[hl][64:70, t0:t0 + 512], in_=QA[32 * hl:32 * hl + 6, :]), reads=["QA"], writes=[("QTa", hl, G)])
                    S.op("dve", lambda e, hl=hl: e.tensor_copy(out=AUG[32 * hl:32 * hl + 6, t0:t0 + 512], in_=KA[64 + 32 * hl:64 + 32 * hl + 6, :]), reads=["KA"], writes=[("AUG", G)])
                    S.op("dve", lambda e, hl=hl: e.tensor_copy(out=AUG[64 + 32 * hl:64 + 32 * hl + 6, t0:t0 + 512], in_=QA[64 + 32 * hl:64 + 32 * hl + 6, :]), reads=["QA"], writes=[("AUG", G)])
        if p == 1:
            TT_ = ngroups * 512
            allG = list(range(ngroups))
            for hl in range(2):
                S.op("dve", lambda e, hl=hl: e.tensor_copy(out=KT[hl][64:70, 0:TT_], in_=AUG[32 * hl:32 * hl + 6, 0:TT_]), reads=[("AUG", G) for G in allG], writes=[("KTa", hl, G) for G in allG])
                S.op("dve", lambda e, hl=hl: e.tensor_copy(out=QT[hl][64:70, 0:TT_], in_=AUG[64 + 32 * hl:64 + 32 * hl + 6, 0:TT_]), reads=[("AUG", G) for G in allG], writes=[("QTa", hl, G) for G in allG])
        units = []
        for G in range(ngroups):
            for hl in range(2):
                us = [(i, i + 1) for i in range(0, 4 * G, 2)] + [(i,) for i in range(4 * G, 4 * G + 4)]
                for n_, u in enumerate(us):
                    units.append((G, hl, u, n_ == len(us) - 1))

        def geom(G, i):
            q_lo = max(G * 512, i * 128)
            return q_lo, (G + 1) * 512 - q_lo

        def QK(ui):
            G, hl, u, _ = units[ui]
            d = 1 + (ui % 3)
            for hh, i in enumerate(u):
                q_lo, n = geom(G, i)
                bi = 2 * d + hh
                S.op("pe", lambda e, i=i, bi=bi, q_lo=q_lo, n=n, hl=hl: e.matmul(banks[bi][:, 0:n], lhsT=KT[hl][0:70, i * 128:(i + 1) * 128], rhs=QT[hl][0:70, q_lo:q_lo + n], start=True, stop=True),
                     reads=[("QT", hl, G), ("QTa", hl, G), ("KT", hl, i // 4), ("KTa", hl, i // 4)], writes=["bank%d" % bi])

        def EXPV(ui):
            G, hl, u, last = units[ui]
            nkb = 4 * G + 4
            ob = hl
            obk = "bank%d" % ob
            d = 1 + (ui % 3)
            P_ = Pt[ui % 3]
            pk = "P%d" % (ui % 3)
            if len(u) == 2:
                S.op("act", lambda e: e.activation(out=P_[:, 0:1024], in_=kb.dbanks[d][:, 0:1024], func=AF.Exp),
                     reads=["bank%d" % (2 * d), "bank%d" % (2 * d + 1)], writes=[pk])
            else:
                q_lo, n = geom(G, u[0])
                S.op("act", lambda e: e.activation(out=P_[:, 0:n], in_=banks[2 * d][:, 0:n], func=AF.Exp), reads=["bank%d" % (2 * d)], writes=[pk])
                S.op("dve", lambda e: e.tensor_tensor(out=P_[:, 0:128], in0=P_[:, 0:128], in1=tri, op=ALU.mult), reads=[pk, "cb"], writes=[pk])
            for hh, i in enumerate(u):
                q_lo, n = geom(G, i)
                c_lo = q_lo - G * 512
                S.op("pe", lambda e, i=i, hh=hh, n=n, c_lo=c_lo: e.matmul(banks[ob][:, c_lo:512], lhsT=V[hl][:, i, :], rhs=P_[:, hh * 512:hh * 512 + n], start=(i == 0), stop=(i == nkb - 1)),
                     reads=[pk, ("V", hl, i // 4), "V%dones" % hl], writes=[obk])
            if last:
                ot = OT[G % 3]
                otk = "OT%d" % (G % 3)
                num = slice(0, 64) if hl == 0 else slice(64, 128)
                den = slice(64, 128) if hl == 0 else slice(0, 64)
                S.op("dve", lambda e: e.reciprocal(out=rec[num, :], in_=banks[ob][den, :]), reads=[obk], writes=[("rec", hl)])
                S.op("dve", lambda e: e.tensor_tensor(out=tmp[num, :], in0=banks[ob][num, :], in1=rec[num, :], op=ALU.mult), reads=[obk, ("rec", hl)], writes=[("tmp", hl)])
                S.op("dve", lambda e: e.tensor_tensor(out=ot[num, :], in0=tmp[num, :], in1=Gt[num, G * 512:(G + 1) * 512], op=ALU.mult), reads=[("tmp", hl), ("Gt", G)], writes=[(otk, hl)])
                if hl == 1:
                    S.dma("pool", oT_dst(p, G), ot[:], reads=[(otk, 0), (otk, 1)], writes=[oT_key(G) if callable(oT_key) else oT_key])
                    S._record(S.lastw[oT_key(G) if callable(oT_key) else oT_key], [(otk, 0), (otk, 1)], [])
                    if "on_oT" in io and p == 1 and G % 4 == 3:
                        io["on_oT"](G // 4)

        for ui in range(len(units) + 2):
            if ui < len(units):
                QK(ui)
            if ui >= 2:
                EXPV(ui - 2)

def consts_C():
    ident = np.eye(128, dtype=np.float32)
    bd = np.zeros((128, 128), np.float32)
    bd[0:64, 0:64] = 1.0 / 64
    bd[64:128, 64:128] = 1.0 / 64
    tri = (np.arange(128)[None, :] >= np.arange(128)[:, None]).astype(np.float32)
    return np.ascontiguousarray(np.concatenate([ident, bd, tri], axis=1).astype(NPBF))


def prep_C(inp, core, hT):
    b, g = core // 4, core % 4
    w_in = inp["odd_w_in"][0]
    cols = []
    for p in range(2):
        hc = (4 * g + 2 * p) * 64
        for base in (0, 1024, 2048, 3072):
            cols.append(w_in[:, base + hc:base + hc + 128])
    cols.append(w_in[:, 4096 + 4 * g:4096 + 4 * g + 4])
    w = np.ascontiguousarray(np.concatenate(cols, axis=1))
    cst = np.zeros((128, 16), np.float32)
    cst[:, 0] = np.tile(inp["fox_qnorm_g"][0], 2)
    cst[:, 1] = np.tile(inp["fox_knorm_g"][0], 2)
    cst[:, 2] = np.repeat(inp["fox_b_f"][0][4 * g:4 * g + 4], 32)
    r = np.arange(128) % 32
    cst[:, 3] = (r == 3)
    cst[:, 4] = (r == 4)
    cst[:, 5] = (r == 5)
    cst[:, 6] = (r < 3)
    cst[:, 7] = -1.0 * (r == 0)
    cst[:, 8] = -1.0 * (r == 1)
    cst[:, 9] = -1.0 * (r == 2)
    cst[:, 10] = (r >= 3) & (r < 6)
    return dict(hT=hT, w=w, cst=cst, cb=consts_C())


def build_A(ngroups=16, stage=9):
    kb = KB()
    for _ in emit_A(kb, {}, ngroups, stage):
        pass
    return kb.finish(["mo"])


def emit_A(kb, io, ngroups=16, stage=9):
    S = kb.S
    hT_d = kb.din("hT", [D, T], BF16) if "hT_src" not in io else None
    w_d = kb.din("w", [D, 768], F32)
    cst_d = kb.din("cst", [128, 8], F32)
    pw_d = kb.din("pw", [128, 128], F32)
    cb_d = kb.din("cb", [128, 1152], BF16)
    rm_d = kb.din("rm", [128, 512], F32)
    mo_d = kb.dout("mo", [256, T], BF16) if "mo_dst" not in io else None
    banks = kb.banks()
    if "hT_src" in io:
        hT_src, hT_key = io["hT_src"], io["hT_key"]
    else:
        hT_v = hT_d.rearrange("(k p) t -> p k t", p=128)
        hT_src, hT_key = (lambda G: hT_v[:, :, G * 512:(G + 1) * 512]), "hT_ext"
    if "mo_dst" in io:
        mo_dst, mo_key = io["mo_dst"], io["mo_key"]
    else:
        mo_v = mo_d.rearrange("(r p) t -> p r t", p=128)
        mo_dst, mo_key = (lambda G: mo_v[:, :, G * 512:(G + 1) * 512]), "mo"

    cst = kb.sb("cst", [128, 8], F32)
    cb = kb.sb("cb", [128, 1152], BF16)
    rm = kb.sb("rm", [128, 512], F32)
    pwf = kb.sb("pwf", [128, 128], F32)
    pwb = kb.sb("pwb", [128, 128], BF16)
    S.dma("sp", cst[:], cst_d[:, :], writes=["cst"])
    S.dma("sp", cb[:], cb_d[:, :], writes=["cb"])
    S.dma("sp", rm[:], rm_d[:, :], writes=["rm"])
    S.dma("sp", pwf[:], pw_d[:, :], writes=["pwf"])
    S.op("dve", lambda e: e.tensor_copy(out=pwb[:], in_=pwf[:]), reads=["pwf"], writes=["pwb"])
    ident = cb[:, 0:128]
    o128 = cb[:, 128:256]
    trim = cb[:, 256:768]
    Bmain = cb[:, 768:896]
    Bprev = cb[:, 896:1024]
    B0 = cb[:, 1024:1152]
    lbt = kb.sb("lbt", [128, 8], F32)
    S.op("act", lambda e: e.activation(out=lbt[:, 0:3], in_=cst[:, 0:3], func=AF.Exp), reads=["cst"], writes=["lbt"])
    S.op("dve", lambda e: e.tensor_tensor(out=lbt[:, 3:4], in0=lbt[:, 0:1], in1=lbt[:, 1:2], op=ALU.add), reads=["lbt"], writes=["lbt"])
    S.op("dve", lambda e: e.tensor_tensor(out=lbt[:, 3:4], in0=lbt[:, 3:4], in1=lbt[:, 2:3], op=ALU.add), reads=["lbt"], writes=["lbt"])
    S.op("dve", lambda e: e.reciprocal(out=lbt[:, 3:4], in_=lbt[:, 3:4]), reads=["lbt"], writes=["lbt"])
    S.op("dve", lambda e: e.tensor_tensor(out=lbt[:, 4:5], in0=lbt[:, 0:1], in1=lbt[:, 3:4], op=ALU.mult), reads=["lbt"], writes=["lbt"])
    S.op("dve", lambda e: e.tensor_scalar(out=lbt[:, 5:6], in0=lbt[:, 4:5], scalar1=-1.0, scalar2=1.0, op0=ALU.mult, op1=ALU.add), reads=["lbt"], writes=["lbt"])
    S.op("dve", lambda e: e.tensor_scalar(out=lbt[:, 6:7], in0=lbt[:, 4:5], scalar1=-1.0, scalar2=None, op0=ALU.add), reads=["lbt"], writes=["lbt"])
    S.op("dve", lambda e: e.memset(lbt[:, 7:8], EPS), reads=[], writes=["lbt7"])
    lbc = lbt[:, 4:5]
    omlb = lbt[:, 5:6]
    nomlb = lbt[:, 6:7]
    epsc = lbt[:, 7:8]

    Wb = kb.sb("Wb", [128, 8, 768], BF16)
    wst = [kb.sb("wst%d" % i, [128, 768], F32) for i in range(2)]
    for k in range(8):
        st = wst[k % 2]
        S.dma("sp", st[:], w_d[k * 128:(k + 1) * 128, :], writes=["wst%d" % (k % 2)])
        S.op("dve", lambda e, k=k, st=st: e.tensor_copy(out=Wb[:, k, :], in_=st[:]), reads=["wst%d" % (k % 2)], writes=["Wb"])

    yield
    hTg = [kb.sb("hTg%d" % i, [128, 8, 512], BF16) for i in range(2)]
    t = {n: kb.sb(n, [128, 512], F32) for n in ["e", "L1", "L2", "logf", "cum", "key", "dq", "dl", "Eq", "Ek", "El", "lnm", "rstd", "ega", "egb", "t1"]}
    sga = [kb.sb("sga%d" % i, [128, 512], F32) for i in range(2)]
    sgb = [kb.sb("sgb%d" % i, [128, 512], F32) for i in range(2)]
    Qs = [kb.sb("Qs%d" % i, [128, 512], BF16) for i in range(2)]
    Ks = [kb.sb("Ks%d" % i, [128, 512], BF16) for i in range(2)]
    Kh = [kb.sb("Kh%d" % i, [128, 512], BF16) for i in range(2)]
    sq = kb.sb("sq", [128, 512], BF16)
    pooledT = kb.sb("pooledT", [128, 512], BF16)
    KhT = kb.sb("KhT", [128, 4, 2, 128], BF16)
    S.op("dve", lambda e: e.memset(KhT[:], 0.0), writes=["KhT"])
    Vt = [kb.sb("Vt%d" % i, [128, 4, 128], BF16) for i in range(2)]
    Ut = [kb.sb("Ut%d" % i, [128, 4, 128], BF16) for i in range(3)]
    At = kb.sb("At", [128, 512], BF16)
    er = [kb.sb("er%d" % i, [128, 16], F32) for i in range(2)]
    state = kb.sb("state", [128, 128], F32)
    stb = [kb.sb("stb%d" % i, [128, 128], BF16) for i in range(2)]
    Ost = [kb.sb("Ost%d" % i, [128, 2, 512], BF16) for i in range(4)]
    S.op("dve", lambda e: e.memset(state[:], 0.0), writes=["state"])

    bst = {"1": 0, "2": 0}

    def nb1():
        b = bst["1"] % 4
        bst["1"] += 1
        return b, "bank%d" % b

    def nb2():
        b = 4 + bst["2"] % 4
        bst["2"] += 1
        return b, "bank%d" % b

    def s1(G):
        pz = G % 2
        hg = hTg[pz]
        hkey = "hTg%d" % pz
        S.dma("sp", hg[:], hT_src(G), reads=[hT_key(G) if callable(hT_key) else hT_key], writes=[hkey])
        yield
        pe_ops = []
        f_ops = []

        def proj_fm(blk):
            b, bk = nb1()
            for k in range(8):
                pe_ops.append(("pe", lambda e, k=k, b=b: e.matmul(banks[b][:, :], lhsT=Wb[:, k, blk * 128:(blk + 1) * 128], rhs=hg[:, k, :], start=(k == 0), stop=(k == 7)),
                               ["Wb", hkey], [bk]))
            return b, bk

        bfm, bfk = proj_fm(1)
        bq, bqk = proj_fm(0)
        biu = []
        for half in range(2):
            b, bk = nb1()
            for jj in range(2):
                j = half * 2 + jj
                for k in range(8):
                    pe_ops.append(("pe", lambda e, k=k, j=j, jj=jj, b=b: e.matmul(banks[b][:, jj * 256:(jj + 1) * 256], lhsT=hg[:, k, j * 128:(j + 1) * 128], rhs=Wb[:, k, 512:768], start=(k == 0), stop=(k == 7)),
                                   ["Wb", hkey], [bk]))
            biu.append((b, bk))
        bga, bgak = proj_fm(2)
        bgb, bgbk = proj_fm(3)
        cum3 = t["cum"][:].rearrange("p (c s) -> p c s", c=8)
        er_ = er[pz]
        erk = "er%d" % pz
        f_ops += [
            ("act", lambda e: e.activation(out=t["e"][:], in_=banks[bfm][:, :], func=AF.Exp, scale=-1.0), [bfk], ["e"]),
            ("act", lambda e: e.activation(out=t["L1"][:], in_=t["e"][:], func=AF.Ln, scale=lbc, bias=rm[:, 1:2]), ["e", "lbt", "rm"], ["L1"]),
            ("act", lambda e: e.activation(out=t["L2"][:], in_=t["e"][:], func=AF.Ln, scale=1.0, bias=rm[:, 1:2]), ["e", "rm"], ["L2"]),
            ("dve", lambda e: e.tensor_tensor(out=t["logf"][:], in0=t["L1"][:], in1=t["L2"][:], op=ALU.subtract), ["L1", "L2"], ["logf"]),
            ("dve", lambda e: e.tensor_tensor_scan(out=t["cum"][:], data0=rm[:], data1=t["logf"][:], initial=0.0, op0=ALU.mult, op1=ALU.add), ["rm", "logf"], ["cum"]),
            ("act", lambda e: e.activation(out=t["e"][:], in_=t["L2"][:], func=AF.Exp, scale=-1.0), ["L2", "L1"], ["e"]),
            ("dve", lambda e: e.tensor_scalar(out=t["key"][:], in0=t["e"][:], scalar1=nomlb, scalar2=omlb, op0=ALU.mult, op1=ALU.add), ["e", "lbt"], ["key"]),
            ("dve", lambda e: e.tensor_tensor(out=t["dq"][:].rearrange("p (c s) -> p c s", c=8), in0=cum3, in1=cum3[:, :, 31:32].to_broadcast([128, 8, 64]), op=ALU.subtract), ["cum"], ["dq"]),
            ("dve", lambda e: e.tensor_tensor(out=t["dl"][:].rearrange("p (c s) -> p c s", c=8), in0=cum3[:, :, 63:64].to_broadcast([128, 8, 64]), in1=cum3, op=ALU.subtract), ["cum"], ["dl"]),
            ("act", lambda e: e.activation(out=t["Eq"][:], in_=t["dq"][:], func=AF.Exp), ["dq"], ["Eq"]),
            ("act", lambda e: e.activation(out=t["Ek"][:], in_=t["dq"][:], func=AF.Exp, scale=-1.0), ["dq"], ["Ek"]),
            ("act", lambda e: e.activation(out=t["El"][:], in_=t["dl"][:], func=AF.Exp), ["dl"], ["El"]),
            ("act", lambda e: e.activation(out=er_[:, 0:8], in_=cum3[:, :, 31], func=AF.Exp), ["cum"], [erk]),
            ("act", lambda e: e.activation(out=er_[:, 8:16], in_=cum3[:, :, 63], func=AF.Exp), ["cum"], [erk]),
            ("dve", lambda e: e.tensor_tensor(out=Qs[pz][:], in0=banks[bq][:, :], in1=t["Eq"][:], op=ALU.mult), [bqk, "Eq"], ["Qs%d" % pz]),
            ("dve", lambda e: e.tensor_tensor(out=Ks[pz][:], in0=t["key"][:], in1=t["Ek"][:], op=ALU.mult), ["key", "Ek"], ["Ks%d" % pz]),
            ("dve", lambda e: e.tensor_tensor(out=Kh[pz][:], in0=t["key"][:], in1=t["El"][:], op=ALU.mult), ["key", "El"], ["Kh%d" % pz]),
        ]
        Uc = Ut[G % 3]
        ukey = "Ut%d" % (G % 3)
        for half in range(2):
            b, bk = biu[half]
            v3 = banks[b][:, :].rearrange("p (j c) -> p j c", j=2)
            f_ops.append(("dve", lambda e, v3=v3, half=half: e.tensor_copy(out=Vt[pz][:, 2 * half:2 * half + 2, :], in_=v3[:, :, 0:128]), [bk], ["Vt%d" % pz]))
            f_ops.append(("dve", lambda e, v3=v3, half=half: e.tensor_copy(out=Uc[:, 2 * half:2 * half + 2, :], in_=v3[:, :, 128:256]), [bk], [ukey]))
        for (bg, bgk, eg, sg, sgk) in ((bga, bgak, "ega", sga[pz], "sga%d" % pz), (bgb, bgbk, "egb", sgb[pz], "sgb%d" % pz)):
            f_ops.append(("act", lambda e, bg=bg, eg=eg: e.activation(out=t[eg][:], in_=banks[bg][:, :], func=AF.Tanh, scale=0.5), [bgk], [eg]))
            f_ops.append(("dve", lambda e, eg=eg: e.tensor_scalar(out=t[eg][:], in0=t[eg][:], scalar1=0.5, scalar2=0.5, op0=ALU.mult, op1=ALU.add), [eg], [eg]))
            f_ops.append(("dve", lambda e, bg=bg, eg=eg, sg=sg: e.tensor_tensor(out=sg[:], in0=banks[bg][:, :], in1=t[eg][:], op=ALU.mult), [bgk, eg], [sgk]))
        pi = 0
        for _ in range(8):
            en, fn, rd, wr = pe_ops[pi]
            S.op(en, fn, reads=rd, writes=wr)
            pi += 1
            yield
        for (en, fn, rd, wr) in f_ops:
            S.op(en, fn, reads=rd, writes=wr)
            yield
            for _ in range(3):
                if pi < len(pe_ops):
                    en2, fn2, rd2, wr2 = pe_ops[pi]
                    S.op(en2, fn2, reads=rd2, writes=wr2)
                    pi += 1
                    yield
        while pi < len(pe_ops):
            en2, fn2, rd2, wr2 = pe_ops[pi]
            S.op(en2, fn2, reads=rd2, writes=wr2)
            pi += 1
            yield

    def s2(G):
        pz = G % 2
        er_ = er[pz]
        erk = "er%d" % pz
        Qk, Kk, Khk, Vk = "Qs%d" % pz, "Ks%d" % pz, "Kh%d" % pz, "Vt%d" % pz
        bt, btk = nb2()
        ptv = banks[bt][:, :].bitcast(BF16)
        for j in range(4):
            S.op("pe", lambda e, j=j: e.transpose(ptv[:, j * 128:(j + 1) * 128], Kh[pz][:, j * 128:(j + 1) * 128], ident), reads=[Khk, "cb"], writes=[btk])
            yield
        ptv3 = ptv[:, 0:512].rearrange("p (j k) -> p j k", j=4)
        S.op("dve", lambda e: e.tensor_copy(out=KhT[0:64, :, 0, :], in_=ptv3[0:64, :, :]), reads=[btk], writes=["KhT"])
        yield
        S.op("dve", lambda e: e.tensor_copy(out=KhT[64:128, :, 1, :], in_=ptv3[64:128, :, :]), reads=[btk], writes=["KhT"])
        yield
        bs, bsk = nb2()
        for pair in range(4):
            S.op("pe", lambda e, pair=pair: e.matmul(banks[bs][:, pair * 128:(pair + 1) * 128], lhsT=Ks[pz][:, pair * 128:(pair + 1) * 128], rhs=Qs[pz][:, pair * 128:(pair + 1) * 128], start=True, stop=True),
                 reads=[Kk, Qk], writes=[bsk])
            yield
        S.op("dve", lambda e: e.tensor_tensor(out=At[:], in0=banks[bs][:, :], in1=trim, op=ALU.mult), reads=[bsk, "cb"], writes=["At"])
        yield
        bz = []
        for zh in range(2):
            b, bk = nb2()
            for cc in range(4):
                c = zh * 4 + cc
                j, half = c // 2, c % 2
                S.op("pe", lambda e, cc=cc, j=j, half=half, b=b: e.matmul(banks[b][:, cc * 128:(cc + 1) * 128], lhsT=KhT[:, j, half, :], rhs=Vt[pz][:, j, :], start=True, stop=True),
                     reads=["KhT", Vk], writes=[bk])
                yield
            bz.append((b, bk))
        bo, bok = nb2()
        for c in range(8):
            gc = 8 * G + c
            j, half = c // 2, c % 2
            sb_ = stb[gc % 2]
            sbk = "stb%d" % (gc % 2)
            S.op("act", lambda e, c=c, sb_=sb_: e.activation(out=sb_[:], in_=state[:], func=AF.Copy, scale=er_[:, c:c + 1]), reads=["state", erk], writes=[sbk])
            if half == 0:
                S.op("pe", lambda e, j=j: e.matmul(banks[bo][:, j * 128:(j + 1) * 128], lhsT=Vt[pz][:, j, :], rhs=At[:, j * 128:(j + 1) * 128], start=True, stop=False),
                     reads=[Vk, "At"], writes=[bok])
            S.op("pe", lambda e, c=c, sb_=sb_, half=half: e.matmul(banks[bo][:, c * 64:(c + 1) * 64], lhsT=sb_[:], rhs=Qs[pz][:, c * 64:(c + 1) * 64], start=False, stop=(half == 1)),
                 reads=[sbk, Qk], writes=[bok])
            zb, zbk = bz[c // 4]
            S.op("dve", lambda e, c=c, zb=zb: e.scalar_tensor_tensor(out=state[:], in0=state[:], scalar=er_[:, 8 + c:9 + c], in1=banks[zb][:, (c % 4) * 128:(c % 4 + 1) * 128], op0=ALU.mult, op1=ALU.add),
                 reads=["state", erk, zbk], writes=["state"])
            yield
        if stage <= 3:
            return
        S.op("act", lambda e: e.activation(out=sq[:], in_=banks[bo][:, :], func=AF.Square), reads=[bok], writes=["sq"])
        yield
        bm, bmk = nb2()
        S.op("pe", lambda e: e.matmul(banks[bm][:, :], lhsT=o128, rhs=sq[:], start=True, stop=True), reads=["sq", "cb"], writes=[bmk])
        yield
        S.op("act", lambda e: e.activation(out=t["lnm"][:], in_=banks[bm][:, :], func=AF.Ln, scale=1.0, bias=epsc), reads=[bmk, "lbt7"], writes=["lnm"])
        yield
        S.op("act", lambda e: e.activation(out=t["rstd"][:], in_=t["lnm"][:], func=AF.Exp, scale=-0.5), reads=["lnm"], writes=["rstd"])
        yield
        S.op("dve", lambda e: e.scalar_tensor_tensor(out=t["t1"][:], in0=banks[bo][:, :], scalar=cst[:, 3:4], in1=t["rstd"][:], op0=ALU.mult, op1=ALU.mult), reads=[bok, "cst", "rstd"], writes=["t1"])
        yield
        os_ = Ost[G % 4]
        osk = "Ost%d" % (G % 4)
        S.op("dve", lambda e: e.tensor_tensor(out=os_[:, 0, :], in0=t["t1"][:], in1=sga[pz][:], op=ALU.mult), reads=["t1", "sga%d" % pz], writes=[osk])
        yield
        bp, bpk = nb2()
        Uc = Ut[G % 3]
        ukey = "Ut%d" % (G % 3)
        Up = Ut[(G - 1) % 3]
        upkey = "Ut%d" % ((G - 1) % 3)
        for j in range(4):
            tj = 4 * G + j
            if tj == 0:
                S.op("pe", lambda e, j=j: e.matmul(banks[bp][:, j * 128:(j + 1) * 128], lhsT=Uc[:, j, :], rhs=B0, start=True, stop=True), reads=[ukey, "cb"], writes=[bpk])
            else:
                S.op("pe", lambda e, j=j: e.matmul(banks[bp][:, j * 128:(j + 1) * 128], lhsT=Uc[:, j, :], rhs=Bmain, start=True, stop=False), reads=[ukey, "cb"], writes=[bpk])
                if j == 0:
                    S.op("pe", lambda e, j=j: e.matmul(banks[bp][:, j * 128:(j + 1) * 128], lhsT=Up[:, 3, :], rhs=Bprev, start=False, stop=True), reads=[upkey, "cb"], writes=[bpk])
                else:
                    S.op("pe", lambda e, j=j: e.matmul(banks[bp][:, j * 128:(j + 1) * 128], lhsT=Uc[:, j - 1, :], rhs=Bprev, start=False, stop=True), reads=[ukey, "cb"], writes=[bpk])
            yield
        S.op("dve", lambda e: e.tensor_copy(out=pooledT[:], in_=banks[bp][:, :]), reads=[bpk], writes=["pooledT"])
        yield
        bob, bobk = nb2()
        S.op("pe", lambda e: e.matmul(banks[bob][:, :], lhsT=pwb[:], rhs=pooledT[:], start=True, stop=True), reads=["pwb", "pooledT"], writes=[bobk])
        yield
        S.op("dve", lambda e: e.scalar_tensor_tensor(out=os_[:, 1, :], in0=banks[bob][:, :], scalar=cst[:, 4:5], in1=sgb[pz][:], op0=ALU.mult, op1=ALU.mult), reads=[bobk, "cst", "sgb%d" % pz], writes=[osk])
        yield
        S.dma("pool", mo_dst(G), os_[:], reads=[osk], writes=[mo_key(G) if callable(mo_key) else mo_key])
        if "on_mo" in io and G % 4 == 3:
            io["on_mo"](G // 4)
        yield

    RA, RB = 1, 2

    def drain(*gens):
        act_ = [g for g in gens if g is not None]
        reps = [RA, RB]
        while act_:
            for gi, g in enumerate(list(act_)):
                for _ in range(reps[gi] if len(gens) > 1 and gi < 2 else 1):
                    try:
                        next(g)
                    except StopIteration:
                        if g in act_:
                            act_.remove(g)
                        break

    drain(s1(0))
    for G in range(ngroups):
        drain(s2(G), s1(G + 1) if G + 1 < ngroups else None)

WINDOWS = (2, 4, 8, 16)


def consts_A(g):
    w = WINDOWS[g]
    ident = np.eye(128, dtype=np.float32)
    o128 = np.full((128, 128), 1.0 / 128, np.float32)
    s = np.arange(128)[:, None]
    t = np.arange(128)[None, :]
    tri1 = ((s // 64 == t // 64) & (s % 64 <= t % 64)).astype(np.float32)
    trim = np.tile(tri1, (1, 4))
    t = np.arange(128)[None, :]
    Bmain = ((s <= t) & (s >= t - w + 1)).astype(np.float32) / w - (s == t)
    Bprev = ((s - 128) >= (t - w + 1)).astype(np.float32) / w
    cnt = np.minimum(t + 1, w).astype(np.float32)
    B0 = ((s <= t) & (s >= t - w + 1)).astype(np.float32) / cnt - (s == t)
    cb = np.concatenate([ident, o128, trim, Bmain, Bprev, B0], axis=1).astype(NPBF)
    rm = np.ones((128, 512), np.float32)
    rm[:, 0::64] = 0.0
    return np.ascontiguousarray(cb), rm


def prep_A(inp, core, hT):
    b, g = core // 4, core % 4
    w_in = inp["even_w_in"][0]
    sl = slice(g * 128, (g + 1) * 128)
    cols = [w_in[:, 0:512][:, sl], w_in[:, 512:1024][:, sl], w_in[:, 1536:2048][:, sl], w_in[:, 2560:3072][:, sl],
            w_in[:, 1024:1536][:, sl], w_in[:, 2048:2560][:, sl]]
    w = np.ascontiguousarray(np.concatenate(cols, axis=1))
    cst = np.zeros((128, 8), np.float32)
    cst[:, 0:3] = inp["hgrn_lb"][:, sl].T
    cst[:, 3] = inp["hgrn_onorm_g"][0][sl]
    cst[:, 4] = inp["pool_scale"][0][sl]
    cb, rm = consts_A(g)
    return dict(hT=hT, w=w, cst=cst, pw=np.ascontiguousarray(inp["pool_w"][0, g]), cb=cb, rm=rm)


_PROGS = {}
GROUPS = [[0, 1, 2, 3], [4, 5, 6, 7]]


def build_fused(upto=99):
    kb = KB()
    nc, S = kb.nc, kb.S
    dt_ = lambda n, sh, d: nc.dram_tensor(n, sh, d).ap()
    h0own = [dt_("i_h0own%d" % g, [D, 512], BF16) for g in range(4)]
    h0all = [dt_("i_h0all%d" % g, [4 * D, 512], BF16) for g in range(4)]
    moown = [dt_("i_moown%d" % q, [256, TOK], BF16) for q in range(4)]
    moall = dt_("i_moall", [4 * D, TOK], BF16)
    x1 = dt_("i_x1", [TOK, D], F32)
    h1own = [dt_("i_h1own%d" % g, [D, 512], BF16) for g in range(4)]
    h1all = [dt_("i_h1all%d" % g, [4 * D, 512], BF16) for g in range(4)]
    oown = [dt_("i_oown%d" % q, [256, TOK], BF16) for q in range(4)]
    oall = dt_("i_oall", [4 * D, TOK], BF16)
    out = nc.dram_tensor("out", [TOK, D], F32, kind="ExternalOutput").ap()
    jq = nc.sync.partition_id() % 4

    def own_dst(ts):
        return lambda g: ts[g].rearrange("(k p) t -> p k t", p=128)

    def all_src(ts):
        return lambda G: ts[G // 4].rearrange("(r k p) t -> p r k t", r=4, p=128)[:, G % 4, :, :]

    def dyn_src(t):
        v = t.rearrange("(q k p) t -> p (q k) t", q=4, p=128)
        return lambda g: v[:, g * 8:(g + 1) * 8, bass.ds(jq * 512, 512)]

    def gather_h(own, al, nm):
        return lambda g: S.collective("AllGather", [own[g][:, :]], [al[g][:, :]], GROUPS, reads=[nm + "own%d" % g], writes=[nm + "all%d" % g])

    def gather_q(own, al, nm):
        return lambda q: S.collective("AllGather", [own[q][:, :]], [al[q * D:(q + 1) * D, :]], GROUPS, reads=[nm + "own%d" % q], writes=[nm + "all%d" % q])

    midA = ExitStack()
    kb.tag, kb.pes = "A_", midA
    genA = emit_A(kb, dict(hT_src=all_src(h0all), hT_key=lambda G: "h0all%d" % (G // 4),
                           mo_dst=(lambda G: moown[G // 4].rearrange("(r p) t -> p r t", p=128)[:, :, (G % 4) * 512:(G % 4 + 1) * 512]),
                           mo_key=lambda G: "moown%d" % (G // 4), on_mo=gather_q(moown, moall, "mo")))
    next(genA)
    kb.tag, kb.pes = "", kb.es
    modown = dt_("i_modown", [128, MC], F32)
    modall = dt_("i_modall", [512, MC], F32)
    with kb.phase("M_"):
        emit_mods(kb, modown)
        S.collective("AllGather", [modown[:, :]], [modall[:, :]], GROUPS, reads=["modown"], writes=["modall"])
    MODS = (modall, "modall")
    if upto <= 0:
        return kb.finish(["modall"])
    with kb.phase("P0_"):
        emit_tok(kb, "P0", dict(mods=MODS, nlayer=0, hT_dst=own_dst(h0own), hT_key=lambda g: "h0own%d" % g, on_hT=gather_h(h0own, h0all, "h0")))
    if upto <= 1:
        return kb.finish(["h0own%d" % g for g in range(4)])
    if upto <= 2:
        return kb.finish(["h0all%d" % g for g in range(4)])
    with kb.phase("A_"):
        for _ in genA:
            pass
    midA.close()
    if upto <= 3:
        return kb.finish(["moown%d" % g for g in range(4)])
    if upto <= 4:
        return kb.finish(["moall%d" % q for q in range(4)])
    midC = ExitStack()
    kb.tag, kb.pes = "C_", midC
    genC = emit_C(kb, dict(hT_src=all_src(h1all), hT_key=lambda G: "h1all%d" % (G // 4),
                           oT_dst=(lambda p, G: oown[G // 4][p * 128:(p + 1) * 128, (G % 4) * 512:(G % 4 + 1) * 512]),
                           oT_key=lambda G: "oown%d" % (G // 4), on_oT=gather_q(oown, oall, "o")))
    next(genC)
    kb.tag, kb.pes = "", kb.es
    with kb.phase("B_"):
        emit_tok(kb, "PB", dict(mods=MODS, layer=0, nlayer=1, mT_src=dyn_src(moall), mT_key=(lambda g: "moall%d" % g), hT_dst=own_dst(h1own), hT_key=lambda g: "h1own%d" % g, on_hT=gather_h(h1own, h1all, "h1"),
                                xo_dst=(lambda ti: x1[ti * 128:(ti + 1) * 128, :]), xo_key="x1"))
    if upto <= 5:
        return kb.finish(["x1"] + ["h1own%d" % g for g in range(4)])
    with kb.phase("C_"):
        for _ in genC:
            pass
    midC.close()
    if upto <= 7:
        return kb.finish(["oown%d" % g for g in range(4)])
    with kb.phase("D_"):
        emit_tok(kb, "PD", dict(mods=MODS, layer=1, mT_src=dyn_src(oall), mT_key=(lambda g: "oall%d" % g), x_src=(lambda ti: x1[ti * 128:(ti + 1) * 128, :]), x_key="x1",
                                xo_dst=(lambda ti: out[ti * 128:(ti + 1) * 128, :]), xo_key="out"))
    return kb.finish(["out"])


def _bc(v, n=128):
    return np.ascontiguousarray(np.broadcast_to(np.asarray(v, np.float32), (n, v.shape[-1])))


def _inputs(x, c, norm_g, ada_w, ada_b, hgrn_lb, even_w_in, hgrn_onorm_g, pool_w, pool_scale,
            even_w_out, odd_w_in, fox_b_f, fox_qnorm_g, fox_knorm_g, odd_w_out):
    f = lambda a: np.asarray(a, np.float32)
    return dict(x=f(x), c=f(c), norm_g=f(norm_g), ada_w=f(ada_w), ada_b=f(ada_b), hgrn_lb=f(hgrn_lb), even_w_in=f(even_w_in),
                hgrn_onorm_g=f(hgrn_onorm_g), pool_w=f(pool_w), pool_scale=f(pool_scale), even_w_out=f(even_w_out),
                odd_w_in=f(odd_w_in), fox_b_f=f(fox_b_f), fox_qnorm_g=f(fox_qnorm_g), fox_knorm_g=f(fox_knorm_g), odd_w_out=f(odd_w_out))


def kernel(_upto=99, **kw):
    inp = _inputs(**kw)
    if "F" not in _PROGS:
        _PROGS["F"] = build_fused(_upto)
    ident = np.ascontiguousarray(np.eye(128, dtype=np.float32).astype(NPBF))
    perm = np.concatenate([np.concatenate([np.arange(g * 128, (g + 1) * 128), 512 + np.arange(g * 128, (g + 1) * 128)]) for g in range(4)])
    wout0 = np.ascontiguousarray(inp["even_w_out"][0][perm, :])
    wout1 = np.ascontiguousarray(inp["odd_w_out"][0])
    aw, ab = inp["ada_w"], inp["ada_b"]
    shared = {
        "P0_ng": _bc(inp["norm_g"][0]), "B_wout": wout0, "B_ng": _bc(inp["norm_g"][1]), "D_wout": wout1,
        "ident": ident,
    }
    maps = []
    Wall = np.concatenate([aw[0], aw[1]], axis=1)
    ball = np.concatenate([ab[0], ab[1]])
    for core in range(NCORES):
        b, j = core // 4, core % 4
        m = dict(shared)
        m["x"] = np.ascontiguousarray(inp["x"][b].reshape(16, 512, D)[j::4].reshape(TOK, D))
        m["cT"] = np.ascontiguousarray(inp["c"][b].reshape(8, 128).T)
        m["M_w"] = np.ascontiguousarray(Wall[:, j * MC:(j + 1) * MC])
        m["M_b"] = _bc(ball[j * MC:(j + 1) * MC])
        pa = prep_A(inp, core, None)
        pc = prep_C(inp, core, None)
        for k, v in pa.items():
            if k != "hT":
                m["A_" + k] = v
        for k, v in pc.items():
            if k != "hT":
                m["C_" + k] = v
        maps.append(m)
    res = run_bass_kernel_spmd(_PROGS["F"], maps, core_ids=list(range(NCORES)))
    r = res.results
    out = np.empty((2, 16, 512, D), np.float32)
    for b in range(2):
        for j in range(4):
            out[b, j::4] = np.asarray(r[b * 4 + j]["out"], np.float32).reshape(4, 512, D)
    return np.ascontiguousarray(out.reshape(2, T, D))
```

```python
import numpy as np
import ml_dtypes
from contextlib import ExitStack
import concourse.bass as bass
import concourse.mybir as mybir
from concourse.bass_utils import run_bass_kernel_spmd

F32 = mybir.dt.float32
BF16 = mybir.dt.bfloat16
AF = mybir.ActivationFunctionType
ALU = mybir.AluOpType
NPBF = ml_dtypes.bfloat16

T = 8192
D = 1024
TOK = 2048
EPS = 1e-6
NCORES = 8
CC_QOS = "P3"


class Sched:
    def __init__(self, nc, es, n_lanes=16):
        self.nc = nc
        self.engs = {"pe": nc.tensor, "act": nc.scalar, "dve": nc.vector, "pool": nc.gpsimd, "sp": nc.sync}
        self.sem = {}
        self.cnt = {}
        for n in ["pe", "act", "dve", "pool", "cc"]:
            self.sem[n] = es.enter_context(nc.semaphore("s_" + n))
            self.cnt[n] = 0
        self.lanes = []
        for i in range(n_lanes):
            nm = "lane%d" % i
            self.sem[nm] = es.enter_context(nc.semaphore("s_" + nm))
            self.cnt[nm] = 0
            self.lanes.append(nm)
        self.planes = []
        for i in range(4):
            nm = "plane%d" % i
            self.sem[nm] = es.enter_context(nc.semaphore("s_" + nm))
            self.cnt[nm] = 0
            self.planes.append(nm)
        self.lane_rr = 0
        self.plane_rr = 0
        self.waited = {n: {} for n in self.engs}
        self.lastw = {}
        self.readers = {}

    def _need(self, reads, writes):
        need = {}

        def add(ev):
            if ev is None:
                return
            s, v = ev
            if need.get(s, 0) < v:
                need[s] = v

        for k in reads:
            add(self.lastw.get(k))
        for k in writes:
            add(self.lastw.get(k))
            for ev in self.readers.get(k, []):
                add(ev)
        return need

    def _emit_waits(self, e, need):
        eng = self.engs[e]
        for s, v in need.items():
            if self.waited[e].get(s, 0) >= v:
                continue
            eng.wait_ge(self.sem[s], v)
            self.waited[e][s] = v

    def _record(self, ev, reads, writes):
        for k in reads:
            lst = self.readers.setdefault(k, [])
            lst.append(ev)
            if len(lst) > 64:
                mx = {}
                for s, v in lst:
                    mx[s] = max(mx.get(s, 0), v)
                self.readers[k] = list(mx.items())
        for k in writes:
            self.lastw[k] = ev
            self.readers[k] = []

    def op(self, e, fn, reads=(), writes=()):
        need = self._need(reads, writes)
        if e == "pe":
            need.pop("pe", None)
        self._emit_waits(e, need)
        ins = fn(self.engs[e])
        self.cnt[e] += 1
        ins.then_inc(self.sem[e], 1)
        self._record((e, self.cnt[e]), reads, writes)
        return ins

    def dma(self, q, out, in_, reads=(), writes=(), **kw):
        if q == "pool":
            lane = self.planes[self.plane_rr % len(self.planes)]
            self.plane_rr += 1
        else:
            lane = self.lanes[self.lane_rr % len(self.lanes)]
            self.lane_rr += 1
        need = self._need(reads, writes)
        if self.cnt[lane] > 0:
            need[lane] = max(need.get(lane, 0), self.cnt[lane])
        self._emit_waits(q, need)
        ins = self.engs[q].dma_start(out=out, in_=in_, **kw)
        self.cnt[lane] += 16
        ins.then_inc(self.sem[lane], 16)
        self._record((lane, self.cnt[lane]), reads, writes)
        return ins

    def barrier(self):
        for e in self.engs:
            need = {n: c for n, c in self.cnt.items() if c > 0 and n != "cc"}
            self._emit_waits(e, need)

    def collective(self, kind, ins, outs, groups, reads=(), writes=()):
        need = self._need(reads, writes)
        self._emit_waits("pool", need)
        ins_ = self.nc.gpsimd.collective_compute(kind, ALU.bypass, replica_groups=groups, ins=ins, outs=outs, dma_qos=CC_QOS)
        self.cnt["cc"] += 1
        ins_.then_inc(self.sem["cc"], 1)
        self._record(("cc", self.cnt["cc"]), reads, writes)
        return ins_

    def wait_all(self, e, keys):
        need = self._need(keys, keys)
        self._emit_waits(e, need)


class _Phase:
    def __init__(self, kb, tag):
        self.kb, self.tag = kb, tag

    def __enter__(self):
        self.kb.tag = self.tag
        self.kb.pes = ExitStack()
        return self

    def __exit__(self, *a):
        self.kb.S.barrier()
        self.kb.pes.close()
        self.kb.pes = self.kb.es
        self.kb.tag = ""
        return False


class BankView:
    def __init__(self, t, off):
        self.t, self.off = t, off

    def __getitem__(self, key):
        rows, cols = key
        a = self.off + (cols.start or 0)
        b = self.off + (512 if cols.stop is None else cols.stop)
        return self.t[rows, a:b]


class KB:
    def __init__(self):
        self.nc = bass.Bass("TRN2", target_bir_lowering=False)
        self.es = ExitStack()
        self.pes = self.es
        self.S = Sched(self.nc, self.es)
        self.outs = []
        self.tag = ""
        self.shared = {}
        self._banks = None

    def phase(self, tag):
        return _Phase(self, tag)

    def din(self, name, shape, dt, shared=False):
        if shared:
            if name not in self.shared:
                self.shared[name] = self.nc.dram_tensor(name, list(shape), dt, kind="ExternalInput").ap()
            return self.shared[name]
        return self.nc.dram_tensor(self.tag + name, list(shape), dt, kind="ExternalInput").ap()

    def dout(self, name, shape, dt):
        self.outs.append(self.tag + name)
        return self.nc.dram_tensor(self.tag + name, list(shape), dt, kind="ExternalOutput").ap()

    def sb(self, name, shape, dt):
        return self.pes.enter_context(self.nc.sbuf_tensor("sb_" + self.tag + name, list(shape), dt))

    def banks(self):
        if self._banks is None:
            self.dbanks = [self.es.enter_context(self.nc.psum_tensor("dbank%d" % i, [128, 1024], F32)) for i in range(4)]
            self._banks = [BankView(self.dbanks[i // 2], (i % 2) * 512) for i in range(8)]
        return self._banks

    def finish(self, out_keys=None):
        self.S.wait_all("sp", out_keys if out_keys is not None else self.outs)
        self.es.close()
        return self.nc


def emit_cond_rep(kb, cT_d):
    S = kb.S
    cT = kb.sb("cT", [128, 8], F32)
    cE = kb.sb("cE", [128, 8], F32)
    cond = kb.sb("cond", [128, 8], F32)
    ones = kb.sb("ones128", [128, 128], F32)
    crep = kb.sb("cond_rep", [128, 8, 128], F32)
    S.dma("sp", cT[:], cT_d[:, :], writes=["cT"])
    S.op("dve", lambda e: e.memset(ones[:], 1.0), writes=["ones128"])
    S.op("act", lambda e: e.activation(out=cE[:], in_=cT[:], func=AF.Exp, scale=-1.0), reads=["cT"], writes=["cE"])
    S.op("dve", lambda e: e.tensor_scalar_add(out=cE[:], in0=cE[:], scalar1=1.0), reads=["cE"], writes=["cE"])
    S.op("dve", lambda e: e.reciprocal(out=cE[:], in_=cE[:]), reads=["cE"], writes=["cE"])
    S.op("dve", lambda e: e.tensor_mul(out=cond[:], in0=cT[:], in1=cE[:]), reads=["cE", "cT"], writes=["cond"])
    for k in range(8):
        S.op("dve", lambda e, k=k: e.tensor_scalar(out=crep[:, k, :], in0=ones[:], scalar1=cond[:, k:k + 1], scalar2=None, op0=ALU.mult),
             reads=["cond", "ones128"], writes=["crep"])
    return crep


def emit_mod_bcast(kb, crep, w_d, b_d, out_t, out_key, ncols, banks, stage, tag, b0=0, c0=0):
    S = kb.S
    nb = ncols // 512
    S.dma("sp", out_t[:, 0:ncols], b_d[:, c0:c0 + ncols], writes=[out_key])
    idx = 0
    for k in range(8):
        st = stage[k % len(stage)]
        skey = "%s_st%d" % (tag, k % len(stage))
        S.dma("sp", st[:, 0:ncols], w_d[k * 128:(k + 1) * 128, c0:c0 + ncols], writes=[skey])
        for n in range(nb):
            S.op("pe", lambda e, n=n, k=k, st=st: e.matmul(banks[b0 + n][:, :], lhsT=crep[:, k, :], rhs=st[:, n * 512:(n + 1) * 512], start=(k == 0), stop=(k == 7)),
                 reads=["crep", skey], writes=["bank%d" % (b0 + n)])
    for n in range(nb):
        S.op("dve", lambda e, n=n: e.tensor_tensor(out=out_t[:, n * 512:(n + 1) * 512], in0=out_t[:, n * 512:(n + 1) * 512], in1=banks[b0 + n][:, :], op=ALU.add),
             reads=["bank%d" % (b0 + n), out_key], writes=[out_key])


MC = 1536


def emit_mods(kb, modown_d):
    S = kb.S
    cT_d = kb.din("cT", [128, 8], F32, shared=True)
    w_d = kb.din("w", [D, MC], F32)
    b_d = kb.din("b", [128, MC], F32)
    banks = kb.banks()
    crep = emit_cond_rep(kb, cT_d)
    stage = [kb.sb("stg%d" % i, [128, MC], F32) for i in range(4)]
    out_t = kb.sb("mo", [128, MC], F32)
    emit_mod_bcast(kb, crep, w_d, b_d, out_t, "mo", MC, banks, stage, "g", b0=0, c0=0)
    S.dma("sp", modown_d[:, :], out_t[:], reads=["mo"], writes=["modown"])


def load_mods(kb, dst, dst_key, modall_d, c0, n):
    S = kb.S
    c = c0
    while c < c0 + n:
        r, lo = c // MC, c % MC
        m = min(MC - lo, c0 + n - c)
        S.dma("sp", dst[:, c - c0:c - c0 + m], modall_d[r * 128:(r + 1) * 128, lo:lo + m], reads=["modall"], writes=[dst_key])
        c += m


def build_tok(mode):
    kb = KB()
    emit_tok(kb, mode, {})
    return kb.finish(["hT", "xo"])


def emit_tok(kb, mode, io):
    S = kb.S
    do_proj = mode in ("PB", "PD")
    do_norm = mode in ("P0", "PB")
    x_d = kb.din("x", [TOK, D], F32, shared=True) if "x_src" not in io else None
    cT_d = kb.din("cT", [128, 8], F32, shared=True)
    ident_d = kb.din("ident", [128, 128], BF16, shared=True)
    if do_proj:
        mT_d = kb.din("mT", [D, TOK], BF16) if "mT_src" not in io else None
        wout_d = kb.din("wout", [D, D], F32)
        gw_d = kb.din("gw", [D, D], F32) if "mods" not in io else None
        gb_d = kb.din("gb", [128, D], F32) if "mods" not in io else None
    if do_norm:
        nw_d = kb.din("nw", [D, 2 * D], F32) if "mods" not in io else None
        nb_d = kb.din("nb", [128, 2 * D], F32) if "mods" not in io else None
        ng_d = kb.din("ng", [128, D], F32)
        hT_d = kb.dout("hT", [D, TOK], BF16) if "hT_dst" not in io else None
    if mode != "P0":
        xo_d = kb.dout("xo", [TOK, D], F32) if "xo_dst" not in io else None
    banks = kb.banks()
    mods = io.get("mods")
    if mods is None:
        crep = emit_cond_rep(kb, cT_d)
    stage = [kb.sb("stg%d" % i, [128, 2048 if mods is None else 1024], F32) for i in range(3)]
    ident = kb.sb("ident", [128, 128], BF16)
    S.dma("sp", ident[:], ident_d[:, :], writes=["ident"])
    if do_proj:
        gate = kb.sb("gate", [128, D], F32)
        if mods is None:
            emit_mod_bcast(kb, crep, gw_d, gb_d, gate, "gate", D, banks, stage, "g")
        else:
            load_mods(kb, gate, "gate", mods[0], io["layer"] * 3 * D + 2 * D, D)
        Wp = kb.sb("Wp", [128, 8, D], BF16)
        for k in range(8):
            st = stage[k % 3]
            skey = "g_st%d" % (k % 3)
            S.dma("sp", st[:, 0:D], wout_d[k * 128:(k + 1) * 128, :], writes=[skey])
            S.op("dve", lambda e, k=k, st=st: e.tensor_tensor(out=Wp[:, k, :], in0=st[:, 0:D], in1=gate[:], op=ALU.mult),
                 reads=[skey, "gate"], writes=["Wp"])
    if do_norm:
        modn = kb.sb("modn", [128, 2 * D], F32)
        if mods is None:
            emit_mod_bcast(kb, crep, nw_d, nb_d, modn, "modn", 2 * D, banks, stage, "g", b0=4)
        else:
            load_mods(kb, modn, "modn", mods[0], io["nlayer"] * 3 * D, 2 * D)
        Gb = kb.sb("Gb", [128, D], F32)
        S.dma("sp", Gb[:], ng_d[:, :], writes=["Gb"])
        S.op("dve", lambda e: e.scalar_tensor_tensor(out=Gb[:], in0=modn[:, D:2 * D], scalar=1.0, in1=Gb[:], op0=ALU.add, op1=ALU.mult),
             reads=["modn", "Gb"], writes=["Gb"])
    xt = [kb.sb("xt%d" % i, [128, D], F32) for i in range(TOK // 128)]
    if do_proj:
        mTt = [kb.sb("mTt%d" % i, [128, 8, 512], BF16) for i in range(2)]
    if do_norm:
        junk = kb.sb("junk", [128, D], F32)
        ss = [kb.sb("ss%d" % i, [128, 4], F32) for i in range(2)]
        tt = kb.sb("tt", [128, D], F32)
        hb = [kb.sb("hb%d" % i, [128, D], BF16) for i in range(2)]
        hst = [kb.sb("hst%d" % i, [128, 8, 512], BF16) for i in range(2)]
        epsb = kb.sb("epsb", [128, 1], F32)
        S.op("dve", lambda e: e.memset(epsb[:], EPS), writes=["epsb"])
    if do_proj:
        if "mT_src" in io:
            mT_src, mT_key = io["mT_src"], io["mT_key"]
        else:
            mT_v = mT_d.rearrange("(k p) t -> p k t", p=128)
            mT_src, mT_key = (lambda g: mT_v[:, :, g * 512:(g + 1) * 512]), "mT_ext"
    if do_norm:
        if "hT_dst" in io:
            hT_dst, hT_key = io["hT_dst"], io["hT_key"]
        else:
            hT_v = hT_d.rearrange("(k p) t -> p k t", p=128)
            hT_dst, hT_key = (lambda g: hT_v[:, :, g * 512:(g + 1) * 512]), "hT"
    if "x_src" in io:
        x_src, x_key = io["x_src"], io["x_key"]
    else:
        x_src, x_key = (lambda ti: x_d[ti * 128:(ti + 1) * 128, :]), "x_ext"
    if mode != "P0":
        if "xo_dst" in io:
            xo_dst, xo_key = io["xo_dst"], io["xo_key"]
        else:
            xo_dst, xo_key = (lambda ti: xo_d[ti * 128:(ti + 1) * 128, :]), "xo"
    NT = TOK // 128

    def part1(ti):
        g, j = ti // 4, ti % 4
        if do_proj:
            mt = mTt[g % 2]
            mkey = "mTt%d" % (g % 2)
            mk = (lambda gg: mT_key(gg)) if callable(mT_key) else (lambda gg: mT_key)
            if j == 0 and g == 0:
                S.dma("sp", mTt[0][:], mT_src(0), reads=[mk(0)], writes=["mTt0"])
            if j == 2 and g + 1 < TOK // 512:
                S.dma("sp", mTt[(g + 1) % 2][:], mT_src(g + 1), reads=[mk(g + 1)], writes=["mTt%d" % ((g + 1) % 2)])
        x_ = xt[ti]
        xkey = "xt%d" % ti
        if ti == 0:
            for t2 in range(NT):
                S.dma("sp", xt[t2][:], x_src(t2), reads=[x_key], writes=["xt%d" % t2])
        cur = x_
        curkey = xkey
        if do_proj:
            for half in range(2):
                bk = banks[half]
                for k in range(8):
                    S.op("pe", lambda e, k=k, half=half, bk=bk, mt=mt, j=j: e.matmul(bk[:, :], lhsT=mt[:, k, j * 128:(j + 1) * 128], rhs=Wp[:, k, half * 512:(half + 1) * 512], start=(k == 0), stop=(k == 7)),
                         reads=[mkey, "Wp"], writes=["bank%d" % half])
            x1 = x_
            x1key = xkey
            for half in range(2):
                S.op("dve", lambda e, half=half, x1=x1, x_=x_: e.tensor_tensor(out=x1[:, half * 512:(half + 1) * 512], in0=x_[:, half * 512:(half + 1) * 512], in1=banks[half][:, :], op=ALU.add),
                     reads=[xkey, "bank%d" % half], writes=[x1key])
            S.dma("sp", xo_dst(ti), x1[:], reads=[x1key], writes=[xo_key])
            cur = x1
            curkey = x1key
        if do_norm:
            s_ = ss[ti % 2]
            skey = "ss%d" % (ti % 2)
            S.op("act", lambda e, cur=cur, s_=s_: e.activation(out=junk[:], in_=cur[:], func=AF.Square, accum_out=s_[:, 0:1]),
                 reads=[curkey], writes=["junk", skey])
            S.op("act", lambda e, s_=s_: e.activation(out=s_[:, 1:2], in_=s_[:, 0:1], func=AF.Ln, scale=1.0 / D, bias=epsb[:, 0:1]),
                 reads=[skey, "epsb"], writes=[skey])
            S.op("act", lambda e, s_=s_: e.activation(out=s_[:, 2:3], in_=s_[:, 1:2], func=AF.Exp, scale=-0.5),
                 reads=[skey], writes=[skey])
            S.op("dve", lambda e, cur=cur, s_=s_: e.scalar_tensor_tensor(out=tt[:], in0=cur[:], scalar=s_[:, 2:3], in1=Gb[:], op0=ALU.mult, op1=ALU.mult),
                 reads=[curkey, skey, "Gb"], writes=["tt"])
            h_ = hb[ti % 2]
            hkey = "hb%d" % (ti % 2)
            S.op("dve", lambda e, h_=h_: e.tensor_tensor(out=h_[:], in0=tt[:], in1=modn[:, 0:D], op=ALU.add),
                 reads=["tt", "modn"], writes=[hkey])

    def part2(ti):
        if not do_norm:
            return
        g, j = ti // 4, ti % 4
        h_ = hb[ti % 2]
        hkey = "hb%d" % (ti % 2)
        pbank = banks[2 + (ti % 2)]
        pkey = "bank%d" % (2 + (ti % 2))
        pT = pbank[:, :].bitcast(BF16)
        for k in range(8):
            S.op("pe", lambda e, k=k, h_=h_, pT=pT: e.transpose(pT[:, k * 128:(k + 1) * 128], h_[:, k * 128:(k + 1) * 128], ident[:]),
                 reads=[hkey, "ident"], writes=[pkey])
        hs = hst[g % 2]
        hskey = "hst%d" % (g % 2)
        S.op("act", lambda e, hs=hs, pT=pT, j=j: e.activation(out=hs[:, :, j * 128:(j + 1) * 128], in_=pT.rearrange("p (k t) -> p k t", k=8), func=AF.Copy),
             reads=[pkey], writes=[hskey])
        if j == 3:
            S.dma("sp", hT_dst(g), hs[:], reads=[hskey], writes=[hT_key(g) if callable(hT_key) else hT_key])
            if "on_hT" in io:
                io["on_hT"](g)

    for ti in range(NT + 1):
        if ti < NT:
            part1(ti)
        if ti >= 1:
            part2(ti - 1)

def build_C(ngroups=16):
    kb = KB()
    for _ in emit_C(kb, {}, ngroups):
        pass
    return kb.finish(["oT"])


def emit_C(kb, io, ngroups=16):
    S = kb.S
    hT_d = kb.din("hT", [D, T], BF16) if "hT_src" not in io else None
    w_d = kb.din("w", [D, 1028], F32)
    cst_d = kb.din("cst", [128, 16], F32)
    cb_d = kb.din("cb", [128, 384], BF16)
    oT_d = kb.dout("oT", [256, T], BF16) if "oT_dst" not in io else None
    banks = kb.banks()
    if "hT_src" in io:
        hT_src, hT_key = io["hT_src"], io["hT_key"]
    else:
        hT_v = hT_d.rearrange("(k p) t -> p k t", p=128)
        hT_src, hT_key = (lambda G: hT_v[:, :, G * 512:(G + 1) * 512]), "hT_ext"
    if "oT_dst" in io:
        oT_dst, oT_key = io["oT_dst"], io["oT_key"]
    else:
        oT_dst, oT_key = (lambda p, G: oT_d[p * 128:(p + 1) * 128, G * 512:(G + 1) * 512]), "oT"

    cst = kb.sb("cst", [128, 16], F32)
    cb = kb.sb("cb", [128, 384], BF16)
    S.dma("sp", cst[:], cst_d[:, :], writes=["cst"])
    S.dma("sp", cb[:], cb_d[:, :], writes=["cb"])
    ident = cb[:, 0:128]
    bd64 = cb[:, 128:256]
    tri = cb[:, 256:384]
    negbf = kb.sb("negbf", [128, 1], F32)
    S.op("dve", lambda e: e.tensor_scalar(out=negbf[:], in0=cst[:, 2:3], scalar1=-1.0, scalar2=None, op0=ALU.mult), reads=["cst"], writes=["negbf"])
    eps64 = kb.sb("eps64", [128, 2], F32)
    S.op("dve", lambda e: e.memset(eps64[:, 0:1], 64.0 * EPS), writes=["eps64"])
    S.op("dve", lambda e: e.memset(eps64[:, 1:2], EPS), writes=["eps64"])
    ones512 = kb.sb("ones512", [128, 512], F32)
    S.op("dve", lambda e: e.memset(ones512[:], 1.0), writes=["ones512"])

    Wb = kb.sb("Wb", [128, 8, 1028], BF16)
    wst = [kb.sb("wst%d" % i, [128, 1028], F32) for i in range(2)]
    for k in range(8):
        st = wst[k % 2]
        S.dma("sp", st[:], w_d[k * 128:(k + 1) * 128, :], writes=["wst%d" % (k % 2)])
        S.op("dve", lambda e, k=k, st=st: e.tensor_copy(out=Wb[:, k, :], in_=st[:]), reads=["wst%d" % (k % 2)], writes=["Wb"])
    Wrep = kb.sb("Wrep", [128, 8, 128], BF16)
    for h in range(4):
        S.op("dve", lambda e, h=h: e.tensor_copy(out=Wrep[:, :, 32 * h:32 * h + 32], in_=Wb[:, :, 1024 + h:1025 + h].to_broadcast([128, 8, 32])),
             reads=["Wb"], writes=["Wrep"])

    yield
    hTg = [kb.sb("hTg%d" % i, [128, 8, 512], BF16) for i in range(2)]
    QT = [kb.sb("QT%d" % i, [128, T], BF16) for i in range(2)]
    KT = [kb.sb("KT%d" % i, [128, T], BF16) for i in range(2)]
    Vall = kb.sb("Vall", [128, 64, 192], BF16)
    V = [Vall[:, :, 0:128], Vall[:, :, 64:192]]
    Gt = kb.sb("Gt", [128, T], BF16)
    sqbs = {n: kb.sb("sqb" + n, [128, 512], BF16) for n in ("q", "k")}
    lnvs = {n: kb.sb("lnv" + n, [128, 512], F32) for n in ("q", "k")}
    rss = {n: kb.sb("rs" + n, [128, 512], F32) for n in ("q", "k")}
    ge = kb.sb("ge", [128, 512], F32)
    fe = kb.sb("fe", [128, 512], F32)
    cl = [kb.sb("cl%d" % i, [128, 512], F32) for i in range(2)]
    c1b = kb.sb("c1b", [128, 512], BF16)
    c2b = kb.sb("c2b", [128, 512], BF16)
    c3b = kb.sb("c3b", [128, 512], BF16)
    r1 = kb.sb("r1", [128, 512], F32)
    r2 = kb.sb("r2", [128, 512], F32)
    KA = kb.sb("KA", [128, 512], BF16)
    QA = kb.sb("QA", [128, 512], BF16)
    r1b = kb.sb("r1b", [128, 512], BF16)
    r2b = kb.sb("r2b", [128, 512], BF16)
    AUG = kb.sb("AUG", [128, T], BF16)
    Pt = [kb.sb("P%d" % i, [128, 1024], BF16) for i in range(3)]
    rec = kb.sb("rec", [128, 512], F32)
    tmp = kb.sb("tmp", [128, 512], F32)
    OT = [kb.sb("OT%d" % i, [128, 512], BF16) for i in range(3)]

    S.op("dve", lambda e: e.memset(Vall[:, :, 64:128], 1.0), writes=["V0ones", "V1ones"])

    bstate = {"i": 0}

    def nextbank():
        b = bstate["i"] % 8
        bstate["i"] += 1
        return b

    for p in range(2):
        for G in range(ngroups):
            t0 = G * 512
            hg = hTg[G % 2]
            hkey = "hTg%d" % (G % 2)
            S.dma("sp", hg[:], hT_src(G), reads=[hT_key(G) if callable(hT_key) else hT_key], writes=[hkey])
            pb = {}
            for name, blk in (("q", 0), ("k", 1), ("g", 3)):
                b = nextbank()
                pb[name] = b
                c0 = p * 512 + blk * 128
                for k in range(8):
                    S.op("pe", lambda e, k=k, b=b, c0=c0, hg=hg: e.matmul(banks[b][:, :], lhsT=Wb[:, k, c0:c0 + 128], rhs=hg[:, k, :], start=(k == 0), stop=(k == 7)),
                         reads=["Wb", hkey], writes=["bank%d" % b])
            bv = nextbank()
            bvk = "bank%d" % bv
            c0 = p * 512 + 256
            for j in range(4):
                for k in range(8):
                    S.op("pe", lambda e, k=k, j=j, bv=bv, c0=c0, hg=hg: e.matmul(banks[bv][:, j * 128:(j + 1) * 128], lhsT=hg[:, k, j * 128:(j + 1) * 128], rhs=Wb[:, k, c0:c0 + 128], start=(k == 0), stop=(k == 7)),
                         reads=["Wb", hkey], writes=[bvk])
            if p == 0:
                bf = nextbank()
                bfk = "bank%d" % bf
                for k in range(8):
                    S.op("pe", lambda e, k=k, bf=bf, hg=hg: e.matmul(banks[bf][:, :], lhsT=Wrep[:, k, :], rhs=hg[:, k, :], start=(k == 0), stop=(k == 7)),
                         reads=["Wrep", hkey], writes=[bfk])
            bm = {}
            for name in ("q", "k"):
                b = pb[name]
                S.op("act", lambda e, b=b, name=name: e.activation(out=sqbs[name][:], in_=banks[b][:, :], func=AF.Square), reads=["bank%d" % b], writes=["sqb" + name])
            for name in ("q", "k"):
                bm[name] = nextbank()
                S.op("pe", lambda e, name=name: e.matmul(banks[bm[name]][:, :], lhsT=bd64, rhs=sqbs[name][:], start=True, stop=True), reads=["cb", "sqb" + name], writes=["bank%d" % bm[name]])
            S.op("act", lambda e: e.activation(out=lnvs["q"][:], in_=banks[bm["q"]][:, :], func=AF.Ln, scale=64.0, bias=eps64[:, 0:1]), reads=["bank%d" % bm["q"], "eps64"], writes=["lnvq"])
            S.op("act", lambda e: e.activation(out=lnvs["k"][:], in_=banks[bm["k"]][:, :], func=AF.Ln, scale=1.0, bias=eps64[:, 1:2]), reads=["bank%d" % bm["k"], "eps64"], writes=["lnvk"])
            for name in ("q", "k"):
                S.op("act", lambda e, name=name: e.activation(out=rss[name][:], in_=lnvs[name][:], func=AF.Exp, scale=-0.5), reads=["lnv" + name], writes=["rs" + name])
            for name, blk, dst, dn in (("q", 0, QT, "QT"), ("k", 1, KT, "KT")):
                b = pb[name]
                for hl in range(2):
                    pr = slice(hl * 64, hl * 64 + 64)
                    S.op("dve", lambda e, hl=hl, pr=pr, b=b, dst=dst, blk=blk, name=name: e.scalar_tensor_tensor(out=dst[hl][0:64, t0:t0 + 512], in0=banks[b][pr, :], scalar=cst[pr, blk:blk + 1], in1=rss[name][pr, :], op0=ALU.mult, op1=ALU.mult),
                         reads=["bank%d" % b, "cst", "rs" + name], writes=[(dn, hl, G)])
            bg = pb["g"]
            bgk = "bank%d" % bg
            S.op("act", lambda e, bg=bg: e.activation(out=ge[:], in_=banks[bg][:, :], func=AF.Exp, scale=-1.0), reads=[bgk], writes=["ge"])
            S.op("act", lambda e: e.activation(out=ge[:], in_=ge[:], func=AF.Ln, scale=1.0, bias=ones512[:, 0:1]), reads=["ge", "ones512"], writes=["ge"])
            S.op("act", lambda e: e.activation(out=ge[:], in_=ge[:], func=AF.Exp, scale=-1.0), reads=["ge"], writes=["ge"])
            S.op("dve", lambda e, bg=bg: e.tensor_tensor(out=Gt[:, t0:t0 + 512], in0=banks[bg][:, :], in1=ge[:], op=ALU.mult), reads=[bgk, "ge"], writes=[("Gt", G)])
            bvv = banks[bv][:, :].rearrange("p (j c) -> p j c", j=4)
            S.op("dve", lambda e, bvv=bvv: e.tensor_copy(out=V[0][:, 4 * G:4 * G + 4, 0:64], in_=bvv[:, :, 0:64]), reads=[bvk], writes=[("V", 0, G)])
            S.op("dve", lambda e, bvv=bvv: e.tensor_copy(out=V[1][:, 4 * G:4 * G + 4, 64:128], in_=bvv[:, :, 64:128]), reads=[bvk], writes=[("V", 1, G)])
            if p == 0:
                S.op("act", lambda e, bf=bf: e.activation(out=fe[:], in_=banks[bf][:, :], func=AF.Exp, scale=-1.0, bias=negbf[:, 0:1]), reads=[bfk, "negbf"], writes=["fe"])
                S.op("act", lambda e: e.activation(out=fe[:], in_=fe[:], func=AF.Ln, scale=1.0, bias=ones512[:, 0:1]), reads=["fe", "ones512"], writes=["fe"])
                c_ = cl[G % 2]
                ck = "cl%d" % (G % 2)
                if G == 0:
                    S.op("dve", lambda e, c_=c_: e.tensor_tensor_scan(out=c_[:], data0=ones512[:], data1=fe[:], initial=0.0, op0=ALU.mult, op1=ALU.add),
                         reads=["fe", "ones512"], writes=[ck])
                else:
                    cp = cl[(G - 1) % 2]
                    S.op("dve", lambda e, c_=c_, cp=cp: e.tensor_tensor_scan(out=c_[:], data0=ones512[:], data1=fe[:], initial=cp[:, 511:512], op0=ALU.mult, op1=ALU.add),
                         reads=["fe", "ones512", "cl%d" % ((G - 1) % 2)], writes=[ck])
                S.op("dve", lambda e, c_=c_: e.tensor_copy(out=c1b[:], in_=c_[:]), reads=[ck], writes=["c1b"])
                S.op("dve", lambda e, c_=c_: e.tensor_tensor(out=r1[:], in0=c_[:], in1=c1b[:], op=ALU.subtract), reads=[ck, "c1b"], writes=["r1"])
                S.op("dve", lambda e: e.tensor_copy(out=c2b[:], in_=r1[:]), reads=["r1"], writes=["c2b"])
                S.op("dve", lambda e: e.tensor_tensor(out=r2[:], in0=r1[:], in1=c2b[:], op=ALU.subtract), reads=["r1", "c2b"], writes=["r2"])
                S.op("dve", lambda e: e.tensor_copy(out=c3b[:], in_=r2[:]), reads=["r2"], writes=["c3b"])
                for (dstt, dk, mc) in ((KA, "KA", 3), (QA, "QA", 7)):
                    S.op("dve", lambda e, mc=mc: e.tensor_scalar(out=r1b[:], in0=c1b[:], scalar1=cst[:, mc:mc + 1], scalar2=cst[:, mc + 3:mc + 4], op0=ALU.mult, op1=ALU.add),
                         reads=["c1b", "cst"], writes=["r1b"])
                    S.op("dve", lambda e, mc=mc: e.scalar_tensor_tensor(out=r2b[:], in0=c2b[:], scalar=cst[:, mc + 1:mc + 2], in1=r1b[:], op0=ALU.mult, op1=ALU.add),
                         reads=["c2b", "cst", "r1b"], writes=["r2b"])
                    S.op("dve", lambda e, mc=mc, dstt=dstt: e.scalar_tensor_tensor(out=dstt[:], in0=c3b[:], scalar=cst[:, mc + 2:mc + 3], in1=r2b[:], op0=ALU.mult, op1=ALU.add),
                         reads=["c3b", "cst", "r2b"], writes=[dk])
                for hl in range(2):
                    S.op("dve", lambda e, hl=hl: e.tensor_copy(out=KT[hl][64:70, t0:t0 + 512], in_=KA[32 * hl:32 * hl + 6, :]), reads=["KA"], writes=[("KTa", hl, G)])
                    S.op("dve", lambda e, hl=hl: e.tensor_copy(out=QT[hl][64:70, t0:t0 + 512], in_=QA[32 * hl:32 * hl + 6, :]), reads=["QA"], writes=[("QTa", hl, G)])
                    S.op("dve", lambda e, hl=hl: e.tensor_copy(out=AUG[32 * hl:32 * hl + 6, t0:t0 + 512], in_=KA[64 + 32 * hl:64 + 32 * hl + 6, :]), reads=["KA"], writes=[("AUG", G)])
                    S.op("dve", lambda e, hl=hl: e.tensor_copy(out=AUG[64 + 32 * hl:64 + 32 * hl + 6, t0:t0 + 512], in_=QA[64 + 32 * hl:64 + 32 * hl + 6, :]), reads=["QA"], writes=[("AUG", G)])
        if p == 1:
            TT_ = ngroups * 512
            allG = list(range(ngroups))
            for hl in range(2):
                S.op("dve", lambda e, hl=hl: e.tensor_copy(out=KT[hl][64:70, 0:TT_], in_=AUG[32 * hl:32 * hl + 6, 0:TT_]), reads=[("AUG", G) for G in allG], writes=[("KTa", hl, G) for G in allG])
                S.op("dve", lambda e, hl=hl: e.tensor_copy(out=QT[hl][64:70, 0:TT_], in_=AUG[64 + 32 * hl:64 + 32 * hl + 6, 0:TT_]), reads=[("AUG", G) for G in allG], writes=[("QTa", hl, G) for G in allG])
        units = []
        for G in range(ngroups):
            for hl in range(2):
                us = [(i, i + 1) for i in range(0, 4 * G, 2)] + [(i,) for i in range(4 * G, 4 * G + 4)]
                for n_, u in enumerate(us):
                    units.append((G, hl, u, n_ == len(us) - 1))

        def geom(G, i):
            q_lo = max(G * 512, i * 128)
            return q_lo, (G + 1) * 512 - q_lo

        def QK(ui):
            G, hl, u, _ = units[ui]
            d = 1 + (ui % 3)
            for hh, i in enumerate(u):
                q_lo, n = geom(G, i)
                bi = 2 * d + hh
                S.op("pe", lambda e, i=i, bi=bi, q_lo=q_lo, n=n, hl=hl: e.matmul(banks[bi][:, 0:n], lhsT=KT[hl][0:70, i * 128:(i + 1) * 128], rhs=QT[hl][0:70, q_lo:q_lo + n], start=True, stop=True),
                     reads=[("QT", hl, G), ("QTa", hl, G), ("KT", hl, i // 4), ("KTa", hl, i // 4)], writes=["bank%d" % bi])

        def EXPV(ui):
            G, hl, u, last = units[ui]
            nkb = 4 * G + 4
            ob = hl
            obk = "bank%d" % ob
            d = 1 + (ui % 3)
            P_ = Pt[ui % 3]
            pk = "P%d" % (ui % 3)
            if len(u) == 2:
                S.op("act", lambda e: e.activation(out=P_[:, 0:1024], in_=kb.dbanks[d][:, 0:1024], func=AF.Exp),
                     reads=["bank%d" % (2 * d), "bank%d" % (2 * d + 1)], writes=[pk])
            else:
                q_lo, n = geom(G, u[0])
                S.op("act", lambda e: e.activation(out=P_[:, 0:n], in_=banks[2 * d][:, 0:n], func=AF.Exp), reads=["bank%d" % (2 * d)], writes=[pk])
                S.op("dve", lambda e: e.tensor_tensor(out=P_[:, 0:128], in0=P_[:, 0:128], in1=tri, op=ALU.mult), reads=[pk, "cb"], writes=[pk])
            for hh, i in enumerate(u):
                q_lo, n = geom(G, i)
                c_lo = q_lo - G * 512
                S.op("pe", lambda e, i=i, hh=hh, n=n, c_lo=c_lo: e.matmul(banks[ob][:, c_lo:512], lhsT=V[hl][:, i, :], rhs=P_[:, hh * 512:hh * 512 + n], start=(i == 0), stop=(i == nkb - 1)),
                     reads=[pk, ("V", hl, i // 4), "V%dones" % hl], writes=[obk])
            if last:
                ot = OT[G % 3]
                otk = "OT%d" % (G % 3)
                num = slice(0, 64) if hl == 0 else slice(64, 128)
                den = slice(64, 128) if hl == 0 else slice(0, 64)
                S.op("dve", lambda e: e.reciprocal(out=rec[num, :], in_=banks[ob][den, :]), reads=[obk], writes=[("rec", hl)])
                S.op("dve", lambda e: e.tensor_tensor(out=tmp[num, :], in0=banks[ob][num, :], in1=rec[num, :], op=ALU.mult), reads=[obk, ("rec", hl)], writes=[("tmp", hl)])
                S.op("dve", lambda e: e.tensor_tensor(out=ot[num, :], in0=tmp[num, :], in1=Gt[num, G * 512:(G + 1) * 512], op=ALU.mult), reads=[("tmp", hl), ("Gt", G)], writes=[(otk, hl)])
                if hl == 1:
                    S.dma("pool", oT_dst(p, G), ot[:], reads=[(otk, 0), (otk, 1)], writes=[oT_key(G) if callable(oT_key) else oT_key])
                    S._record(S.lastw[oT_key(G) if callable(oT_key) else oT_key], [(otk, 0), (otk, 1)], [])
                    if "on_oT" in io and p == 1 and G % 4 == 3:
                        io["on_oT"](G // 4)

        for ui in range(len(units) + 2):
            if ui < len(units):
                QK(ui)
            if ui >= 2:
                EXPV(ui - 2)

def consts_C():
    ident = np.eye(128, dtype=np.float32)
    bd = np.zeros((128, 128), np.float32)
    bd[0:64, 0:64] = 1.0 / 64
    bd[64:128, 64:128] = 1.0 / 64
    tri = (np.arange(128)[None, :] >= np.arange(128)[:, None]).astype(np.float32)
    return np.ascontiguousarray(np.concatenate([ident, bd, tri], axis=1).astype(NPBF))


def prep_C(inp, core, hT):
    b, g = core // 4, core % 4
    w_in = inp["odd_w_in"][0]
    cols = []
    for p in range(2):
        hc = (4 * g + 2 * p) * 64
        for base in (0, 1024, 2048, 3072):
            cols.append(w_in[:, base + hc:base + hc + 128])
    cols.append(w_in[:, 4096 + 4 * g:4096 + 4 * g + 4])
    w = np.ascontiguousarray(np.concatenate(cols, axis=1))
    cst = np.zeros((128, 16), np.float32)
    cst[:, 0] = np.tile(inp["fox_qnorm_g"][0], 2)
    cst[:, 1] = np.tile(inp["fox_knorm_g"][0], 2)
    cst[:, 2] = np.repeat(inp["fox_b_f"][0][4 * g:4 * g + 4], 32)
    r = np.arange(128) % 32
    cst[:, 3] = (r == 3)
    cst[:, 4] = (r == 4)
    cst[:, 5] = (r == 5)
    cst[:, 6] = (r < 3)
    cst[:, 7] = -1.0 * (r == 0)
    cst[:, 8] = -1.0 * (r == 1)
    cst[:, 9] = -1.0 * (r == 2)
    cst[:, 10] = (r >= 3) & (r < 6)
    return dict(hT=hT, w=w, cst=cst, cb=consts_C())


def build_A(ngroups=16, stage=9):
    kb = KB()
    for _ in emit_A(kb, {}, ngroups, stage):
        pass
    return kb.finish(["mo"])


def emit_A(kb, io, ngroups=16, stage=9):
    S = kb.S
    hT_d = kb.din("hT", [D, T], BF16) if "hT_src" not in io else None
    w_d = kb.din("w", [D, 768], F32)
    cst_d = kb.din("cst", [128, 8], F32)
    pw_d = kb.din("pw", [128, 128], F32)
    cb_d = kb.din("cb", [128, 1152], BF16)
    rm_d = kb.din("rm", [128, 512], F32)
    mo_d = kb.dout("mo", [256, T], BF16) if "mo_dst" not in io else None
    banks = kb.banks()
    if "hT_src" in io:
        hT_src, hT_key = io["hT_src"], io["hT_key"]
    else:
        hT_v = hT_d.rearrange("(k p) t -> p k t", p=128)
        hT_src, hT_key = (lambda G: hT_v[:, :, G * 512:(G + 1) * 512]), "hT_ext"
    if "mo_dst" in io:
        mo_dst, mo_key = io["mo_dst"], io["mo_key"]
    else:
        mo_v = mo_d.rearrange("(r p) t -> p r t", p=128)
        mo_dst, mo_key = (lambda G: mo_v[:, :, G * 512:(G + 1) * 512]), "mo"

    cst = kb.sb("cst", [128, 8], F32)
    cb = kb.sb("cb", [128, 1152], BF16)
    rm = kb.sb("rm", [128, 512], F32)
    pwf = kb.sb("pwf", [128, 128], F32)
    pwb = kb.sb("pwb", [128, 128], BF16)
    S.dma("sp", cst[:], cst_d[:, :], writes=["cst"])
    S.dma("sp", cb[:], cb_d[:, :], writes=["cb"])
    S.dma("sp", rm[:], rm_d[:, :], writes=["rm"])
    S.dma("sp", pwf[:], pw_d[:, :], writes=["pwf"])
    S.op("dve", lambda e: e.tensor_copy(out=pwb[:], in_=pwf[:]), reads=["pwf"], writes=["pwb"])
    ident = cb[:, 0:128]
    o128 = cb[:, 128:256]
    trim = cb[:, 256:768]
    Bmain = cb[:, 768:896]
    Bprev = cb[:, 896:1024]
    B0 = cb[:, 1024:1152]
    lbt = kb.sb("lbt", [128, 8], F32)
    S.op("act", lambda e: e.activation(out=lbt[:, 0:3], in_=cst[:, 0:3], func=AF.Exp), reads=["cst"], writes=["lbt"])
    S.op("dve", lambda e: e.tensor_tensor(out=lbt[:, 3:4], in0=lbt[:, 0:1], in1=lbt[:, 1:2], op=ALU.add), reads=["lbt"], writes=["lbt"])
    S.op("dve", lambda e: e.tensor_tensor(out=lbt[:, 3:4], in0=lbt[:, 3:4], in1=lbt[:, 2:3], op=ALU.add), reads=["lbt"], writes=["lbt"])
    S.op("dve", lambda e: e.reciprocal(out=lbt[:, 3:4], in_=lbt[:, 3:4]), reads=["lbt"], writes=["lbt"])
    S.op("dve", lambda e: e.tensor_tensor(out=lbt[:, 4:5], in0=lbt[:, 0:1], in1=lbt[:, 3:4], op=ALU.mult), reads=["lbt"], writes=["lbt"])
    S.op("dve", lambda e: e.tensor_scalar(out=lbt[:, 5:6], in0=lbt[:, 4:5], scalar1=-1.0, scalar2=1.0, op0=ALU.mult, op1=ALU.add), reads=["lbt"], writes=["lbt"])
    S.op("dve", lambda e: e.tensor_scalar(out=lbt[:, 6:7], in0=lbt[:, 4:5], scalar1=-1.0, scalar2=None, op0=ALU.add), reads=["lbt"], writes=["lbt"])
    S.op("dve", lambda e: e.memset(lbt[:, 7:8], EPS), reads=[], writes=["lbt7"])
    lbc = lbt[:, 4:5]
    omlb = lbt[:, 5:6]
    nomlb = lbt[:, 6:7]
    epsc = lbt[:, 7:8]

    Wb = kb.sb("Wb", [128, 8, 768], BF16)
    wst = [kb.sb("wst%d" % i, [128, 768], F32) for i in range(2)]
    for k in range(8):
        st = wst[k % 2]
        S.dma("sp", st[:], w_d[k * 128:(k + 1) * 128, :], writes=["wst%d" % (k % 2)])
        S.op("dve", lambda e, k=k, st=st: e.tensor_copy(out=Wb[:, k, :], in_=st[:]), reads=["wst%d" % (k % 2)], writes=["Wb"])

    yield
    hTg = [kb.sb("hTg%d" % i, [128, 8, 512], BF16) for i in range(2)]
    t = {n: kb.sb(n, [128, 512], F32) for n in ["e", "L1", "L2", "logf", "cum", "key", "dq", "dl", "Eq", "Ek", "El", "lnm", "rstd", "ega", "egb", "t1"]}
    sga = [kb.sb("sga%d" % i, [128, 512], F32) for i in range(2)]
    sgb = [kb.sb("sgb%d" % i, [128, 512], F32) for i in range(2)]
    Qs = [kb.sb("Qs%d" % i, [128, 512], BF16) for i in range(2)]
    Ks = [kb.sb("Ks%d" % i, [128, 512], BF16) for i in range(2)]
    Kh = [kb.sb("Kh%d" % i, [128, 512], BF16) for i in range(2)]
    sq = kb.sb("sq", [128, 512], BF16)
    pooledT = kb.sb("pooledT", [128, 512], BF16)
    KhT = kb.sb("KhT", [128, 4, 2, 128], BF16)
    S.op("dve", lambda e: e.memset(KhT[:], 0.0), writes=["KhT"])
    Vt = [kb.sb("Vt%d" % i, [128, 4, 128], BF16) for i in range(2)]
    Ut = [kb.sb("Ut%d" % i, [128, 4, 128], BF16) for i in range(3)]
    At = kb.sb("At", [128, 512], BF16)
    er = [kb.sb("er%d" % i, [128, 16], F32) for i in range(2)]
    state = kb.sb("state", [128, 128], F32)
    stb = [kb.sb("stb%d" % i, [128, 128], BF16) for i in range(2)]
    Ost = [kb.sb("Ost%d" % i, [128, 2, 512], BF16) for i in range(4)]
    S.op("dve", lambda e: e.memset(state[:], 0.0), writes=["state"])

    bst = {"1": 0, "2": 0}

    def nb1():
        b = bst["1"] % 4
        bst["1"] += 1
        return b, "bank%d" % b

    def nb2():
        b = 4 + bst["2"] % 4
        bst["2"] += 1
        return b, "bank%d" % b

    def s1(G):
        pz = G % 2
        hg = hTg[pz]
        hkey = "hTg%d" % pz
        S.dma("sp", hg[:], hT_src(G), reads=[hT_key(G) if callable(hT_key) else hT_key], writes=[hkey])
        yield
        pe_ops = []
        f_ops = []

        def proj_fm(blk):
            b, bk = nb1()
            for k in range(8):
                pe_ops.append(("pe", lambda e, k=k, b=b: e.matmul(banks[b][:, :], lhsT=Wb[:, k, blk * 128:(blk + 1) * 128], rhs=hg[:, k, :], start=(k == 0), stop=(k == 7)),
                               ["Wb", hkey], [bk]))
            return b, bk

        bfm, bfk = proj_fm(1)
        bq, bqk = proj_fm(0)
        biu = []
        for half in range(2):
            b, bk = nb1()
            for jj in range(2):
                j = half * 2 + jj
                for k in range(8):
                    pe_ops.append(("pe", lambda e, k=k, j=j, jj=jj, b=b: e.matmul(banks[b][:, jj * 256:(jj + 1) * 256], lhsT=hg[:, k, j * 128:(j + 1) * 128], rhs=Wb[:, k, 512:768], start=(k == 0), stop=(k == 7)),
                                   ["Wb", hkey], [bk]))
            biu.append((b, bk))
        bga, bgak = proj_fm(2)
        bgb, bgbk = proj_fm(3)
        cum3 = t["cum"][:].rearrange("p (c s) -> p c s", c=8)
        er_ = er[pz]
        erk = "er%d" % pz
        f_ops += [
            ("act", lambda e: e.activation(out=t["e"][:], in_=banks[bfm][:, :], func=AF.Exp, scale=-1.0), [bfk], ["e"]),
            ("act", lambda e: e.activation(out=t["L1"][:], in_=t["e"][:], func=AF.Ln, scale=lbc, bias=rm[:, 1:2]), ["e", "lbt", "rm"], ["L1"]),
            ("act", lambda e: e.activation(out=t["L2"][:], in_=t["e"][:], func=AF.Ln, scale=1.0, bias=rm[:, 1:2]), ["e", "rm"], ["L2"]),
            ("dve", lambda e: e.tensor_tensor(out=t["logf"][:], in0=t["L1"][:], in1=t["L2"][:], op=ALU.subtract), ["L1", "L2"], ["logf"]),
            ("dve", lambda e: e.tensor_tensor_scan(out=t["cum"][:], data0=rm[:], data1=t["logf"][:], initial=0.0, op0=ALU.mult, op1=ALU.add), ["rm", "logf"], ["cum"]),
            ("act", lambda e: e.activation(out=t["e"][:], in_=t["L2"][:], func=AF.Exp, scale=-1.0), ["L2", "L1"], ["e"]),
            ("dve", lambda e: e.tensor_scalar(out=t["key"][:], in0=t["e"][:], scalar1=nomlb, scalar2=omlb, op0=ALU.mult, op1=ALU.add), ["e", "lbt"], ["key"]),
            ("dve", lambda e: e.tensor_tensor(out=t["dq"][:].rearrange("p (c s) -> p c s", c=8), in0=cum3, in1=cum3[:, :, 31:32].to_broadcast([128, 8, 64]), op=ALU.subtract), ["cum"], ["dq"]),
            ("dve", lambda e: e.tensor_tensor(out=t["dl"][:].rearrange("p (c s) -> p c s", c=8), in0=cum3[:, :, 63:64].to_broadcast([128, 8, 64]), in1=cum3, op=ALU.subtract), ["cum"], ["dl"]),
            ("act", lambda e: e.activation(out=t["Eq"][:], in_=t["dq"][:], func=AF.Exp), ["dq"], ["Eq"]),
            ("act", lambda e: e.activation(out=t["Ek"][:], in_=t["dq"][:], func=AF.Exp, scale=-1.0), ["dq"], ["Ek"]),
            ("act", lambda e: e.activation(out=t["El"][:], in_=t["dl"][:], func=AF.Exp), ["dl"], ["El"]),
            ("act", lambda e: e.activation(out=er_[:, 0:8], in_=cum3[:, :, 31], func=AF.Exp), ["cum"], [erk]),
            ("act", lambda e: e.activation(out=er_[:, 8:16], in_=cum3[:, :, 63], func=AF.Exp), ["cum"], [erk]),
            ("dve", lambda e: e.tensor_tensor(out=Qs[pz][:], in0=banks[bq][:, :], in1=t["Eq"][:], op=ALU.mult), [bqk, "Eq"], ["Qs%d" % pz]),
            ("dve", lambda e: e.tensor_tensor(out=Ks[pz][:], in0=t["key"][:], in1=t["Ek"][:], op=ALU.mult), ["key", "Ek"], ["Ks%d" % pz]),
            ("dve", lambda e: e.tensor_tensor(out=Kh[pz][:], in0=t["key"][:], in1=t["El"][:], op=ALU.mult), ["key", "El"], ["Kh%d" % pz]),
        ]
        Uc = Ut[G % 3]
        ukey = "Ut%d" % (G % 3)
        for half in range(2):
            b, bk = biu[half]
            v3 = banks[b][:, :].rearrange("p (j c) -> p j c", j=2)
            f_ops.append(("dve", lambda e, v3=v3, half=half: e.tensor_copy(out=Vt[pz][:, 2 * half:2 * half + 2, :], in_=v3[:, :, 0:128]), [bk], ["Vt%d" % pz]))
            f_ops.append(("dve", lambda e, v3=v3, half=half: e.tensor_copy(out=Uc[:, 2 * half:2 * half + 2, :], in_=v3[:, :, 128:256]), [bk], [ukey]))
        for (bg, bgk, eg, sg, sgk) in ((bga, bgak, "ega", sga[pz], "sga%d" % pz), (bgb, bgbk, "egb", sgb[pz], "sgb%d" % pz)):
            f_ops.append(("act", lambda e, bg=bg, eg=eg: e.activation(out=t[eg][:], in_=banks[bg][:, :], func=AF.Exp, scale=-1.0), [bgk], [eg]))
            f_ops.append(("act", lambda e, eg=eg: e.activation(out=t[eg][:], in_=t[eg][:], func=AF.Ln, scale=1.0, bias=rm[:, 1:2]), [eg, "rm"], [eg]))
            f_ops.append(("act", lambda e, eg=eg: e.activation(out=t[eg][:], in_=t[eg][:], func=AF.Exp, scale=-1.0), [eg], [eg]))
            f_ops.append(("dve", lambda e, bg=bg, eg=eg, sg=sg: e.tensor_tensor(out=sg[:], in0=banks[bg][:, :], in1=t[eg][:], op=ALU.mult), [bgk, eg], [sgk]))
        pi = 0
        for _ in range(8):
            en, fn, rd, wr = pe_ops[pi]
            S.op(en, fn, reads=rd, writes=wr)
            pi += 1
            yield
        for (en, fn, rd, wr) in f_ops:
            S.op(en, fn, reads=rd, writes=wr)
            yield
            for _ in range(3):
                if pi < len(pe_ops):
                    en2, fn2, rd2, wr2 = pe_ops[pi]
                    S.op(en2, fn2, reads=rd2, writes=wr2)
                    pi += 1
                    yield
        while pi < len(pe_ops):
            en2, fn2, rd2, wr2 = pe_ops[pi]
            S.op(en2, fn2, reads=rd2, writes=wr2)
            pi += 1
            yield

    def s2(G):
        pz = G % 2
        er_ = er[pz]
        erk = "er%d" % pz
        Qk, Kk, Khk, Vk = "Qs%d" % pz, "Ks%d" % pz, "Kh%d" % pz, "Vt%d" % pz
        bt, btk = nb2()
        ptv = banks[bt][:, :].bitcast(BF16)
        for j in range(4):
            S.op("pe", lambda e, j=j: e.transpose(ptv[:, j * 128:(j + 1) * 128], Kh[pz][:, j * 128:(j + 1) * 128], ident), reads=[Khk, "cb"], writes=[btk])
            yield
        ptv3 = ptv[:, 0:512].rearrange("p (j k) -> p j k", j=4)
        S.op("dve", lambda e: e.tensor_copy(out=KhT[0:64, :, 0, :], in_=ptv3[0:64, :, :]), reads=[btk], writes=["KhT"])
        yield
        S.op("dve", lambda e: e.tensor_copy(out=KhT[64:128, :, 1, :], in_=ptv3[64:128, :, :]), reads=[btk], writes=["KhT"])
        yield
        bs, bsk = nb2()
        for pair in range(4):
            S.op("pe", lambda e, pair=pair: e.matmul(banks[bs][:, pair * 128:(pair + 1) * 128], lhsT=Ks[pz][:, pair * 128:(pair + 1) * 128], rhs=Qs[pz][:, pair * 128:(pair + 1) * 128], start=True, stop=True),
                 reads=[Kk, Qk], writes=[bsk])
            yield
        S.op("dve", lambda e: e.tensor_tensor(out=At[:], in0=banks[bs][:, :], in1=trim, op=ALU.mult), reads=[bsk, "cb"], writes=["At"])
        yield
        bz = []
        for zh in range(2):
            b, bk = nb2()
            for cc in range(4):
                c = zh * 4 + cc
                j, half = c // 2, c % 2
                S.op("pe", lambda e, cc=cc, j=j, half=half, b=b: e.matmul(banks[b][:, cc * 128:(cc + 1) * 128], lhsT=KhT[:, j, half, :], rhs=Vt[pz][:, j, :], start=True, stop=True),
                     reads=["KhT", Vk], writes=[bk])
                yield
            bz.append((b, bk))
        bo, bok = nb2()
        for c in range(8):
            gc = 8 * G + c
            j, half = c // 2, c % 2
            sb_ = stb[gc % 2]
            sbk = "stb%d" % (gc % 2)
            S.op("act", lambda e, c=c, sb_=sb_: e.activation(out=sb_[:], in_=state[:], func=AF.Copy, scale=er_[:, c:c + 1]), reads=["state", erk], writes=[sbk])
            if half == 0:
                S.op("pe", lambda e, j=j: e.matmul(banks[bo][:, j * 128:(j + 1) * 128], lhsT=Vt[pz][:, j, :], rhs=At[:, j * 128:(j + 1) * 128], start=True, stop=False),
                     reads=[Vk, "At"], writes=[bok])
            S.op("pe", lambda e, c=c, sb_=sb_, half=half: e.matmul(banks[bo][:, c * 64:(c + 1) * 64], lhsT=sb_[:], rhs=Qs[pz][:, c * 64:(c + 1) * 64], start=False, stop=(half == 1)),
                 reads=[sbk, Qk], writes=[bok])
            zb, zbk = bz[c // 4]
            S.op("dve", lambda e, c=c, zb=zb: e.scalar_tensor_tensor(out=state[:], in0=state[:], scalar=er_[:, 8 + c:9 + c], in1=banks[zb][:, (c % 4) * 128:(c % 4 + 1) * 128], op0=ALU.mult, op1=ALU.add),
                 reads=["state", erk, zbk], writes=["state"])
            yield
        if stage <= 3:
            return
        S.op("act", lambda e: e.activation(out=sq[:], in_=banks[bo][:, :], func=AF.Square), reads=[bok], writes=["sq"])
        yield
        bm, bmk = nb2()
        S.op("pe", lambda e: e.matmul(banks[bm][:, :], lhsT=o128, rhs=sq[:], start=True, stop=True), reads=["sq", "cb"], writes=[bmk])
        yield
        S.op("act", lambda e: e.activation(out=t["lnm"][:], in_=banks[bm][:, :], func=AF.Ln, scale=1.0, bias=epsc), reads=[bmk, "lbt7"], writes=["lnm"])
        yield
        S.op("act", lambda e: e.activation(out=t["rstd"][:], in_=t["lnm"][:], func=AF.Exp, scale=-0.5), reads=["lnm"], writes=["rstd"])
        yield
        S.op("dve", lambda e: e.scalar_tensor_tensor(out=t["t1"][:], in0=banks[bo][:, :], scalar=cst[:, 3:4], in1=t["rstd"][:], op0=ALU.mult, op1=ALU.mult), reads=[bok, "cst", "rstd"], writes=["t1"])
        yield
        os_ = Ost[G % 4]
        osk = "Ost%d" % (G % 4)
        S.op("dve", lambda e: e.tensor_tensor(out=os_[:, 0, :], in0=t["t1"][:], in1=sga[pz][:], op=ALU.mult), reads=["t1", "sga%d" % pz], writes=[osk])
        yield
        bp, bpk = nb2()
        Uc = Ut[G % 3]
        ukey = "Ut%d" % (G % 3)
        Up = Ut[(G - 1) % 3]
        upkey = "Ut%d" % ((G - 1) % 3)
        for j in range(4):
            tj = 4 * G + j
            if tj == 0:
                S.op("pe", lambda e, j=j: e.matmul(banks[bp][:, j * 128:(j + 1) * 128], lhsT=Uc[:, j, :], rhs=B0, start=True, stop=True), reads=[ukey, "cb"], writes=[bpk])
            else:
                S.op("pe", lambda e, j=j: e.matmul(banks[bp][:, j * 128:(j + 1) * 128], lhsT=Uc[:, j, :], rhs=Bmain, start=True, stop=False), reads=[ukey, "cb"], writes=[bpk])
                if j == 0:
                    S.op("pe", lambda e, j=j: e.matmul(banks[bp][:, j * 128:(j + 1) * 128], lhsT=Up[:, 3, :], rhs=Bprev, start=False, stop=True), reads=[upkey, "cb"], writes=[bpk])
                else:
                    S.op("pe", lambda e, j=j: e.matmul(banks[bp][:, j * 128:(j + 1) * 128], lhsT=Uc[:, j - 1, :], rhs=Bprev, start=False, stop=True), reads=[ukey, "cb"], writes=[bpk])
            yield
        S.op("dve", lambda e: e.tensor_copy(out=pooledT[:], in_=banks[bp][:, :]), reads=[bpk], writes=["pooledT"])
        yield
        bob, bobk = nb2()
        S.op("pe", lambda e: e.matmul(banks[bob][:, :], lhsT=pwb[:], rhs=pooledT[:], start=True, stop=True), reads=["pwb", "pooledT"], writes=[bobk])
        yield
        S.op("dve", lambda e: e.scalar_tensor_tensor(out=os_[:, 1, :], in0=banks[bob][:, :], scalar=cst[:, 4:5], in1=sgb[pz][:], op0=ALU.mult, op1=ALU.mult), reads=[bobk, "cst", "sgb%d" % pz], writes=[osk])
        yield
        S.dma("pool", mo_dst(G), os_[:], reads=[osk], writes=[mo_key(G) if callable(mo_key) else mo_key])
        if "on_mo" in io and G % 4 == 3:
            io["on_mo"](G // 4)
        yield

    RA, RB = 1, 2

    def drain(*gens):
        act_ = [g for g in gens if g is not None]
        reps = [RA, RB]
        while act_:
            for gi, g in enumerate(list(act_)):
                for _ in range(reps[gi] if len(gens) > 1 and gi < 2 else 1):
                    try:
                        next(g)
                    except StopIteration:
                        if g in act_:
                            act_.remove(g)
                        break

    drain(s1(0))
    for G in range(ngroups):
        drain(s2(G), s1(G + 1) if G + 1 < ngroups else None)

WINDOWS = (2, 4, 8, 16)


def consts_A(g):
    w = WINDOWS[g]
    ident = np.eye(128, dtype=np.float32)
    o128 = np.full((128, 128), 1.0 / 128, np.float32)
    s = np.arange(128)[:, None]
    t = np.arange(128)[None, :]
    tri1 = ((s // 64 == t // 64) & (s % 64 <= t % 64)).astype(np.float32)
    trim = np.tile(tri1, (1, 4))
    t = np.arange(128)[None, :]
    Bmain = ((s <= t) & (s >= t - w + 1)).astype(np.float32) / w - (s == t)
    Bprev = ((s - 128) >= (t - w + 1)).astype(np.float32) / w
    cnt = np.minimum(t + 1, w).astype(np.float32)
    B0 = ((s <= t) & (s >= t - w + 1)).astype(np.float32) / cnt - (s == t)
    cb = np.concatenate([ident, o128, trim, Bmain, Bprev, B0], axis=1).astype(NPBF)
    rm = np.ones((128, 512), np.float32)
    rm[:, 0::64] = 0.0
    return np.ascontiguousarray(cb), rm


def prep_A(inp, core, hT):
    b, g = core // 4, core % 4
    w_in = inp["even_w_in"][0]
    sl = slice(g * 128, (g + 1) * 128)
    cols = [w_in[:, 0:512][:, sl], w_in[:, 512:1024][:, sl], w_in[:, 1536:2048][:, sl], w_in[:, 2560:3072][:, sl],
            w_in[:, 1024:1536][:, sl], w_in[:, 2048:2560][:, sl]]
    w = np.ascontiguousarray(np.concatenate(cols, axis=1))
    cst = np.zeros((128, 8), np.float32)
    cst[:, 0:3] = inp["hgrn_lb"][:, sl].T
    cst[:, 3] = inp["hgrn_onorm_g"][0][sl]
    cst[:, 4] = inp["pool_scale"][0][sl]
    cb, rm = consts_A(g)
    return dict(hT=hT, w=w, cst=cst, pw=np.ascontiguousarray(inp["pool_w"][0, g]), cb=cb, rm=rm)


_PROGS = {}
GROUPS = [[0, 1, 2, 3], [4, 5, 6, 7]]


def build_fused(upto=99):
    kb = KB()
    nc, S = kb.nc, kb.S
    dt_ = lambda n, sh, d: nc.dram_tensor(n, sh, d).ap()
    h0own = [dt_("i_h0own%d" % g, [D, 512], BF16) for g in range(4)]
    h0all = [dt_("i_h0all%d" % g, [4 * D, 512], BF16) for g in range(4)]
    moown = [dt_("i_moown%d" % q, [256, TOK], BF16) for q in range(4)]
    moall = dt_("i_moall", [4 * D, TOK], BF16)
    x1 = dt_("i_x1", [TOK, D], F32)
    h1own = [dt_("i_h1own%d" % g, [D, 512], BF16) for g in range(4)]
    h1all = [dt_("i_h1all%d" % g, [4 * D, 512], BF16) for g in range(4)]
    oown = [dt_("i_oown%d" % q, [256, TOK], BF16) for q in range(4)]
    oall = dt_("i_oall", [4 * D, TOK], BF16)
    out = nc.dram_tensor("out", [TOK, D], F32, kind="ExternalOutput").ap()
    jq = nc.sync.partition_id() % 4

    def own_dst(ts):
        return lambda g: ts[g].rearrange("(k p) t -> p k t", p=128)

    def all_src(ts):
        return lambda G: ts[G // 4].rearrange("(r k p) t -> p r k t", r=4, p=128)[:, G % 4, :, :]

    def dyn_src(t):
        v = t.rearrange("(q k p) t -> p (q k) t", q=4, p=128)
        return lambda g: v[:, g * 8:(g + 1) * 8, bass.ds(jq * 512, 512)]

    def gather_h(own, al, nm):
        return lambda g: S.collective("AllGather", [own[g][:, :]], [al[g][:, :]], GROUPS, reads=[nm + "own%d" % g], writes=[nm + "all%d" % g])

    def gather_q(own, al, nm):
        return lambda q: S.collective("AllGather", [own[q][:, :]], [al[q * D:(q + 1) * D, :]], GROUPS, reads=[nm + "own%d" % q], writes=[nm + "all%d" % q])

    midA = ExitStack()
    kb.tag, kb.pes = "A_", midA
    genA = emit_A(kb, dict(hT_src=all_src(h0all), hT_key=lambda G: "h0all%d" % (G // 4),
                           mo_dst=(lambda G: moown[G // 4].rearrange("(r p) t -> p r t", p=128)[:, :, (G % 4) * 512:(G % 4 + 1) * 512]),
                           mo_key=lambda G: "moown%d" % (G // 4), on_mo=gather_q(moown, moall, "mo")))
    next(genA)
    kb.tag, kb.pes = "", kb.es
    modown = dt_("i_modown", [128, MC], F32)
    modall = dt_("i_modall", [512, MC], F32)
    with kb.phase("M_"):
        emit_mods(kb, modown)
        S.collective("AllGather", [modown[:, :]], [modall[:, :]], GROUPS, reads=["modown"], writes=["modall"])
    MODS = (modall, "modall")
    if upto <= 0:
        return kb.finish(["modall"])
    with kb.phase("P0_"):
        emit_tok(kb, "P0", dict(mods=MODS, nlayer=0, hT_dst=own_dst(h0own), hT_key=lambda g: "h0own%d" % g, on_hT=gather_h(h0own, h0all, "h0")))
    if upto <= 1:
        return kb.finish(["h0own%d" % g for g in range(4)])
    if upto <= 2:
        return kb.finish(["h0all%d" % g for g in range(4)])
    with kb.phase("A_"):
        for _ in genA:
            pass
    midA.close()
    if upto <= 3:
        return kb.finish(["moown%d" % g for g in range(4)])
    if upto <= 4:
        return kb.finish(["moall%d" % q for q in range(4)])
    midC = ExitStack()
    kb.tag, kb.pes = "C_", midC
    genC = emit_C(kb, dict(hT_src=all_src(h1all), hT_key=lambda G: "h1all%d" % (G // 4),
                           oT_dst=(lambda p, G: oown[G // 4][p * 128:(p + 1) * 128, (G % 4) * 512:(G % 4 + 1) * 512]),
                           oT_key=lambda G: "oown%d" % (G // 4), on_oT=gather_q(oown, oall, "o")))
    next(genC)
    kb.tag, kb.pes = "", kb.es
    with kb.phase("B_"):
        emit_tok(kb, "PB", dict(mods=MODS, layer=0, nlayer=1, mT_src=dyn_src(moall), mT_key=(lambda g: "moall%d" % g), hT_dst=own_dst(h1own), hT_key=lambda g: "h1own%d" % g, on_hT=gather_h(h1own, h1all, "h1"),
                                xo_dst=(lambda ti: x1[ti * 128:(ti + 1) * 128, :]), xo_key="x1"))
    if upto <= 5:
        return kb.finish(["x1"] + ["h1own%d" % g for g in range(4)])
    with kb.phase("C_"):
        for _ in genC:
            pass
    midC.close()
    if upto <= 7:
        return kb.finish(["oown%d" % g for g in range(4)])
    with kb.phase("D_"):
        emit_tok(kb, "PD", dict(mods=MODS, layer=1, mT_src=dyn_src(oall), mT_key=(lambda g: "oall%d" % g), x_src=(lambda ti: x1[ti * 128:(ti + 1) * 128, :]), x_key="x1",
                                xo_dst=(lambda ti: out[ti * 128:(ti + 1) * 128, :]), xo_key="out"))
    return kb.finish(["out"])


def _bc(v, n=128):
    return np.ascontiguousarray(np.broadcast_to(np.asarray(v, np.float32), (n, v.shape[-1])))


def _inputs(x, c, norm_g, ada_w, ada_b, hgrn_lb, even_w_in, hgrn_onorm_g, pool_w, pool_scale,
            even_w_out, odd_w_in, fox_b_f, fox_qnorm_g, fox_knorm_g, odd_w_out):
    f = lambda a: np.asarray(a, np.float32)
    return dict(x=f(x), c=f(c), norm_g=f(norm_g), ada_w=f(ada_w), ada_b=f(ada_b), hgrn_lb=f(hgrn_lb), even_w_in=f(even_w_in),
                hgrn_onorm_g=f(hgrn_onorm_g), pool_w=f(pool_w), pool_scale=f(pool_scale), even_w_out=f(even_w_out),
                odd_w_in=f(odd_w_in), fox_b_f=f(fox_b_f), fox_qnorm_g=f(fox_qnorm_g), fox_knorm_g=f(fox_knorm_g), odd_w_out=f(odd_w_out))


def kernel(_upto=99, **kw):
    inp = _inputs(**kw)
    if "F" not in _PROGS:
        _PROGS["F"] = build_fused(_upto)
    ident = np.ascontiguousarray(np.eye(128, dtype=np.float32).astype(NPBF))
    perm = np.concatenate([np.concatenate([np.arange(g * 128, (g + 1) * 128), 512 + np.arange(g * 128, (g + 1) * 128)]) for g in range(4)])
    wout0 = np.ascontiguousarray(inp["even_w_out"][0][perm, :])
    wout1 = np.ascontiguousarray(inp["odd_w_out"][0])
    aw, ab = inp["ada_w"], inp["ada_b"]
    shared = {
        "P0_ng": _bc(inp["norm_g"][0]), "B_wout": wout0, "B_ng": _bc(inp["norm_g"][1]), "D_wout": wout1,
        "ident": ident,
    }
    maps = []
    Wall = np.concatenate([aw[0], aw[1]], axis=1)
    ball = np.concatenate([ab[0], ab[1]])
    for core in range(NCORES):
        b, j = core // 4, core % 4
        m = dict(shared)
        m["x"] = np.ascontiguousarray(inp["x"][b].reshape(16, 512, D)[j::4].reshape(TOK, D))
        m["cT"] = np.ascontiguousarray(inp["c"][b].reshape(8, 128).T)
        m["M_w"] = np.ascontiguousarray(Wall[:, j * MC:(j + 1) * MC])
        m["M_b"] = _bc(ball[j * MC:(j + 1) * MC])
        pa = prep_A(inp, core, None)
        pc = prep_C(inp, core, None)
        for k, v in pa.items():
            if k != "hT":
                m["A_" + k] = v
        for k, v in pc.items():
            if k != "hT":
                m["C_" + k] = v
        maps.append(m)
    res = run_bass_kernel_spmd(_PROGS["F"], maps, core_ids=list(range(NCORES)))
    r = res.results
    out = np.empty((2, 16, 512, D), np.float32)
    for b in range(2):
        for j in range(4):
            out[b, j::4] = np.asarray(r[b * 4 + j]["out"], np.float32).reshape(4, 512, D)
    return np.ascontiguousarray(out.reshape(2, T, D))
```

```python
import numpy as np
import ml_dtypes
from contextlib import ExitStack
import concourse.bass as bass
import concourse.mybir as mybir
from concourse.bass_utils import run_bass_kernel_spmd

F32 = mybir.dt.float32
BF16 = mybir.dt.bfloat16
AF = mybir.ActivationFunctionType
ALU = mybir.AluOpType
NPBF = ml_dtypes.bfloat16

T = 8192
D = 1024
TOK = 2048
EPS = 1e-6
NCORES = 8
CC_QOS = "P3"


class Sched:
    def __init__(self, nc, es, n_lanes=16):
        self.nc = nc
        self.engs = {"pe": nc.tensor, "act": nc.scalar, "dve": nc.vector, "pool": nc.gpsimd, "sp": nc.sync}
        self.sem = {}
        self.cnt = {}
        for n in ["pe", "act", "dve", "pool", "cc"]:
            self.sem[n] = es.enter_context(nc.semaphore("s_" + n))
            self.cnt[n] = 0
        self.lanes = []
        for i in range(n_lanes):
            nm = "lane%d" % i
            self.sem[nm] = es.enter_context(nc.semaphore("s_" + nm))
            self.cnt[nm] = 0
            self.lanes.append(nm)
        self.planes = []
        for i in range(4):
            nm = "plane%d" % i
            self.sem[nm] = es.enter_context(nc.semaphore("s_" + nm))
            self.cnt[nm] = 0
            self.planes.append(nm)
        self.lane_rr = 0
        self.plane_rr = 0
        self.waited = {n: {} for n in self.engs}
        self.lastw = {}
        self.readers = {}

    def _need(self, reads, writes):
        need = {}

        def add(ev):
            if ev is None:
                return
            s, v = ev
            if need.get(s, 0) < v:
                need[s] = v

        for k in reads:
            add(self.lastw.get(k))
        for k in writes:
            add(self.lastw.get(k))
            for ev in self.readers.get(k, []):
                add(ev)
        return need

    def _emit_waits(self, e, need):
        eng = self.engs[e]
        for s, v in need.items():
            if self.waited[e].get(s, 0) >= v:
                continue
            eng.wait_ge(self.sem[s], v)
            self.waited[e][s] = v

    def _record(self, ev, reads, writes):
        for k in reads:
            lst = self.readers.setdefault(k, [])
            lst.append(ev)
            if len(lst) > 64:
                mx = {}
                for s, v in lst:
                    mx[s] = max(mx.get(s, 0), v)
                self.readers[k] = list(mx.items())
        for k in writes:
            self.lastw[k] = ev
            self.readers[k] = []

    def op(self, e, fn, reads=(), writes=()):
        need = self._need(reads, writes)
        if e == "pe":
            need.pop("pe", None)
        self._emit_waits(e, need)
        ins = fn(self.engs[e])
        self.cnt[e] += 1
        ins.then_inc(self.sem[e], 1)
        self._record((e, self.cnt[e]), reads, writes)
        return ins

    def dma(self, q, out, in_, reads=(), writes=(), **kw):
        if q == "pool":
            lane = self.planes[self.plane_rr % len(self.planes)]
            self.plane_rr += 1
        else:
            lane = self.lanes[self.lane_rr % len(self.lanes)]
            self.lane_rr += 1
        need = self._need(reads, writes)
        if self.cnt[lane] > 0:
            need[lane] = max(need.get(lane, 0), self.cnt[lane])
        self._emit_waits(q, need)
        ins = self.engs[q].dma_start(out=out, in_=in_, **kw)
        self.cnt[lane] += 16
        ins.then_inc(self.sem[lane], 16)
        self._record((lane, self.cnt[lane]), reads, writes)
        return ins

    def barrier(self):
        for e in self.engs:
            need = {n: c for n, c in self.cnt.items() if c > 0 and n != "cc"}
            self._emit_waits(e, need)

    def collective(self, kind, ins, outs, groups, reads=(), writes=()):
        need = self._need(reads, writes)
        self._emit_waits("pool", need)
        ins_ = self.nc.gpsimd.collective_compute(kind, ALU.bypass, replica_groups=groups, ins=ins, outs=outs, dma_qos=CC_QOS)
        self.cnt["cc"] += 1
        ins_.then_inc(self.sem["cc"], 1)
        self._record(("cc", self.cnt["cc"]), reads, writes)
        return ins_

    def wait_all(self, e, keys):
        need = self._need(keys, keys)
        self._emit_waits(e, need)


class _Phase:
    def __init__(self, kb, tag):
        self.kb, self.tag = kb, tag

    def __enter__(self):
        self.kb.tag = self.tag
        self.kb.pes = ExitStack()
        return self

    def __exit__(self, *a):
        self.kb.S.barrier()
        self.kb.pes.close()
        self.kb.pes = self.kb.es
        self.kb.tag = ""
        return False


class BankView:
    def __init__(self, t, off):
        self.t, self.off = t, off

    def __getitem__(self, key):
        rows, cols = key
        a = self.off + (cols.start or 0)
        b = self.off + (512 if cols.stop is None else cols.stop)
        return self.t[rows, a:b]


class KB:
    def __init__(self):
        self.nc = bass.Bass("TRN2", target_bir_lowering=False)
        self.es = ExitStack()
        self.pes = self.es
        self.S = Sched(self.nc, self.es)
        self.outs = []
        self.tag = ""
        self.shared = {}
        self._banks = None

    def phase(self, tag):
        return _Phase(self, tag)

    def din(self, name, shape, dt, shared=False):
        if shared:
            if name not in self.shared:
                self.shared[name] = self.nc.dram_tensor(name, list(shape), dt, kind="ExternalInput").ap()
            return self.shared[name]
        return self.nc.dram_tensor(self.tag + name, list(shape), dt, kind="ExternalInput").ap()

    def dout(self, name, shape, dt):
        self.outs.append(self.tag + name)
        return self.nc.dram_tensor(self.tag + name, list(shape), dt, kind="ExternalOutput").ap()

    def sb(self, name, shape, dt):
        return self.pes.enter_context(self.nc.sbuf_tensor("sb_" + self.tag + name, list(shape), dt))

    def banks(self):
        if self._banks is None:
            self.dbanks = [self.es.enter_context(self.nc.psum_tensor("dbank%d" % i, [128, 1024], F32)) for i in range(4)]
            self._banks = [BankView(self.dbanks[i // 2], (i % 2) * 512) for i in range(8)]
        return self._banks

    def finish(self, out_keys=None):
        self.S.wait_all("sp", out_keys if out_keys is not None else self.outs)
        self.es.close()
        return self.nc


def emit_cond_rep(kb, cT_d):
    S = kb.S
    cT = kb.sb("cT", [128, 8], F32)
    cE = kb.sb("cE", [128, 8], F32)
    cond = kb.sb("cond", [128, 8], F32)
    ones = kb.sb("ones128", [128, 128], F32)
    crep = kb.sb("cond_rep", [128, 8, 128], F32)
    S.dma("sp", cT[:], cT_d[:, :], writes=["cT"])
    S.op("dve", lambda e: e.memset(ones[:], 1.0), writes=["ones128"])
    S.op("act", lambda e: e.activation(out=cE[:], in_=cT[:], func=AF.Exp, scale=-1.0), reads=["cT"], writes=["cE"])
    S.op("dve", lambda e: e.tensor_scalar_add(out=cE[:], in0=cE[:], scalar1=1.0), reads=["cE"], writes=["cE"])
    S.op("dve", lambda e: e.reciprocal(out=cE[:], in_=cE[:]), reads=["cE"], writes=["cE"])
    S.op("dve", lambda e: e.tensor_mul(out=cond[:], in0=cT[:], in1=cE[:]), reads=["cE", "cT"], writes=["cond"])
    for k in range(8):
        S.op("dve", lambda e, k=k: e.tensor_scalar(out=crep[:, k, :], in0=ones[:], scalar1=cond[:, k:k + 1], scalar2=None, op0=ALU.mult),
             reads=["cond", "ones128"], writes=["crep"])
    return crep


def emit_mod_bcast(kb, crep, w_d, b_d, out_t, out_key, ncols, banks, stage, tag, b0=0, c0=0):
    S = kb.S
    nb = ncols // 512
    S.dma("sp", out_t[:, 0:ncols], b_d[:, c0:c0 + ncols], writes=[out_key])
    idx = 0
    for k in range(8):
        st = stage[k % len(stage)]
        skey = "%s_st%d" % (tag, k % len(stage))
        S.dma("sp", st[:, 0:ncols], w_d[k * 128:(k + 1) * 128, c0:c0 + ncols], writes=[skey])
        for n in range(nb):
            S.op("pe", lambda e, n=n, k=k, st=st: e.matmul(banks[b0 + n][:, :], lhsT=crep[:, k, :], rhs=st[:, n * 512:(n + 1) * 512], start=(k == 0), stop=(k == 7)),
                 reads=["crep", skey], writes=["bank%d" % (b0 + n)])
    for n in range(nb):
        S.op("dve", lambda e, n=n: e.tensor_tensor(out=out_t[:, n * 512:(n + 1) * 512], in0=out_t[:, n * 512:(n + 1) * 512], in1=banks[b0 + n][:, :], op=ALU.add),
             reads=["bank%d" % (b0 + n), out_key], writes=[out_key])


MC = 1536


def emit_mods(kb, modown_d):
    S = kb.S
    cT_d = kb.din("cT", [128, 8], F32, shared=True)
    w_d = kb.din("w", [D, MC], F32)
    b_d = kb.din("b", [128, MC], F32)
    banks = kb.banks()
    crep = emit_cond_rep(kb, cT_d)
    stage = [kb.sb("stg%d" % i, [128, MC], F32) for i in range(4)]
    out_t = kb.sb("mo", [128, MC], F32)
    emit_mod_bcast(kb, crep, w_d, b_d, out_t, "mo", MC, banks, stage, "g", b0=0, c0=0)
    S.dma("sp", modown_d[:, :], out_t[:], reads=["mo"], writes=["modown"])


def load_mods(kb, dst, dst_key, modall_d, c0, n):
    S = kb.S
    c = c0
    while c < c0 + n:
        r, lo = c // MC, c % MC
        m = min(MC - lo, c0 + n - c)
        S.dma("sp", dst[:, c - c0:c - c0 + m], modall_d[r * 128:(r + 1) * 128, lo:lo + m], reads=["modall"], writes=[dst_key])
        c += m


def build_tok(mode):
    kb = KB()
    emit_tok(kb, mode, {})
    return kb.finish(["hT", "xo"])


def emit_tok(kb, mode, io):
    S = kb.S
    do_proj = mode in ("PB", "PD")
    do_norm = mode in ("P0", "PB")
    x_d = kb.din("x", [TOK, D], F32, shared=True) if "x_src" not in io else None
    cT_d = kb.din("cT", [128, 8], F32, shared=True)
    ident_d = kb.din("ident", [128, 128], BF16, shared=True)
    if do_proj:
        mT_d = kb.din("mT", [D, TOK], BF16) if "mT_src" not in io else None
        wout_d = kb.din("wout", [D, D], F32)
        gw_d = kb.din("gw", [D, D], F32) if "mods" not in io else None
        gb_d = kb.din("gb", [128, D], F32) if "mods" not in io else None
    if do_norm:
        nw_d = kb.din("nw", [D, 2 * D], F32) if "mods" not in io else None
        nb_d = kb.din("nb", [128, 2 * D], F32) if "mods" not in io else None
        ng_d = kb.din("ng", [128, D], F32)
        hT_d = kb.dout("hT", [D, TOK], BF16) if "hT_dst" not in io else None
    if mode != "P0":
        xo_d = kb.dout("xo", [TOK, D], F32) if "xo_dst" not in io else None
    banks = kb.banks()
    mods = io.get("mods")
    if mods is None:
        crep = emit_cond_rep(kb, cT_d)
    stage = [kb.sb("stg%d" % i, [128, 2048 if mods is None else 1024], F32) for i in range(3)]
    ident = kb.sb("ident", [128, 128], BF16)
    S.dma("sp", ident[:], ident_d[:, :], writes=["ident"])
    if do_proj:
        gate = kb.sb("gate", [128, D], F32)
        if mods is None:
            emit_mod_bcast(kb, crep, gw_d, gb_d, gate, "gate", D, banks, stage, "g")
        else:
            load_mods(kb, gate, "gate", mods[0], io["layer"] * 3 * D + 2 * D, D)
        Wp = kb.sb("Wp", [128, 8, D], BF16)
        for k in range(8):
            st = stage[k % 3]
            skey = "g_st%d" % (k % 3)
            S.dma("sp", st[:, 0:D], wout_d[k * 128:(k + 1) * 128, :], writes=[skey])
            S.op("dve", lambda e, k=k, st=st: e.tensor_tensor(out=Wp[:, k, :], in0=st[:, 0:D], in1=gate[:], op=ALU.mult),
                 reads=[skey, "gate"], writes=["Wp"])
    if do_norm:
        modn = kb.sb("modn", [128, 2 * D], F32)
        if mods is None:
            emit_mod_bcast(kb, crep, nw_d, nb_d, modn, "modn", 2 * D, banks, stage, "g", b0=4)
        else:
            load_mods(kb, modn, "modn", mods[0], io["nlayer"] * 3 * D, 2 * D)
        Gb = kb.sb("Gb", [128, D], F32)
        S.dma("sp", Gb[:], ng_d[:, :], writes=["Gb"])
        S.op("dve", lambda e: e.scalar_tensor_tensor(out=Gb[:], in0=modn[:, D:2 * D], scalar=1.0, in1=Gb[:], op0=ALU.add, op1=ALU.mult),
             reads=["modn", "Gb"], writes=["Gb"])
    xt = [kb.sb("xt%d" % i, [128, D], F32) for i in range(TOK // 128)]
    if do_proj:
        mTt = [kb.sb("mTt%d" % i, [128, 8, 512], BF16) for i in range(2)]
    if do_norm:
        junk = kb.sb("junk", [128, D], F32)
        ss = [kb.sb("ss%d" % i, [128, 4], F32) for i in range(2)]
        tt = kb.sb("tt", [128, D], F32)
        hb = [kb.sb("hb%d" % i, [128, D], BF16) for i in range(2)]
        hst = [kb.sb("hst%d" % i, [128, 8, 512], BF16) for i in range(2)]
        epsb = kb.sb("epsb", [128, 1], F32)
        S.op("dve", lambda e: e.memset(epsb[:], EPS), writes=["epsb"])
    if do_proj:
        if "mT_src" in io:
            mT_src, mT_key = io["mT_src"], io["mT_key"]
        else:
            mT_v = mT_d.rearrange("(k p) t -> p k t", p=128)
            mT_src, mT_key = (lambda g: mT_v[:, :, g * 512:(g + 1) * 512]), "mT_ext"
    if do_norm:
        if "hT_dst" in io:
            hT_dst, hT_key = io["hT_dst"], io["hT_key"]
        else:
            hT_v = hT_d.rearrange("(k p) t -> p k t", p=128)
            hT_dst, hT_key = (lambda g: hT_v[:, :, g * 512:(g + 1) * 512]), "hT"
    if "x_src" in io:
        x_src, x_key = io["x_src"], io["x_key"]
    else:
        x_src, x_key = (lambda ti: x_d[ti * 128:(ti + 1) * 128, :]), "x_ext"
    if mode != "P0":
        if "xo_dst" in io:
            xo_dst, xo_key = io["xo_dst"], io["xo_key"]
        else:
            xo_dst, xo_key = (lambda ti: xo_d[ti * 128:(ti + 1) * 128, :]), "xo"
    NT = TOK // 128

    def part1(ti):
        g, j = ti // 4, ti % 4
        if do_proj:
            mt = mTt[g % 2]
            mkey = "mTt%d" % (g % 2)
            mk = (lambda gg: mT_key(gg)) if callable(mT_key) else (lambda gg: mT_key)
            if j == 0 and g == 0:
                S.dma("sp", mTt[0][:], mT_src(0), reads=[mk(0)], writes=["mTt0"])
            if j == 2 and g + 1 < TOK // 512:
                S.dma("sp", mTt[(g + 1) % 2][:], mT_src(g + 1), reads=[mk(g + 1)], writes=["mTt%d" % ((g + 1) % 2)])
        x_ = xt[ti]
        xkey = "xt%d" % ti
        if ti == 0:
            for t2 in range(NT):
                S.dma("sp", xt[t2][:], x_src(t2), reads=[x_key], writes=["xt%d" % t2])
        cur = x_
        curkey = xkey
        if do_proj:
            for half in range(2):
                bk = banks[half]
                for k in range(8):
                    S.op("pe", lambda e, k=k, half=half, bk=bk, mt=mt, j=j: e.matmul(bk[:, :], lhsT=mt[:, k, j * 128:(j + 1) * 128], rhs=Wp[:, k, half * 512:(half + 1) * 512], start=(k == 0), stop=(k == 7)),
                         reads=[mkey, "Wp"], writes=["bank%d" % half])
            x1 = x_
            x1key = xkey
            for half in range(2):
                S.op("dve", lambda e, half=half, x1=x1, x_=x_: e.tensor_tensor(out=x1[:, half * 512:(half + 1) * 512], in0=x_[:, half * 512:(half + 1) * 512], in1=banks[half][:, :], op=ALU.add),
                     reads=[xkey, "bank%d" % half], writes=[x1key])
            S.dma("sp", xo_dst(ti), x1[:], reads=[x1key], writes=[xo_key])
            cur = x1
            curkey = x1key
        if do_norm:
            s_ = ss[ti % 2]
            skey = "ss%d" % (ti % 2)
            S.op("act", lambda e, cur=cur, s_=s_: e.activation(out=junk[:], in_=cur[:], func=AF.Square, accum_out=s_[:, 0:1]),
                 reads=[curkey], writes=["junk", skey])
            S.op("act", lambda e, s_=s_: e.activation(out=s_[:, 1:2], in_=s_[:, 0:1], func=AF.Ln, scale=1.0 / D, bias=epsb[:, 0:1]),
                 reads=[skey, "epsb"], writes=[skey])
            S.op("act", lambda e, s_=s_: e.activation(out=s_[:, 2:3], in_=s_[:, 1:2], func=AF.Exp, scale=-0.5),
                 reads=[skey], writes=[skey])
            S.op("dve", lambda e, cur=cur, s_=s_: e.scalar_tensor_tensor(out=tt[:], in0=cur[:], scalar=s_[:, 2:3], in1=Gb[:], op0=ALU.mult, op1=ALU.mult),
                 reads=[curkey, skey, "Gb"], writes=["tt"])
            h_ = hb[ti % 2]
            hkey = "hb%d" % (ti % 2)
            S.op("dve", lambda e, h_=h_: e.tensor_tensor(out=h_[:], in0=tt[:], in1=modn[:, 0:D], op=ALU.add),
                 reads=["tt", "modn"], writes=[hkey])

    def part2(ti):
        if not do_norm:
            return
        g, j = ti // 4, ti % 4
        h_ = hb[ti % 2]
        hkey = "hb%d" % (ti % 2)
        pbank = banks[2 + (ti % 2)]
        pkey = "bank%d" % (2 + (ti % 2))
        pT = pbank[:, :].bitcast(BF16)
        for k in range(8):
            S.op("pe", lambda e, k=k, h_=h_, pT=pT: e.transpose(pT[:, k * 128:(k + 1) * 128], h_[:, k * 128:(k + 1) * 128], ident[:]),
                 reads=[hkey, "ident"], writes=[pkey])
        hs = hst[g % 2]
        hskey = "hst%d" % (g % 2)
        S.op("act", lambda e, hs=hs, pT=pT, j=j: e.activation(out=hs[:, :, j * 128:(j + 1) * 128], in_=pT.rearrange("p (k t) -> p k t", k=8), func=AF.Copy),
             reads=[pkey], writes=[hskey])
        if j == 3:
            S.dma("sp", hT_dst(g), hs[:], reads=[hskey], writes=[hT_key(g) if callable(hT_key) else hT_key])
            if "on_hT" in io:
                io["on_hT"](g)

    for ti in range(NT + 1):
        if ti < NT:
            part1(ti)
        if ti >= 1:
            part2(ti - 1)

def build_C(ngroups=16):
    kb = KB()
    for _ in emit_C(kb, {}, ngroups):
        pass
    return kb.finish(["oT"])


def emit_C(kb, io, ngroups=16):
    S = kb.S
    hT_d = kb.din("hT", [D, T], BF16) if "hT_src" not in io else None
    w_d = kb.din("w", [D, 1028], F32)
    cst_d = kb.din("cst", [128, 16], F32)
    cb_d = kb.din("cb", [128, 384], BF16)
    oT_d = kb.dout("oT", [256, T], BF16) if "oT_dst" not in io else None
    banks = kb.banks()
    if "hT_src" in io:
        hT_src, hT_key = io["hT_src"], io["hT_key"]
    else:
        hT_v = hT_d.rearrange("(k p) t -> p k t", p=128)
        hT_src, hT_key = (lambda G: hT_v[:, :, G * 512:(G + 1) * 512]), "hT_ext"
    if "oT_dst" in io:
        oT_dst, oT_key = io["oT_dst"], io["oT_key"]
    else:
        oT_dst, oT_key = (lambda p, G: oT_d[p * 128:(p + 1) * 128, G * 512:(G + 1) * 512]), "oT"

    cst = kb.sb("cst", [128, 16], F32)
    cb = kb.sb("cb", [128, 384], BF16)
    S.dma("sp", cst[:], cst_d[:, :], writes=["cst"])
    S.dma("sp", cb[:], cb_d[:, :], writes=["cb"])
    ident = cb[:, 0:128]
    bd64 = cb[:, 128:256]
    tri = cb[:, 256:384]
    negbf = kb.sb("negbf", [128, 1], F32)
    S.op("dve", lambda e: e.tensor_scalar(out=negbf[:], in0=cst[:, 2:3], scalar1=-1.0, scalar2=None, op0=ALU.mult), reads=["cst"], writes=["negbf"])
    eps64 = kb.sb("eps64", [128, 2], F32)
    S.op("dve", lambda e: e.memset(eps64[:, 0:1], 64.0 * EPS), writes=["eps64"])
    S.op("dve", lambda e: e.memset(eps64[:, 1:2], EPS), writes=["eps64"])
    ones512 = kb.sb("ones512", [128, 512], F32)
    S.op("dve", lambda e: e.memset(ones512[:], 1.0), writes=["ones512"])

    Wb = kb.sb("Wb", [128, 8, 1028], BF16)
    wst = [kb.sb("wst%d" % i, [128, 1028], F32) for i in range(2)]
    for k in range(8):
        st = wst[k % 2]
        S.dma("sp", st[:], w_d[k * 128:(k + 1) * 128, :], writes=["wst%d" % (k % 2)])
        S.op("dve", lambda e, k=k, st=st: e.tensor_copy(out=Wb[:, k, :], in_=st[:]), reads=["wst%d" % (k % 2)], writes=["Wb"])
    Wrep = kb.sb("Wrep", [128, 8, 128], BF16)
    for h in range(4):
        S.op("dve", lambda e, h=h: e.tensor_copy(out=Wrep[:, :, 32 * h:32 * h + 32], in_=Wb[:, :, 1024 + h:1025 + h].to_broadcast([128, 8, 32])),
             reads=["Wb"], writes=["Wrep"])

    yield
    hTg = [kb.sb("hTg%d" % i, [128, 8, 512], BF16) for i in range(2)]
    QT = [kb.sb("QT%d" % i, [128, T], BF16) for i in range(2)]
    KT = [kb.sb("KT%d" % i, [128, T], BF16) for i in range(2)]
    Vall = kb.sb("Vall", [128, 64, 192], BF16)
    V = [Vall[:, :, 0:128], Vall[:, :, 64:192]]
    Gt = kb.sb("Gt", [128, T], BF16)
    sqbs = {n: kb.sb("sqb" + n, [128, 512], BF16) for n in ("q", "k")}
    lnvs = {n: kb.sb("lnv" + n, [128, 512], F32) for n in ("q", "k")}
    rss = {n: kb.sb("rs" + n, [128, 512], F32) for n in ("q", "k")}
    ge = kb.sb("ge", [128, 512], F32)
    fe = kb.sb("fe", [128, 512], F32)
    cl = [kb.sb("cl%d" % i, [128, 512], F32) for i in range(2)]
    c1b = kb.sb("c1b", [128, 512], BF16)
    c2b = kb.sb("c2b", [128, 512], BF16)
    c3b = kb.sb("c3b", [128, 512], BF16)
    r1 = kb.sb("r1", [128, 512], F32)
    r2 = kb.sb("r2", [128, 512], F32)
    KA = kb.sb("KA", [128, 512], BF16)
    QA = kb.sb("QA", [128, 512], BF16)
    r1b = kb.sb("r1b", [128, 512], BF16)
    r2b = kb.sb("r2b", [128, 512], BF16)
    AUG = kb.sb("AUG", [128, T], BF16)
    Pt = [kb.sb("P%d" % i, [128, 1024], BF16) for i in range(3)]
    rec = kb.sb("rec", [128, 512], F32)
    tmp = kb.sb("tmp", [128, 512], F32)
    OT = [kb.sb("OT%d" % i, [128, 512], BF16) for i in range(3)]

    S.op("dve", lambda e: e.memset(Vall[:, :, 64:128], 1.0), writes=["V0ones", "V1ones"])

    bstate = {"i": 0}

    def nextbank():
        b = bstate["i"] % 8
        bstate["i"] += 1
        return b

    for p in range(2):
        for G in range(ngroups):
            t0 = G * 512
            hg = hTg[G % 2]
            hkey = "hTg%d" % (G % 2)
            S.dma("sp", hg[:], hT_src(G), reads=[hT_key(G) if callable(hT_key) else hT_key], writes=[hkey])
            pb = {}
            for name, blk in (("q", 0), ("k", 1), ("g", 3)):
                b = nextbank()
                pb[name] = b
                c0 = p * 512 + blk * 128
                for k in range(8):
                    S.op("pe", lambda e, k=k, b=b, c0=c0, hg=hg: e.matmul(banks[b][:, :], lhsT=Wb[:, k, c0:c0 + 128], rhs=hg[:, k, :], start=(k == 0), stop=(k == 7)),
                         reads=["Wb", hkey], writes=["bank%d" % b])
            bv = nextbank()
            bvk = "bank%d" % bv
            c0 = p * 512 + 256
            for j in range(4):
                for k in range(8):
                    S.op("pe", lambda e, k=k, j=j, bv=bv, c0=c0, hg=hg: e.matmul(banks[bv][:, j * 128:(j + 1) * 128], lhsT=hg[:, k, j * 128:(j + 1) * 128], rhs=Wb[:, k, c0:c0 + 128], start=(k == 0), stop=(k == 7)),
                         reads=["Wb", hkey], writes=[bvk])
            if p == 0:
                bf = nextbank()
                bfk = "bank%d" % bf
                for k in range(8):
                    S.op("pe", lambda e, k=k, bf=bf, hg=hg: e.matmul(banks[bf][:, :], lhsT=Wrep[:, k, :], rhs=hg[:, k, :], start=(k == 0), stop=(k == 7)),
                         reads=["Wrep", hkey], writes=[bfk])
            bm = {}
            for name in ("q", "k"):
                b = pb[name]
                S.op("act", lambda e, b=b, name=name: e.activation(out=sqbs[name][:], in_=banks[b][:, :], func=AF.Square), reads=["bank%d" % b], writes=["sqb" + name])
            for name in ("q", "k"):
                bm[name] = nextbank()
                S.op("pe", lambda e, name=name: e.matmul(banks[bm[name]][:, :], lhsT=bd64, rhs=sqbs[name][:], start=True, stop=True), reads=["cb", "sqb" + name], writes=["bank%d" % bm[name]])
            S.op("act", lambda e: e.activation(out=lnvs["q"][:], in_=banks[bm["q"]][:, :], func=AF.Ln, scale=64.0, bias=eps64[:, 0:1]), reads=["bank%d" % bm["q"], "eps64"], writes=["lnvq"])
            S.op("act", lambda e: e.activation(out=lnvs["k"][:], in_=banks[bm["k"]][:, :], func=AF.Ln, scale=1.0, bias=eps64[:, 1:2]), reads=["bank%d" % bm["k"], "eps64"], writes=["lnvk"])
            for name in ("q", "k"):
                S.op("act", lambda e, name=name: e.activation(out=rss[name][:], in_=lnvs[name][:], func=AF.Exp, scale=-0.5), reads=["lnv" + name], writes=["rs" + name])
            for name, blk, dst, dn in (("q", 0, QT, "QT"), ("k", 1, KT, "KT")):
                b = pb[name]
                for hl in range(2):
                    pr = slice(hl * 64, hl * 64 + 64)
                    S.op("dve", lambda e, hl=hl, pr=pr, b=b, dst=dst, blk=blk, name=name: e.scalar_tensor_tensor(out=dst[hl][0:64, t0:t0 + 512], in0=banks[b][pr, :], scalar=cst[pr, blk:blk + 1], in1=rss[name][pr, :], op0=ALU.mult, op1=ALU.mult),
                         reads=["bank%d" % b, "cst", "rs" + name], writes=[(dn, hl, G)])
            bg = pb["g"]
            bgk = "bank%d" % bg
            S.op("act", lambda e, bg=bg: e.activation(out=ge[:], in_=banks[bg][:, :], func=AF.Exp, scale=-1.0), reads=[bgk], writes=["ge"])
            S.op("act", lambda e: e.activation(out=ge[:], in_=ge[:], func=AF.Ln, scale=1.0, bias=ones512[:, 0:1]), reads=["ge", "ones512"], writes=["ge"])
            S.op("act", lambda e: e.activation(out=ge[:], in_=ge[:], func=AF.Exp, scale=-1.0), reads=["ge"], writes=["ge"])
            S.op("dve", lambda e, bg=bg: e.tensor_tensor(out=Gt[:, t0:t0 + 512], in0=banks[bg][:, :], in1=ge[:], op=ALU.mult), reads=[bgk, "ge"], writes=[("Gt", G)])
            bvv = banks[bv][:, :].rearrange("p (j c) -> p j c", j=4)
            S.op("dve", lambda e, bvv=bvv: e.tensor_copy(out=V[0][:, 4 * G:4 * G + 4, 0:64], in_=bvv[:, :, 0:64]), reads=[bvk], writes=[("V", 0, G)])
            S.op("dve", lambda e, bvv=bvv: e.tensor_copy(out=V[1][:, 4 * G:4 * G + 4, 64:128], in_=bvv[:, :, 64:128]), reads=[bvk], writes=[("V", 1, G)])
            if p == 0:
                S.op("act", lambda e, bf=bf: e.activation(out=fe[:], in_=banks[bf][:, :], func=AF.Exp, scale=-1.0, bias=negbf[:, 0:1]), reads=[bfk, "negbf"], writes=["fe"])
                S.op("act", lambda e: e.activation(out=fe[:], in_=fe[:], func=AF.Ln, scale=1.0, bias=ones512[:, 0:1]), reads=["fe", "ones512"], writes=["fe"])
                c_ = cl[G % 2]
                ck = "cl%d" % (G % 2)
                if G == 0:
                    S.op("dve", lambda e, c_=c_: e.tensor_tensor_scan(out=c_[:], data0=ones512[:], data1=fe[:], initial=0.0, op0=ALU.mult, op1=ALU.add),
                         reads=["fe", "ones512"], writes=[ck])
                else:
                    cp = cl[(G - 1) % 2]
                    S.op("dve", lambda e, c_=c_, cp=cp: e.tensor_tensor_scan(out=c_[:], data0=ones512[:], data1=fe[:], initial=cp[:, 511:512], op0=ALU.mult, op1=ALU.add),
                         reads=["fe", "ones512", "cl%d" % ((G - 1) % 2)], writes=[ck])
                S.op("dve", lambda e, c_=c_: e.tensor_copy(out=c1b[:], in_=c_[:]), reads=[ck], writes=["c1b"])
                S.op("dve", lambda e, c_=c_: e.tensor_tensor(out=r1[:], in0=c_[:], in1=c1b[:], op=ALU.subtract), reads=[ck, "c1b"], writes=["r1"])
                S.op("dve", lambda e: e.tensor_copy(out=c2b[:], in_=r1[:]), reads=["r1"], writes=["c2b"])
                S.op("dve", lambda e: e.tensor_tensor(out=r2[:], in0=r1[:], in1=c2b[:], op=ALU.subtract), reads=["r1", "c2b"], writes=["r2"])
                S.op("dve", lambda e: e.tensor_copy(out=c3b[:], in_=r2[:]), reads=["r2"], writes=["c3b"])
                for (dstt, dk, mc) in ((KA, "KA", 3), (QA, "QA", 7)):
                    S.op("dve", lambda e, mc=mc: e.tensor_scalar(out=r1b[:], in0=c1b[:], scalar1=cst[:, mc:mc + 1], scalar2=cst[:, mc + 3:mc + 4], op0=ALU.mult, op1=ALU.add),
                         reads=["c1b", "cst"], writes=["r1b"])
                    S.op("dve", lambda e, mc=mc: e.scalar_tensor_tensor(out=r2b[:], in0=c2b[:], scalar=cst[:, mc + 1:mc + 2], in1=r1b[:], op0=ALU.mult, op1=ALU.add),
                         reads=["c2b", "cst", "r1b"], writes=["r2b"])
                    S.op("dve", lambda e, mc=mc, dstt=dstt: e.scalar_tensor_tensor(out=dstt[:], in0=c3b[:], scalar=cst[:, mc + 2:mc + 3], in1=r2b[:], op0=ALU.mult, op1=ALU.add),
                         reads=["c3b", "cst", "r2b"], writes=[dk])
                for hl in range(2):
                    S.op("dve", lambda e, hl=hl: e.tensor_copy(out=KT[hl][64:70, t0:t0 + 512], in_=KA[32 * hl:32 * hl + 6, :]), reads=["KA"], writes=[("KTa", hl, G)])
                    S.op("dve", lambda e, hl=hl: e.tensor_copy(out=QT[hl][64:70, t0:t0 + 512], in_=QA[32 * hl:32 * hl + 6, :]), reads=["QA"], writes=[("QTa", hl, G)])
                    S.op("dve", lambda e, hl=hl: e.tensor_copy(out=AUG[32 * hl:32 * hl + 6, t0:t0 + 512], in_=KA[64 + 32 * hl:64 + 32 * hl + 6, :]), reads=["KA"], writes=[("AUG", G)])
                    S.op("dve", lambda e, hl=hl: e.tensor_copy(out=AUG[64 + 32 * hl:64 + 32 * hl + 6, t0:t0 + 512], in_=QA[64 + 32 * hl:64 + 32 * hl + 6, :]), reads=["QA"], writes=[("AUG", G)])
        if p == 1:
            TT_ = ngroups * 512
            allG = list(range(ngroups))
            for hl in range(2):
                S.op("dve", lambda e, hl=hl: e.tensor_copy(out=KT[hl][64:70, 0:TT_], in_=AUG[32 * hl:32 * hl + 6, 0:TT_]), reads=[("AUG", G) for G in allG], writes=[("KTa", hl, G) for G in allG])
                S.op("dve", lambda e, hl=hl: e.tensor_copy(out=QT[hl][64:70, 0:TT_], in_=AUG[64 + 32 * hl:64 + 32 * hl + 6, 0:TT_]), reads=[("AUG", G) for G in allG], writes=[("QTa", hl, G) for G in allG])
        units = []
        for G in range(ngroups):
            for hl in range(2):
                us = [(i, i + 1) for i in range(0, 4 * G, 2)] + [(i,) for i in range(4 * G, 4 * G + 4)]
                for n_, u in enumerate(us):
                    units.append((G, hl, u, n_ == len(us) - 1))

        def geom(G, i):
            q_lo = max(G * 512, i * 128)
            return q_lo, (G + 1) * 512 - q_lo

        def QK(ui):
            G, hl, u, _ = units[ui]
            d = 1 + (ui % 3)
            for hh, i in enumerate(u):
                q_lo, n = geom(G, i)
                bi = 2 * d + hh
                S.op("pe", lambda e, i=i, bi=bi, q_lo=q_lo, n=n, hl=hl: e.matmul(banks[bi][:, 0:n], lhsT=KT[hl][0:70, i * 128:(i + 1) * 128], rhs=QT[hl][0:70, q_lo:q_lo + n], start=True, stop=True),
                     reads=[("QT", hl, G), ("QTa", hl, G), ("KT", hl, i // 4), ("KTa", hl, i // 4)], writes=["bank%d" % bi])

        def EXPV(ui):
            G, hl, u, last = units[ui]
            nkb = 4 * G + 4
            ob = hl
            obk = "bank%d" % ob
            d = 1 + (ui % 3)
            P_ = Pt[ui % 3]
            pk = "P%d" % (ui % 3)
            if len(u) == 2:
                S.op("act", lambda e: e.activation(out=P_[:, 0:1024], in_=kb.dbanks[d][:, 0:1024], func=AF.Exp),
                     reads=["bank%d" % (2 * d), "bank%d" % (2 * d + 1)], writes=[pk])
            else:
                q_lo, n = geom(G, u[0])
                S.op("act", lambda e: e.activation(out=P_[:, 0:n], in_=banks[2 * d][:, 0:n], func=AF.Exp), reads=["bank%d" % (2 * d)], writes=[pk])
                S.op("dve", lambda e: e.tensor_tensor(out=P_[:, 0:128], in0=P_[:, 0:128], in1=tri, op=ALU.mult), reads=[pk, "cb"], writes=[pk])
            for hh, i in enumerate(u):
                q_lo, n = geom(G, i)
                c_lo = q_lo - G * 512
                S.op("pe", lambda e, i=i, hh=hh, n=n, c_lo=c_lo: e.matmul(banks[ob][:, c_lo:512], lhsT=V[hl][:, i, :], rhs=P_[:, hh * 512:hh * 512 + n], start=(i == 0), stop=(i == nkb - 1)),
                     reads=[pk, ("V", hl, i // 4), "V%dones" % hl], writes=[obk])
            if last:
                ot = OT[G % 3]
                otk = "OT%d" % (G % 3)
                num = slice(0, 64) if hl == 0 else slice(64, 128)
                den = slice(64, 128) if hl == 0 else slice(0, 64)
                S.op("dve", lambda e: e.reciprocal(out=rec[num, :], in_=banks[ob][den, :]), reads=[obk], writes=[("rec", hl)])
                S.op("dve", lambda e: e.tensor_tensor(out=tmp[num, :], in0=banks[ob][num, :], in1=rec[num, :], op=ALU.mult), reads=[obk, ("rec", hl)], writes=[("tmp", hl)])
                S.op("dve", lambda e: e.tensor_tensor(out=ot[num, :], in0=tmp[num, :], in1=Gt[num, G * 512:(G + 1) * 512], op=ALU.mult), reads=[("tmp", hl), ("Gt", G)], writes=[(otk, hl)])
                if hl == 1:
                    S.dma("pool", oT_dst(p, G), ot[:], reads=[(otk, 0), (otk, 1)], writes=[oT_key(G) if callable(oT_key) else oT_key])
                    S._record(S.lastw[oT_key(G) if callable(oT_key) else oT_key], [(otk, 0), (otk, 1)], [])
                    if "on_oT" in io and p == 1 and G % 4 == 3:
                        io["on_oT"](G // 4)

        for ui in range(len(units) + 2):
            if ui < len(units):
                QK(ui)
            if ui >= 2:
                EXPV(ui - 2)

def consts_C():
    ident = np.eye(128, dtype=np.float32)
    bd = np.zeros((128, 128), np.float32)
    bd[0:64, 0:64] = 1.0 / 64
    bd[64:128, 64:128] = 1.0 / 64
    tri = (np.arange(128)[None, :] >= np.arange(128)[:, None]).astype(np.float32)
    return np.ascontiguousarray(np.concatenate([ident, bd, tri], axis=1).astype(NPBF))


def prep_C(inp, core, hT):
    b, g = core // 4, core % 4
    w_in = inp["odd_w_in"][0]
    cols = []
    for p in range(2):
        hc = (4 * g + 2 * p) * 64
        for base in (0, 1024, 2048, 3072):
            cols.append(w_in[:, base + hc:base + hc + 128])
    cols.append(w_in[:, 4096 + 4 * g:4096 + 4 * g + 4])
    w = np.ascontiguousarray(np.concatenate(cols, axis=1))
    cst = np.zeros((128, 16), np.float32)
    cst[:, 0] = np.tile(inp["fox_qnorm_g"][0], 2)
    cst[:, 1] = np.tile(inp["fox_knorm_g"][0], 2)
    cst[:, 2] = np.repeat(inp["fox_b_f"][0][4 * g:4 * g + 4], 32)
    r = np.arange(128) % 32
    cst[:, 3] = (r == 3)
    cst[:, 4] = (r == 4)
    cst[:, 5] = (r == 5)
    cst[:, 6] = (r < 3)
    cst[:, 7] = -1.0 * (r == 0)
    cst[:, 8] = -1.0 * (r == 1)
    cst[:, 9] = -1.0 * (r == 2)
    cst[:, 10] = (r >= 3) & (r < 6)
    return dict(hT=hT, w=w, cst=cst, cb=consts_C())


def build_A(ngroups=16, stage=9):
    kb = KB()
    for _ in emit_A(kb, {}, ngroups, stage):
        pass
    return kb.finish(["mo"])


def emit_A(kb, io, ngroups=16, stage=9):
    S = kb.S
    hT_d = kb.din("hT", [D, T], BF16) if "hT_src" not in io else None
    w_d = kb.din("w", [D, 768], F32)
    cst_d = kb.din("cst", [128, 8], F32)
    pw_d = kb.din("pw", [128, 128], F32)
    cb_d = kb.din("cb", [128, 1152], BF16)
    rm_d = kb.din("rm", [128, 512], F32)
    mo_d = kb.dout("mo", [256, T], BF16) if "mo_dst" not in io else None
    banks = kb.banks()
    if "hT_src" in io:
        hT_src, hT_key = io["hT_src"], io["hT_key"]
    else:
        hT_v = hT_d.rearrange("(k p) t -> p k t", p=128)
        hT_src, hT_key = (lambda G: hT_v[:, :, G * 512:(G + 1) * 512]), "hT_ext"
    if "mo_dst" in io:
        mo_dst, mo_key = io["mo_dst"], io["mo_key"]
    else:
        mo_v = mo_d.rearrange("(r p) t -> p r t", p=128)
        mo_dst, mo_key = (lambda G: mo_v[:, :, G * 512:(G + 1) * 512]), "mo"

    cst = kb.sb("cst", [128, 8], F32)
    cb = kb.sb("cb", [128, 1152], BF16)
    rm = kb.sb("rm", [128, 512], F32)
    pwf = kb.sb("pwf", [128, 128], F32)
    pwb = kb.sb("pwb", [128, 128], BF16)
    S.dma("sp", cst[:], cst_d[:, :], writes=["cst"])
    S.dma("sp", cb[:], cb_d[:, :], writes=["cb"])
    S.dma("sp", rm[:], rm_d[:, :], writes=["rm"])
    S.dma("sp", pwf[:], pw_d[:, :], writes=["pwf"])
    S.op("dve", lambda e: e.tensor_copy(out=pwb[:], in_=pwf[:]), reads=["pwf"], writes=["pwb"])
    ident = cb[:, 0:128]
    o128 = cb[:, 128:256]
    trim = cb[:, 256:768]
    Bmain = cb[:, 768:896]
    Bprev = cb[:, 896:1024]
    B0 = cb[:, 1024:1152]
    lbt = kb.sb("lbt", [128, 8], F32)
    S.op("act", lambda e: e.activation(out=lbt[:, 0:3], in_=cst[:, 0:3], func=AF.Exp), reads=["cst"], writes=["lbt"])
    S.op("dve", lambda e: e.tensor_tensor(out=lbt[:, 3:4], in0=lbt[:, 0:1], in1=lbt[:, 1:2], op=ALU.add), reads=["lbt"], writes=["lbt"])
    S.op("dve", lambda e: e.tensor_tensor(out=lbt[:, 3:4], in0=lbt[:, 3:4], in1=lbt[:, 2:3], op=ALU.add), reads=["lbt"], writes=["lbt"])
    S.op("dve", lambda e: e.reciprocal(out=lbt[:, 3:4], in_=lbt[:, 3:4]), reads=["lbt"], writes=["lbt"])
    S.op("dve", lambda e: e.tensor_tensor(out=lbt[:, 4:5], in0=lbt[:, 0:1], in1=lbt[:, 3:4], op=ALU.mult), reads=["lbt"], writes=["lbt"])
    S.op("dve", lambda e: e.tensor_scalar(out=lbt[:, 5:6], in0=lbt[:, 4:5], scalar1=-1.0, scalar2=1.0, op0=ALU.mult, op1=ALU.add), reads=["lbt"], writes=["lbt"])
    S.op("dve", lambda e: e.tensor_scalar(out=lbt[:, 6:7], in0=lbt[:, 4:5], scalar1=-1.0, scalar2=None, op0=ALU.add), reads=["lbt"], writes=["lbt"])
    S.op("dve", lambda e: e.memset(lbt[:, 7:8], EPS), reads=[], writes=["lbt7"])
    lbc = lbt[:, 4:5]
    omlb = lbt[:, 5:6]
    nomlb = lbt[:, 6:7]
    epsc = lbt[:, 7:8]

    Wb = kb.sb("Wb", [128, 8, 768], BF16)
    wst = [kb.sb("wst%d" % i, [128, 768], F32) for i in range(2)]
    for k in range(8):
        st = wst[k % 2]
        S.dma("sp", st[:], w_d[k * 128:(k + 1) * 128, :], writes=["wst%d" % (k % 2)])
        S.op("dve", lambda e, k=k, st=st: e.tensor_copy(out=Wb[:, k, :], in_=st[:]), reads=["wst%d" % (k % 2)], writes=["Wb"])

    yield
    hTg = [kb.sb("hTg%d" % i, [128, 8, 512], BF16) for i in range(2)]
    t = {n: kb.sb(n, [128, 512], F32) for n in ["e", "L1", "L2", "logf", "cum", "key", "dq", "dl", "Eq", "Ek", "El", "lnm", "rstd", "ega", "egb", "t1"]}
    sga = [kb.sb("sga%d" % i, [128, 512], F32) for i in range(2)]
    sgb = [kb.sb("sgb%d" % i, [128, 512], F32) for i in range(2)]
    Qs = [kb.sb("Qs%d" % i, [128, 512], BF16) for i in range(2)]
    Ks = [kb.sb("Ks%d" % i, [128, 512], BF16) for i in range(2)]
    Kh = [kb.sb("Kh%d" % i, [128, 512], BF16) for i in range(2)]
    sq = kb.sb("sq", [128, 512], BF16)
    pooledT = kb.sb("pooledT", [128, 512], BF16)
    KhT = kb.sb("KhT", [128, 4, 2, 128], BF16)
    S.op("dve", lambda e: e.memset(KhT[:], 0.0), writes=["KhT"])
    Vt = [kb.sb("Vt%d" % i, [128, 4, 128], BF16) for i in range(2)]
    Ut = [kb.sb("Ut%d" % i, [128, 4, 128], BF16) for i in range(3)]
    At = kb.sb("At", [128, 512], BF16)
    er = [kb.sb("er%d" % i, [128, 16], F32) for i in range(2)]
    state = kb.sb("state", [128, 128], F32)
    stb = [kb.sb("stb%d" % i, [128, 128], BF16) for i in range(8)]
    Ost = [kb.sb("Ost%d" % i, [128, 2, 512], BF16) for i in range(4)]
    S.op("dve", lambda e: e.memset(state[:], 0.0), writes=["state"])

    bst = {"1": 0, "2": 0}

    def nb1():
        b = bst["1"] % 4
        bst["1"] += 1
        return b, "bank%d" % b

    def nb2():
        b = 4 + bst["2"] % 4
        bst["2"] += 1
        return b, "bank%d" % b

    def s1(G):
        pz = G % 2
        hg = hTg[pz]
        hkey = "hTg%d" % pz
        S.dma("sp", hg[:], hT_src(G), reads=[hT_key(G) if callable(hT_key) else hT_key], writes=[hkey])
        yield
        pe_ops = []
        f_ops = []

        def proj_fm(blk):
            b, bk = nb1()
            for k in range(8):
                pe_ops.append(("pe", lambda e, k=k, b=b: e.matmul(banks[b][:, :], lhsT=Wb[:, k, blk * 128:(blk + 1) * 128], rhs=hg[:, k, :], start=(k == 0), stop=(k == 7)),
                               ["Wb", hkey], [bk]))
            return b, bk

        bfm, bfk = proj_fm(1)
        bq, bqk = proj_fm(0)
        biu = []
        for half in range(2):
            b, bk = nb1()
            for jj in range(2):
                j = half * 2 + jj
                for k in range(8):
                    pe_ops.append(("pe", lambda e, k=k, j=j, jj=jj, b=b: e.matmul(banks[b][:, jj * 256:(jj + 1) * 256], lhsT=hg[:, k, j * 128:(j + 1) * 128], rhs=Wb[:, k, 512:768], start=(k == 0), stop=(k == 7)),
                                   ["Wb", hkey], [bk]))
            biu.append((b, bk))
        bga, bgak = proj_fm(2)
        bgb, bgbk = proj_fm(3)
        cum3 = t["cum"][:].rearrange("p (c s) -> p c s", c=8)
        er_ = er[pz]
        erk = "er%d" % pz
        f_ops += [
            ("act", lambda e: e.activation(out=t["e"][:], in_=banks[bfm][:, :], func=AF.Exp, scale=-1.0), [bfk], ["e"]),
            ("act", lambda e: e.activation(out=t["L1"][:], in_=t["e"][:], func=AF.Ln, scale=lbc, bias=rm[:, 1:2]), ["e", "lbt", "rm"], ["L1"]),
            ("act", lambda e: e.activation(out=t["L2"][:], in_=t["e"][:], func=AF.Ln, scale=1.0, bias=rm[:, 1:2]), ["e", "rm"], ["L2"]),
            ("dve", lambda e: e.tensor_tensor(out=t["logf"][:], in0=t["L1"][:], in1=t["L2"][:], op=ALU.subtract), ["L1", "L2"], ["logf"]),
            ("dve", lambda e: e.tensor_tensor_scan(out=t["cum"][:], data0=rm[:], data1=t["logf"][:], initial=0.0, op0=ALU.mult, op1=ALU.add), ["rm", "logf"], ["cum"]),
            ("act", lambda e: e.activation(out=t["e"][:], in_=t["L2"][:], func=AF.Exp, scale=-1.0), ["L2", "L1"], ["e"]),
            ("dve", lambda e: e.tensor_scalar(out=t["key"][:], in0=t["e"][:], scalar1=nomlb, scalar2=omlb, op0=ALU.mult, op1=ALU.add), ["e", "lbt"], ["key"]),
            ("dve", lambda e: e.tensor_tensor(out=t["dq"][:].rearrange("p (c s) -> p c s", c=8), in0=cum3, in1=cum3[:, :, 31:32].to_broadcast([128, 8, 64]), op=ALU.subtract), ["cum"], ["dq"]),
            ("dve", lambda e: e.tensor_tensor(out=t["dl"][:].rearrange("p (c s) -> p c s", c=8), in0=cum3[:, :, 63:64].to_broadcast([128, 8, 64]), in1=cum3, op=ALU.subtract), ["cum"], ["dl"]),
            ("act", lambda e: e.activation(out=t["Eq"][:], in_=t["dq"][:], func=AF.Exp), ["dq"], ["Eq"]),
            ("act", lambda e: e.activation(out=t["Ek"][:], in_=t["dq"][:], func=AF.Exp, scale=-1.0), ["dq"], ["Ek"]),
            ("act", lambda e: e.activation(out=t["El"][:], in_=t["dl"][:], func=AF.Exp), ["dl"], ["El"]),
            ("act", lambda e: e.activation(out=er_[:, 0:8], in_=cum3[:, :, 31], func=AF.Exp), ["cum"], [erk]),
            ("act", lambda e: e.activation(out=er_[:, 8:16], in_=cum3[:, :, 63], func=AF.Exp), ["cum"], [erk]),
            ("dve", lambda e: e.tensor_tensor(out=Qs[pz][:], in0=banks[bq][:, :], in1=t["Eq"][:], op=ALU.mult), [bqk, "Eq"], ["Qs%d" % pz]),
            ("dve", lambda e: e.tensor_tensor(out=Ks[pz][:], in0=t["key"][:], in1=t["Ek"][:], op=ALU.mult), ["key", "Ek"], ["Ks%d" % pz]),
            ("dve", lambda e: e.tensor_tensor(out=Kh[pz][:], in0=t["key"][:], in1=t["El"][:], op=ALU.mult), ["key", "El"], ["Kh%d" % pz]),
        ]
        Uc = Ut[G % 3]
        ukey = "Ut%d" % (G % 3)
        for half in range(2):
            b, bk = biu[half]
            v3 = banks[b][:, :].rearrange("p (j c) -> p j c", j=2)
            f_ops.append(("act", lambda e, v3=v3, half=half: e.activation(out=Vt[pz][:, 2 * half:2 * half + 2, :], in_=v3[:, :, 0:128], func=AF.Copy), [bk], ["Vt%d" % pz]))
            f_ops.append(("act", lambda e, v3=v3, half=half: e.activation(out=Uc[:, 2 * half:2 * half + 2, :], in_=v3[:, :, 128:256], func=AF.Copy), [bk], [ukey]))
        for (bg, bgk, eg, sg, sgk) in ((bga, bgak, "ega", sga[pz], "sga%d" % pz), (bgb, bgbk, "egb", sgb[pz], "sgb%d" % pz)):
            f_ops.append(("act", lambda e, bg=bg, eg=eg: e.activation(out=t[eg][:], in_=banks[bg][:, :], func=AF.Exp, scale=-1.0), [bgk], [eg]))
            f_ops.append(("act", lambda e, eg=eg: e.activation(out=t[eg][:], in_=t[eg][:], func=AF.Ln, scale=1.0, bias=rm[:, 1:2]), [eg, "rm"], [eg]))
            f_ops.append(("act", lambda e, eg=eg: e.activation(out=t[eg][:], in_=t[eg][:], func=AF.Exp, scale=-1.0), [eg], [eg]))
            f_ops.append(("dve", lambda e, bg=bg, eg=eg, sg=sg: e.tensor_tensor(out=sg[:], in0=banks[bg][:, :], in1=t[eg][:], op=ALU.mult), [bgk, eg], [sgk]))
        pi = 0
        for _ in range(8):
            en, fn, rd, wr = pe_ops[pi]
            S.op(en, fn, reads=rd, writes=wr)
            pi += 1
            yield
        for (en, fn, rd, wr) in f_ops:
            S.op(en, fn, reads=rd, writes=wr)
            yield
            for _ in range(3):
                if pi < len(pe_ops):
                    en2, fn2, rd2, wr2 = pe_ops[pi]
                    S.op(en2, fn2, reads=rd2, writes=wr2)
                    pi += 1
                    yield
        while pi < len(pe_ops):
            en2, fn2, rd2, wr2 = pe_ops[pi]
            S.op(en2, fn2, reads=rd2, writes=wr2)
            pi += 1
            yield

    def s2(G):
        pz = G % 2
        er_ = er[pz]
        erk = "er%d" % pz
        Qk, Kk, Khk, Vk = "Qs%d" % pz, "Ks%d" % pz, "Kh%d" % pz, "Vt%d" % pz
        bt, btk = nb2()
        ptv = banks[bt][:, :].bitcast(BF16)
        for j in range(4):
            S.op("pe", lambda e, j=j: e.transpose(ptv[:, j * 128:(j + 1) * 128], Kh[pz][:, j * 128:(j + 1) * 128], ident), reads=[Khk, "cb"], writes=[btk])
            yield
        ptv3 = ptv[:, 0:512].rearrange("p (j k) -> p j k", j=4)
        S.op("dve", lambda e: e.tensor_copy(out=KhT[0:64, :, 0, :], in_=ptv3[0:64, :, :]), reads=[btk], writes=["KhT"])
        yield
        S.op("dve", lambda e: e.tensor_copy(out=KhT[64:128, :, 1, :], in_=ptv3[64:128, :, :]), reads=[btk], writes=["KhT"])
        yield
        bs, bsk = nb2()
        for pair in range(4):
            S.op("pe", lambda e, pair=pair: e.matmul(banks[bs][:, pair * 128:(pair + 1) * 128], lhsT=Ks[pz][:, pair * 128:(pair + 1) * 128], rhs=Qs[pz][:, pair * 128:(pair + 1) * 128], start=True, stop=True),
                 reads=[Kk, Qk], writes=[bsk])
            yield
        S.op("dve", lambda e: e.tensor_tensor(out=At[:], in0=banks[bs][:, :], in1=trim, op=ALU.mult), reads=[bsk, "cb"], writes=["At"])
        yield
        bz = []
        for zh in range(2):
            b, bk = nb2()
            for cc in range(4):
                c = zh * 4 + cc
                j, half = c // 2, c % 2
                S.op("pe", lambda e, cc=cc, j=j, half=half, b=b: e.matmul(banks[b][:, cc * 128:(cc + 1) * 128], lhsT=KhT[:, j, half, :], rhs=Vt[pz][:, j, :], start=True, stop=True),
                     reads=["KhT", Vk], writes=[bk])
                yield
            bz.append((b, bk))
        bo, bok = nb2()
        for c in range(8):
            gc = 8 * G + c
            j, half = c // 2, c % 2
            sb_ = stb[gc % 8]
            sbk = "stb%d" % (gc % 8)
            S.op("dve", lambda e, c=c, sb_=sb_: e.tensor_scalar(out=sb_[:], in0=state[:], scalar1=er_[:, c:c + 1], scalar2=None, op0=ALU.mult), reads=["state", erk], writes=[sbk])
            if half == 0:
                S.op("pe", lambda e, j=j: e.matmul(banks[bo][:, j * 128:(j + 1) * 128], lhsT=Vt[pz][:, j, :], rhs=At[:, j * 128:(j + 1) * 128], start=True, stop=False),
                     reads=[Vk, "At"], writes=[bok])
            S.op("pe", lambda e, c=c, sb_=sb_, half=half: e.matmul(banks[bo][:, c * 64:(c + 1) * 64], lhsT=sb_[:], rhs=Qs[pz][:, c * 64:(c + 1) * 64], start=False, stop=(half == 1)),
                 reads=[sbk, Qk], writes=[bok])
            zb, zbk = bz[c // 4]
            S.op("dve", lambda e, c=c, zb=zb: e.scalar_tensor_tensor(out=state[:], in0=state[:], scalar=er_[:, 8 + c:9 + c], in1=banks[zb][:, (c % 4) * 128:(c % 4 + 1) * 128], op0=ALU.mult, op1=ALU.add),
                 reads=["state", erk, zbk], writes=["state"])
            yield
        if stage <= 3:
            return
        S.op("act", lambda e: e.activation(out=sq[:], in_=banks[bo][:, :], func=AF.Square), reads=[bok], writes=["sq"])
        yield
        bm, bmk = nb2()
        S.op("pe", lambda e: e.matmul(banks[bm][:, :], lhsT=o128, rhs=sq[:], start=True, stop=True), reads=["sq", "cb"], writes=[bmk])
        yield
        S.op("act", lambda e: e.activation(out=t["lnm"][:], in_=banks[bm][:, :], func=AF.Ln, scale=1.0, bias=epsc), reads=[bmk, "lbt7"], writes=["lnm"])
        yield
        S.op("act", lambda e: e.activation(out=t["rstd"][:], in_=t["lnm"][:], func=AF.Exp, scale=-0.5), reads=["lnm"], writes=["rstd"])
        yield
        S.op("dve", lambda e: e.scalar_tensor_tensor(out=t["t1"][:], in0=banks[bo][:, :], scalar=cst[:, 3:4], in1=t["rstd"][:], op0=ALU.mult, op1=ALU.mult), reads=[bok, "cst", "rstd"], writes=["t1"])
        yield
        os_ = Ost[G % 4]
        osk = "Ost%d" % (G % 4)
        S.op("dve", lambda e: e.tensor_tensor(out=os_[:, 0, :], in0=t["t1"][:], in1=sga[pz][:], op=ALU.mult), reads=["t1", "sga%d" % pz], writes=[osk])
        yield
        bp, bpk = nb2()
        Uc = Ut[G % 3]
        ukey = "Ut%d" % (G % 3)
        Up = Ut[(G - 1) % 3]
        upkey = "Ut%d" % ((G - 1) % 3)
        for j in range(4):
            tj = 4 * G + j
            if tj == 0:
                S.op("pe", lambda e, j=j: e.matmul(banks[bp][:, j * 128:(j + 1) * 128], lhsT=Uc[:, j, :], rhs=B0, start=True, stop=True), reads=[ukey, "cb"], writes=[bpk])
            else:
                S.op("pe", lambda e, j=j: e.matmul(banks[bp][:, j * 128:(j + 1) * 128], lhsT=Uc[:, j, :], rhs=Bmain, start=True, stop=False), reads=[ukey, "cb"], writes=[bpk])
                if j == 0:
                    S.op("pe", lambda e, j=j: e.matmul(banks[bp][:, j * 128:(j + 1) * 128], lhsT=Up[:, 3, :], rhs=Bprev, start=False, stop=True), reads=[upkey, "cb"], writes=[bpk])
                else:
                    S.op("pe", lambda e, j=j: e.matmul(banks[bp][:, j * 128:(j + 1) * 128], lhsT=Uc[:, j - 1, :], rhs=Bprev, start=False, stop=True), reads=[ukey, "cb"], writes=[bpk])
            yield
        S.op("act", lambda e: e.activation(out=pooledT[:], in_=banks[bp][:, :], func=AF.Copy), reads=[bpk], writes=["pooledT"])
        yield
        bob, bobk = nb2()
        S.op("pe", lambda e: e.matmul(banks[bob][:, :], lhsT=pwb[:], rhs=pooledT[:], start=True, stop=True), reads=["pwb", "pooledT"], writes=[bobk])
        yield
        S.op("dve", lambda e: e.scalar_tensor_tensor(out=os_[:, 1, :], in0=banks[bob][:, :], scalar=cst[:, 4:5], in1=sgb[pz][:], op0=ALU.mult, op1=ALU.mult), reads=[bobk, "cst", "sgb%d" % pz], writes=[osk])
        yield
        S.dma("pool", mo_dst(G), os_[:], reads=[osk], writes=[mo_key(G) if callable(mo_key) else mo_key])
        if "on_mo" in io and G % 4 == 3:
            io["on_mo"](G // 4)
        yield

    RA, RB = 1, 2

    def drain(*gens):
        act_ = [g for g in gens if g is not None]
        reps = [RA, RB]
        while act_:
            for gi, g in enumerate(list(act_)):
                for _ in range(reps[gi] if len(gens) > 1 and gi < 2 else 1):
                    try:
                        next(g)
                    except StopIteration:
                        if g in act_:
                            act_.remove(g)
                        break

    drain(s1(0))
    for G in range(ngroups):
        drain(s2(G), s1(G + 1) if G + 1 < ngroups else None)

WINDOWS = (2, 4, 8, 16)


def consts_A(g):
    w = WINDOWS[g]
    ident = np.eye(128, dtype=np.float32)
    o128 = np.full((128, 128), 1.0 / 128, np.float32)
    s = np.arange(128)[:, None]
    t = np.arange(128)[None, :]
    tri1 = ((s // 64 == t // 64) & (s % 64 <= t % 64)).astype(np.float32)
    trim = np.tile(tri1, (1, 4))
    t = np.arange(128)[None, :]
    Bmain = ((s <= t) & (s >= t - w + 1)).astype(np.float32) / w - (s == t)
    Bprev = ((s - 128) >= (t - w + 1)).astype(np.float32) / w
    cnt = np.minimum(t + 1, w).astype(np.float32)
    B0 = ((s <= t) & (s >= t - w + 1)).astype(np.float32) / cnt - (s == t)
    cb = np.concatenate([ident, o128, trim, Bmain, Bprev, B0], axis=1).astype(NPBF)
    rm = np.ones((128, 512), np.float32)
    rm[:, 0::64] = 0.0
    return np.ascontiguousarray(cb), rm


def prep_A(inp, core, hT):
    b, g = core // 4, core % 4
    w_in = inp["even_w_in"][0]
    sl = slice(g * 128, (g + 1) * 128)
    cols = [w_in[:, 0:512][:, sl], w_in[:, 512:1024][:, sl], w_in[:, 1536:2048][:, sl], w_in[:, 2560:3072][:, sl],
            w_in[:, 1024:1536][:, sl], w_in[:, 2048:2560][:, sl]]
    w = np.ascontiguousarray(np.concatenate(cols, axis=1))
    cst = np.zeros((128, 8), np.float32)
    cst[:, 0:3] = inp["hgrn_lb"][:, sl].T
    cst[:, 3] = inp["hgrn_onorm_g"][0][sl]
    cst[:, 4] = inp["pool_scale"][0][sl]
    cb, rm = consts_A(g)
    return dict(hT=hT, w=w, cst=cst, pw=np.ascontiguousarray(inp["pool_w"][0, g]), cb=cb, rm=rm)


_PROGS = {}
GROUPS = [[0, 1, 2, 3], [4, 5, 6, 7]]


def build_fused(upto=99):
    kb = KB()
    nc, S = kb.nc, kb.S
    dt_ = lambda n, sh, d: nc.dram_tensor(n, sh, d).ap()
    h0own = [dt_("i_h0own%d" % g, [D, 512], BF16) for g in range(4)]
    h0all = [dt_("i_h0all%d" % g, [4 * D, 512], BF16) for g in range(4)]
    moown = [dt_("i_moown%d" % q, [256, TOK], BF16) for q in range(4)]
    moall = dt_("i_moall", [4 * D, TOK], BF16)
    x1 = dt_("i_x1", [TOK, D], F32)
    h1own = [dt_("i_h1own%d" % g, [D, 512], BF16) for g in range(4)]
    h1all = [dt_("i_h1all%d" % g, [4 * D, 512], BF16) for g in range(4)]
    oown = [dt_("i_oown%d" % q, [256, TOK], BF16) for q in range(4)]
    oall = dt_("i_oall", [4 * D, TOK], BF16)
    out = nc.dram_tensor("out", [TOK, D], F32, kind="ExternalOutput").ap()
    jq = nc.sync.partition_id() % 4

    def own_dst(ts):
        return lambda g: ts[g].rearrange("(k p) t -> p k t", p=128)

    def all_src(ts):
        return lambda G: ts[G // 4].rearrange("(r k p) t -> p r k t", r=4, p=128)[:, G % 4, :, :]

    def dyn_src(t):
        v = t.rearrange("(q k p) t -> p (q k) t", q=4, p=128)
        return lambda g: v[:, g * 8:(g + 1) * 8, bass.ds(jq * 512, 512)]

    def gather_h(own, al, nm):
        return lambda g: S.collective("AllGather", [own[g][:, :]], [al[g][:, :]], GROUPS, reads=[nm + "own%d" % g], writes=[nm + "all%d" % g])

    def gather_q(own, al, nm):
        return lambda q: S.collective("AllGather", [own[q][:, :]], [al[q * D:(q + 1) * D, :]], GROUPS, reads=[nm + "own%d" % q], writes=[nm + "all%d" % q])

    midA = ExitStack()
    kb.tag, kb.pes = "A_", midA
    genA = emit_A(kb, dict(hT_src=all_src(h0all), hT_key=lambda G: "h0all%d" % (G // 4),
                           mo_dst=(lambda G: moown[G // 4].rearrange("(r p) t -> p r t", p=128)[:, :, (G % 4) * 512:(G % 4 + 1) * 512]),
                           mo_key=lambda G: "moown%d" % (G // 4), on_mo=gather_q(moown, moall, "mo")))
    next(genA)
    kb.tag, kb.pes = "", kb.es
    modown = dt_("i_modown", [128, MC], F32)
    modall = dt_("i_modall", [512, MC], F32)
    with kb.phase("M_"):
        emit_mods(kb, modown)
        S.collective("AllGather", [modown[:, :]], [modall[:, :]], GROUPS, reads=["modown"], writes=["modall"])
    MODS = (modall, "modall")
    if upto <= 0:
        return kb.finish(["modall"])
    with kb.phase("P0_"):
        emit_tok(kb, "P0", dict(mods=MODS, nlayer=0, hT_dst=own_dst(h0own), hT_key=lambda g: "h0own%d" % g, on_hT=gather_h(h0own, h0all, "h0")))
    if upto <= 1:
        return kb.finish(["h0own%d" % g for g in range(4)])
    if upto <= 2:
        return kb.finish(["h0all%d" % g for g in range(4)])
    with kb.phase("A_"):
        for _ in genA:
            pass
    midA.close()
    if upto <= 3:
        return kb.finish(["moown%d" % g for g in range(4)])
    if upto <= 4:
        return kb.finish(["moall%d" % q for q in range(4)])
    midC = ExitStack()
    kb.tag, kb.pes = "C_", midC
    genC = emit_C(kb, dict(hT_src=all_src(h1all), hT_key=lambda G: "h1all%d" % (G // 4),
                           oT_dst=(lambda p, G: oown[G // 4][p * 128:(p + 1) * 128, (G % 4) * 512:(G % 4 + 1) * 512]),
                           oT_key=lambda G: "oown%d" % (G // 4), on_oT=gather_q(oown, oall, "o")))
    next(genC)
    kb.tag, kb.pes = "", kb.es
    with kb.phase("B_"):
        emit_tok(kb, "PB", dict(mods=MODS, layer=0, nlayer=1, mT_src=dyn_src(moall), mT_key=(lambda g: "moall%d" % g), hT_dst=own_dst(h1own), hT_key=lambda g: "h1own%d" % g, on_hT=gather_h(h1own, h1all, "h1"),
                                xo_dst=(lambda ti: x1[ti * 128:(ti + 1) * 128, :]), xo_key="x1"))
    if upto <= 5:
        return kb.finish(["x1"] + ["h1own%d" % g for g in range(4)])
    with kb.phase("C_"):
        for _ in genC:
            pass
    midC.close()
    if upto <= 7:
        return kb.finish(["oown%d" % g for g in range(4)])
    with kb.phase("D_"):
        emit_tok(kb, "PD", dict(mods=MODS, layer=1, mT_src=dyn_src(oall), mT_key=(lambda g: "oall%d" % g), x_src=(lambda ti: x1[ti * 128:(ti + 1) * 128, :]), x_key="x1",
                                xo_dst=(lambda ti: out[ti * 128:(ti + 1) * 128, :]), xo_key="out"))
    return kb.finish(["out"])


def _bc(v, n=128):
    return np.ascontiguousarray(np.broadcast_to(np.asarray(v, np.float32), (n, v.shape[-1])))


def _inputs(x, c, norm_g, ada_w, ada_b, hgrn_lb, even_w_in, hgrn_onorm_g, pool_w, pool_scale,
            even_w_out, odd_w_in, fox_b_f, fox_qnorm_g, fox_knorm_g, odd_w_out):
    f = lambda a: np.asarray(a, np.float32)
    return dict(x=f(x), c=f(c), norm_g=f(norm_g), ada_w=f(ada_w), ada_b=f(ada_b), hgrn_lb=f(hgrn_lb), even_w_in=f(even_w_in),
                hgrn_onorm_g=f(hgrn_onorm_g), pool_w=f(pool_w), pool_scale=f(pool_scale), even_w_out=f(even_w_out),
                odd_w_in=f(odd_w_in), fox_b_f=f(fox_b_f), fox_qnorm_g=f(fox_qnorm_g), fox_knorm_g=f(fox_knorm_g), odd_w_out=f(odd_w_out))


def kernel(_upto=99, **kw):
    inp = _inputs(**kw)
    if "F" not in _PROGS:
        _PROGS["F"] = build_fused(_upto)
    ident = np.ascontiguousarray(np.eye(128, dtype=np.float32).astype(NPBF))
    perm = np.concatenate([np.concatenate([np.arange(g * 128, (g + 1) * 128), 512 + np.arange(g * 128, (g + 1) * 128)]) for g in range(4)])
    wout0 = np.ascontiguousarray(inp["even_w_out"][0][perm, :])
    wout1 = np.ascontiguousarray(inp["odd_w_out"][0])
    aw, ab = inp["ada_w"], inp["ada_b"]
    shared = {
        "P0_ng": _bc(inp["norm_g"][0]), "B_wout": wout0, "B_ng": _bc(inp["norm_g"][1]), "D_wout": wout1,
        "ident": ident,
    }
    maps = []
    Wall = np.concatenate([aw[0], aw[1]], axis=1)
    ball = np.concatenate([ab[0], ab[1]])
    for core in range(NCORES):
        b, j = core // 4, core % 4
        m = dict(shared)
        m["x"] = np.ascontiguousarray(inp["x"][b].reshape(16, 512, D)[j::4].reshape(TOK, D))
        m["cT"] = np.ascontiguousarray(inp["c"][b].reshape(8, 128).T)
        m["M_w"] = np.ascontiguousarray(Wall[:, j * MC:(j + 1) * MC])
        m["M_b"] = _bc(ball[j * MC:(j + 1) * MC])
        pa = prep_A(inp, core, None)
        pc = prep_C(inp, core, None)
        for k, v in pa.items():
            if k != "hT":
                m["A_" + k] = v
        for k, v in pc.items():
            if k != "hT":
                m["C_" + k] = v
        maps.append(m)
    res = run_bass_kernel_spmd(_PROGS["F"], maps, core_ids=list(range(NCORES)))
    r = res.results
    out = np.empty((2, 16, 512, D), np.float32)
    for b in range(2):
        for j in range(4):
            out[b, j::4] = np.asarray(r[b * 4 + j]["out"], np.float32).reshape(4, 512, D)
    return np.ascontiguousarray(out.reshape(2, T, D))
```

```python
import numpy as np
import ml_dtypes
from contextlib import ExitStack
import concourse.bass as bass
import concourse.mybir as mybir
from concourse.bass_utils import run_bass_kernel_spmd

F32 = mybir.dt.float32
BF16 = mybir.dt.bfloat16
AF = mybir.ActivationFunctionType
ALU = mybir.AluOpType
NPBF = ml_dtypes.bfloat16

T = 8192
D = 1024
TOK = 2048
EPS = 1e-6
NCORES = 8
CC_QOS = "P3"


class Sched:
    def __init__(self, nc, es, n_lanes=16):
        self.nc = nc
        self.engs = {"pe": nc.tensor, "act": nc.scalar, "dve": nc.vector, "pool": nc.gpsimd, "sp": nc.sync}
        self.sem = {}
        self.cnt = {}
        for n in ["pe", "act", "dve", "pool", "cc"]:
            self.sem[n] = es.enter_context(nc.semaphore("s_" + n))
            self.cnt[n] = 0
        self.lanes = []
        for i in range(n_lanes):
            nm = "lane%d" % i
            self.sem[nm] = es.enter_context(nc.semaphore("s_" + nm))
            self.cnt[nm] = 0
            self.lanes.append(nm)
        self.planes = []
        for i in range(4):
            nm = "plane%d" % i
            self.sem[nm] = es.enter_context(nc.semaphore("s_" + nm))
            self.cnt[nm] = 0
            self.planes.append(nm)
        self.lane_rr = 0
        self.plane_rr = 0
        self.waited = {n: {} for n in self.engs}
        self.lastw = {}
        self.readers = {}

    def _need(self, reads, writes):
        need = {}

        def add(ev):
            if ev is None:
                return
            s, v = ev
            if need.get(s, 0) < v:
                need[s] = v

        for k in reads:
            add(self.lastw.get(k))
        for k in writes:
            add(self.lastw.get(k))
            for ev in self.readers.get(k, []):
                add(ev)
        return need

    def _emit_waits(self, e, need):
        eng = self.engs[e]
        for s, v in need.items():
            if self.waited[e].get(s, 0) >= v:
                continue
            eng.wait_ge(self.sem[s], v)
            self.waited[e][s] = v

    def _record(self, ev, reads, writes):
        for k in reads:
            lst = self.readers.setdefault(k, [])
            lst.append(ev)
            if len(lst) > 64:
                mx = {}
                for s, v in lst:
                    mx[s] = max(mx.get(s, 0), v)
                self.readers[k] = list(mx.items())
        for k in writes:
            self.lastw[k] = ev
            self.readers[k] = []

    def op(self, e, fn, reads=(), writes=()):
        need = self._need(reads, writes)
        if e == "pe":
            need.pop("pe", None)
        self._emit_waits(e, need)
        ins = fn(self.engs[e])
        self.cnt[e] += 1
        ins.then_inc(self.sem[e], 1)
        self._record((e, self.cnt[e]), reads, writes)
        return ins

    def dma(self, q, out, in_, reads=(), writes=(), **kw):
        if q == "pool":
            lane = self.planes[self.plane_rr % len(self.planes)]
            self.plane_rr += 1
        else:
            lane = self.lanes[self.lane_rr % len(self.lanes)]
            self.lane_rr += 1
        need = self._need(reads, writes)
        if self.cnt[lane] > 0:
            need[lane] = max(need.get(lane, 0), self.cnt[lane])
        self._emit_waits(q, need)
        ins = self.engs[q].dma_start(out=out, in_=in_, **kw)
        self.cnt[lane] += 16
        ins.then_inc(self.sem[lane], 16)
        self._record((lane, self.cnt[lane]), reads, writes)
        return ins

    def barrier(self):
        for e in self.engs:
            need = {n: c for n, c in self.cnt.items() if c > 0 and n != "cc"}
            self._emit_waits(e, need)

    def collective(self, kind, ins, outs, groups, reads=(), writes=()):
        need = self._need(reads, writes)
        self._emit_waits("pool", need)
        ins_ = self.nc.gpsimd.collective_compute(kind, ALU.bypass, replica_groups=groups, ins=ins, outs=outs, dma_qos=CC_QOS)
        self.cnt["cc"] += 1
        ins_.then_inc(self.sem["cc"], 1)
        self._record(("cc", self.cnt["cc"]), reads, writes)
        return ins_

    def wait_all(self, e, keys):
        need = self._need(keys, keys)
        self._emit_waits(e, need)


class _Phase:
    def __init__(self, kb, tag):
        self.kb, self.tag = kb, tag

    def __enter__(self):
        self.kb.tag = self.tag
        self.kb.pes = ExitStack()
        return self

    def __exit__(self, *a):
        self.kb.S.barrier()
        self.kb.pes.close()
        self.kb.pes = self.kb.es
        self.kb.tag = ""
        return False


class BankView:
    def __init__(self, t, off):
        self.t, self.off = t, off

    def __getitem__(self, key):
        rows, cols = key
        a = self.off + (cols.start or 0)
        b = self.off + (512 if cols.stop is None else cols.stop)
        return self.t[rows, a:b]


class KB:
    def __init__(self):
        self.nc = bass.Bass("TRN2", target_bir_lowering=False)
        self.es = ExitStack()
        self.pes = self.es
        self.S = Sched(self.nc, self.es)
        self.outs = []
        self.tag = ""
        self.shared = {}
        self._banks = None

    def phase(self, tag):
        return _Phase(self, tag)

    def din(self, name, shape, dt, shared=False):
        if shared:
            if name not in self.shared:
                self.shared[name] = self.nc.dram_tensor(name, list(shape), dt, kind="ExternalInput").ap()
            return self.shared[name]
        return self.nc.dram_tensor(self.tag + name, list(shape), dt, kind="ExternalInput").ap()

    def dout(self, name, shape, dt):
        self.outs.append(self.tag + name)
        return self.nc.dram_tensor(self.tag + name, list(shape), dt, kind="ExternalOutput").ap()

    def sb(self, name, shape, dt):
        return self.pes.enter_context(self.nc.sbuf_tensor("sb_" + self.tag + name, list(shape), dt))

    def banks(self):
        if self._banks is None:
            self.dbanks = [self.es.enter_context(self.nc.psum_tensor("dbank%d" % i, [128, 1024], F32)) for i in range(4)]
            self._banks = [BankView(self.dbanks[i // 2], (i % 2) * 512) for i in range(8)]
        return self._banks

    def finish(self, out_keys=None):
        self.S.wait_all("sp", out_keys if out_keys is not None else self.outs)
        self.es.close()
        return self.nc


def emit_cond_rep(kb, cT_d):
    S = kb.S
    cT = kb.sb("cT", [128, 8], F32)
    cE = kb.sb("cE", [128, 8], F32)
    cond = kb.sb("cond", [128, 8], F32)
    ones = kb.sb("ones128", [128, 128], F32)
    crep = kb.sb("cond_rep", [128, 8, 128], F32)
    S.dma("sp", cT[:], cT_d[:, :], writes=["cT"])
    S.op("dve", lambda e: e.memset(ones[:], 1.0), writes=["ones128"])
    S.op("act", lambda e: e.activation(out=cE[:], in_=cT[:], func=AF.Exp, scale=-1.0), reads=["cT"], writes=["cE"])
    S.op("dve", lambda e: e.tensor_scalar_add(out=cE[:], in0=cE[:], scalar1=1.0), reads=["cE"], writes=["cE"])
    S.op("dve", lambda e: e.reciprocal(out=cE[:], in_=cE[:]), reads=["cE"], writes=["cE"])
    S.op("dve", lambda e: e.tensor_mul(out=cond[:], in0=cT[:], in1=cE[:]), reads=["cE", "cT"], writes=["cond"])
    for k in range(8):
        S.op("dve", lambda e, k=k: e.tensor_scalar(out=crep[:, k, :], in0=ones[:], scalar1=cond[:, k:k + 1], scalar2=None, op0=ALU.mult),
             reads=["cond", "ones128"], writes=["crep"])
    return crep


def emit_mod_bcast(kb, crep, w_d, b_d, out_t, out_key, ncols, banks, stage, tag, b0=0, c0=0):
    S = kb.S
    nb = ncols // 512
    S.dma("sp", out_t[:, 0:ncols], b_d[:, c0:c0 + ncols], writes=[out_key])
    idx = 0
    for k in range(8):
        st = stage[k % len(stage)]
        skey = "%s_st%d" % (tag, k % len(stage))
        S.dma("sp", st[:, 0:ncols], w_d[k * 128:(k + 1) * 128, c0:c0 + ncols], writes=[skey])
        for n in range(nb):
            S.op("pe", lambda e, n=n, k=k, st=st: e.matmul(banks[b0 + n][:, :], lhsT=crep[:, k, :], rhs=st[:, n * 512:(n + 1) * 512], start=(k == 0), stop=(k == 7)),
                 reads=["crep", skey], writes=["bank%d" % (b0 + n)])
    for n in range(nb):
        S.op("dve", lambda e, n=n: e.tensor_tensor(out=out_t[:, n * 512:(n + 1) * 512], in0=out_t[:, n * 512:(n + 1) * 512], in1=banks[b0 + n][:, :], op=ALU.add),
             reads=["bank%d" % (b0 + n), out_key], writes=[out_key])


MC = 1536


MCA, MCB = 512, 1024


def emit_mods(kb, modownA_d, modownB_d, on_A, on_B):
    S = kb.S
    cT_d = kb.din("cT", [128, 8], F32, shared=True)
    w_d = kb.din("w", [D, MC], F32)
    b_d = kb.din("b", [128, MC], F32)
    banks = kb.banks()
    crep = emit_cond_rep(kb, cT_d)
    stage = [kb.sb("stg%d" % i, [128, MCB], F32) for i in range(4)]
    outA = kb.sb("moA", [128, MCA], F32)
    outB = kb.sb("moB", [128, MCB], F32)
    emit_mod_bcast(kb, crep, w_d, b_d, outA, "moA", MCA, banks, stage, "g", b0=0, c0=0)
    S.dma("sp", modownA_d[:, :], outA[:], reads=["moA"], writes=["modownA"])
    on_A()
    emit_mod_bcast(kb, crep, w_d, b_d, outB, "moB", MCB, banks, stage, "g", b0=1, c0=MCA)
    S.dma("sp", modownB_d[:, :], outB[:], reads=["moB"], writes=["modownB"])
    on_B()


def load_mods(kb, dst, dst_key, mods, c0, n):
    S = kb.S
    c = c0
    while c < c0 + n:
        if c < 2 * D:
            ap_, key_, w_ = mods["A"][0], mods["A"][1], MCA
            cc = c
            lim = 2 * D
        else:
            ap_, key_, w_ = mods["B"][0], mods["B"][1], MCB
            cc = c - 2 * D
            lim = 6 * D
        r, lo = cc // w_, cc % w_
        m = min(w_ - lo, c0 + n - c, lim - c)
        S.dma("sp", dst[:, c - c0:c - c0 + m], ap_[r * 128:(r + 1) * 128, lo:lo + m], reads=[key_], writes=[dst_key])
        c += m


def build_tok(mode):
    kb = KB()
    emit_tok(kb, mode, {})
    return kb.finish(["hT", "xo"])


def emit_tok(kb, mode, io):
    S = kb.S
    do_proj = mode in ("PB", "PD")
    do_norm = mode in ("P0", "PB")
    x_d = kb.din("x", [TOK, D], F32, shared=True) if "x_src" not in io else None
    cT_d = kb.din("cT", [128, 8], F32, shared=True)
    ident_d = kb.din("ident", [128, 128], BF16, shared=True)
    if do_proj:
        mT_d = kb.din("mT", [D, TOK], BF16) if "mT_src" not in io else None
        wout_d = kb.din("wout", [D, D], F32)
        gw_d = kb.din("gw", [D, D], F32) if "mods" not in io else None
        gb_d = kb.din("gb", [128, D], F32) if "mods" not in io else None
    if do_norm:
        nw_d = kb.din("nw", [D, 2 * D], F32) if "mods" not in io else None
        nb_d = kb.din("nb", [128, 2 * D], F32) if "mods" not in io else None
        ng_d = kb.din("ng", [128, D], F32)
        hT_d = kb.dout("hT", [D, TOK], BF16) if "hT_dst" not in io else None
    if mode != "P0":
        xo_d = kb.dout("xo", [TOK, D], F32) if "xo_dst" not in io else None
    banks = kb.banks()
    mods = io.get("mods")
    if mods is None:
        crep = emit_cond_rep(kb, cT_d)
    stage = [kb.sb("stg%d" % i, [128, 2048 if mods is None else 1024], F32) for i in range(3)]
    ident = kb.sb("ident", [128, 128], BF16)
    S.dma("sp", ident[:], ident_d[:, :], writes=["ident"])
    if do_proj:
        gate = kb.sb("gate", [128, D], F32)
        if mods is None:
            emit_mod_bcast(kb, crep, gw_d, gb_d, gate, "gate", D, banks, stage, "g")
        else:
            load_mods(kb, gate, "gate", mods, io["layer"] * 3 * D + 2 * D, D)
        Wp = kb.sb("Wp", [128, 8, D], BF16)
        for k in range(8):
            st = stage[k % 3]
            skey = "g_st%d" % (k % 3)
            S.dma("sp", st[:, 0:D], wout_d[k * 128:(k + 1) * 128, :], writes=[skey])
            S.op("dve", lambda e, k=k, st=st: e.tensor_tensor(out=Wp[:, k, :], in0=st[:, 0:D], in1=gate[:], op=ALU.mult),
                 reads=[skey, "gate"], writes=["Wp"])
    if do_norm:
        modn = kb.sb("modn", [128, 2 * D], F32)
        if mods is None:
            emit_mod_bcast(kb, crep, nw_d, nb_d, modn, "modn", 2 * D, banks, stage, "g", b0=4)
        else:
            load_mods(kb, modn, "modn", mods, io["nlayer"] * 3 * D, 2 * D)
        Gb = kb.sb("Gb", [128, D], F32)
        S.dma("sp", Gb[:], ng_d[:, :], writes=["Gb"])
        S.op("dve", lambda e: e.scalar_tensor_tensor(out=Gb[:], in0=modn[:, D:2 * D], scalar=1.0, in1=Gb[:], op0=ALU.add, op1=ALU.mult),
             reads=["modn", "Gb"], writes=["Gb"])
    xt = [kb.sb("xt%d" % i, [128, D], F32) for i in range(TOK // 128)]
    if do_proj:
        mTt = [kb.sb("mTt%d" % i, [128, 8, 512], BF16) for i in range(2)]
    if do_norm:
        junk = kb.sb("junk", [128, D], F32)
        ss = [kb.sb("ss%d" % i, [128, 4], F32) for i in range(2)]
        tt = kb.sb("tt", [128, D], F32)
        hb = [kb.sb("hb%d" % i, [128, D], BF16) for i in range(2)]
        hst = [kb.sb("hst%d" % i, [128, 8, 512], BF16) for i in range(2)]
        epsb = kb.sb("epsb", [128, 1], F32)
        S.op("dve", lambda e: e.memset(epsb[:], EPS), writes=["epsb"])
    if do_proj:
        if "mT_src" in io:
            mT_src, mT_key = io["mT_src"], io["mT_key"]
        else:
            mT_v = mT_d.rearrange("(k p) t -> p k t", p=128)
            mT_src, mT_key = (lambda g: mT_v[:, :, g * 512:(g + 1) * 512]), "mT_ext"
    if do_norm:
        if "hT_dst" in io:
            hT_dst, hT_key = io["hT_dst"], io["hT_key"]
        else:
            hT_v = hT_d.rearrange("(k p) t -> p k t", p=128)
            hT_dst, hT_key = (lambda g: hT_v[:, :, g * 512:(g + 1) * 512]), "hT"
    if "x_src" in io:
        x_src, x_key = io["x_src"], io["x_key"]
    else:
        x_src, x_key = (lambda ti: x_d[ti * 128:(ti + 1) * 128, :]), "x_ext"
    if mode != "P0":
        if "xo_dst" in io:
            xo_dst, xo_key = io["xo_dst"], io["xo_key"]
        else:
            xo_dst, xo_key = (lambda ti: xo_d[ti * 128:(ti + 1) * 128, :]), "xo"
    NT = TOK // 128

    def part1(ti):
        g, j = ti // 4, ti % 4
        if do_proj:
            mt = mTt[g % 2]
            mkey = "mTt%d" % (g % 2)
            mk = (lambda gg: mT_key(gg)) if callable(mT_key) else (lambda gg: mT_key)
            if j == 0 and g == 0:
                S.dma("sp", mTt[0][:], mT_src(0), reads=[mk(0)], writes=["mTt0"])
            if j == 2 and g + 1 < TOK // 512:
                S.dma("sp", mTt[(g + 1) % 2][:], mT_src(g + 1), reads=[mk(g + 1)], writes=["mTt%d" % ((g + 1) % 2)])
        x_ = xt[ti]
        xkey = "xt%d" % ti
        if ti == 0:
            for t2 in range(NT):
                S.dma("sp", xt[t2][:], x_src(t2), reads=[x_key], writes=["xt%d" % t2])
        cur = x_
        curkey = xkey
        if do_proj:
            for half in range(2):
                bk = banks[half]
                for k in range(8):
                    S.op("pe", lambda e, k=k, half=half, bk=bk, mt=mt, j=j: e.matmul(bk[:, :], lhsT=mt[:, k, j * 128:(j + 1) * 128], rhs=Wp[:, k, half * 512:(half + 1) * 512], start=(k == 0), stop=(k == 7)),
                         reads=[mkey, "Wp"], writes=["bank%d" % half])
            x1 = x_
            x1key = xkey
            for half in range(2):
                S.op("dve", lambda e, half=half, x1=x1, x_=x_: e.tensor_tensor(out=x1[:, half * 512:(half + 1) * 512], in0=x_[:, half * 512:(half + 1) * 512], in1=banks[half][:, :], op=ALU.add),
                     reads=[xkey, "bank%d" % half], writes=[x1key])
            S.dma("sp", xo_dst(ti), x1[:], reads=[x1key], writes=[xo_key])
            cur = x1
            curkey = x1key
        if do_norm:
            s_ = ss[ti % 2]
            skey = "ss%d" % (ti % 2)
            S.op("act", lambda e, cur=cur, s_=s_: e.activation(out=junk[:], in_=cur[:], func=AF.Square, accum_out=s_[:, 0:1]),
                 reads=[curkey], writes=["junk", skey])
            S.op("act", lambda e, s_=s_: e.activation(out=s_[:, 1:2], in_=s_[:, 0:1], func=AF.Ln, scale=1.0 / D, bias=epsb[:, 0:1]),
                 reads=[skey, "epsb"], writes=[skey])
            S.op("act", lambda e, s_=s_: e.activation(out=s_[:, 2:3], in_=s_[:, 1:2], func=AF.Exp, scale=-0.5),
                 reads=[skey], writes=[skey])
            S.op("dve", lambda e, cur=cur, s_=s_: e.scalar_tensor_tensor(out=tt[:], in0=cur[:], scalar=s_[:, 2:3], in1=Gb[:], op0=ALU.mult, op1=ALU.mult),
                 reads=[curkey, skey, "Gb"], writes=["tt"])
            h_ = hb[ti % 2]
            hkey = "hb%d" % (ti % 2)
            S.op("dve", lambda e, h_=h_: e.tensor_tensor(out=h_[:], in0=tt[:], in1=modn[:, 0:D], op=ALU.add),
                 reads=["tt", "modn"], writes=[hkey])

    def part2(ti):
        if not do_norm:
            return
        g, j = ti // 4, ti % 4
        h_ = hb[ti % 2]
        hkey = "hb%d" % (ti % 2)
        pbank = banks[2 + (ti % 2)]
        pkey = "bank%d" % (2 + (ti % 2))
        pT = pbank[:, :].bitcast(BF16)
        for k in range(8):
            S.op("pe", lambda e, k=k, h_=h_, pT=pT: e.transpose(pT[:, k * 128:(k + 1) * 128], h_[:, k * 128:(k + 1) * 128], ident[:]),
                 reads=[hkey, "ident"], writes=[pkey])
        hs = hst[g % 2]
        hskey = "hst%d" % (g % 2)
        S.op("act", lambda e, hs=hs, pT=pT, j=j: e.activation(out=hs[:, :, j * 128:(j + 1) * 128], in_=pT.rearrange("p (k t) -> p k t", k=8), func=AF.Copy),
             reads=[pkey], writes=[hskey])
        if j == 3:
            S.dma("sp", hT_dst(g), hs[:], reads=[hskey], writes=[hT_key(g) if callable(hT_key) else hT_key])
            if "on_hT" in io:
                io["on_hT"](g)

    for ti in range(NT + 1):
        if ti < NT:
            part1(ti)
        if ti >= 1:
            part2(ti - 1)

def build_C(ngroups=16):
    kb = KB()
    for _ in emit_C(kb, {}, ngroups):
        pass
    return kb.finish(["oT"])


def emit_C(kb, io, ngroups=16):
    S = kb.S
    hT_d = kb.din("hT", [D, T], BF16) if "hT_src" not in io else None
    w_d = kb.din("w", [D, 1028], F32)
    cst_d = kb.din("cst", [128, 16], F32)
    cb_d = kb.din("cb", [128, 384], BF16)
    oT_d = kb.dout("oT", [256, T], BF16) if "oT_dst" not in io else None
    banks = kb.banks()
    if "hT_src" in io:
        hT_src, hT_key = io["hT_src"], io["hT_key"]
    else:
        hT_v = hT_d.rearrange("(k p) t -> p k t", p=128)
        hT_src, hT_key = (lambda G: hT_v[:, :, G * 512:(G + 1) * 512]), "hT_ext"
    if "oT_dst" in io:
        oT_dst, oT_key = io["oT_dst"], io["oT_key"]
    else:
        oT_dst, oT_key = (lambda p, G: oT_d[p * 128:(p + 1) * 128, G * 512:(G + 1) * 512]), "oT"

    cst = kb.sb("cst", [128, 16], F32)
    cb = kb.sb("cb", [128, 384], BF16)
    S.dma("sp", cst[:], cst_d[:, :], writes=["cst"])
    S.dma("sp", cb[:], cb_d[:, :], writes=["cb"])
    ident = cb[:, 0:128]
    bd64 = cb[:, 128:256]
    tri = cb[:, 256:384]
    negbf = kb.sb("negbf", [128, 1], F32)
    S.op("dve", lambda e: e.tensor_scalar(out=negbf[:], in0=cst[:, 2:3], scalar1=-1.0, scalar2=None, op0=ALU.mult), reads=["cst"], writes=["negbf"])
    eps64 = kb.sb("eps64", [128, 2], F32)
    S.op("dve", lambda e: e.memset(eps64[:, 0:1], 64.0 * EPS), writes=["eps64"])
    S.op("dve", lambda e: e.memset(eps64[:, 1:2], EPS), writes=["eps64"])
    ones512 = kb.sb("ones512", [128, 512], F32)
    S.op("dve", lambda e: e.memset(ones512[:], 1.0), writes=["ones512"])

    Wb = kb.sb("Wb", [128, 8, 1028], BF16)
    wst = [kb.sb("wst%d" % i, [128, 1028], F32) for i in range(2)]
    for k in range(8):
        st = wst[k % 2]
        S.dma("sp", st[:], w_d[k * 128:(k + 1) * 128, :], writes=["wst%d" % (k % 2)])
        S.op("dve", lambda e, k=k, st=st: e.tensor_copy(out=Wb[:, k, :], in_=st[:]), reads=["wst%d" % (k % 2)], writes=["Wb"])
    Wrep = kb.sb("Wrep", [128, 8, 128], BF16)
    for h in range(4):
        S.op("dve", lambda e, h=h: e.tensor_copy(out=Wrep[:, :, 32 * h:32 * h + 32], in_=Wb[:, :, 1024 + h:1025 + h].to_broadcast([128, 8, 32])),
             reads=["Wb"], writes=["Wrep"])

    yield
    hTg = [kb.sb("hTg%d" % i, [128, 8, 512], BF16) for i in range(2)]
    QT = [kb.sb("QT%d" % i, [128, T], BF16) for i in range(2)]
    KT = [kb.sb("KT%d" % i, [128, T], BF16) for i in range(2)]
    Vall = kb.sb("Vall", [128, 64, 192], BF16)
    V = [Vall[:, :, 0:128], Vall[:, :, 64:192]]
    Gt = kb.sb("Gt", [128, T], BF16)
    sqbs = {n: kb.sb("sqb" + n, [128, 512], BF16) for n in ("q", "k")}
    lnvs = {n: kb.sb("lnv" + n, [128, 512], F32) for n in ("q", "k")}
    rss = {n: kb.sb("rs" + n, [128, 512], F32) for n in ("q", "k")}
    ge = kb.sb("ge", [128, 512], F32)
    fe = kb.sb("fe", [128, 512], F32)
    cl = [kb.sb("cl%d" % i, [128, 512], F32) for i in range(2)]
    c1b = kb.sb("c1b", [128, 512], BF16)
    c2b = kb.sb("c2b", [128, 512], BF16)
    c3b = kb.sb("c3b", [128, 512], BF16)
    r1 = kb.sb("r1", [128, 512], F32)
    r2 = kb.sb("r2", [128, 512], F32)
    KA = kb.sb("KA", [128, 512], BF16)
    QA = kb.sb("QA", [128, 512], BF16)
    r1b = kb.sb("r1b", [128, 512], BF16)
    r2b = kb.sb("r2b", [128, 512], BF16)
    AUG = kb.sb("AUG", [128, T], BF16)
    Pt = [kb.sb("P%d" % i, [128, 1024], BF16) for i in range(3)]
    rec = kb.sb("rec", [128, 512], F32)
    tmp = kb.sb("tmp", [128, 512], F32)
    OT = [kb.sb("OT%d" % i, [128, 512], BF16) for i in range(3)]

    S.op("dve", lambda e: e.memset(Vall[:, :, 64:128], 1.0), writes=["V0ones", "V1ones"])

    bstate = {"i": 0}

    def nextbank():
        b = bstate["i"] % 8
        bstate["i"] += 1
        return b

    for p in range(2):
        for G in range(ngroups):
            t0 = G * 512
            hg = hTg[G % 2]
            hkey = "hTg%d" % (G % 2)
            S.dma("sp", hg[:], hT_src(G), reads=[hT_key(G) if callable(hT_key) else hT_key], writes=[hkey])
            pb = {}
            for name, blk in (("q", 0), ("k", 1), ("g", 3)):
                b = nextbank()
                pb[name] = b
                c0 = p * 512 + blk * 128
                for k in range(8):
                    S.op("pe", lambda e, k=k, b=b, c0=c0, hg=hg: e.matmul(banks[b][:, :], lhsT=Wb[:, k, c0:c0 + 128], rhs=hg[:, k, :], start=(k == 0), stop=(k == 7)),
                         reads=["Wb", hkey], writes=["bank%d" % b])
            bv = nextbank()
            bvk = "bank%d" % bv
            c0 = p * 512 + 256
            for j in range(4):
                for k in range(8):
                    S.op("pe", lambda e, k=k, j=j, bv=bv, c0=c0, hg=hg: e.matmul(banks[bv][:, j * 128:(j + 1) * 128], lhsT=hg[:, k, j * 128:(j + 1) * 128], rhs=Wb[:, k, c0:c0 + 128], start=(k == 0), stop=(k == 7)),
                         reads=["Wb", hkey], writes=[bvk])
            if p == 0:
                bf = nextbank()
                bfk = "bank%d" % bf
                for k in range(8):
                    S.op("pe", lambda e, k=k, bf=bf, hg=hg: e.matmul(banks[bf][:, :], lhsT=Wrep[:, k, :], rhs=hg[:, k, :], start=(k == 0), stop=(k == 7)),
                         reads=["Wrep", hkey], writes=[bfk])
            bm = {}
            for name in ("q", "k"):
                b = pb[name]
                S.op("act", lambda e, b=b, name=name: e.activation(out=sqbs[name][:], in_=banks[b][:, :], func=AF.Square), reads=["bank%d" % b], writes=["sqb" + name])
            for name in ("q", "k"):
                bm[name] = nextbank()
                S.op("pe", lambda e, name=name: e.matmul(banks[bm[name]][:, :], lhsT=bd64, rhs=sqbs[name][:], start=True, stop=True), reads=["cb", "sqb" + name], writes=["bank%d" % bm[name]])
            S.op("act", lambda e: e.activation(out=lnvs["q"][:], in_=banks[bm["q"]][:, :], func=AF.Ln, scale=64.0, bias=eps64[:, 0:1]), reads=["bank%d" % bm["q"], "eps64"], writes=["lnvq"])
            S.op("act", lambda e: e.activation(out=lnvs["k"][:], in_=banks[bm["k"]][:, :], func=AF.Ln, scale=1.0, bias=eps64[:, 1:2]), reads=["bank%d" % bm["k"], "eps64"], writes=["lnvk"])
            for name in ("q", "k"):
                S.op("act", lambda e, name=name: e.activation(out=rss[name][:], in_=lnvs[name][:], func=AF.Exp, scale=-0.5), reads=["lnv" + name], writes=["rs" + name])
            for name, blk, dst, dn in (("q", 0, QT, "QT"), ("k", 1, KT, "KT")):
                b = pb[name]
                for hl in range(2):
                    pr = slice(hl * 64, hl * 64 + 64)
                    S.op("dve", lambda e, hl=hl, pr=pr, b=b, dst=dst, blk=blk, name=name: e.scalar_tensor_tensor(out=dst[hl][0:64, t0:t0 + 512], in0=banks[b][pr, :], scalar=cst[pr, blk:blk + 1], in1=rss[name][pr, :], op0=ALU.mult, op1=ALU.mult),
                         reads=["bank%d" % b, "cst", "rs" + name], writes=[(dn, hl, G)])
            bg = pb["g"]
            bgk = "bank%d" % bg
            S.op("act", lambda e, bg=bg: e.activation(out=ge[:], in_=banks[bg][:, :], func=AF.Exp, scale=-1.0), reads=[bgk], writes=["ge"])
            S.op("act", lambda e: e.activation(out=ge[:], in_=ge[:], func=AF.Ln, scale=1.0, bias=ones512[:, 0:1]), reads=["ge", "ones512"], writes=["ge"])
            S.op("act", lambda e: e.activation(out=ge[:], in_=ge[:], func=AF.Exp, scale=-1.0), reads=["ge"], writes=["ge"])
            S.op("dve", lambda e, bg=bg: e.tensor_tensor(out=Gt[:, t0:t0 + 512], in0=banks[bg][:, :], in1=ge[:], op=ALU.mult), reads=[bgk, "ge"], writes=[("Gt", G)])
            bvv = banks[bv][:, :].rearrange("p (j c) -> p j c", j=4)
            S.op("dve", lambda e, bvv=bvv: e.tensor_copy(out=V[0][:, 4 * G:4 * G + 4, 0:64], in_=bvv[:, :, 0:64]), reads=[bvk], writes=[("V", 0, G)])
            S.op("dve", lambda e, bvv=bvv: e.tensor_copy(out=V[1][:, 4 * G:4 * G + 4, 64:128], in_=bvv[:, :, 64:128]), reads=[bvk], writes=[("V", 1, G)])
            if p == 0:
                S.op("act", lambda e, bf=bf: e.activation(out=fe[:], in_=banks[bf][:, :], func=AF.Exp, scale=-1.0, bias=negbf[:, 0:1]), reads=[bfk, "negbf"], writes=["fe"])
                S.op("act", lambda e: e.activation(out=fe[:], in_=fe[:], func=AF.Ln, scale=1.0, bias=ones512[:, 0:1]), reads=["fe", "ones512"], writes=["fe"])
                c_ = cl[G % 2]
                ck = "cl%d" % (G % 2)
                if G == 0:
                    S.op("dve", lambda e, c_=c_: e.tensor_tensor_scan(out=c_[:], data0=ones512[:], data1=fe[:], initial=0.0, op0=ALU.mult, op1=ALU.add),
                         reads=["fe", "ones512"], writes=[ck])
                else:
                    cp = cl[(G - 1) % 2]
                    S.op("dve", lambda e, c_=c_, cp=cp: e.tensor_tensor_scan(out=c_[:], data0=ones512[:], data1=fe[:], initial=cp[:, 511:512], op0=ALU.mult, op1=ALU.add),
                         reads=["fe", "ones512", "cl%d" % ((G - 1) % 2)], writes=[ck])
                S.op("dve", lambda e, c_=c_: e.tensor_copy(out=c1b[:], in_=c_[:]), reads=[ck], writes=["c1b"])
                S.op("dve", lambda e, c_=c_: e.tensor_tensor(out=r1[:], in0=c_[:], in1=c1b[:], op=ALU.subtract), reads=[ck, "c1b"], writes=["r1"])
                S.op("dve", lambda e: e.tensor_copy(out=c2b[:], in_=r1[:]), reads=["r1"], writes=["c2b"])
                S.op("dve", lambda e: e.tensor_tensor(out=r2[:], in0=r1[:], in1=c2b[:], op=ALU.subtract), reads=["r1", "c2b"], writes=["r2"])
                S.op("dve", lambda e: e.tensor_copy(out=c3b[:], in_=r2[:]), reads=["r2"], writes=["c3b"])
                for (dstt, dk, mc) in ((KA, "KA", 3), (QA, "QA", 7)):
                    S.op("dve", lambda e, mc=mc: e.tensor_scalar(out=r1b[:], in0=c1b[:], scalar1=cst[:, mc:mc + 1], scalar2=cst[:, mc + 3:mc + 4], op0=ALU.mult, op1=ALU.add),
                         reads=["c1b", "cst"], writes=["r1b"])
                    S.op("dve", lambda e, mc=mc: e.scalar_tensor_tensor(out=r2b[:], in0=c2b[:], scalar=cst[:, mc + 1:mc + 2], in1=r1b[:], op0=ALU.mult, op1=ALU.add),
                         reads=["c2b", "cst", "r1b"], writes=["r2b"])
                    S.op("dve", lambda e, mc=mc, dstt=dstt: e.scalar_tensor_tensor(out=dstt[:], in0=c3b[:], scalar=cst[:, mc + 2:mc + 3], in1=r2b[:], op0=ALU.mult, op1=ALU.add),
                         reads=["c3b", "cst", "r2b"], writes=[dk])
                for hl in range(2):
                    S.op("dve", lambda e, hl=hl: e.tensor_copy(out=KT[hl][64:70, t0:t0 + 512], in_=KA[32 * hl:32 * hl + 6, :]), reads=["KA"], writes=[("KTa", hl, G)])
                    S.op("dve", lambda e, hl=hl: e.tensor_copy(out=QT[hl][64:70, t0:t0 + 512], in_=QA[32 * hl:32 * hl + 6, :]), reads=["QA"], writes=[("QTa", hl, G)])
                    S.op("dve", lambda e, hl=hl: e.tensor_copy(out=AUG[32 * hl:32 * hl + 6, t0:t0 + 512], in_=KA[64 + 32 * hl:64 + 32 * hl + 6, :]), reads=["KA"], writes=[("AUG", G)])
                    S.op("dve", lambda e, hl=hl: e.tensor_copy(out=AUG[64 + 32 * hl:64 + 32 * hl + 6, t0:t0 + 512], in_=QA[64 + 32 * hl:64 + 32 * hl + 6, :]), reads=["QA"], writes=[("AUG", G)])
        if p == 1:
            TT_ = ngroups * 512
            allG = list(range(ngroups))
            for hl in range(2):
                S.op("dve", lambda e, hl=hl: e.tensor_copy(out=KT[hl][64:70, 0:TT_], in_=AUG[32 * hl:32 * hl + 6, 0:TT_]), reads=[("AUG", G) for G in allG], writes=[("KTa", hl, G) for G in allG])
                S.op("dve", lambda e, hl=hl: e.tensor_copy(out=QT[hl][64:70, 0:TT_], in_=AUG[64 + 32 * hl:64 + 32 * hl + 6, 0:TT_]), reads=[("AUG", G) for G in allG], writes=[("QTa", hl, G) for G in allG])
        units = []
        for G in range(ngroups):
            for hl in range(2):
                us = [(i, i + 1) for i in range(0, 4 * G, 2)] + [(i,) for i in range(4 * G, 4 * G + 4)]
                for n_, u in enumerate(us):
                    units.append((G, hl, u, n_ == len(us) - 1))

        def geom(G, i):
            q_lo = max(G * 512, i * 128)
            return q_lo, (G + 1) * 512 - q_lo

        def QK(ui):
            G, hl, u, _ = units[ui]
            d = 1 + (ui % 3)
            for hh, i in enumerate(u):
                q_lo, n = geom(G, i)
                bi = 2 * d + hh
                S.op("pe", lambda e, i=i, bi=bi, q_lo=q_lo, n=n, hl=hl: e.matmul(banks[bi][:, 0:n], lhsT=KT[hl][0:70, i * 128:(i + 1) * 128], rhs=QT[hl][0:70, q_lo:q_lo + n], start=True, stop=True),
                     reads=[("QT", hl, G), ("QTa", hl, G), ("KT", hl, i // 4), ("KTa", hl, i // 4)], writes=["bank%d" % bi])

        def EXPV(ui):
            G, hl, u, last = units[ui]
            nkb = 4 * G + 4
            ob = hl
            obk = "bank%d" % ob
            d = 1 + (ui % 3)
            P_ = Pt[ui % 3]
            pk = "P%d" % (ui % 3)
            if len(u) == 2:
                S.op("act", lambda e: e.activation(out=P_[:, 0:1024], in_=kb.dbanks[d][:, 0:1024], func=AF.Exp),
                     reads=["bank%d" % (2 * d), "bank%d" % (2 * d + 1)], writes=[pk])
            else:
                q_lo, n = geom(G, u[0])
                S.op("act", lambda e: e.activation(out=P_[:, 0:n], in_=banks[2 * d][:, 0:n], func=AF.Exp), reads=["bank%d" % (2 * d)], writes=[pk])
                S.op("dve", lambda e: e.tensor_tensor(out=P_[:, 0:128], in0=P_[:, 0:128], in1=tri, op=ALU.mult), reads=[pk, "cb"], writes=[pk])
            for hh, i in enumerate(u):
                q_lo, n = geom(G, i)
                c_lo = q_lo - G * 512
                S.op("pe", lambda e, i=i, hh=hh, n=n, c_lo=c_lo: e.matmul(banks[ob][:, c_lo:512], lhsT=V[hl][:, i, :], rhs=P_[:, hh * 512:hh * 512 + n], start=(i == 0), stop=(i == nkb - 1)),
                     reads=[pk, ("V", hl, i // 4), "V%dones" % hl], writes=[obk])
            if last:
                ot = OT[G % 3]
                otk = "OT%d" % (G % 3)
                num = slice(0, 64) if hl == 0 else slice(64, 128)
                den = slice(64, 128) if hl == 0 else slice(0, 64)
                S.op("dve", lambda e: e.reciprocal(out=rec[num, :], in_=banks[ob][den, :]), reads=[obk], writes=[("rec", hl)])
                S.op("dve", lambda e: e.tensor_tensor(out=tmp[num, :], in0=banks[ob][num, :], in1=rec[num, :], op=ALU.mult), reads=[obk, ("rec", hl)], writes=[("tmp", hl)])
                S.op("dve", lambda e: e.tensor_tensor(out=ot[num, :], in0=tmp[num, :], in1=Gt[num, G * 512:(G + 1) * 512], op=ALU.mult), reads=[("tmp", hl), ("Gt", G)], writes=[(otk, hl)])
                if hl == 1:
                    S.dma("pool", oT_dst(p, G), ot[:], reads=[(otk, 0), (otk, 1)], writes=[oT_key(G) if callable(oT_key) else oT_key])
                    S._record(S.lastw[oT_key(G) if callable(oT_key) else oT_key], [(otk, 0), (otk, 1)], [])
                    if "on_oT" in io and p == 1 and G % 4 == 3:
                        io["on_oT"](G // 4)

        for ui in range(len(units) + 2):
            if ui < len(units):
                QK(ui)
            if ui >= 2:
                EXPV(ui - 2)

def consts_C():
    ident = np.eye(128, dtype=np.float32)
    bd = np.zeros((128, 128), np.float32)
    bd[0:64, 0:64] = 1.0 / 64
    bd[64:128, 64:128] = 1.0 / 64
    tri = (np.arange(128)[None, :] >= np.arange(128)[:, None]).astype(np.float32)
    return np.ascontiguousarray(np.concatenate([ident, bd, tri], axis=1).astype(NPBF))


def prep_C(inp, core, hT):
    b, g = core // 4, core % 4
    w_in = inp["odd_w_in"][0]
    cols = []
    for p in range(2):
        hc = (4 * g + 2 * p) * 64
        for base in (0, 1024, 2048, 3072):
            cols.append(w_in[:, base + hc:base + hc + 128])
    cols.append(w_in[:, 4096 + 4 * g:4096 + 4 * g + 4])
    w = np.ascontiguousarray(np.concatenate(cols, axis=1))
    cst = np.zeros((128, 16), np.float32)
    cst[:, 0] = np.tile(inp["fox_qnorm_g"][0], 2)
    cst[:, 1] = np.tile(inp["fox_knorm_g"][0], 2)
    cst[:, 2] = np.repeat(inp["fox_b_f"][0][4 * g:4 * g + 4], 32)
    r = np.arange(128) % 32
    cst[:, 3] = (r == 3)
    cst[:, 4] = (r == 4)
    cst[:, 5] = (r == 5)
    cst[:, 6] = (r < 3)
    cst[:, 7] = -1.0 * (r == 0)
    cst[:, 8] = -1.0 * (r == 1)
    cst[:, 9] = -1.0 * (r == 2)
    cst[:, 10] = (r >= 3) & (r < 6)
    return dict(hT=hT, w=w, cst=cst, cb=consts_C())


def build_A(ngroups=16, stage=9):
    kb = KB()
    for _ in emit_A(kb, {}, ngroups, stage):
        pass
    return kb.finish(["mo"])


def emit_A(kb, io, ngroups=16, stage=9):
    S = kb.S
    hT_d = kb.din("hT", [D, T], BF16) if "hT_src" not in io else None
    w_d = kb.din("w", [D, 768], F32)
    cst_d = kb.din("cst", [128, 8], F32)
    pw_d = kb.din("pw", [128, 128], F32)
    cb_d = kb.din("cb", [128, 1152], BF16)
    rm_d = kb.din("rm", [128, 512], F32)
    mo_d = kb.dout("mo", [256, T], BF16) if "mo_dst" not in io else None
    banks = kb.banks()
    if "hT_src" in io:
        hT_src, hT_key = io["hT_src"], io["hT_key"]
    else:
        hT_v = hT_d.rearrange("(k p) t -> p k t", p=128)
        hT_src, hT_key = (lambda G: hT_v[:, :, G * 512:(G + 1) * 512]), "hT_ext"
    if "mo_dst" in io:
        mo_dst, mo_key = io["mo_dst"], io["mo_key"]
    else:
        mo_v = mo_d.rearrange("(r p) t -> p r t", p=128)
        mo_dst, mo_key = (lambda G: mo_v[:, :, G * 512:(G + 1) * 512]), "mo"

    cst = kb.sb("cst", [128, 8], F32)
    cb = kb.sb("cb", [128, 1152], BF16)
    rm = kb.sb("rm", [128, 512], F32)
    pwf = kb.sb("pwf", [128, 128], F32)
    pwb = kb.sb("pwb", [128, 128], BF16)
    S.dma("sp", cst[:], cst_d[:, :], writes=["cst"])
    S.dma("sp", cb[:], cb_d[:, :], writes=["cb"])
    S.dma("sp", rm[:], rm_d[:, :], writes=["rm"])
    S.dma("sp", pwf[:], pw_d[:, :], writes=["pwf"])
    S.op("dve", lambda e: e.tensor_copy(out=pwb[:], in_=pwf[:]), reads=["pwf"], writes=["pwb"])
    ident = cb[:, 0:128]
    o128 = cb[:, 128:256]
    trim = cb[:, 256:768]
    Bmain = cb[:, 768:896]
    Bprev = cb[:, 896:1024]
    B0 = cb[:, 1024:1152]
    lbt = kb.sb("lbt", [128, 8], F32)
    S.op("act", lambda e: e.activation(out=lbt[:, 0:3], in_=cst[:, 0:3], func=AF.Exp), reads=["cst"], writes=["lbt"])
    S.op("dve", lambda e: e.tensor_tensor(out=lbt[:, 3:4], in0=lbt[:, 0:1], in1=lbt[:, 1:2], op=ALU.add), reads=["lbt"], writes=["lbt"])
    S.op("dve", lambda e: e.tensor_tensor(out=lbt[:, 3:4], in0=lbt[:, 3:4], in1=lbt[:, 2:3], op=ALU.add), reads=["lbt"], writes=["lbt"])
    S.op("dve", lambda e: e.reciprocal(out=lbt[:, 3:4], in_=lbt[:, 3:4]), reads=["lbt"], writes=["lbt"])
    S.op("dve", lambda e: e.tensor_tensor(out=lbt[:, 4:5], in0=lbt[:, 0:1], in1=lbt[:, 3:4], op=ALU.mult), reads=["lbt"], writes=["lbt"])
    S.op("dve", lambda e: e.tensor_scalar(out=lbt[:, 5:6], in0=lbt[:, 4:5], scalar1=-1.0, scalar2=1.0, op0=ALU.mult, op1=ALU.add), reads=["lbt"], writes=["lbt"])
    S.op("dve", lambda e: e.tensor_scalar(out=lbt[:, 6:7], in0=lbt[:, 4:5], scalar1=-1.0, scalar2=None, op0=ALU.add), reads=["lbt"], writes=["lbt"])
    S.op("dve", lambda e: e.memset(lbt[:, 7:8], EPS), reads=[], writes=["lbt7"])
    lbc = lbt[:, 4:5]
    omlb = lbt[:, 5:6]
    nomlb = lbt[:, 6:7]
    epsc = lbt[:, 7:8]

    Wb = kb.sb("Wb", [128, 8, 768], BF16)
    wst = [kb.sb("wst%d" % i, [128, 768], F32) for i in range(2)]
    for k in range(8):
        st = wst[k % 2]
        S.dma("sp", st[:], w_d[k * 128:(k + 1) * 128, :], writes=["wst%d" % (k % 2)])
        S.op("dve", lambda e, k=k, st=st: e.tensor_copy(out=Wb[:, k, :], in_=st[:]), reads=["wst%d" % (k % 2)], writes=["Wb"])

    yield
    hTg = [kb.sb("hTg%d" % i, [128, 8, 512], BF16) for i in range(2)]
    t = {n: kb.sb(n, [128, 512], F32) for n in ["e", "L1", "L2", "logf", "cum", "key", "dq", "dl", "Eq", "Ek", "El", "lnm", "rstd", "ega", "egb", "t1"]}
    sga = [kb.sb("sga%d" % i, [128, 512], F32) for i in range(2)]
    sgb = [kb.sb("sgb%d" % i, [128, 512], F32) for i in range(2)]
    Qs = [kb.sb("Qs%d" % i, [128, 512], BF16) for i in range(2)]
    Ks = [kb.sb("Ks%d" % i, [128, 512], BF16) for i in range(2)]
    Kh = [kb.sb("Kh%d" % i, [128, 512], BF16) for i in range(2)]
    sq = kb.sb("sq", [128, 512], BF16)
    pooledT = kb.sb("pooledT", [128, 512], BF16)
    KhT = kb.sb("KhT", [128, 4, 2, 128], BF16)
    S.op("dve", lambda e: e.memset(KhT[:], 0.0), writes=["KhT"])
    Vt = [kb.sb("Vt%d" % i, [128, 4, 128], BF16) for i in range(2)]
    Ut = [kb.sb("Ut%d" % i, [128, 4, 128], BF16) for i in range(3)]
    At = kb.sb("At", [128, 512], BF16)
    er = [kb.sb("er%d" % i, [128, 16], F32) for i in range(2)]
    state = kb.sb("state", [128, 128], F32)
    stb = [kb.sb("stb%d" % i, [128, 128], BF16) for i in range(8)]
    Ost = [kb.sb("Ost%d" % i, [128, 2, 512], BF16) for i in range(4)]
    S.op("dve", lambda e: e.memset(state[:], 0.0), writes=["state"])

    bst = {"1": 0, "2": 0}

    def nb1():
        b = bst["1"] % 4
        bst["1"] += 1
        return b, "bank%d" % b

    def nb2():
        b = 4 + bst["2"] % 4
        bst["2"] += 1
        return b, "bank%d" % b

    def s1(G):
        pz = G % 2
        hg = hTg[pz]
        hkey = "hTg%d" % pz
        S.dma("sp", hg[:], hT_src(G), reads=[hT_key(G) if callable(hT_key) else hT_key], writes=[hkey])
        yield
        pe_ops = []
        f_ops = []

        def proj_fm(blk):
            b, bk = nb1()
            for k in range(8):
                pe_ops.append(("pe", lambda e, k=k, b=b: e.matmul(banks[b][:, :], lhsT=Wb[:, k, blk * 128:(blk + 1) * 128], rhs=hg[:, k, :], start=(k == 0), stop=(k == 7)),
                               ["Wb", hkey], [bk]))
            return b, bk

        bfm, bfk = proj_fm(1)
        bq, bqk = proj_fm(0)
        biu = []
        for half in range(2):
            b, bk = nb1()
            for jj in range(2):
                j = half * 2 + jj
                for k in range(8):
                    pe_ops.append(("pe", lambda e, k=k, j=j, jj=jj, b=b: e.matmul(banks[b][:, jj * 256:(jj + 1) * 256], lhsT=hg[:, k, j * 128:(j + 1) * 128], rhs=Wb[:, k, 512:768], start=(k == 0), stop=(k == 7)),
                                   ["Wb", hkey], [bk]))
            biu.append((b, bk))
        bga, bgak = proj_fm(2)
        bgb, bgbk = proj_fm(3)
        cum3 = t["cum"][:].rearrange("p (c s) -> p c s", c=8)
        er_ = er[pz]
        erk = "er%d" % pz
        f_ops += [
            ("act", lambda e: e.activation(out=t["e"][:], in_=banks[bfm][:, :], func=AF.Exp, scale=-1.0), [bfk], ["e"]),
            ("act", lambda e: e.activation(out=t["L1"][:], in_=t["e"][:], func=AF.Ln, scale=lbc, bias=rm[:, 1:2]), ["e", "lbt", "rm"], ["L1"]),
            ("act", lambda e: e.activation(out=t["L2"][:], in_=t["e"][:], func=AF.Ln, scale=1.0, bias=rm[:, 1:2]), ["e", "rm"], ["L2"]),
            ("dve", lambda e: e.tensor_tensor(out=t["logf"][:], in0=t["L1"][:], in1=t["L2"][:], op=ALU.subtract), ["L1", "L2"], ["logf"]),
            ("dve", lambda e: e.tensor_tensor_scan(out=t["cum"][:], data0=rm[:], data1=t["logf"][:], initial=0.0, op0=ALU.mult, op1=ALU.add), ["rm", "logf"], ["cum"]),
            ("act", lambda e: e.activation(out=t["e"][:], in_=t["L2"][:], func=AF.Exp, scale=-1.0), ["L2", "L1"], ["e"]),
            ("dve", lambda e: e.tensor_scalar(out=t["key"][:], in0=t["e"][:], scalar1=nomlb, scalar2=omlb, op0=ALU.mult, op1=ALU.add), ["e", "lbt"], ["key"]),
            ("dve", lambda e: e.tensor_tensor(out=t["dq"][:].rearrange("p (c s) -> p c s", c=8), in0=cum3, in1=cum3[:, :, 31:32].to_broadcast([128, 8, 64]), op=ALU.subtract), ["cum"], ["dq"]),
            ("dve", lambda e: e.tensor_tensor(out=t["dl"][:].rearrange("p (c s) -> p c s", c=8), in0=cum3[:, :, 63:64].to_broadcast([128, 8, 64]), in1=cum3, op=ALU.subtract), ["cum"], ["dl"]),
            ("act", lambda e: e.activation(out=t["Eq"][:], in_=t["dq"][:], func=AF.Exp), ["dq"], ["Eq"]),
            ("act", lambda e: e.activation(out=t["Ek"][:], in_=t["dq"][:], func=AF.Exp, scale=-1.0), ["dq"], ["Ek"]),
            ("act", lambda e: e.activation(out=t["El"][:], in_=t["dl"][:], func=AF.Exp), ["dl"], ["El"]),
            ("act", lambda e: e.activation(out=er_[:, 0:8], in_=cum3[:, :, 31], func=AF.Exp), ["cum"], [erk]),
            ("act", lambda e: e.activation(out=er_[:, 8:16], in_=cum3[:, :, 63], func=AF.Exp), ["cum"], [erk]),
            ("dve", lambda e: e.tensor_tensor(out=Qs[pz][:], in0=banks[bq][:, :], in1=t["Eq"][:], op=ALU.mult), [bqk, "Eq"], ["Qs%d" % pz]),
            ("dve", lambda e: e.tensor_tensor(out=Ks[pz][:], in0=t["key"][:], in1=t["Ek"][:], op=ALU.mult), ["key", "Ek"], ["Ks%d" % pz]),
            ("dve", lambda e: e.tensor_tensor(out=Kh[pz][:], in0=t["key"][:], in1=t["El"][:], op=ALU.mult), ["key", "El"], ["Kh%d" % pz]),
        ]
        Uc = Ut[G % 3]
        ukey = "Ut%d" % (G % 3)
        for half in range(2):
            b, bk = biu[half]
            v3 = banks[b][:, :].rearrange("p (j c) -> p j c", j=2)
            f_ops.append(("act", lambda e, v3=v3, half=half: e.activation(out=Vt[pz][:, 2 * half:2 * half + 2, :], in_=v3[:, :, 0:128], func=AF.Copy), [bk], ["Vt%d" % pz]))
            f_ops.append(("act", lambda e, v3=v3, half=half: e.activation(out=Uc[:, 2 * half:2 * half + 2, :], in_=v3[:, :, 128:256], func=AF.Copy), [bk], [ukey]))
        for (bg, bgk, eg, sg, sgk) in ((bga, bgak, "ega", sga[pz], "sga%d" % pz), (bgb, bgbk, "egb", sgb[pz], "sgb%d" % pz)):
            f_ops.append(("act", lambda e, bg=bg, eg=eg: e.activation(out=t[eg][:], in_=banks[bg][:, :], func=AF.Exp, scale=-1.0), [bgk], [eg]))
            f_ops.append(("act", lambda e, eg=eg: e.activation(out=t[eg][:], in_=t[eg][:], func=AF.Ln, scale=1.0, bias=rm[:, 1:2]), [eg, "rm"], [eg]))
            f_ops.append(("act", lambda e, eg=eg: e.activation(out=t[eg][:], in_=t[eg][:], func=AF.Exp, scale=-1.0), [eg], [eg]))
            f_ops.append(("dve", lambda e, bg=bg, eg=eg, sg=sg: e.tensor_tensor(out=sg[:], in0=banks[bg][:, :], in1=t[eg][:], op=ALU.mult), [bgk, eg], [sgk]))
        pi = 0
        for _ in range(8):
            en, fn, rd, wr = pe_ops[pi]
            S.op(en, fn, reads=rd, writes=wr)
            pi += 1
            yield
        for (en, fn, rd, wr) in f_ops:
            S.op(en, fn, reads=rd, writes=wr)
            yield
            for _ in range(3):
                if pi < len(pe_ops):
                    en2, fn2, rd2, wr2 = pe_ops[pi]
                    S.op(en2, fn2, reads=rd2, writes=wr2)
                    pi += 1
                    yield
        while pi < len(pe_ops):
            en2, fn2, rd2, wr2 = pe_ops[pi]
            S.op(en2, fn2, reads=rd2, writes=wr2)
            pi += 1
            yield

    def s2(G):
        pz = G % 2
        er_ = er[pz]
        erk = "er%d" % pz
        Qk, Kk, Khk, Vk = "Qs%d" % pz, "Ks%d" % pz, "Kh%d" % pz, "Vt%d" % pz
        bt, btk = nb2()
        ptv = banks[bt][:, :].bitcast(BF16)
        for j in range(4):
            S.op("pe", lambda e, j=j: e.transpose(ptv[:, j * 128:(j + 1) * 128], Kh[pz][:, j * 128:(j + 1) * 128], ident), reads=[Khk, "cb"], writes=[btk])
            yield
        ptv3 = ptv[:, 0:512].rearrange("p (j k) -> p j k", j=4)
        S.op("dve", lambda e: e.tensor_copy(out=KhT[0:64, :, 0, :], in_=ptv3[0:64, :, :]), reads=[btk], writes=["KhT"])
        yield
        S.op("dve", lambda e: e.tensor_copy(out=KhT[64:128, :, 1, :], in_=ptv3[64:128, :, :]), reads=[btk], writes=["KhT"])
        yield
        bs, bsk = nb2()
        for pair in range(4):
            S.op("pe", lambda e, pair=pair: e.matmul(banks[bs][:, pair * 128:(pair + 1) * 128], lhsT=Ks[pz][:, pair * 128:(pair + 1) * 128], rhs=Qs[pz][:, pair * 128:(pair + 1) * 128], start=True, stop=True),
                 reads=[Kk, Qk], writes=[bsk])
            yield
        S.op("dve", lambda e: e.tensor_tensor(out=At[:], in0=banks[bs][:, :], in1=trim, op=ALU.mult), reads=[bsk, "cb"], writes=["At"])
        yield
        bz = []
        for zh in range(2):
            b, bk = nb2()
            for cc in range(4):
                c = zh * 4 + cc
                j, half = c // 2, c % 2
                S.op("pe", lambda e, cc=cc, j=j, half=half, b=b: e.matmul(banks[b][:, cc * 128:(cc + 1) * 128], lhsT=KhT[:, j, half, :], rhs=Vt[pz][:, j, :], start=True, stop=True),
                     reads=["KhT", Vk], writes=[bk])
                yield
            bz.append((b, bk))
        bo, bok = nb2()
        for c in range(8):
            gc = 8 * G + c
            j, half = c // 2, c % 2
            sb_ = stb[gc % 8]
            sbk = "stb%d" % (gc % 8)
            S.op("dve", lambda e, c=c, sb_=sb_: e.tensor_scalar(out=sb_[:], in0=state[:], scalar1=er_[:, c:c + 1], scalar2=None, op0=ALU.mult), reads=["state", erk], writes=[sbk])
            if half == 0:
                S.op("pe", lambda e, j=j: e.matmul(banks[bo][:, j * 128:(j + 1) * 128], lhsT=Vt[pz][:, j, :], rhs=At[:, j * 128:(j + 1) * 128], start=True, stop=False),
                     reads=[Vk, "At"], writes=[bok])
            S.op("pe", lambda e, c=c, sb_=sb_, half=half: e.matmul(banks[bo][:, c * 64:(c + 1) * 64], lhsT=sb_[:], rhs=Qs[pz][:, c * 64:(c + 1) * 64], start=False, stop=(half == 1)),
                 reads=[sbk, Qk], writes=[bok])
            zb, zbk = bz[c // 4]
            S.op("dve", lambda e, c=c, zb=zb: e.scalar_tensor_tensor(out=state[:], in0=state[:], scalar=er_[:, 8 + c:9 + c], in1=banks[zb][:, (c % 4) * 128:(c % 4 + 1) * 128], op0=ALU.mult, op1=ALU.add),
                 reads=["state", erk, zbk], writes=["state"])
            yield
        if stage <= 3:
            return
        S.op("act", lambda e: e.activation(out=sq[:], in_=banks[bo][:, :], func=AF.Square), reads=[bok], writes=["sq"])
        yield
        bm, bmk = nb2()
        S.op("pe", lambda e: e.matmul(banks[bm][:, :], lhsT=o128, rhs=sq[:], start=True, stop=True), reads=["sq", "cb"], writes=[bmk])
        yield
        S.op("act", lambda e: e.activation(out=t["lnm"][:], in_=banks[bm][:, :], func=AF.Ln, scale=1.0, bias=epsc), reads=[bmk, "lbt7"], writes=["lnm"])
        yield
        S.op("act", lambda e: e.activation(out=t["rstd"][:], in_=t["lnm"][:], func=AF.Exp, scale=-0.5), reads=["lnm"], writes=["rstd"])
        yield
        S.op("dve", lambda e: e.scalar_tensor_tensor(out=t["t1"][:], in0=banks[bo][:, :], scalar=cst[:, 3:4], in1=t["rstd"][:], op0=ALU.mult, op1=ALU.mult), reads=[bok, "cst", "rstd"], writes=["t1"])
        yield
        os_ = Ost[G % 4]
        osk = "Ost%d" % (G % 4)
        S.op("dve", lambda e: e.tensor_tensor(out=os_[:, 0, :], in0=t["t1"][:], in1=sga[pz][:], op=ALU.mult), reads=["t1", "sga%d" % pz], writes=[osk])
        yield
        bp, bpk = nb2()
        Uc = Ut[G % 3]
        ukey = "Ut%d" % (G % 3)
        Up = Ut[(G - 1) % 3]
        upkey = "Ut%d" % ((G - 1) % 3)
        for j in range(4):
            tj = 4 * G + j
            if tj == 0:
                S.op("pe", lambda e, j=j: e.matmul(banks[bp][:, j * 128:(j + 1) * 128], lhsT=Uc[:, j, :], rhs=B0, start=True, stop=True), reads=[ukey, "cb"], writes=[bpk])
            else:
                S.op("pe", lambda e, j=j: e.matmul(banks[bp][:, j * 128:(j + 1) * 128], lhsT=Uc[:, j, :], rhs=Bmain, start=True, stop=False), reads=[ukey, "cb"], writes=[bpk])
                if j == 0:
                    S.op("pe", lambda e, j=j: e.matmul(banks[bp][:, j * 128:(j + 1) * 128], lhsT=Up[:, 3, :], rhs=Bprev, start=False, stop=True), reads=[upkey, "cb"], writes=[bpk])
                else:
                    S.op("pe", lambda e, j=j: e.matmul(banks[bp][:, j * 128:(j + 1) * 128], lhsT=Uc[:, j - 1, :], rhs=Bprev, start=False, stop=True), reads=[ukey, "cb"], writes=[bpk])
            yield
        S.op("act", lambda e: e.activation(out=pooledT[:], in_=banks[bp][:, :], func=AF.Copy), reads=[bpk], writes=["pooledT"])
        yield
        bob, bobk = nb2()
        S.op("pe", lambda e: e.matmul(banks[bob][:, :], lhsT=pwb[:], rhs=pooledT[:], start=True, stop=True), reads=["pwb", "pooledT"], writes=[bobk])
        yield
        S.op("dve", lambda e: e.scalar_tensor_tensor(out=os_[:, 1, :], in0=banks[bob][:, :], scalar=cst[:, 4:5], in1=sgb[pz][:], op0=ALU.mult, op1=ALU.mult), reads=[bobk, "cst", "sgb%d" % pz], writes=[osk])
        yield
        S.dma("pool", mo_dst(G), os_[:], reads=[osk], writes=[mo_key(G) if callable(mo_key) else mo_key])
        if "on_mo" in io and G % 4 == 3:
            io["on_mo"](G // 4)
        yield

    RA, RB = 1, 2

    def drain(*gens):
        act_ = [g for g in gens if g is not None]
        reps = [RA, RB]
        while act_:
            for gi, g in enumerate(list(act_)):
                for _ in range(reps[gi] if len(gens) > 1 and gi < 2 else 1):
                    try:
                        next(g)
                    except StopIteration:
                        if g in act_:
                            act_.remove(g)
                        break

    drain(s1(0))
    for G in range(ngroups):
        drain(s2(G), s1(G + 1) if G + 1 < ngroups else None)

WINDOWS = (2, 4, 8, 16)


def consts_A(g):
    w = WINDOWS[g]
    ident = np.eye(128, dtype=np.float32)
    o128 = np.full((128, 128), 1.0 / 128, np.float32)
    s = np.arange(128)[:, None]
    t = np.arange(128)[None, :]
    tri1 = ((s // 64 == t // 64) & (s % 64 <= t % 64)).astype(np.float32)
    trim = np.tile(tri1, (1, 4))
    t = np.arange(128)[None, :]
    Bmain = ((s <= t) & (s >= t - w + 1)).astype(np.float32) / w - (s == t)
    Bprev = ((s - 128) >= (t - w + 1)).astype(np.float32) / w
    cnt = np.minimum(t + 1, w).astype(np.float32)
    B0 = ((s <= t) & (s >= t - w + 1)).astype(np.float32) / cnt - (s == t)
    cb = np.concatenate([ident, o128, trim, Bmain, Bprev, B0], axis=1).astype(NPBF)
    rm = np.ones((128, 512), np.float32)
    rm[:, 0::64] = 0.0
    return np.ascontiguousarray(cb), rm


def prep_A(inp, core, hT):
    b, g = core // 4, core % 4
    w_in = inp["even_w_in"][0]
    sl = slice(g * 128, (g + 1) * 128)
    cols = [w_in[:, 0:512][:, sl], w_in[:, 512:1024][:, sl], w_in[:, 1536:2048][:, sl], w_in[:, 2560:3072][:, sl],
            w_in[:, 1024:1536][:, sl], w_in[:, 2048:2560][:, sl]]
    w = np.ascontiguousarray(np.concatenate(cols, axis=1))
    cst = np.zeros((128, 8), np.float32)
    cst[:, 0:3] = inp["hgrn_lb"][:, sl].T
    cst[:, 3] = inp["hgrn_onorm_g"][0][sl]
    cst[:, 4] = inp["pool_scale"][0][sl]
    cb, rm = consts_A(g)
    return dict(hT=hT, w=w, cst=cst, pw=np.ascontiguousarray(inp["pool_w"][0, g]), cb=cb, rm=rm)


_PROGS = {}
GROUPS = [[0, 1, 2, 3], [4, 5, 6, 7]]


def build_fused(upto=99):
    kb = KB()
    nc, S = kb.nc, kb.S
    dt_ = lambda n, sh, d: nc.dram_tensor(n, sh, d).ap()
    h0own = [dt_("i_h0own%d" % g, [D, 512], BF16) for g in range(4)]
    h0all = [dt_("i_h0all%d" % g, [4 * D, 512], BF16) for g in range(4)]
    moown = [dt_("i_moown%d" % q, [256, TOK], BF16) for q in range(4)]
    moall = dt_("i_moall", [4 * D, TOK], BF16)
    x1 = dt_("i_x1", [TOK, D], F32)
    h1own = [dt_("i_h1own%d" % g, [D, 512], BF16) for g in range(4)]
    h1all = [dt_("i_h1all%d" % g, [4 * D, 512], BF16) for g in range(4)]
    oown = [dt_("i_oown%d" % q, [256, TOK], BF16) for q in range(4)]
    oall = dt_("i_oall", [4 * D, TOK], BF16)
    out = nc.dram_tensor("out", [TOK, D], F32, kind="ExternalOutput").ap()
    jq = nc.sync.partition_id() % 4

    def own_dst(ts):
        return lambda g: ts[g].rearrange("(k p) t -> p k t", p=128)

    def all_src(ts):
        return lambda G: ts[G // 4].rearrange("(r k p) t -> p r k t", r=4, p=128)[:, G % 4, :, :]

    def dyn_src(t):
        v = t.rearrange("(q k p) t -> p (q k) t", q=4, p=128)
        return lambda g: v[:, g * 8:(g + 1) * 8, bass.ds(jq * 512, 512)]

    def gather_h(own, al, nm):
        return lambda g: S.collective("AllGather", [own[g][:, :]], [al[g][:, :]], GROUPS, reads=[nm + "own%d" % g], writes=[nm + "all%d" % g])

    def gather_q(own, al, nm):
        return lambda q: S.collective("AllGather", [own[q][:, :]], [al[q * D:(q + 1) * D, :]], GROUPS, reads=[nm + "own%d" % q], writes=[nm + "all%d" % q])

    modownA = dt_("i_modownA", [128, MCA], F32)
    modallA = dt_("i_modallA", [512, MCA], F32)
    modownB = dt_("i_modownB", [128, MCB], F32)
    modallB = dt_("i_modallB", [512, MCB], F32)
    with kb.phase("M_"):
        emit_mods(kb, modownA, modownB,
                  lambda: S.collective("AllGather", [modownA[:, :]], [modallA[:, :]], GROUPS, reads=["modownA"], writes=["modallA"]),
                  lambda: S.collective("AllGather", [modownB[:, :]], [modallB[:, :]], GROUPS, reads=["modownB"], writes=["modallB"]))
    MODS = {"A": (modallA, "modallA"), "B": (modallB, "modallB")}
    if upto <= 0:
        return kb.finish(["modallA", "modallB"])
    midA = ExitStack()
    kb.tag, kb.pes = "A_", midA
    genA = emit_A(kb, dict(hT_src=all_src(h0all), hT_key=lambda G: "h0all%d" % (G // 4),
                           mo_dst=(lambda G: moown[G // 4].rearrange("(r p) t -> p r t", p=128)[:, :, (G % 4) * 512:(G % 4 + 1) * 512]),
                           mo_key=lambda G: "moown%d" % (G // 4), on_mo=gather_q(moown, moall, "mo")))
    next(genA)
    kb.tag, kb.pes = "", kb.es
    with kb.phase("P0_"):
        emit_tok(kb, "P0", dict(mods=MODS, nlayer=0, hT_dst=own_dst(h0own), hT_key=lambda g: "h0own%d" % g, on_hT=gather_h(h0own, h0all, "h0")))
    if upto <= 1:
        return kb.finish(["h0own%d" % g for g in range(4)])
    if upto <= 2:
        return kb.finish(["h0all%d" % g for g in range(4)])
    with kb.phase("A_"):
        for _ in genA:
            pass
    midA.close()
    if upto <= 3:
        return kb.finish(["moown%d" % g for g in range(4)])
    if upto <= 4:
        return kb.finish(["moall%d" % q for q in range(4)])
    midC = ExitStack()
    kb.tag, kb.pes = "C_", midC
    genC = emit_C(kb, dict(hT_src=all_src(h1all), hT_key=lambda G: "h1all%d" % (G // 4),
                           oT_dst=(lambda p, G: oown[G // 4][p * 128:(p + 1) * 128, (G % 4) * 512:(G % 4 + 1) * 512]),
                           oT_key=lambda G: "oown%d" % (G // 4), on_oT=gather_q(oown, oall, "o")))
    next(genC)
    kb.tag, kb.pes = "", kb.es
    with kb.phase("B_"):
        emit_tok(kb, "PB", dict(mods=MODS, layer=0, nlayer=1, mT_src=dyn_src(moall), mT_key=(lambda g: "moall%d" % g), hT_dst=own_dst(h1own), hT_key=lambda g: "h1own%d" % g, on_hT=gather_h(h1own, h1all, "h1"),
                                xo_dst=(lambda ti: x1[ti * 128:(ti + 1) * 128, :]), xo_key="x1"))
    if upto <= 5:
        return kb.finish(["x1"] + ["h1own%d" % g for g in range(4)])
    with kb.phase("C_"):
        for _ in genC:
            pass
    midC.close()
    if upto <= 7:
        return kb.finish(["oown%d" % g for g in range(4)])
    with kb.phase("D_"):
        emit_tok(kb, "PD", dict(mods=MODS, layer=1, mT_src=dyn_src(oall), mT_key=(lambda g: "oall%d" % g), x_src=(lambda ti: x1[ti * 128:(ti + 1) * 128, :]), x_key="x1",
                                xo_dst=(lambda ti: out[ti * 128:(ti + 1) * 128, :]), xo_key="out"))
    return kb.finish(["out"])


def _bc(v, n=128):
    return np.ascontiguousarray(np.broadcast_to(np.asarray(v, np.float32), (n, v.shape[-1])))


def _inputs(x, c, norm_g, ada_w, ada_b, hgrn_lb, even_w_in, hgrn_onorm_g, pool_w, pool_scale,
            even_w_out, odd_w_in, fox_b_f, fox_qnorm_g, fox_knorm_g, odd_w_out):
    f = lambda a: np.asarray(a, np.float32)
    return dict(x=f(x), c=f(c), norm_g=f(norm_g), ada_w=f(ada_w), ada_b=f(ada_b), hgrn_lb=f(hgrn_lb), even_w_in=f(even_w_in),
                hgrn_onorm_g=f(hgrn_onorm_g), pool_w=f(pool_w), pool_scale=f(pool_scale), even_w_out=f(even_w_out),
                odd_w_in=f(odd_w_in), fox_b_f=f(fox_b_f), fox_qnorm_g=f(fox_qnorm_g), fox_knorm_g=f(fox_knorm_g), odd_w_out=f(odd_w_out))


def kernel(_upto=99, **kw):
    inp = _inputs(**kw)
    if "F" not in _PROGS:
        _PROGS["F"] = build_fused(_upto)
    ident = np.ascontiguousarray(np.eye(128, dtype=np.float32).astype(NPBF))
    perm = np.concatenate([np.concatenate([np.arange(g * 128, (g + 1) * 128), 512 + np.arange(g * 128, (g + 1) * 128)]) for g in range(4)])
    wout0 = np.ascontiguousarray(inp["even_w_out"][0][perm, :])
    wout1 = np.ascontiguousarray(inp["odd_w_out"][0])
    aw, ab = inp["ada_w"], inp["ada_b"]
    shared = {
        "P0_ng": _bc(inp["norm_g"][0]), "B_wout": wout0, "B_ng": _bc(inp["norm_g"][1]), "D_wout": wout1,
        "ident": ident,
    }
    maps = []
    Wall = np.concatenate([aw[0], aw[1]], axis=1)
    ball = np.concatenate([ab[0], ab[1]])
    for core in range(NCORES):
        b, j = core // 4, core % 4
        m = dict(shared)
        m["x"] = np.ascontiguousarray(inp["x"][b].reshape(16, 512, D)[j::4].reshape(TOK, D))
        m["cT"] = np.ascontiguousarray(inp["c"][b].reshape(8, 128).T)
        m["M_w"] = np.ascontiguousarray(np.concatenate([Wall[:, j * MCA:(j + 1) * MCA], Wall[:, 2 * D + j * MCB:2 * D + (j + 1) * MCB]], axis=1))
        m["M_b"] = _bc(np.concatenate([ball[j * MCA:(j + 1) * MCA], ball[2 * D + j * MCB:2 * D + (j + 1) * MCB]]))
        pa = prep_A(inp, core, None)
        pc = prep_C(inp, core, None)
        for k, v in pa.items():
            if k != "hT":
                m["A_" + k] = v
        for k, v in pc.items():
            if k != "hT":
                m["C_" + k] = v
        maps.append(m)
    res = run_bass_kernel_spmd(_PROGS["F"], maps, core_ids=list(range(NCORES)))
    r = res.results
    out = np.empty((2, 16, 512, D), np.float32)
    for b in range(2):
        for j in range(4):
            out[b, j::4] = np.asarray(r[b * 4 + j]["out"], np.float32).reshape(4, 512, D)
    return np.ascontiguousarray(out.reshape(2, T, D))
```
